# Optimizing a Trainium2 kernel written in Bass

```python
import jax, jax.numpy as jnp
from jax import lax
import numpy as np

D_MODEL = 1024
BATCH = 4
SEQ = 4096
DEPTH = 2

PLE_DIM = 256
D_FF = 2816
LN_EPS = 1e-5
RMS_EPS = 1e-6
DEEPNORM_ALPHA = (2 * DEPTH) ** 0.25
DEEPNORM_BETA = (8 * DEPTH) ** -0.25

GDN_HEADS = 8
GDN_DK = 64
GDN_DV = 64
GDN_CONV = 4
GDN_CHUNK = 64

SB_HEADS = 4
SB_DIM = 64

MLA_HEADS = 4
MLA_NOPE = 64
MLA_ROPE = 32
MLA_V = 64
MLA_Q_RANK = 256
MLA_KV_RANK = 128
ROPE_BASE = 10000.0

Q_BLOCK = 128

MIX_WIDTH = GDN_HEADS * GDN_DV + SB_HEADS * SB_DIM + MLA_HEADS * MLA_V
IN_WIDTHS = (
    GDN_HEADS * GDN_DK, GDN_HEADS * GDN_DK, GDN_HEADS * GDN_DV,
    GDN_HEADS * GDN_DV,
    GDN_HEADS, GDN_HEADS,
    SB_HEADS * SB_DIM, SB_HEADS * SB_DIM, SB_HEADS * SB_DIM,
    MLA_Q_RANK,
    MLA_KV_RANK + MLA_ROPE,
)
IN_TOTAL = int(sum(IN_WIDTHS))
IN_SPLITS = tuple(int(s) for s in np.cumsum(IN_WIDTHS)[:-1])
GDN_CONV_CH = 2 * GDN_HEADS * GDN_DK + GDN_HEADS * GDN_DV

kernel_name = "hybrid_gdn_stickbreak_mla_macaron_deepnorm"


def layer_norm(x, g, b):
    xf = x.astype(jnp.float32)
    mu = jnp.mean(xf, axis=-1, keepdims=True)
    var = jnp.mean(jnp.square(xf - mu), axis=-1, keepdims=True)
    y = (xf - mu) * lax.rsqrt(var + LN_EPS)
    return (y * g.astype(jnp.float32) + b.astype(jnp.float32)).astype(x.dtype)


def rms_norm(x, w):
    xf = x.astype(jnp.float32)
    y = xf * lax.rsqrt(jnp.mean(jnp.square(xf), axis=-1, keepdims=True) + RMS_EPS)
    return (y * w.astype(jnp.float32)).astype(x.dtype)


def l2_normalize(x):
    xf = x.astype(jnp.float32)
    return xf * lax.rsqrt(jnp.sum(jnp.square(xf), axis=-1, keepdims=True) + RMS_EPS)


def swiglu(h, w_in, w_out):
    gate, up = jnp.split(h @ w_in, 2, axis=-1)
    return (jax.nn.silu(gate) * up) @ w_out


def causal_depthwise_conv(x, w):
    k_width, ch = w.shape
    return lax.conv_general_dilated(
        x, w[:, None, :].astype(x.dtype), window_strides=(1,), padding=[(k_width - 1, 0)],
        dimension_numbers=("NWC", "WIO", "NWC"), feature_group_count=ch)


def rope_tables(positions):
    inv = 1.0 / (ROPE_BASE ** (jnp.arange(0, MLA_ROPE, 2, dtype=jnp.float32) / MLA_ROPE))
    ang = positions.astype(jnp.float32)[..., None] * inv
    return jnp.cos(ang), jnp.sin(ang)


def apply_rope(x, cos, sin):
    x1, x2 = jnp.split(x.astype(jnp.float32), 2, axis=-1)
    return jnp.concatenate([x1 * cos - x2 * sin, x2 * cos + x1 * sin], axis=-1).astype(x.dtype)


def chunk_gated_delta_rule(q, k, v, g, beta):
    B, S, H, DK = q.shape
    DV = v.shape[-1]
    C = GDN_CHUNK
    N = S // C
    f32 = jnp.float32

    def chunks(t):
        t = t.astype(f32).reshape((B, N, C) + t.shape[2:])
        return jnp.moveaxis(jnp.moveaxis(t, 3, 2), 1, 0)

    qc, kc, vc, gc, bc = (chunks(t) for t in (q, k, v, g, beta))
    gc = jnp.cumsum(gc, axis=-1)
    incl = jnp.tril(jnp.ones((C, C), dtype=bool))
    strict = jnp.tril(jnp.ones((C, C), dtype=bool), -1)
    diff = gc[..., :, None] - gc[..., None, :]
    decay = jnp.where(incl, jnp.exp(jnp.where(incl, diff, 0.0)), 0.0)
    kb = kc * bc[..., None]
    lhs = jnp.where(strict, jnp.einsum('nbhid,nbhjd->nbhij', kb, kc) * decay, 0.0) + jnp.eye(C, dtype=f32)
    u = lax.linalg.triangular_solve(lhs, vc * bc[..., None], left_side=True, lower=True, unit_diagonal=True)
    w = lax.linalg.triangular_solve(lhs, kb * jnp.exp(gc)[..., None], left_side=True, lower=True, unit_diagonal=True)
    qk = jnp.where(incl, jnp.einsum('nbhid,nbhjd->nbhij', qc, kc) * decay, 0.0)
    q_dec = qc * jnp.exp(gc)[..., None]
    k_dec = kc * jnp.exp(gc[..., -1:] - gc)[..., None]
    g_last = jnp.exp(gc[..., -1])

    def step(state, xs):
        u_n, w_n, qk_n, qd_n, kd_n, gl_n = xs
        v_new = u_n - jnp.einsum('bhck,bhkv->bhcv', w_n, state)
        o_n = jnp.einsum('bhck,bhkv->bhcv', qd_n, state) + jnp.einsum('bhij,bhjv->bhiv', qk_n, v_new)
        state = state * gl_n[..., None, None] + jnp.einsum('bhck,bhcv->bhkv', kd_n, v_new)
        return state, o_n

    s0 = jnp.zeros((B, H, DK, DV), f32)
    _, o = lax.scan(step, s0, (u, w, qk, q_dec, k_dec, g_last))
    return jnp.transpose(o, (1, 0, 3, 2, 4)).reshape(B, S, H, DV)


def gated_deltanet(gq, gk, gv, gz, ga, gb, conv_w, a_log, dt_bias, norm_w):
    B, S, _ = gq.shape
    f32 = jnp.float32
    qkv = jax.nn.silu(causal_depthwise_conv(jnp.concatenate([gq, gk, gv], axis=-1), conv_w))
    q, k, v = jnp.split(qkv, [GDN_HEADS * GDN_DK, 2 * GDN_HEADS * GDN_DK], axis=-1)
    q = l2_normalize(q.reshape(B, S, GDN_HEADS, GDN_DK)) * (GDN_DK ** -0.5)
    k = l2_normalize(k.reshape(B, S, GDN_HEADS, GDN_DK))
    v = v.reshape(B, S, GDN_HEADS, GDN_DV)
    beta = jax.nn.sigmoid(gb.astype(f32))
    g = -jnp.exp(a_log.astype(f32)) * jax.nn.softplus(ga.astype(f32) + dt_bias.astype(f32))
    o = chunk_gated_delta_rule(q, k, v, g, beta)
    o = rms_norm(o, norm_w) * jax.nn.silu(gz.reshape(B, S, GDN_HEADS, GDN_DV).astype(f32))
    return o.reshape(B, S, GDN_HEADS * GDN_DV).astype(gq.dtype)


def to_query_blocks(t):
    B, S = t.shape[:2]
    return jnp.swapaxes(t.reshape((B, S // Q_BLOCK, Q_BLOCK) + t.shape[2:]), 0, 1)


def from_query_blocks(t):
    t = jnp.swapaxes(t, 0, 1)
    return t.reshape((t.shape[0], t.shape[1] * t.shape[2]) + t.shape[3:])


def stick_breaking_attention(q, k, v):
    S = q.shape[1]
    scale = SB_DIM ** -0.5
    kpos = jnp.arange(S)

    def block(args):
        qb, i = args
        z = jnp.einsum('bqhd,bkhd->bhqk', qb, k).astype(jnp.float32) * scale
        qpos = i * Q_BLOCK + jnp.arange(Q_BLOCK)
        mask = kpos[None, :] < qpos[:, None]
        log_1m = jnp.where(mask, jax.nn.log_sigmoid(-z), 0.0)
        rest = lax.cumsum(log_1m, axis=3, reverse=True) - log_1m
        wts = jnp.where(mask, jnp.exp(jax.nn.log_sigmoid(z) + rest), 0.0)
        return jnp.einsum('bhqk,bkhd->bqhd', wts.astype(v.dtype), v)

    nb = S // Q_BLOCK
    out = lax.map(block, (to_query_blocks(q), jnp.arange(nb)))
    return from_query_blocks(out)


def mla_attention(q_nope, q_rope, k_nope, k_rope, v):
    S = q_nope.shape[1]
    scale = (MLA_NOPE + MLA_ROPE) ** -0.5
    kpos = jnp.arange(S)

    def block(args):
        qn, qr, i = args
        s = (jnp.einsum('bqhd,bkhd->bhqk', qn, k_nope) + jnp.einsum('bqhd,bkd->bhqk', qr, k_rope)).astype(jnp.float32) * scale
        qpos = i * Q_BLOCK + jnp.arange(Q_BLOCK)
        s = jnp.where(kpos[None, :] <= qpos[:, None], s, -jnp.inf)
        pr = jax.nn.softmax(s, axis=-1)
        return jnp.einsum('bhqk,bkhd->bqhd', pr.astype(v.dtype), v)

    nb = S // Q_BLOCK
    out = lax.map(block, (to_query_blocks(q_nope), to_query_blocks(q_rope), jnp.arange(nb)))
    return from_query_blocks(out)


def hybrid_mixer(h, cos, sin, w_in, conv_w, a_log, dt_bias, gdn_norm_w, q_norm_w, kv_norm_w, w_uq, w_ukv, w_o):
    B, S, _ = h.shape
    proj = h @ w_in
    gq, gk, gv, gz, ga, gb, sq, sk, sv, mq, mkv = jnp.split(proj, IN_SPLITS, axis=-1)
    o_gdn = gated_deltanet(gq, gk, gv, gz, ga, gb, conv_w, a_log, dt_bias, gdn_norm_w)
    shp = (B, S, SB_HEADS, SB_DIM)
    o_sb = stick_breaking_attention(sq.reshape(shp), sk.reshape(shp), sv.reshape(shp)).reshape(B, S, SB_HEADS * SB_DIM)
    qf = (rms_norm(mq, q_norm_w) @ w_uq).reshape(B, S, MLA_HEADS, MLA_NOPE + MLA_ROPE)
    q_nope, q_rope = jnp.split(qf, [MLA_NOPE], axis=-1)
    ckv, k_rope = jnp.split(mkv, [MLA_KV_RANK], axis=-1)
    kvf = (rms_norm(ckv, kv_norm_w) @ w_ukv).reshape(B, S, MLA_HEADS, MLA_NOPE + MLA_V)
    k_nope, v_mla = jnp.split(kvf, [MLA_NOPE], axis=-1)
    q_rope = apply_rope(q_rope, cos[:, :, None, :], sin[:, :, None, :])
    k_rope = apply_rope(k_rope, cos, sin)
    o_mla = mla_attention(q_nope, q_rope, k_nope, k_rope, v_mla).reshape(B, S, MLA_HEADS * MLA_V)
    return jnp.concatenate([o_gdn, o_sb, o_mla], axis=-1) @ w_o


def setup_inputs(seed: int = 0) -> dict:
    key = jax.random.key(seed)
    ks = jax.random.split(key, 24)
    f32 = jnp.float32
    L = DEPTH

    def nrm(k, shape, scale):
        return jax.random.normal(k, shape, f32) * scale

    x = nrm(ks[0], (BATCH, SEQ, D_MODEL), 1.0)
    p = nrm(ks[1], (DEPTH, BATCH, SEQ, PLE_DIM), 1.0)
    positions = (jnp.arange(SEQ, dtype=jnp.int32)[None, :]
                 + jax.random.randint(ks[2], (BATCH, 1), 0, 1024, dtype=jnp.int32))
    a_init = jax.random.uniform(ks[7], (L, GDN_HEADS), f32, 1.0, 16.0)
    dt = jnp.exp(jax.random.uniform(ks[8], (L, GDN_HEADS), f32, np.log(1e-3), np.log(1e-1)))
    return {
        "x": x,
        "p": p,
        "positions": positions,
        "ffa_w_in": nrm(ks[3], (L, D_MODEL, 2 * D_FF), D_MODEL ** -0.5),
        "ffa_w_out": nrm(ks[4], (L, D_FF, D_MODEL), D_FF ** -0.5 * DEEPNORM_BETA),
        "mix_w_in": nrm(ks[5], (L, D_MODEL, IN_TOTAL), D_MODEL ** -0.5),
        "gdn_conv_w": nrm(ks[6], (L, GDN_CONV, GDN_CONV_CH), GDN_CONV ** -0.5),
        "gdn_a_log": jnp.log(a_init),
        "gdn_dt_bias": dt + jnp.log(-jnp.expm1(-dt)),
        "gdn_norm_w": 1.0 + nrm(ks[9], (L, GDN_DV), 0.02),
        "mla_q_norm_w": 1.0 + nrm(ks[10], (L, MLA_Q_RANK), 0.02),
        "mla_kv_norm_w": 1.0 + nrm(ks[11], (L, MLA_KV_RANK), 0.02),
        "mla_w_uq": nrm(ks[12], (L, MLA_Q_RANK, MLA_HEADS * (MLA_NOPE + MLA_ROPE)), MLA_Q_RANK ** -0.5),
        "mla_w_ukv": nrm(ks[13], (L, MLA_KV_RANK, MLA_HEADS * (MLA_NOPE + MLA_V)), MLA_KV_RANK ** -0.5),
        "mix_w_o": nrm(ks[14], (L, MIX_WIDTH, D_MODEL), MIX_WIDTH ** -0.5 * DEEPNORM_BETA),
        "ffb_w_in": nrm(ks[15], (L, D_MODEL, 2 * D_FF), D_MODEL ** -0.5),
        "ffb_w_out": nrm(ks[16], (L, D_FF, D_MODEL), D_FF ** -0.5 * DEEPNORM_BETA),
        "ln_g": 1.0 + nrm(ks[17], (L, 3, D_MODEL), 0.02),
        "ln_b": nrm(ks[18], (L, 3, D_MODEL), 0.02),
        "ple_w_gate": nrm(ks[19], (L, D_MODEL, D_MODEL), D_MODEL ** -0.5),
        "ple_w_proj": nrm(ks[20], (L, PLE_DIM, D_MODEL), PLE_DIM ** -0.5 * DEEPNORM_BETA),
    }


def reference(x, p, positions, ffa_w_in, ffa_w_out, mix_w_in, gdn_conv_w, gdn_a_log, gdn_dt_bias,
              gdn_norm_w, mla_q_norm_w, mla_kv_norm_w, mla_w_uq, mla_w_ukv, mix_w_o,
              ffb_w_in, ffb_w_out, ln_g, ln_b, ple_w_gate, ple_w_proj):
    cos, sin = rope_tables(positions)
    h = x
    for i in range(DEPTH):
        h = layer_norm(DEEPNORM_ALPHA * h + 0.5 * swiglu(h, ffa_w_in[i], ffa_w_out[i]), ln_g[i, 0], ln_b[i, 0])
        mix = hybrid_mixer(h, cos, sin, mix_w_in[i], gdn_conv_w[i], gdn_a_log[i], gdn_dt_bias[i], gdn_norm_w[i],
                           mla_q_norm_w[i], mla_kv_norm_w[i], mla_w_uq[i], mla_w_ukv[i], mix_w_o[i])
        h = layer_norm(DEEPNORM_ALPHA * h + mix, ln_g[i, 1], ln_b[i, 1])
        h = layer_norm(DEEPNORM_ALPHA * h + 0.5 * swiglu(h, ffb_w_in[i], ffb_w_out[i]), ln_g[i, 2], ln_b[i, 2])
        h = h + jax.nn.sigmoid(h @ ple_w_gate[i]) * (p[i] @ ple_w_proj[i])
    return h
```

```python
import numpy as np
from contextlib import ExitStack
import concourse.bass as bass
import concourse.mybir as mybir
from concourse.bass_utils import run_bass_kernel_spmd

F32 = mybir.dt.float32
BF16 = mybir.dt.bfloat16
I32 = mybir.dt.int32
AF = mybir.ActivationFunctionType
ALU = mybir.AluOpType
AX = mybir.AxisListType

D = 1024
DFF = 2816
PLE = 256
DEPTH = 2
ALPHA = (2 * DEPTH) ** 0.25
LN_EPS = 1e-5
RMS_EPS = 1e-6
IN_TOTAL = 3248
C_GQ, C_GK, C_GV, C_GZ, C_GA, C_GB = 0, 512, 1024, 1536, 2048, 2056
C_SQ, C_SK, C_SV, C_MQ, C_MKV, C_KR = 2064, 2320, 2576, 2832, 3088, 3216
SEM_ROT = 1 << 30
TWO_PI = 6.283185307179586


class Prog:
    def __init__(self, nc, es):
        self.nc = nc
        self.es = es
        self.engs = {"pe": nc.tensor, "act": nc.scalar, "dve": nc.vector, "pool": nc.gpsimd, "sp": nc.sync}
        self.nsem = 0
        self.esem = {}
        self.ecnt = {}
        self.allsems = []
        for e in ("pe", "act", "dve", "pool"):
            self.esem[e] = self._newsem("e_" + e)
            self.ecnt[e] = 0
        self.dsem = {}
        self.dcnt = {}
        self.dfree = []
        self.bar1 = None
        self.bar2 = None
        self.nbar = 0
        self.last_w = {}
        self.readers = {}
        self.known = {e: {} for e in self.engs}
        self.nwaits = 0
        self.ninst = 0
        import os
        self.limit = int(os.environ["OP_LIMIT"]) if "OP_LIMIT" in os.environ else None

    def _newsem(self, name):
        self.nsem += 1
        s = self.es.enter_context(self.nc.semaphore(f"{name}_{self.nsem}"))
        self.allsems.append([s, 0])
        return s

    def _bump(self, sem, val):
        for r in self.allsems:
            if r[0] is sem:
                r[1] = max(r[1], val)
                return

    def _deps(self, reads, writes):
        deps = {}

        def add(t):
            cur = deps.get(id(t[0]))
            if cur is None or cur[1] < t[1]:
                deps[id(t[0])] = (t[0], t[1])

        for k in reads:
            t = self.last_w.get(k)
            if t is not None:
                add(t)
        for k in writes:
            t = self.last_w.get(k)
            if t is not None:
                add(t)
            for t in self.readers.get(k, ()):
                add(t)
        return deps

    def _wait(self, eng, deps, skip_sem=None):
        e = self.engs[eng]
        kn = self.known[eng]
        for sid, (sem, val) in deps.items():
            if skip_sem is not None and sem is skip_sem:
                continue
            if kn.get(sid, 0) >= val:
                continue
            e.wait_ge(sem, val)
            self.nwaits += 1
            kn[sid] = val

    def _commit(self, reads, writes, tok):
        for k in writes:
            self.last_w[k] = tok
            self.readers[k] = []
        for k in reads:
            if k in writes:
                continue
            self.readers.setdefault(k, []).append(tok)

    def op(self, eng, fn, reads=(), writes=()):
        if self.limit is not None and self.ninst >= self.limit:
            return None
        pr = [k for k in reads if k in PSUM_KEYS]
        if pr:
            writes = list(writes) + pr
        deps = self._deps(reads, writes)
        self._wait(eng, deps, skip_sem=self.esem["pe"] if eng == "pe" else None)
        inst = fn()
        if self.ecnt[eng] >= SEM_ROT:
            self.esem[eng] = self._newsem("e_" + eng)
            self.ecnt[eng] = 0
        self.ecnt[eng] += 1
        sem = self.esem[eng]
        inst.then_inc(sem, 1)
        tok = (sem, self.ecnt[eng])
        self._bump(sem, self.ecnt[eng])
        self._commit(reads, writes, tok)
        self.ninst += 1
        return tok

    def dma(self, q, out, in_, reads=(), writes=(), chan=None, **kw):
        if self.limit is not None and self.ninst >= self.limit:
            return None
        deps = self._deps(reads, writes)
        self._wait(q, deps)
        if chan is None:
            chan = "d_" + (list(writes) + list(reads))[0]
        if q == "pool":
            chan = "sw_" + chan
            assert chan not in self.dsem
            self.dsem[chan] = self._newsem("sw")
            self.dcnt[chan] = 0
        elif chan not in self.dsem:
            self.dsem[chan] = self.dfree.pop() if self.dfree else self._newsem("d")
            self.dcnt[chan] = 0
        inst = self.engs[q].dma_start(out=out, in_=in_, **kw)
        self.dcnt[chan] += 16
        sem = self.dsem[chan]
        inst.then_inc(sem, 16)
        tok = (sem, self.dcnt[chan])
        self._bump(sem, self.dcnt[chan])
        self._commit(reads, writes, tok)
        self.ninst += 1
        return tok

    def barrier(self):
        for eng in self.engs:
            kn = self.known[eng]
            for sem, val in self.allsems:
                if val > 0 and kn.get(id(sem), 0) < val:
                    self.engs[eng].wait_ge(sem, val)
                    kn[id(sem)] = val
                    self.nwaits += 1
        self.last_w.clear()
        self.readers.clear()
        if not self.dsem:
            return
        if self.bar1 is None:
            self.bar1 = self.es.enter_context(self.nc.semaphore("bar1"))
            self.bar2 = self.es.enter_context(self.nc.semaphore("bar2"))
        self.nbar += 1
        for eng in self.engs:
            self.engs[eng].sem_inc(self.bar1, 1)
        self.engs["pool"].wait_ge(self.bar1, len(self.engs) * self.nbar)
        for chan, sem in self.dsem.items():
            if chan.startswith("sw_"):
                continue
            self.engs["pool"].sem_clear(sem)
            self.dfree.append(sem)
            for r in self.allsems:
                if r[0] is sem:
                    r[1] = 0
            for eng in self.engs:
                self.known[eng].pop(id(sem), None)
        self.engs["pool"].sem_inc(self.bar2, 1)
        for eng in self.engs:
            self.engs[eng].wait_ge(self.bar2, self.nbar)
        self.dsem.clear()
        self.dcnt.clear()


class Ctx:
    pass


_UID = [0]


def _tiles(nc, pes, P=None):
    _UID[0] += 1
    u = _UID[0]
    sb = lambda n, s, d: pes.enter_context(nc.sbuf_tensor(f"{n}_u{u}", s, d))

    def ps(n, s, d):
        PSUM_KEYS.add(n)
        return pes.enter_context(nc.psum_tensor(f"{n}_u{u}", [128, 2048 // mybir.dt.size(d)], d))
    return sb, ps


PSUM_KEYS = set()


def load_w(c, dst, src_rows, key, kchunks, c0, c1):
    n = c1 - c0
    nblk = (n + 1023) // 1024
    while n % nblk:
        nblk += 1
    w = n // nblk
    for i in range(nblk):
        c.P.dma("pool", dst[:, 0:kchunks, i * w:(i + 1) * w], src_rows[0:kchunks * 128, c0 + i * w:c0 + (i + 1) * w].rearrange("(k p) f -> p k f", p=128),
                writes=[f"{key}{kc}" for kc in range(kchunks)], chan=f"{key}_b{i}", max_dma_last_dim=4096)


def consts_phase(c, pes):
    nc, P = c.nc, c.P
    sb, ps = _tiles(nc, pes)
    return


def phase_ffn(c, l, which, src, dst, dstT=None):
    nc, P, S = c.nc, c.P, c.S
    w_in = (c.ffa_w_in if which == 0 else c.ffb_w_in)[l]
    w_out = (c.ffa_w_out if which == 0 else c.ffb_w_out)[l]
    lni = 0 if which == 0 else 2
    KC, FC, NG = D // 128, c.DFF // 128, min(512, S)
    NT = NG // 128
    dff = c.DFF
    with ExitStack() as pes:
        sb, ps = _tiles(nc, pes)
        win = sb("win", [128, KC, 2 * dff], BF16)
        wout = sb("wout", [128, FC, D], BF16)
        ident = sb("ident", [128, 128], F32)
        gb = sb("gb", [128, 2, D], F32)
        x = sb("x", [128, NT, D], F32)
        xT = sb("xT", [128, KC, NG], BF16)
        actT = sb("actT", [128, FC, NG], BF16)
        sg = [sb(f"sg{i}", [128, NG], F32) for i in range(2)]
        y = sb("y", [128, D], F32)
        o = sb("o", [128, D], F32)
        st = sb("st", [128, 2, 6], F32)
        mv = sb("mv", [128, 2], F32)
        rstd = sb("rstd", [128, 1], F32)
        xts = sb("xts", [128, 8, 128], BF16)
        pt = [ps(f"pt{i}", [128, 512], F32) for i in range(2)]
        pg = [ps(f"pg{i}", [128, 512], F32) for i in range(2)]
        pu = [ps(f"pu{i}", [128, 512], F32) for i in range(2)]
        po = [ps(f"po{i}", [128, 512], F32) for i in range(2)]
        P.dma("sp", ident[:], c.consts[:, c.CO["ident"]:c.CO["ident"] + 128], writes=["ident"])
        P.dma("sp", gb[:, 0, :], c.ln_g[l, lni, :].partition_broadcast(128), writes=["gb0"])
        P.dma("sp", gb[:, 1, :], c.ln_b[l, lni, :].partition_broadcast(128), writes=["gb1"])
        load_w(c, win, w_in, "win", KC, 0, 2 * dff)
        load_w(c, wout, w_out, "wout", FC, 0, D)
        for g in range(S // NG):
            t0 = g * NG
            for tt in range(NT):
                P.dma("sp", x[:, tt, :], src[t0 + tt * 128: t0 + (tt + 1) * 128, :], writes=[f"x{tt}"])
            for kc in range(KC):
                p_, pk = pt[kc % 2], f"pt{kc % 2}"
                for tt in range(NT):
                    P.op("pe", lambda p_=p_, tt=tt, kc=kc: nc.tensor.transpose(p_[:, tt * 128:(tt + 1) * 128], x[:, tt, kc * 128:(kc + 1) * 128], ident[:]),
                         reads=[f"x{tt}", "ident"], writes=[pk])
                if kc % 2 == 0:
                    P.op("act", lambda p_=p_, kc=kc: nc.scalar.copy(xT[:, kc, :], p_[:, 0:NG]), reads=[pk], writes=[f"xT{kc}"])
                else:
                    P.op("dve", lambda p_=p_, kc=kc: nc.vector.tensor_copy(xT[:, kc, :], p_[:, 0:NG]), reads=[pk], writes=[f"xT{kc}"])
            for j in range(FC):
                b = j % 2
                for kc in range(KC):
                    P.op("pe", lambda b=b, kc=kc, j=j: nc.tensor.matmul(pg[b][:, 0:NG], lhsT=win[:, kc, j * 128:(j + 1) * 128], rhs=xT[:, kc, :],
                                                                       start=(kc == 0), stop=(kc == KC - 1)),
                         reads=[f"win{kc}", f"xT{kc}"], writes=[f"pg{b}"])
                for kc in range(KC):
                    P.op("pe", lambda b=b, kc=kc, j=j: nc.tensor.matmul(pu[b][:, 0:NG], lhsT=win[:, kc, dff + j * 128: dff + (j + 1) * 128], rhs=xT[:, kc, :],
                                                                       start=(kc == 0), stop=(kc == KC - 1)),
                         reads=[f"win{kc}", f"xT{kc}"], writes=[f"pu{b}"])
                P.op("act", lambda b=b: nc.scalar.activation(out=sg[b][:], in_=pg[b][:, 0:NG], func=AF.Silu), reads=[f"pg{b}"], writes=[f"sg{b}"])
                P.op("dve", lambda b=b, j=j: nc.vector.tensor_tensor(out=actT[:, j, :], in0=pu[b][:, 0:NG], in1=sg[b][:], op=ALU.mult),
                     reads=[f"pu{b}", f"sg{b}"], writes=[f"actT{j}"])
            for tt in range(NT):
                for dh in range(2):
                    for j in range(FC):
                        P.op("pe", lambda tt=tt, dh=dh, j=j: nc.tensor.matmul(po[dh][:, :], lhsT=actT[:, j, tt * 128:(tt + 1) * 128],
                                                                             rhs=wout[:, j, dh * 512:(dh + 1) * 512], start=(j == 0), stop=(j == FC - 1)),
                             reads=[f"actT{j}", f"wout{j}"], writes=[f"po{dh}"])
                P.op("act", lambda tt=tt: nc.scalar.mul(y[:], x[:, tt, :], ALPHA), reads=[f"x{tt}"], writes=["y"])
                for dh in range(2):
                    P.op("dve", lambda dh=dh: nc.vector.scalar_tensor_tensor(out=y[:, dh * 512:(dh + 1) * 512], in0=po[dh][:, :], scalar=0.5,
                                                                           in1=y[:, dh * 512:(dh + 1) * 512], op0=ALU.mult, op1=ALU.add),
                         reads=[f"po{dh}", "y"], writes=["y"])
                layer_norm_tile(c, y, o, st, mv, rstd, gb)
                P.dma("sp", dst[t0 + tt * 128: t0 + (tt + 1) * 128, :], o[:], reads=["o"], writes=["dst"], chan="d_o")
                if dstT is not None:
                    for half in range(2):
                        for q in range(4):
                            kc = half * 4 + q
                            P.op("pe", lambda half=half, q=q, kc=kc: nc.tensor.transpose(pt[half][:, q * 128:(q + 1) * 128], o[:, kc * 128:(kc + 1) * 128], ident[:]),
                                 reads=["o", "ident"], writes=[f"pt{half}"])
                        if half == 0:
                            P.op("act", lambda: nc.scalar.copy(xts[:, 0:4, :].rearrange("p a b -> p (a b)"), pt[0][:, :]), reads=["pt0"], writes=["xts0"])
                        else:
                            P.op("dve", lambda: nc.vector.tensor_copy(xts[:, 4:8, :].rearrange("p a b -> p (a b)"), pt[1][:, :]), reads=["pt1"], writes=["xts1"])
                    tok = slice(t0 + tt * 128, t0 + (tt + 1) * 128)
                    P.dma("sp", dstT[:, tok].rearrange("(k p) t -> p k t", p=128), xts[:], reads=["xts0", "xts1"], writes=["dstT"], chan="d_xts")
    P.barrier()


def layer_norm_tile(c, y, o, st, mv, rstd, gb, sfx=""):
    nc, P = c.nc, c.P
    ky, ko, kst, kmv, krs = "y" + sfx, "o" + sfx, "st" + sfx, "mv" + sfx, "rstd" + sfx
    for dh in range(2):
        P.op("dve", lambda dh=dh: nc.vector.bn_stats(out=st[:, dh, :], in_=y[:, dh * 512:(dh + 1) * 512]), reads=[ky], writes=[kst])
    P.op("dve", lambda: nc.vector.bn_aggr(out=mv[:], in_=st[:].rearrange("p a b -> p (a b)")), reads=[kst], writes=[kmv])
    P.op("act", lambda: nc.scalar.activation(out=rstd[:], in_=mv[:, 1:2], func=AF.Sqrt, bias=LN_EPS, scale=1.0), reads=[kmv], writes=[krs])
    P.op("dve", lambda: nc.vector.reciprocal(out=rstd[:], in_=rstd[:]), reads=[krs], writes=[krs])
    P.op("dve", lambda: nc.vector.tensor_scalar(out=y[:], in0=y[:], scalar1=mv[:, 0:1], scalar2=rstd[:, 0:1], op0=ALU.subtract, op1=ALU.mult),
         reads=[ky, kmv, krs], writes=[ky])
    P.op("pool", lambda: nc.gpsimd.tensor_tensor(out=o[:], in0=y[:], in1=gb[:, 0, :], op=ALU.mult), reads=[ky, "gb0"], writes=[ko])
    P.op("pool", lambda: nc.gpsimd.tensor_tensor(out=o[:], in0=o[:], in1=gb[:, 1, :], op=ALU.add), reads=[ko, "gb1"], writes=[ko])


def phase_transpose(c, src, dstT):
    nc, P, S = c.nc, c.P, c.S
    with ExitStack() as pes:
        sb, ps = _tiles(nc, pes)
        ident = sb("ident", [128, 128], F32)
        xs = [sb(f"x{i}", [128, D], F32) for i in range(2)]
        xts = [sb(f"xt{i}", [128, 8, 128], BF16) for i in range(2)]
        pt = [ps(f"pt{i}", [128, 512], F32) for i in range(4)]
        P.dma("sp", ident[:], c.consts[:, c.CO["ident"]:c.CO["ident"] + 128], writes=["ident"])
        for tt in range(S // 128):
            b = tt % 2
            P.dma("sp", xs[b][:], src[tt * 128:(tt + 1) * 128, :], writes=[f"x{b}"])
            for half in range(2):
                pp, pk = pt[2 * b + half], f"pt{2 * b + half}"
                for q in range(4):
                    kc = half * 4 + q
                    P.op("pe", lambda pp=pp, q=q, kc=kc, b=b: nc.tensor.transpose(pp[:, q * 128:(q + 1) * 128], xs[b][:, kc * 128:(kc + 1) * 128], ident[:]),
                         reads=[f"x{b}", "ident"], writes=[pk])
                eng = "act" if half == 0 else "dve"
                if half == 0:
                    P.op("act", lambda pp=pp, b=b: nc.scalar.copy(xts[b][:, 0:4, :].rearrange("p a b -> p (a b)"), pp[:, :]), reads=[pk], writes=[f"xt{b}h0"])
                else:
                    P.op("dve", lambda pp=pp, b=b: nc.vector.tensor_copy(xts[b][:, 4:8, :].rearrange("p a b -> p (a b)"), pp[:, :]), reads=[pk], writes=[f"xt{b}h1"])
            P.dma("sp", dstT[:, tt * 128:(tt + 1) * 128].rearrange("(k p) t -> p k t", p=128), xts[b][:], reads=[f"xt{b}h0", f"xt{b}h1"],
                  writes=["dstT"], chan=f"d_xt{b}")
    P.barrier()


def phase_rope(c):
    nc, P, S = c.nc, c.P, c.S
    with ExitStack() as pes:
        sb, ps = _tiles(nc, pes)
        pi_ = sb("pi", [32, S], I32)
        ang = sb("ang", [32, S], F32)
        k = sb("k", [32, S], F32)
        r = sb("r", [32, S], F32)
        rc = sb("rc", [32, S], F32)
        m = sb("m", [32, S], F32)
        cs = sb("cs", [32, S], F32)
        sn = sb("sn", [32, S], F32)
        cst = sb("cst", [32, 2], F32)
        MAGIC = 12582912.0
        C1 = 6.28125
        C2 = TWO_PI - C1
        PI_LO = 3.1415925
        P.dma("sp", pi_[:], c.pos[0, :].partition_broadcast(32), writes=["pi"])
        P.dma("sp", cst[:], c.consts[0:32, c.CO["rope"]:c.CO["rope"] + 2], writes=["cst"])
        P.op("dve", lambda: nc.vector.tensor_copy(ang[:], pi_[:]), reads=["pi"], writes=["ang"])
        P.op("dve", lambda: nc.vector.tensor_scalar(out=ang[:], in0=ang[:], scalar1=cst[:, 0:1], scalar2=None, op0=ALU.mult), reads=["ang", "cst"], writes=["ang"])
        P.op("dve", lambda: nc.vector.tensor_scalar(out=k[:], in0=ang[:], scalar1=1.0 / TWO_PI, scalar2=MAGIC, op0=ALU.mult, op1=ALU.add), reads=["ang"], writes=["k"])
        P.op("dve", lambda: nc.vector.tensor_scalar(out=k[:], in0=k[:], scalar1=MAGIC, scalar2=None, op0=ALU.subtract), reads=["k"], writes=["k"])
        P.op("dve", lambda: nc.vector.scalar_tensor_tensor(out=r[:], in0=k[:], scalar=-C1, in1=ang[:], op0=ALU.mult, op1=ALU.add), reads=["k", "ang"], writes=["r"])
        P.op("dve", lambda: nc.vector.scalar_tensor_tensor(out=r[:], in0=k[:], scalar=-C2, in1=r[:], op0=ALU.mult, op1=ALU.add), reads=["k", "r"], writes=["r"])
        P.op("dve", lambda: nc.vector.tensor_scalar(out=rc[:], in0=r[:], scalar1=np.pi / 2, scalar2=None, op0=ALU.add), reads=["r"], writes=["rc"])
        P.op("dve", lambda: nc.vector.tensor_scalar(out=m[:], in0=rc[:], scalar1=np.pi, scalar2=-TWO_PI, op0=ALU.is_gt, op1=ALU.mult), reads=["rc"], writes=["m"])
        P.op("dve", lambda: nc.vector.tensor_tensor(out=rc[:], in0=rc[:], in1=m[:], op=ALU.add), reads=["rc", "m"], writes=["rc"])
        for t_, nm in ((r, "r"), (rc, "rc")):
            P.op("dve", lambda t_=t_: nc.vector.tensor_scalar(out=t_[:], in0=t_[:], scalar1=PI_LO, scalar2=-PI_LO, op0=ALU.min, op1=ALU.max), reads=[nm], writes=[nm])
        import os
        SINF = AF.Identity if os.environ.get("ROPE_DBG") == "nosin" else AF.Sin
        P.op("act", lambda: nc.scalar.activation(out=sn[:], in_=r[:], func=SINF), reads=["r"], writes=["sn"])
        P.op("act", lambda: nc.scalar.activation(out=cs[:], in_=rc[:], func=SINF), reads=["rc"], writes=["cs"])
        P.op("dve", lambda: nc.vector.tensor_scalar(out=sn[:], in0=sn[:], scalar1=cst[:, 1:2], scalar2=None, op0=ALU.mult), reads=["sn", "cst"], writes=["sn"])
        P.dma("sp", c.ropeT[0, :, :], cs[:], reads=["cs"], writes=["ropeT0"])
        P.dma("sp", c.ropeT[1, :, :], sn[:], reads=["sn"], writes=["ropeT1"])
    P.barrier()


def phase_mla(c, l):
    nc, P, S = c.nc, c.P, c.S
    NB = max(1, S // 512)
    NG = min(512, S)
    NTT = S // 128
    SCALE = 96 ** -0.5
    H = 4
    w_in = c.mix_w_in[l]
    with ExitStack() as pes:
        sb, ps = _tiles(nc, pes)
        xTb = [sb(f"xT{i}", [128, 8, NG], BF16) for i in range(2)]
        wq = sb("wq", [128, 8, 256], BF16)
        wkv = sb("wkv", [128, 8, 128], BF16)
        wkr = sb("wkr", [128, 8, 192], BF16)
        wuq = sb("wuq", [128, 2, 768], BF16)
        wukk = sb("wukk", [128, 1, 256], BF16)
        wukv = sb("wukv", [128, 1, 256], BF16)
        nwq = sb("nwq", [128, 2], F32)
        nwkv = sb("nwkv", [128, 1], F32)
        ones = sb("ones", [128, 128], BF16)
        onesf = sb("onesf", [128, 128], F32)
        mk = sb("mk", [128, 1, 2048], BF16)
        cs = sb("cs", [96, NG], F32)
        sn = sb("sn", [96, NG], F32)
        cqn = sb("cqn", [128, 2, NG], BF16)
        ckvn = sb("ckvn", [128, NG], BF16)
        krT = sb("krT", [96, NG], BF16)
        qf = [sb(f"qf{h}", [96, S], BF16) for h in range(H)]
        kf = [sb(f"kf{h}", [96, S], BF16) for h in range(H)]
        V = sb("V", [128, NTT, 256], BF16)
        tmp = [sb(f"tmp{i}", [128, NG], F32) for i in range(3)]
        sq = [sb(f"sq{i}", [128, NG], BF16) for i in range(2)]
        rbc = sb("rbc", [128, NG], F32)
        pe_ = [sb(f"pe{i}", [128, NG], BF16) for i in range(5)]
        rec = [sb(f"rec{i}", [64, NG], F32) for i in range(2)]
        oT = [sb(f"oT{i}", [64, NG], BF16) for i in range(2)]
        pA = [ps(f"pA{i}", [128, 512], F32) for i in range(2)]
        pB = ps("pB", [128, 512], F32)
        pS = [ps(f"pS{i}", [128, 512], F32) for i in range(2)]
        pL = ps("pL", [64, 512], F32)
        CO = c.CO
        load_w(c, wq, w_in, "wq", 8, C_MQ, C_MQ + 256)
        load_w(c, wkv, w_in, "wkv", 8, C_MKV, C_MKV + 128)
        load_w(c, wkr, c.w_kr[l], "wkr", 8, 0, 192)
        load_w(c, wuq, c.w_uq_p[l], "wuq", 2, 0, 768)
        load_w(c, wukk, c.w_ukv_k[l], "wukk", 1, 0, 256)
        load_w(c, wukv, c.w_ukv_v[l], "wukv", 1, 0, 256)
        P.dma("sp", nwq[:], c.q_norm_wT[l], writes=["nwq"])
        P.dma("sp", nwkv[:], c.kv_norm_wT[l], writes=["nwkv"])
        P.dma("sp", onesf[:], c.consts[:, CO["ones"]:CO["ones"] + 128], writes=["onesf"])
        P.op("dve", lambda: nc.vector.tensor_copy(ones[:], onesf[:]), reads=["onesf"], writes=["ones"])
        load_w(c, mk, c.consts, "mk", 1, CO["maskle"], CO["maskle"] + 2048)
        wkeys = lambda n, k: [f"{n}{i}" for i in range(k)]
        for nb in range(NB):
            cols = slice(nb * NG, (nb + 1) * NG)
            xT = xTb[nb % 2]
            xk = f"xT{nb % 2}"
            P.dma("sp", xT[:], c.h1T[:, cols].rearrange("(k p) t -> p k t", p=128), writes=[xk])
            P.dma("sp", cs[64:96, :], c.ropeT[0, :, cols], writes=["cs"])
            P.dma("sp", sn[64:96, :], c.ropeT[1, :, cols], writes=["sn"])
            for j in range(2):
                pa = pA[j]
                for kc in range(8):
                    P.op("pe", lambda pa=pa, kc=kc, j=j: nc.tensor.matmul(pa[:, 0:NG], lhsT=wq[:, kc, j * 128:(j + 1) * 128], rhs=xT[:, kc, :],
                                                                         start=(kc == 0), stop=(kc == 7)), reads=[f"wq{kc}", xk], writes=[f"pA{j}"])
                P.op("act", lambda pa=pa, j=j: nc.scalar.activation(out=sq[j][:], in_=pa[:, 0:NG], func=AF.Square), reads=[f"pA{j}"], writes=[f"sq{j}"])
                P.op("dve", lambda pa=pa, j=j: nc.vector.tensor_scalar(out=tmp[j][:], in0=pa[:, 0:NG], scalar1=nwq[:, j:j + 1], scalar2=None, op0=ALU.mult),
                     reads=[f"pA{j}", "nwq"], writes=[f"tmp{j}"])
            for j in range(2):
                P.op("pe", lambda j=j: nc.tensor.matmul(pB[:, 0:NG], lhsT=ones[:], rhs=sq[j][:], start=(j == 0), stop=(j == 1)),
                     reads=["ones", f"sq{j}"], writes=["pB"])
            P.op("act", lambda: nc.scalar.activation(out=rbc[:], in_=pB[:, 0:NG], func=AF.Sqrt, bias=RMS_EPS, scale=1.0 / 256), reads=["pB"], writes=["rbc"])
            P.op("dve", lambda: nc.vector.reciprocal(out=rbc[:], in_=rbc[:]), reads=["rbc"], writes=["rbc"])
            for j in range(2):
                P.op("dve", lambda j=j: nc.vector.tensor_tensor(out=cqn[:, j, :], in0=tmp[j][:], in1=rbc[:], op=ALU.mult),
                     reads=[f"tmp{j}", "rbc"], writes=["cqn"])
            pa = pA[0]
            for kc in range(8):
                P.op("pe", lambda kc=kc: nc.tensor.matmul(pa[:, 0:NG], lhsT=wkv[:, kc, :], rhs=xT[:, kc, :], start=(kc == 0), stop=(kc == 7)),
                     reads=[f"wkv{kc}", xk], writes=["pA0"])
            P.op("act", lambda: nc.scalar.activation(out=sq[0][:], in_=pa[:, 0:NG], func=AF.Square), reads=["pA0"], writes=["sq0"])
            P.op("dve", lambda: nc.vector.tensor_scalar(out=tmp[0][:], in0=pa[:, 0:NG], scalar1=nwkv[:, 0:1], scalar2=None, op0=ALU.mult),
                 reads=["pA0", "nwkv"], writes=["tmp0"])
            P.op("pe", lambda: nc.tensor.matmul(pB[:, 0:NG], lhsT=ones[:], rhs=sq[0][:], start=True, stop=True), reads=["ones", "sq0"], writes=["pB"])
            P.op("act", lambda: nc.scalar.activation(out=rbc[:], in_=pB[:, 0:NG], func=AF.Sqrt, bias=RMS_EPS, scale=1.0 / 128), reads=["pB"], writes=["rbc"])
            P.op("dve", lambda: nc.vector.reciprocal(out=rbc[:], in_=rbc[:]), reads=["rbc"], writes=["rbc"])
            P.op("dve", lambda: nc.vector.tensor_tensor(out=ckvn[:, :], in0=tmp[0][:], in1=rbc[:], op=ALU.mult), reads=["tmp0", "rbc"], writes=["ckvn"])
            pa = pA[1]
            for kc in range(8):
                P.op("pe", lambda kc=kc: nc.tensor.matmul(pa[0:96, 0:NG], lhsT=wkr[:, kc, 0:96], rhs=xT[:, kc, :], start=(kc == 0), stop=(kc == 7)),
                     reads=[f"wkr{kc}", xk], writes=["pA1"])
            for kc in range(8):
                P.op("pe", lambda kc=kc: nc.tensor.matmul(pB[0:96, 0:NG], lhsT=wkr[:, kc, 96:192], rhs=xT[:, kc, :], start=(kc == 0), stop=(kc == 7)),
                     reads=[f"wkr{kc}", xk], writes=["pB"])
            P.op("dve", lambda: nc.vector.tensor_tensor(out=tmp[1][64:96, :], in0=pa[64:96, 0:NG], in1=cs[64:96, :], op=ALU.mult), reads=["pA1", "cs"], writes=["tmp1"])
            P.op("dve", lambda: nc.vector.tensor_tensor(out=tmp[2][64:96, :], in0=pB[64:96, 0:NG], in1=sn[64:96, :], op=ALU.mult), reads=["pB", "sn"], writes=["tmp2"])
            P.op("pool", lambda: nc.gpsimd.tensor_tensor(out=krT[64:96, :], in0=tmp[1][64:96, :], in1=tmp[2][64:96, :], op=ALU.add), reads=["tmp1", "tmp2"], writes=["krT"])
            for h in range(H):
                pa = pA[h % 2]
                pak = f"pA{h % 2}"
                for j in range(2):
                    P.op("pe", lambda pa=pa, j=j, h=h: nc.tensor.matmul(pa[0:96, 0:NG], lhsT=wuq[:, j, h * 192:h * 192 + 96], rhs=cqn[:, j, :],
                                                                       start=(j == 0), stop=(j == 1)), reads=[f"wuq{j}", "cqn"], writes=[pak])
                for j in range(2):
                    P.op("pe", lambda j=j, h=h: nc.tensor.matmul(pB[0:96, 0:NG], lhsT=wuq[:, j, h * 192 + 96:h * 192 + 192], rhs=cqn[:, j, :],
                                                                start=(j == 0), stop=(j == 1)), reads=[f"wuq{j}", "cqn"], writes=["pB"])
                P.op("act", lambda pa=pa, h=h: nc.scalar.copy(qf[h][0:64, cols], pa[0:64, 0:NG]), reads=[pak], writes=[f"qf{h}"])
                P.op("dve", lambda pa=pa: nc.vector.tensor_tensor(out=tmp[1][64:96, :], in0=pa[64:96, 0:NG], in1=cs[64:96, :], op=ALU.mult), reads=[pak, "cs"], writes=["tmp1"])
                P.op("dve", lambda: nc.vector.tensor_tensor(out=tmp[2][64:96, :], in0=pB[64:96, 0:NG], in1=sn[64:96, :], op=ALU.mult), reads=["pB", "sn"], writes=["tmp2"])
                P.op("pool", lambda h=h: nc.gpsimd.tensor_tensor(out=qf[h][64:96, cols], in0=tmp[1][64:96, :], in1=tmp[2][64:96, :], op=ALU.add),
                     reads=["tmp1", "tmp2"], writes=[f"qf{h}"])
                P.op("pe", lambda pa=pa, h=h: nc.tensor.matmul(pa[0:64, 0:NG], lhsT=wukk[:, 0, h * 64:(h + 1) * 64], rhs=ckvn[:, :], start=True, stop=True),
                     reads=["wukk0", "ckvn"], writes=[pak])
                P.op("act", lambda pa=pa, h=h: nc.scalar.copy(kf[h][0:64, cols], pa[0:64, 0:NG]), reads=[pak], writes=[f"kf{h}"])
                P.op("pool", lambda h=h: nc.gpsimd.tensor_copy(kf[h][64:96, cols], krT[64:96, :]), reads=["krT"], writes=[f"kf{h}"])
            for q in range(NG // 128):
                tt = nb * (NG // 128) + q
                P.op("pe", lambda tt=tt: nc.tensor.matmul(pB[:, 0:256], lhsT=ckvn[:, q * 128:(q + 1) * 128], rhs=wukv[:, 0, :], start=True, stop=True),
                     reads=["ckvn", "wukv0"], writes=["pB"])
                P.op("act", lambda tt=tt: nc.scalar.copy(V[:, tt, :], pB[:, 0:256]), reads=["pB"], writes=["V"])
        LA = 3
        NPE = 5
        pOb = (pA[0], pA[1])
        pOk = ("pA0", "pA1")
        pLb = (pB, pL)
        pLk = ("pB", "pL")
        for hp in range(2):
            meta = []
            for tq in range(NB):
                nkb = (tq + 1) * (NG // 128)
                for kb in range(nkb):
                    for hh in range(2):
                        meta.append(dict(tq=tq, kb=kb, hh=hh, nkb=nkb, zb=len(meta) % 2, eb=len(meta) % NPE))

            def stageA(m):
                tq, kb, hh, zb, eb = m["tq"], m["kb"], m["hh"], m["zb"], m["eb"]
                h = hp * 2 + hh
                qcols = slice(tq * NG, (tq + 1) * NG)
                P.op("pe", lambda: nc.tensor.matmul(pS[zb][:, 0:NG], lhsT=kf[h][:, kb * 128:(kb + 1) * 128], rhs=qf[h][:, qcols], start=True, stop=True),
                     reads=[f"kf{h}", f"qf{h}"], writes=[f"pS{zb}"])
                P.op("act", lambda: nc.scalar.activation(out=pe_[eb][:], in_=pS[zb][:, 0:NG], func=AF.Exp, scale=SCALE), reads=[f"pS{zb}"], writes=[f"pe{eb}"])
                r = kb - tq * (NG // 128)
                if r >= 0:
                    P.op("pool", lambda: nc.gpsimd.tensor_tensor(out=pe_[eb][:], in0=pe_[eb][:], in1=mk[:, 0, r * 512:r * 512 + NG], op=ALU.mult),
                         reads=[f"pe{eb}", "mk0"], writes=[f"pe{eb}"])

            def stageB(m):
                tq, kb, hh, nkb, eb = m["tq"], m["kb"], m["hh"], m["nkb"], m["eb"]
                h = hp * 2 + hh
                qcols = slice(tq * NG, (tq + 1) * NG)
                P.op("pe", lambda: nc.tensor.matmul(pOb[hh][0:64, 0:NG], lhsT=V[:, kb, h * 64:(h + 1) * 64], rhs=pe_[eb][:], start=(kb == 0), stop=(kb == nkb - 1)),
                     reads=["V", f"pe{eb}"], writes=[pOk[hh]])
                P.op("pe", lambda: nc.tensor.matmul(pLb[hh][0:64, 0:NG], lhsT=ones[:, 0:64], rhs=pe_[eb][:], start=(kb == 0), stop=(kb == nkb - 1)),
                     reads=["ones", f"pe{eb}"], writes=[pLk[hh]])
                if kb == nkb - 1:
                    P.op("dve", lambda: nc.vector.reciprocal(out=rec[hh][:], in_=pLb[hh][0:64, 0:NG]), reads=[pLk[hh]], writes=[f"rec{hh}"])
                    P.op("dve", lambda: nc.vector.tensor_tensor(out=oT[hh][:], in0=pOb[hh][0:64, 0:NG], in1=rec[hh][:], op=ALU.mult), reads=[pOk[hh], f"rec{hh}"], writes=[f"oT{hh}"])
                    P.dma("sp", c.mixT[768 + h * 64: 768 + (h + 1) * 64, qcols], oT[hh][:], reads=[f"oT{hh}"], writes=["mixT"], chan=f"d_oT{hh}")

            n = len(meta)
            for i in range(min(LA, n)):
                stageA(meta[i])
            for i in range(n):
                if i + LA < n:
                    stageA(meta[i + LA])
                stageB(meta[i])
    P.barrier()


def make_consts():
    CO = {}
    cols = []
    off = 0

    def add(name, arr):
        nonlocal off
        a = np.zeros((128, arr.shape[1]), np.float32)
        a[:arr.shape[0]] = arr
        CO[name] = off
        off += arr.shape[1]
        cols.append(a)

    add("ident", np.eye(128, dtype=np.float32))
    add("ones", np.ones((128, 128), np.float32))
    s = np.arange(128)[:, None]
    t = np.arange(512)[None, :]
    add("maskle", np.concatenate([((r * 128 + s) <= t).astype(np.float32) for r in range(4)], 1))
    add("masklt", np.concatenate([((r * 128 + s) < t).astype(np.float32) for r in range(4)], 1))
    j = np.arange(128)[:, None]
    i = np.arange(128)[None, :]
    add("trineg", -(j >= i).astype(np.float32))
    add("compneg", -(j < i).astype(np.float32))
    add("triinc", (j <= i).astype(np.float32))
    add("upincl", (i >= j).astype(np.float32))
    add("upstrict", (i > j).astype(np.float32))
    inv = (1.0 / (10000.0 ** (np.arange(0, 32, 2, dtype=np.float32) / 32))).astype(np.float32)
    rope = np.zeros((32, 2), np.float32)
    rope[:, 0] = np.concatenate([inv, inv])
    rope[:, 1] = np.concatenate([-np.ones(16), np.ones(16)])
    add("rope", rope)
    return np.concatenate(cols, 1), CO


def prep_weights(inp, L):
    f = lambda a: np.ascontiguousarray(a, dtype=np.float32)
    w_in = inp["mix_w_in"]
    out = {}
    z64 = np.zeros(w_in.shape[:2] + (64,), np.float32)
    out["w_kr"] = f(np.concatenate([z64, w_in[:, :, C_KR:C_KR + 32], z64, w_in[:, :, C_KR + 16:C_KR + 32], w_in[:, :, C_KR:C_KR + 16]], -1))
    uq = inp["mla_w_uq"]
    parts = []
    for h in range(4):
        b = h * 96
        parts += [uq[:, :, b:b + 64], uq[:, :, b + 64:b + 96], np.zeros(uq.shape[:2] + (64,), np.float32), uq[:, :, b + 80:b + 96], uq[:, :, b + 64:b + 80]]
    out["w_uq_p"] = f(np.concatenate(parts, -1))
    ukv = inp["mla_w_ukv"]
    out["w_ukv_k"] = f(np.concatenate([ukv[:, :, h * 128:h * 128 + 64] for h in range(4)], -1))
    out["w_ukv_v"] = f(np.concatenate([ukv[:, :, h * 128 + 64:h * 128 + 128] for h in range(4)], -1))
    out["q_norm_wT"] = f(inp["mla_q_norm_w"].reshape(L, 2, 128).transpose(0, 2, 1))
    out["kv_norm_wT"] = f(inp["mla_kv_norm_w"].reshape(L, 1, 128).transpose(0, 2, 1))
    out["conv_wT"] = f(inp["gdn_conv_w"].reshape(L, 4, 12, 128).transpose(0, 3, 2, 1))
    for k in ("ffa_w_in", "ffa_w_out", "mix_w_in", "gdn_a_log", "gdn_dt_bias", "gdn_norm_w", "mix_w_o", "ffb_w_in", "ffb_w_out",
              "ln_g", "ln_b", "ple_w_gate", "ple_w_proj"):
        out[k] = f(inp[k])
    return out


WSHAPES = lambda L, dff: {
    "ffa_w_in": [L, D, 2 * dff], "ffa_w_out": [L, dff, D], "mix_w_in": [L, D, IN_TOTAL], "w_kr": [L, D, 192],
    "w_uq_p": [L, 256, 768], "w_ukv_k": [L, 128, 256], "w_ukv_v": [L, 128, 256], "q_norm_wT": [L, 128, 2], "kv_norm_wT": [L, 128, 1],
    "conv_wT": [L, 128, 12, 4], "gdn_a_log": [L, 8], "gdn_dt_bias": [L, 8], "gdn_norm_w": [L, 64], "mix_w_o": [L, D, D],
    "ffb_w_in": [L, D, 2 * dff], "ffb_w_out": [L, dff, D], "ln_g": [L, 3, D], "ln_b": [L, 3, D], "ple_w_gate": [L, D, D], "ple_w_proj": [L, PLE, D],
}


def build(S, L, phases, dff=DFF, debug=False):
    nc = bass.Bass("TRN2", target_bir_lowering=False)
    c = Ctx()
    c.nc, c.S, c.DFF, c.L = nc, S, dff, L
    consts_np, c.CO = make_consts()
    din = lambda n, s, d=F32: nc.dram_tensor(n, s, d, kind="ExternalInput").ap()
    c.x = din("x", [S, D])
    c.p = din("p", [L, S, PLE])
    c.pos = din("pos", [1, S], I32)
    c.consts = din("consts", list(consts_np.shape))
    for k, shp in WSHAPES(L, dff).items():
        setattr(c, k, din(k, shp))
    kind = "ExternalOutput" if debug else "Internal"
    scr = lambda n, s, d: nc.dram_tensor(n, s, d, kind=kind).ap()
    c.hb = [scr("hb0", [S, D], F32), scr("hb1", [S, D], F32)]
    c.h1T = scr("h1T", [D, S], BF16)
    c.mixT = scr("mixT", [D, S], BF16)
    c.ropeT = scr("ropeT", [2, 32, S], F32)
    c.out = nc.dram_tensor("out", [S, D], F32, kind="ExternalOutput").ap()
    with ExitStack() as es:
        c.P = Prog(nc, es)
        for ph in phases:
            ph(c)
        c.P.barrier()
        print("ninst", c.P.ninst, "nwaits", c.P.nwaits, "nsem", c.P.nsem)
    return nc, consts_np


def phase_sb(c, l):
    nc, P, S = c.nc, c.P, c.S
    NG = min(512, S)
    NB = S // NG
    NTT = S // 128
    R = NG // 128
    H = 4
    w_in = c.mix_w_in[l]
    CO = c.CO
    with ExitStack() as pes:
        sb, ps = _tiles(nc, pes)
        xTb = [sb(f"xT{i}", [128, 8, NG], BF16) for i in range(2)]
        wq = sb("wq", [128, 8, 256], BF16)
        wk = sb("wk", [128, 8, 256], BF16)
        wv = sb("wv", [128, 8, 256], BF16)
        qT = [sb(f"qT{j}", [128, S], BF16) for j in range(2)]
        kT = [sb(f"kT{j}", [128, S], BF16) for j in range(2)]
        V = sb("V", [128, NTT, 256], BF16)
        cf = sb("cf", [128, 256], F32)
        tri = sb("tri", [128, 128], BF16)
        comp = sb("comp", [128, 128], BF16)
        mkf = sb("mkf", [128, 4, 512], F32)
        e = [sb(f"e{i}", [128, NG], F32) for i in range(5)]
        sp = [sb(f"sp{i}", [128, NG], BF16) for i in range(10)]
        eC = [sb(f"eC{i}", [128, NG], F32) for i in range(2)]
        A = [sb(f"A{i}", [128, NG], BF16) for i in range(4)]
        oT = [sb(f"oT{i}", [64, NG], BF16) for i in range(2)]
        pZ = [ps(f"pZ{i}", [128, 512], F32) for i in range(2)]
        pC = [ps(f"pC{i}", [128, 512], F32) for i in range(2)]
        pO = [ps(f"pO{i}", [128, 512], F32) for i in range(2)]
        pX = ps("pX", [128, 512], F32)
        load_w(c, wq, w_in, "wq", 8, C_SQ, C_SQ + 256)
        load_w(c, wk, w_in, "wk", 8, C_SK, C_SK + 256)
        load_w(c, wv, w_in, "wv", 8, C_SV, C_SV + 256)
        P.dma("sp", cf[:, 0:128], c.consts[:, CO["trineg"]:CO["trineg"] + 128], writes=["cf"])
        P.dma("sp", cf[:, 128:256], c.consts[:, CO["compneg"]:CO["compneg"] + 128], writes=["cf"])
        P.op("dve", lambda: nc.vector.tensor_copy(tri[:], cf[:, 0:128]), reads=["cf"], writes=["tri"])
        P.op("dve", lambda: nc.vector.tensor_copy(comp[:], cf[:, 128:256]), reads=["cf"], writes=["comp"])
        P.dma("sp", mkf[:].rearrange("p a b -> p (a b)"), c.consts[:, CO["masklt"]:CO["masklt"] + 2048], writes=["mkf"])
        for nb in range(NB):
            cols = slice(nb * NG, (nb + 1) * NG)
            xT = xTb[nb % 2]
            xk = f"xT{nb % 2}"
            P.dma("sp", xT[:], c.h1T[:, cols].rearrange("(k p) t -> p k t", p=128), writes=[xk])
            for (w_, dst, nm) in ((wq, qT, "qT"), (wk, kT, "kT")):
                for j in range(2):
                    for kc in range(8):
                        P.op("pe", lambda w_=w_, kc=kc, j=j: nc.tensor.matmul(pX[:, 0:NG], lhsT=w_[:, kc, j * 128:(j + 1) * 128], rhs=xT[:, kc, :],
                                                                             start=(kc == 0), stop=(kc == 7)), reads=[xk] + [f"wq{kc}", f"wk{kc}"], writes=["pX"])
                    P.op("act", lambda dst=dst, j=j: nc.scalar.copy(dst[j][:, cols], pX[:, 0:NG]), reads=["pX"], writes=[f"{nm}{j}"])
            for q in range(R):
                tt = nb * R + q
                for kc in range(8):
                    P.op("pe", lambda q=q, kc=kc: nc.tensor.matmul(pX[:, 0:256], lhsT=xT[:, kc, q * 128:(q + 1) * 128], rhs=wv[:, kc, :],
                                                                    start=(kc == 0), stop=(kc == 7)), reads=[xk, f"wv{kc}"], writes=["pX"])
                P.op("dve", lambda tt=tt: nc.vector.tensor_copy(V[:, tt, :], pX[:, 0:256]), reads=["pX"], writes=["V"])
        LA = 3
        NE, NSP = 5, 5
        for hp in range(2):
            items = []
            for tq in range(NB):
                nkb = (tq + 1) * R
                for kb in range(nkb - 1, -1, -1):
                    for hh in range(2):
                        items.append((tq, kb, hh, nkb))
            percnt = [0, 0]
            meta = []
            for idx, (tq, kb, hh, nkb) in enumerate(items):
                itn = percnt[hh]
                percnt[hh] += 1
                meta.append(dict(tq=tq, kb=kb, hh=hh, nkb=nkb, zb=idx % 2, eb=idx % NE, ab=idx % 4, spb=hh * NSP + itn % NSP, spp=hh * NSP + (itn - 1) % NSP))

            def stageA(m):
                tq, kb, hh = m["tq"], m["kb"], m["hh"]
                qcols = slice(tq * NG, (tq + 1) * NG)
                pb, j, zb, eb, spb = hh * 64, hp, m["zb"], m["eb"], m["spb"]
                r = kb - tq * R
                P.op("pe", lambda: nc.tensor.matmul(pZ[zb][:, 0:NG], lhsT=kT[j][pb:pb + 64, kb * 128:(kb + 1) * 128], rhs=qT[j][pb:pb + 64, qcols], start=True, stop=True),
                     reads=[f"kT{j}", f"qT{j}"], writes=[f"pZ{zb}"])
                P.op("act", lambda: nc.scalar.activation(out=e[eb][:], in_=pZ[zb][:, 0:NG], func=AF.Exp, scale=0.125), reads=[f"pZ{zb}"], writes=[f"e{eb}"])
                if r >= 0:
                    P.op("dve", lambda: nc.vector.tensor_tensor(out=e[eb][:], in0=e[eb][:], in1=mkf[:, r, 0:NG], op=ALU.mult), reads=[f"e{eb}", "mkf"], writes=[f"e{eb}"])
                P.op("act", lambda: nc.scalar.activation(out=sp[spb][:], in_=e[eb][:], func=AF.Ln, bias=1.0, scale=1.0), reads=[f"e{eb}"], writes=[f"sp{spb}"])

            def stageB(m):
                tq, kb, hh, nkb = m["tq"], m["kb"], m["hh"], m["nkb"]
                qcols = slice(tq * NG, (tq + 1) * NG)
                h = hp * 2 + hh
                eb, ab, spb, spp = m["eb"], m["ab"], m["spb"], m["spp"]
                first = kb == nkb - 1
                if not first:
                    P.op("pe", lambda: nc.tensor.matmul(pC[hh][:, 0:NG], lhsT=comp[:], rhs=sp[spp][:], start=False, stop=False, skip_group_check=True),
                         reads=["comp", f"sp{spp}"], writes=[f"pC{hh}"])
                P.op("pe", lambda: nc.tensor.matmul(pC[hh][:, 0:NG], lhsT=tri[:], rhs=sp[spb][:], start=first, stop=True, skip_group_check=True),
                     reads=["tri", f"sp{spb}"], writes=[f"pC{hh}"])
                P.op("act", lambda: nc.scalar.activation(out=eC[hh][:], in_=pC[hh][:, 0:NG], func=AF.Exp), reads=[f"pC{hh}"], writes=[f"eC{hh}"])
                P.op("dve", lambda: nc.vector.tensor_tensor(out=A[ab][:], in0=e[eb][:], in1=eC[hh][:], op=ALU.mult), reads=[f"e{eb}", f"eC{hh}"], writes=[f"A{ab}"])

            def stageB2(m):
                tq, kb, hh, nkb = m["tq"], m["kb"], m["hh"], m["nkb"]
                qcols = slice(tq * NG, (tq + 1) * NG)
                h = hp * 2 + hh
                ab = m["ab"]
                first = kb == nkb - 1
                P.op("pe", lambda: nc.tensor.matmul(pO[hh][0:64, 0:NG], lhsT=V[:, kb, h * 64:(h + 1) * 64], rhs=A[ab][:], start=first, stop=(kb == 0)),
                     reads=["V", f"A{ab}"], writes=[f"pO{hh}"])
                if kb == 0:
                    P.op("dve", lambda: nc.vector.tensor_copy(oT[hh][:], pO[hh][0:64, 0:NG]), reads=[f"pO{hh}"], writes=[f"oT{hh}"])
                    P.dma("sp", c.mixT[512 + h * 64: 512 + (h + 1) * 64, qcols], oT[hh][:], reads=[f"oT{hh}"], writes=["mixT"], chan=f"d_oT{hh}")

            n = len(meta)
            for i in range(min(LA, n)):
                stageA(meta[i])
            for i in range(n):
                if i + LA < n:
                    stageA(meta[i + LA])
                stageB(meta[i])
                if i >= 2:
                    stageB2(meta[i - 2])
            for i in range(max(0, n - 2), n):
                stageB2(meta[i])
    P.barrier()


def phase_gdn(c, l):
    nc, P, S = c.nc, c.P, c.S
    NG = min(512, S)
    NB = S // NG
    R = NG // 128
    w_in = c.mix_w_in[l]
    CO = c.CO
    with ExitStack() as pes:
        sb, ps = _tiles(nc, pes)
        xTb = [sb(f"xT{i}", [128, 8, NG], BF16) for i in range(2)]
        wqkv = sb("wqkv", [128, 8, 1536], BF16)
        wz = sb("wz", [128, 8, 512], BF16)
        wab = sb("wab", [128, 8, 16], BF16)
        cw = sb("cw", [128, 12, 4], F32)
        ident = sb("ident", [128, 128], F32)
        identb = sb("identb", [128, 128], BF16)
        onesf = sb("onesf", [128, 128], F32)
        triinc = sb("triinc", [128, 128], F32)
        upi = sb("upi", [128, 128], F32)
        ups = sb("ups", [128, 128], F32)
        dtb = sb("dtb", [128, 8], F32)
        nea = sb("nea", [128, 8], F32)
        nw = sb("nw", [128, 64], F32)
        halo = sb("halo", [128, 12, 3], F32)
        raw = [sb(f"raw{i}", [128, NG + 3], F32) for i in range(2)]
        acc = [sb(f"acc{i}", [128, NG], F32) for i in range(2)]
        csl = sb("csl", [128, 12, NG], F32)
        QKV = sb("QKV", [128, 1536], F32)
        sqt = sb("sqt", [128, 1024], F32)
        ss = sb("ss", [128, 16], F32)
        kqT = sb("kqT", [64, 8, 2, 128], BF16)
        gab = sb("gab", [128, 16], F32)
        beta = sb("beta", [128, 8], F32)
        nbeta = sb("nbeta", [128, 8], F32)
        g = sb("g", [128, 8], F32)
        gc = sb("gc", [128, 8], F32)
        gam = sb("gam", [128, 8], F32)
        kap = sb("kap", [128, 8], F32)
        gl = sb("gl", [128, 8], F32)
        nbk = sb("nbk", [128, 8], F32)
        G1 = sb("G1", [128, 8, 128], F32)
        dm = sb("dm", [128, 8, 128], F32)
        dTs = sb("dTs", [128, 8, 128], F32)
        dTi = sb("dTi", [128, 8, 128], F32)
        CD = F32
        A0 = sb("A0", [128, 8, 128], CD)
        qkTp = sb("qkTp", [128, 8, 128], BF16)
        Am = [[sb(f"Am{g}{i}", [128, 4, 128], BF16) for i in range(2)] for g in range(2)]
        Bm = [[sb(f"Bm{g}{i}", [128, 4, 128], BF16) for i in range(2)] for g in range(2)]
        Amf = [[sb(f"Amf{g}{i}", [128, 4, 128], F32) for i in range(2)] for g in range(2)]
        Bmf = [[sb(f"Bmf{g}{i}", [128, 4, 128], F32) for i in range(2)] for g in range(2)]
        Yb = [sb(f"Yb{g}", [128, 4, 128], BF16) for g in range(2)]
        Yf = [sb(f"Yf{g}", [128, 4, 128], F32) for g in range(2)]
        Xp = sb("Xp", [128, 8, 128], BF16)
        St = sb("St", [64, 8, 64], F32)
        Sb = sb("Sb", [64, 8, 64], BF16)
        rp = sb("rp", [128, 8, 64], BF16)
        vt = sb("vt", [128, 512], BF16)
        kd = sb("kd", [128, 512], BF16)
        o1 = sb("o1", [128, 8, 64], F32)
        of = sb("of", [128, 8, 64], F32)
        zs = sb("zs", [128, 512], F32)
        og = sb("og", [128, 512], F32)
        oTs = sb("oTs", [128, 4, 128], BF16)
        B = [ps(f"B{i}", [128, 512], F32) for i in range(8)]
        bk = lambda i: f"B{i}"

        load_w(c, wqkv, w_in, "wqkv", 8, C_GQ, C_GQ + 1536)
        load_w(c, wz, w_in, "wz", 8, C_GZ, C_GZ + 512)
        load_w(c, wab, w_in, "wab", 8, C_GA, C_GA + 16)
        P.dma("sp", cw[:], c.conv_wT[l], writes=["cw"])
        for (t_, nm) in ((ident, "ident"), (onesf, "ones"), (triinc, "triinc"), (upi, "upincl"), (ups, "upstrict")):
            P.dma("sp", t_[:], c.consts[:, CO[nm]:CO[nm] + 128], writes=[nm])
        P.op("dve", lambda: nc.vector.tensor_copy(identb[:], ident[:]), reads=["ident"], writes=["identb"])
        P.dma("sp", dtb[:], c.gdn_dt_bias[l, :].partition_broadcast(128), writes=["dtb"])
        P.dma("sp", nea[:], c.gdn_a_log[l, :].partition_broadcast(128), writes=["nea"])
        P.dma("sp", nw[:], c.gdn_norm_w[l, :].partition_broadcast(128), writes=["nw"])
        P.op("act", lambda: nc.scalar.activation(out=nea[:], in_=nea[:], func=AF.Exp), reads=["nea"], writes=["nea"])
        P.op("dve", lambda: nc.vector.tensor_scalar(out=nea[:], in0=nea[:], scalar1=-1.0, scalar2=None, op0=ALU.mult), reads=["nea"], writes=["nea"])
        P.op("pool", lambda: nc.gpsimd.memset(halo[:], 0.0), writes=["halo"])
        P.op("pool", lambda: nc.gpsimd.memset(St[:], 0.0), writes=["St"])
        P.op("pool", lambda: nc.gpsimd.memset(Sb[:], 0.0), writes=["Sb"])

        def bc_h(ap2d, n, d):
            return ap2d.unsqueeze(2).to_broadcast([128, n, d])

        def bc_m(ap2d, n, d):
            return ap2d.unsqueeze(1).to_broadcast([128, n, d])

        for nb in range(NB):
            cols = slice(nb * NG, (nb + 1) * NG)
            xT = xTb[nb % 2]
            xk = f"xT{nb % 2}"
            P.dma("sp", xT[:], c.h1T[:, cols].rearrange("(k p) t -> p k t", p=128), writes=[xk])
            for ch in range(12):
                rb = ch % 2
                for kc in range(8):
                    P.op("pe", lambda kc=kc, ch=ch: nc.tensor.matmul(B[0][:, 0:NG], lhsT=wqkv[:, kc, ch * 128:(ch + 1) * 128], rhs=xT[:, kc, :],
                                                                    start=(kc == 0), stop=(kc == 7)), reads=[xk, f"wqkv{kc}"], writes=[bk(0)])
                P.op("pool", lambda rb=rb, ch=ch: nc.gpsimd.tensor_copy(raw[rb][:, 0:3], halo[:, ch, :]), reads=["halo"], writes=[f"raw{rb}"])
                P.op("act", lambda rb=rb: nc.scalar.copy(raw[rb][:, 3:NG + 3], B[0][:, 0:NG]), reads=[bk(0)], writes=[f"raw{rb}"])
                P.op("pool", lambda rb=rb, ch=ch: nc.gpsimd.tensor_copy(halo[:, ch, :], raw[rb][:, NG:NG + 3]), reads=[f"raw{rb}"], writes=["halo"])
                P.op("dve", lambda rb=rb, ch=ch: nc.vector.tensor_scalar(out=acc[rb][:], in0=raw[rb][:, 0:NG], scalar1=cw[:, ch, 0:1], scalar2=None, op0=ALU.mult),
                     reads=[f"raw{rb}", "cw"], writes=[f"acc{rb}"])
                for j in range(1, 4):
                    P.op("dve", lambda rb=rb, ch=ch, j=j: nc.vector.scalar_tensor_tensor(out=acc[rb][:], in0=raw[rb][:, j:NG + j], scalar=cw[:, ch, j:j + 1],
                                                                                       in1=acc[rb][:], op0=ALU.mult, op1=ALU.add),
                         reads=[f"raw{rb}", "cw", f"acc{rb}"], writes=[f"acc{rb}"])
                P.op("act", lambda rb=rb, ch=ch: nc.scalar.activation(out=csl[:, ch, :], in_=acc[rb][:], func=AF.Silu), reads=[f"acc{rb}"], writes=[f"csl{ch}"])
            for q in range(R):
                tt = nb * R + q
                tcols = slice(tt * 128, (tt + 1) * 128)
                for grp in range(3):
                    for cc in range(4):
                        ch = grp * 4 + cc
                        P.op("pe", lambda cc=cc, ch=ch, q=q: nc.tensor.transpose(B[1][:, cc * 128:(cc + 1) * 128], csl[:, ch, q * 128:(q + 1) * 128], ident[:]),
                             reads=[f"csl{ch}", "ident"], writes=[bk(1)])
                    if grp % 2 == 0:
                        P.op("act", lambda grp=grp: nc.scalar.copy(QKV[:, grp * 512:(grp + 1) * 512], B[1][:, :]), reads=[bk(1)], writes=[f"QKV{grp}"])
                    else:
                        P.op("dve", lambda grp=grp: nc.vector.tensor_copy(QKV[:, grp * 512:(grp + 1) * 512], B[1][:, :]), reads=[bk(1)], writes=[f"QKV{grp}"])
                P.op("pool", lambda: nc.gpsimd.tensor_tensor(out=sqt[:], in0=QKV[:, 0:1024], in1=QKV[:, 0:1024], op=ALU.mult), reads=["QKV0", "QKV1"], writes=["sqt"])
                P.op("dve", lambda: nc.vector.tensor_reduce(out=ss[:], in_=sqt[:].rearrange("p (h d) -> p h d", d=64), axis=AX.X, op=ALU.add), reads=["sqt"], writes=["ss"])
                P.op("act", lambda: nc.scalar.activation(out=ss[:], in_=ss[:], func=AF.Sqrt, bias=RMS_EPS, scale=1.0), reads=["ss"], writes=["ss"])
                P.op("dve", lambda: nc.vector.reciprocal(out=ss[:], in_=ss[:]), reads=["ss"], writes=["ss"])
                P.op("dve", lambda: nc.vector.tensor_scalar(out=ss[:, 0:8], in0=ss[:, 0:8], scalar1=0.125, scalar2=None, op0=ALU.mult), reads=["ss"], writes=["ss"])
                P.op("dve", lambda: nc.vector.tensor_tensor(out=QKV[:, 0:1024].rearrange("p (h d) -> p h d", d=64), in0=QKV[:, 0:1024].rearrange("p (h d) -> p h d", d=64),
                                                            in1=bc_h(ss[:, 0:16], 16, 64), op=ALU.mult), reads=["QKV0", "QKV1", "ss"], writes=["QKV0", "QKV1"])
                for (src0, slot) in ((512, 0), (0, 1)):
                    for g4 in range(2):
                        for hl in range(4):
                            h = g4 * 4 + hl
                            P.op("pe", lambda hl=hl, h=h, src0=src0: nc.tensor.transpose(B[1][0:64, hl * 128:(hl + 1) * 128], QKV[:, src0 + h * 64: src0 + (h + 1) * 64], ident[:]),
                                 reads=["QKV0", "QKV1", "ident"], writes=[bk(1)])
                        if g4 == 0:
                            P.op("act", lambda slot=slot, g4=g4: nc.scalar.copy(kqT[:, g4 * 4:(g4 + 1) * 4, slot, :], B[1][0:64, :].rearrange("p (a b) -> p a b", a=4)), reads=[bk(1)], writes=["kqT"])
                        else:
                            P.op("dve", lambda slot=slot, g4=g4: nc.vector.tensor_copy(kqT[:, g4 * 4:(g4 + 1) * 4, slot, :], B[1][0:64, :].rearrange("p (a b) -> p a b", a=4)), reads=[bk(1)], writes=["kqT"])
                for kc in range(8):
                    P.op("pe", lambda kc=kc: nc.tensor.matmul(B[2][:, 0:16], lhsT=xT[:, kc, q * 128:(q + 1) * 128], rhs=wab[:, kc, :], start=(kc == 0), stop=(kc == 7)),
                         reads=[xk, f"wab{kc}"], writes=[bk(2)])
                P.op("dve", lambda: nc.vector.tensor_copy(gab[:], B[2][:, 0:16]), reads=[bk(2)], writes=["gab"])
                P.op("act", lambda: nc.scalar.activation(out=beta[:], in_=gab[:, 8:16], func=AF.Exp, scale=-1.0), reads=["gab"], writes=["beta"])
                P.op("dve", lambda: nc.vector.tensor_scalar(out=beta[:], in0=beta[:], scalar1=1.0, scalar2=None, op0=ALU.add), reads=["beta"], writes=["beta"])
                P.op("dve", lambda: nc.vector.reciprocal(out=beta[:], in_=beta[:]), reads=["beta"], writes=["beta"])
                P.op("dve", lambda: nc.vector.tensor_scalar(out=nbeta[:], in0=beta[:], scalar1=-1.0, scalar2=None, op0=ALU.mult), reads=["beta"], writes=["nbeta"])
                P.op("dve", lambda: nc.vector.tensor_tensor(out=g[:], in0=gab[:, 0:8], in1=dtb[:], op=ALU.add), reads=["gab", "dtb"], writes=["g"])
                P.op("act", lambda: nc.scalar.activation(out=g[:], in_=g[:], func=AF.Exp), reads=["g"], writes=["g"])
                P.op("act", lambda: nc.scalar.activation(out=g[:], in_=g[:], func=AF.Ln, bias=1.0, scale=1.0), reads=["g"], writes=["g"])
                P.op("dve", lambda: nc.vector.tensor_tensor(out=g[:], in0=g[:], in1=nea[:], op=ALU.mult), reads=["g", "nea"], writes=["g"])
                P.op("pe", lambda: nc.tensor.matmul(B[2][:, 16:24], lhsT=triinc[:], rhs=g[:], start=True, stop=True), reads=["triinc", "g"], writes=[bk(2)])
                P.op("pe", lambda: nc.tensor.matmul(B[2][:, 24:32], lhsT=onesf[:], rhs=g[:], start=True, stop=True), reads=["ones", "g"], writes=[bk(2)])
                P.op("dve", lambda: nc.vector.tensor_copy(gc[:], B[2][:, 16:24]), reads=[bk(2)], writes=["gc"])
                P.op("dve", lambda: nc.vector.tensor_tensor(out=kap[:], in0=B[2][:, 24:32], in1=gc[:], op=ALU.subtract), reads=[bk(2), "gc"], writes=["kap"])
                P.op("act", lambda: nc.scalar.activation(out=gl[:], in_=B[2][:, 24:32], func=AF.Exp), reads=[bk(2)], writes=["gl"])
                P.op("act", lambda: nc.scalar.activation(out=kap[:], in_=kap[:], func=AF.Exp), reads=["kap"], writes=["kap"])
                P.op("act", lambda: nc.scalar.activation(out=gam[:], in_=gc[:], func=AF.Exp), reads=["gc"], writes=["gam"])
                P.op("dve", lambda: nc.vector.tensor_tensor(out=nbk[:], in0=nbeta[:], in1=kap[:], op=ALU.mult), reads=["nbeta", "kap"], writes=["nbk"])
                P.op("pool", lambda: nc.gpsimd.tensor_tensor(out=G1[:], in0=bc_m(triinc[:], 8, 128), in1=bc_h(g[:], 8, 128), op=ALU.mult), reads=["triinc", "g"], writes=["G1"])
                for half in range(2):
                    P.op("pe", lambda half=half: nc.tensor.matmul(B[3 + half][:, :], lhsT=onesf[:], rhs=G1[:, half * 4:(half + 1) * 4, :].rearrange("p a b -> p (a b)"),
                                                                 start=True, stop=True), reads=["ones", "G1"], writes=[bk(3 + half)])
                    P.op("dve", lambda half=half: nc.vector.tensor_tensor(out=dm[:, half * 4:(half + 1) * 4, :], in0=B[3 + half][:, :].rearrange("p (a b) -> p a b", a=4),
                                                                         in1=bc_h(gc[:, half * 4:(half + 1) * 4], 4, 128), op=ALU.subtract),
                         reads=[bk(3 + half), "gc"], writes=["dm"])
                P.op("pool", lambda: nc.gpsimd.tensor_scalar(out=dm[:], in0=dm[:], scalar1=0.0, scalar2=None, op0=ALU.min), reads=["dm"], writes=["dm"])
                P.op("act", lambda: nc.scalar.activation(out=dm[:], in_=dm[:], func=AF.Exp), reads=["dm"], writes=["dm"])
                P.op("pool", lambda: nc.gpsimd.tensor_tensor(out=dTs[:], in0=dm[:], in1=bc_m(ups[:], 8, 128), op=ALU.mult), reads=["dm", "upstrict"], writes=["dTs"])
                P.op("pool", lambda: nc.gpsimd.tensor_tensor(out=dTi[:], in0=dm[:], in1=bc_m(upi[:], 8, 128), op=ALU.mult), reads=["dm", "upincl"], writes=["dTi"])
                for j2 in range(4):
                    for hh in range(2):
                        h = j2 * 2 + hh
                        P.op("pe", lambda h=h, hh=hh: nc.tensor.matmul(B[7][:, hh * 256:(hh + 1) * 256], lhsT=kqT[:, h, 0, :],
                                                                     rhs=kqT[:, h, :, :].rearrange("p a b -> p (a b)"), start=True, stop=True),
                             reads=["kqT"], writes=[bk(7)])
                    for hh in range(2):
                        h = j2 * 2 + hh
                        P.op("dve", lambda hh=hh, h=h: nc.vector.scalar_tensor_tensor(out=A0[:, h, :], in0=B[7][:, hh * 256: hh * 256 + 128], scalar=beta[:, h:h + 1],
                                                                                    in1=dTs[:, h, :], op0=ALU.mult, op1=ALU.mult),
                             reads=[bk(7), "beta", "dTs"], writes=["A0"])
                        P.op("dve", lambda hh=hh, h=h: nc.vector.scalar_tensor_tensor(out=qkTp[:, h, :], in0=B[7][:, hh * 256 + 128: hh * 256 + 256], scalar=nbeta[:, h:h + 1],
                                                                                    in1=dTi[:, h, :], op0=ALU.mult, op1=ALU.mult),
                             reads=[bk(7), "nbeta", "dTi"], writes=["qkTp"])
                NF = 4
                GB = ((5, 6, 7), (3, 4, 0))
                for g4 in range(2):
                    hs = slice(g4 * 4, (g4 + 1) * 4)
                    bB = GB[g4][1]
                    for hl in range(4):
                        P.op("pe", lambda hl=hl, g4=g4, bB=bB: nc.tensor.matmul(B[bB][:, hl * 128:(hl + 1) * 128], lhsT=A0[:, g4 * 4 + hl, :], rhs=ident[:], start=True, stop=True),
                             reads=["A0", "ident"], writes=[bk(bB)])
                    P.op("dve", lambda g4=g4, bB=bB: nc.vector.tensor_copy(Bmf[g4][0][:].rearrange("p a b -> p (a b)"), B[bB][:, :]), reads=[bk(bB)], writes=[f"Bmf{g4}0"])
                    P.op("pool", lambda hs=hs, g4=g4: nc.gpsimd.tensor_tensor(out=Yf[g4][:], in0=bc_m(ident[:], 4, 128), in1=A0[:, hs, :], op=ALU.subtract),
                         reads=["ident", "A0"], writes=[f"Yf{g4}"])
                for k in range(1, 7):
                    pv, cu = (k - 1) % 2, k % 2
                    f32lvl = k <= NF
                    for g4 in range(2):
                        hs = slice(g4 * 4, (g4 + 1) * 4)
                        bA, bB, bY = GB[g4]
                        if k == 1:
                            Aprev = lambda hl, g4=g4: A0[:, g4 * 4 + hl, :]
                            akey = "A0"
                        elif f32lvl:
                            Aprev = lambda hl, g4=g4, pv=pv: Amf[g4][pv][:, hl, :]
                            akey = f"Amf{g4}{pv}"
                        else:
                            Aprev = lambda hl, g4=g4, pv=pv: Am[g4][pv][:, hl, :]
                            akey = f"Am{g4}{pv}"
                        Bprev = (lambda hl, g4=g4, pv=pv: Bmf[g4][pv][:, hl, :]) if f32lvl else (lambda hl, g4=g4, pv=pv: Bm[g4][pv][:, hl, :])
                        bkey = f"Bmf{g4}{pv}" if f32lvl else f"Bm{g4}{pv}"
                        if k <= 5:
                            for hl in range(4):
                                P.op("pe", lambda hl=hl, Aprev=Aprev, Bprev=Bprev, bA=bA: nc.tensor.matmul(B[bA][:, hl * 128:(hl + 1) * 128], lhsT=Bprev(hl), rhs=Aprev(hl), start=True, stop=True),
                                     reads=[bkey, akey], writes=[bk(bA)])
                        if k <= NF:
                            P.op("act", lambda cu=cu, g4=g4, bA=bA: nc.scalar.copy(Amf[g4][cu][:].rearrange("p a b -> p (a b)"), B[bA][:, :]), reads=[bk(bA)], writes=[f"Amf{g4}{cu}"])
                        if NF <= k <= 5:
                            P.op("dve", lambda cu=cu, g4=g4, bA=bA: nc.vector.tensor_copy(Am[g4][cu][:].rearrange("p a b -> p (a b)"), B[bA][:, :]), reads=[bk(bA)], writes=[f"Am{g4}{cu}"])
                        for hl in range(4):
                            P.op("pe", lambda hl=hl, Aprev=Aprev, Bprev=Bprev, bB=bB: nc.tensor.matmul(B[bB][:, hl * 128:(hl + 1) * 128], lhsT=Aprev(hl), rhs=Bprev(hl), start=True, stop=True),
                                 reads=[bkey, akey], writes=[bk(bB)])
                        if k <= NF:
                            P.op("dve", lambda cu=cu, g4=g4, bB=bB: nc.vector.tensor_copy(Bmf[g4][cu][:].rearrange("p a b -> p (a b)"), B[bB][:, :]), reads=[bk(bB)], writes=[f"Bmf{g4}{cu}"])
                        if k >= NF:
                            P.op("act", lambda cu=cu, g4=g4, bB=bB: nc.scalar.copy(Bm[g4][cu][:].rearrange("p a b -> p (a b)"), B[bB][:, :]), reads=[bk(bB)], writes=[f"Bm{g4}{cu}"])
                        for hl in range(4):
                            if f32lvl:
                                P.op("pe", lambda hl=hl, cu=cu, g4=g4, bY=bY: nc.tensor.matmul(B[bY][:, hl * 128:(hl + 1) * 128], lhsT=Bmf[g4][cu][:, hl, :], rhs=Yf[g4][:, hl, :], start=True, stop=True),
                                     reads=[f"Bmf{g4}{cu}", f"Yf{g4}"], writes=[bk(bY)])
                            else:
                                P.op("pe", lambda hl=hl, cu=cu, g4=g4, bY=bY: nc.tensor.matmul(B[bY][:, hl * 128:(hl + 1) * 128], lhsT=Bm[g4][cu][:, hl, :], rhs=Yb[g4][:, hl, :], start=True, stop=True),
                                     reads=[f"Bm{g4}{cu}", f"Yb{g4}"], writes=[bk(bY)])
                        if k < 6:
                            P.op("dve", lambda g4=g4, bY=bY: nc.vector.tensor_tensor(out=Yf[g4][:].rearrange("p a b -> p (a b)"), in0=B[bY][:, :], in1=Yf[g4][:].rearrange("p a b -> p (a b)"), op=ALU.add),
                                 reads=[bk(bY), f"Yf{g4}"], writes=[f"Yf{g4}"])
                            if k >= NF:
                                P.op("act", lambda g4=g4: nc.scalar.copy(Yb[g4][:].rearrange("p a b -> p (a b)"), Yf[g4][:].rearrange("p a b -> p (a b)")), reads=[f"Yf{g4}"], writes=[f"Yb{g4}"])
                        else:
                            P.op("dve", lambda g4=g4, bY=bY, hs=hs: nc.vector.tensor_tensor(out=Xp[:, hs, :].rearrange("p a b -> p (a b)"), in0=B[bY][:, :], in1=Yf[g4][:].rearrange("p a b -> p (a b)"), op=ALU.add),
                                 reads=[bk(bY), f"Yf{g4}"], writes=["Xp"])
                for h in range(8):
                    j2, pb = h // 2, (h % 2) * 64
                    bnk = 5 + h // 4
                    col = (h % 4) * 128
                    for slot in range(2):
                        P.op("pe", lambda bnk=bnk, col=col, slot=slot, h=h: nc.tensor.matmul(B[bnk][:, col + slot * 64: col + slot * 64 + 64], lhsT=kqT[:, h, slot, :],
                                                                                           rhs=Sb[:, h, :], start=True, stop=True),
                             reads=["kqT", "Sb"], writes=[bk(bnk)])
                for h in range(8):
                    bnk, col = 5 + h // 4, (h % 4) * 128
                    P.op("dve", lambda h=h, bnk=bnk, col=col: nc.vector.scalar_tensor_tensor(out=rp[:, h, :], in0=B[bnk][:, col:col + 64], scalar=gam[:, h:h + 1],
                                                                                           in1=QKV[:, 1024 + h * 64: 1024 + (h + 1) * 64], op0=ALU.mult, op1=ALU.subtract),
                         reads=[bk(bnk), "gam", "QKV2"], writes=["rp"])
                    P.op("act", lambda h=h, bnk=bnk, col=col: nc.scalar.activation(out=o1[:, h, :], in_=B[bnk][:, col + 64:col + 128], func=AF.Identity, scale=gam[:, h:h + 1]),
                         reads=[bk(bnk), "gam"], writes=["o1"])
                for h in range(8):
                    P.op("pe", lambda h=h: nc.tensor.matmul(B[4][:, h * 64:(h + 1) * 64], lhsT=Xp[:, h, :], rhs=rp[:, h, :], start=True, stop=True),
                         reads=["Xp", "rp"], writes=[bk(4)])
                P.op("act", lambda: nc.scalar.copy(vt[:], B[4][:, :]), reads=[bk(4)], writes=["vt"])
                P.op("pool", lambda: nc.gpsimd.tensor_tensor(out=kd[:].rearrange("p (h d) -> p h d", d=64), in0=QKV[:, 512:1024].rearrange("p (h d) -> p h d", d=64),
                                                            in1=bc_h(nbk[:], 8, 64), op=ALU.mult), reads=["QKV1", "nbk"], writes=["kd"])
                for h in range(8):
                    P.op("pe", lambda h=h: nc.tensor.matmul(B[3][:, h * 64:(h + 1) * 64], lhsT=qkTp[:, h, :], rhs=vt[:, h * 64:(h + 1) * 64], start=True, stop=True),
                         reads=["qkTp", "vt"], writes=[bk(3)])
                for h in range(8):
                    j2 = h // 2
                    P.op("pe", lambda h=h: nc.tensor.matmul(B[2][0:64, h * 64:(h + 1) * 64], lhsT=kd[:, h * 64:(h + 1) * 64], rhs=vt[:, h * 64:(h + 1) * 64], start=True, stop=True),
                         reads=["kd", "vt"], writes=[bk(2)])
                P.op("dve", lambda: nc.vector.tensor_tensor(out=of[:].rearrange("p a b -> p (a b)"), in0=B[3][:, :], in1=o1[:].rearrange("p a b -> p (a b)"), op=ALU.add),
                     reads=[bk(3), "o1"], writes=["of"])
                P.op("pool", lambda: nc.gpsimd.tensor_tensor(out=St[:], in0=St[:], in1=gl[0:64, :].unsqueeze(2).to_broadcast([64, 8, 64]), op=ALU.mult), reads=["St", "gl"], writes=["St"])
                P.op("dve", lambda: nc.vector.tensor_tensor(out=St[:].rearrange("p a b -> p (a b)"), in0=St[:].rearrange("p a b -> p (a b)"), in1=B[2][0:64, :], op=ALU.add),
                     reads=["St", bk(2)], writes=["St"])
                P.op("act", lambda: nc.scalar.copy(Sb[:].rearrange("p a b -> p (a b)"), St[:].rearrange("p a b -> p (a b)")), reads=["St"], writes=["Sb"])
                for kc in range(8):
                    P.op("pe", lambda kc=kc: nc.tensor.matmul(B[0][:, :], lhsT=xT[:, kc, q * 128:(q + 1) * 128], rhs=wz[:, kc, :], start=(kc == 0), stop=(kc == 7)),
                         reads=[xk, f"wz{kc}"], writes=[bk(0)])
                P.op("act", lambda: nc.scalar.activation(out=zs[:], in_=B[0][:, :], func=AF.Silu), reads=[bk(0)], writes=["zs"])
                P.op("pool", lambda: nc.gpsimd.tensor_tensor(out=sqt[:, 0:512], in0=of[:].rearrange("p a b -> p (a b)"), in1=of[:].rearrange("p a b -> p (a b)"), op=ALU.mult),
                     reads=["of"], writes=["sqt"])
                P.op("dve", lambda: nc.vector.tensor_reduce(out=ss[:, 0:8], in_=sqt[:, 0:512].rearrange("p (h d) -> p h d", d=64), axis=AX.X, op=ALU.add), reads=["sqt"], writes=["ss"])
                P.op("act", lambda: nc.scalar.activation(out=ss[:, 0:8], in_=ss[:, 0:8], func=AF.Sqrt, bias=RMS_EPS, scale=1.0 / 64), reads=["ss"], writes=["ss"])
                P.op("dve", lambda: nc.vector.reciprocal(out=ss[:, 0:8], in_=ss[:, 0:8]), reads=["ss"], writes=["ss"])
                P.op("dve", lambda: nc.vector.tensor_tensor(out=og[:].rearrange("p (h d) -> p h d", d=64), in0=of[:], in1=bc_h(ss[:, 0:8], 8, 64), op=ALU.mult),
                     reads=["of", "ss"], writes=["og"])
                P.op("pool", lambda: nc.gpsimd.tensor_tensor(out=og[:].rearrange("p (h d) -> p h d", d=64), in0=og[:].rearrange("p (h d) -> p h d", d=64), in1=bc_m(nw[:], 8, 64), op=ALU.mult),
                     reads=["og", "nw"], writes=["og"])
                P.op("pool", lambda: nc.gpsimd.tensor_tensor(out=og[:], in0=og[:], in1=zs[:], op=ALU.mult), reads=["og", "zs"], writes=["og"])
                for j2 in range(4):
                    P.op("pe", lambda j2=j2: nc.tensor.transpose(B[1][:, j2 * 128:(j2 + 1) * 128], og[:, j2 * 128:(j2 + 1) * 128], ident[:]), reads=["og", "ident"], writes=[bk(1)])
                P.op("act", lambda: nc.scalar.copy(oTs[:].rearrange("p a b -> p (a b)"), B[1][:, :]), reads=[bk(1)], writes=["oTs"])
                P.dma("sp", c.mixT[0:512, tcols].rearrange("(a p) t -> p a t", p=128), oTs[:], reads=["oTs"], writes=["mixT"], chan="d_oTs")
    P.barrier()


def phase_outproj(c, l, src, dst):
    nc, P, S = c.nc, c.P, c.S
    with ExitStack() as pes:
        sb, ps = _tiles(nc, pes)
        wo = sb("wo", [128, 8, D], BF16)
        gb = sb("gb", [128, 2, D], F32)
        xs = [sb(f"x{i}", [128, D], F32) for i in range(2)]
        ms = [sb(f"m{i}", [128, 8, 128], BF16) for i in range(2)]
        ys = [sb(f"y{i}", [128, D], F32) for i in range(2)]
        os_ = [sb(f"o{i}", [128, D], F32) for i in range(2)]
        sts = [sb(f"st{i}", [128, 2, 6], F32) for i in range(2)]
        mvs = [sb(f"mv{i}", [128, 2], F32) for i in range(2)]
        rstds = [sb(f"rstd{i}", [128, 1], F32) for i in range(2)]
        po = [ps(f"po{i}", [128, 512], F32) for i in range(4)]
        load_w(c, wo, c.mix_w_o[l], "wo", 8, 0, D)
        P.dma("sp", gb[:, 0, :], c.ln_g[l, 1, :].partition_broadcast(128), writes=["gb0"])
        P.dma("sp", gb[:, 1, :], c.ln_b[l, 1, :].partition_broadcast(128), writes=["gb1"])
        for tt in range(S // 128):
            b = tt % 2
            tc_ = slice(tt * 128, (tt + 1) * 128)
            P.dma("sp", xs[b][:], src[tc_, :], writes=[f"x{b}"])
            P.dma("sp", ms[b][:], c.mixT[:, tc_].rearrange("(k p) t -> p k t", p=128), writes=[f"m{b}"])
            y, o = ys[b], os_[b]
            for dh in range(2):
                pb_ = 2 * b + dh
                for kc in range(8):
                    P.op("pe", lambda b=b, dh=dh, kc=kc, pb_=pb_: nc.tensor.matmul(po[pb_][:, :], lhsT=ms[b][:, kc, :], rhs=wo[:, kc, dh * 512:(dh + 1) * 512],
                                                                                   start=(kc == 0), stop=(kc == 7)), reads=[f"m{b}", f"wo{kc}"], writes=[f"po{pb_}"])
            P.op("act", lambda b=b, y=y: nc.scalar.mul(y[:], xs[b][:], ALPHA), reads=[f"x{b}"], writes=[f"y{b}"])
            for dh in range(2):
                pb_ = 2 * b + dh
                P.op("dve", lambda dh=dh, y=y, pb_=pb_: nc.vector.tensor_tensor(out=y[:, dh * 512:(dh + 1) * 512], in0=po[pb_][:, :], in1=y[:, dh * 512:(dh + 1) * 512], op=ALU.add),
                     reads=[f"po{pb_}", f"y{b}"], writes=[f"y{b}"])
            layer_norm_tile(c, y, o, sts[b], mvs[b], rstds[b], gb, sfx=str(b))
            P.dma("sp", dst[tc_, :], o[:], reads=[f"o{b}"], writes=["dst"], chan=f"d_o{b}")
    P.barrier()


def phase_ple(c, l, src, dst):
    nc, P, S = c.nc, c.P, c.S
    with ExitStack() as pes:
        sb, ps = _tiles(nc, pes)
        wg = sb("wg", [128, 8, D], BF16)
        wp = sb("wp", [128, 2, D], BF16)
        ident = sb("ident", [128, 128], F32)
        xs = [sb(f"x{i}", [128, D], F32) for i in range(2)]
        pls = [sb(f"pl{i}", [128, PLE], F32) for i in range(2)]
        hTs = [sb(f"hT{i}", [128, 10, 128], BF16) for i in range(2)]
        sgs = [sb(f"sg{i}", [128, D], F32) for i in range(2)]
        os_ = [sb(f"o{i}", [128, D], F32) for i in range(2)]
        pt = [ps(f"pt{i}", [128, 512], F32) for i in range(3)]
        pg = [ps(f"pg{i}", [128, 512], F32) for i in range(2)]
        pp = [ps(f"pp{i}", [128, 512], F32) for i in range(2)]
        load_w(c, wg, c.ple_w_gate[l], "wg", 8, 0, D)
        load_w(c, wp, c.ple_w_proj[l], "wp", 2, 0, D)
        P.dma("sp", ident[:], c.consts[:, c.CO["ident"]:c.CO["ident"] + 128], writes=["ident"])
        for tt in range(S // 128):
            b = tt % 2
            tc_ = slice(tt * 128, (tt + 1) * 128)
            hT, sg, o = hTs[b], sgs[b], os_[b]
            P.dma("sp", xs[b][:], src[tc_, :], writes=[f"x{b}"])
            P.dma("sp", pls[b][:], c.p[l, tc_, :], writes=[f"pl{b}"])
            for grp in range(3):
                n = 4 if grp < 2 else 2
                for q in range(n):
                    kc = grp * 4 + q
                    if grp < 2:
                        P.op("pe", lambda grp=grp, q=q, kc=kc, b=b: nc.tensor.transpose(pt[grp][:, q * 128:(q + 1) * 128], xs[b][:, kc * 128:(kc + 1) * 128], ident[:]),
                             reads=[f"x{b}", "ident"], writes=[f"pt{grp}"])
                    else:
                        P.op("pe", lambda grp=grp, q=q, b=b: nc.tensor.transpose(pt[grp][:, q * 128:(q + 1) * 128], pls[b][:, q * 128:(q + 1) * 128], ident[:]),
                             reads=[f"pl{b}", "ident"], writes=[f"pt{grp}"])
                eng = ("act", "dve", "act")[grp]
                if eng == "act":
                    P.op("act", lambda grp=grp, n=n: nc.scalar.copy(hT[:, grp * 4:grp * 4 + n, :].rearrange("p a b -> p (a b)"), pt[grp][:, 0:n * 128]),
                         reads=[f"pt{grp}"], writes=[f"hT{b}_{grp}"])
                else:
                    P.op("dve", lambda grp=grp, n=n: nc.vector.tensor_copy(hT[:, grp * 4:grp * 4 + n, :].rearrange("p a b -> p (a b)"), pt[grp][:, 0:n * 128]),
                         reads=[f"pt{grp}"], writes=[f"hT{b}_{grp}"])
            for dh in range(2):
                for kc in range(8):
                    P.op("pe", lambda dh=dh, kc=kc: nc.tensor.matmul(pg[dh][:, :], lhsT=hT[:, kc, :], rhs=wg[:, kc, dh * 512:(dh + 1) * 512], start=(kc == 0), stop=(kc == 7)),
                         reads=[f"hT{b}_0", f"hT{b}_1", f"wg{kc}"], writes=[f"pg{dh}"])
                for kc in range(2):
                    P.op("pe", lambda dh=dh, kc=kc: nc.tensor.matmul(pp[dh][:, :], lhsT=hT[:, 8 + kc, :], rhs=wp[:, kc, dh * 512:(dh + 1) * 512], start=(kc == 0), stop=(kc == 1)),
                         reads=[f"hT{b}_2", f"wp{kc}"], writes=[f"pp{dh}"])
                P.op("act", lambda dh=dh: nc.scalar.activation(out=sg[:, dh * 512:(dh + 1) * 512], in_=pg[dh][:, :], func=AF.Sigmoid), reads=[f"pg{dh}"], writes=[f"sg{b}_{dh}"])
                P.op("dve", lambda dh=dh: nc.vector.tensor_tensor(out=sg[:, dh * 512:(dh + 1) * 512], in0=pp[dh][:, :], in1=sg[:, dh * 512:(dh + 1) * 512], op=ALU.mult),
                     reads=[f"pp{dh}", f"sg{b}_{dh}"], writes=[f"sg{b}_{dh}"])
            P.op("pool", lambda b=b: nc.gpsimd.tensor_tensor(out=o[:], in0=sg[:], in1=xs[b][:], op=ALU.add), reads=[f"sg{b}_0", f"sg{b}_1", f"x{b}"], writes=[f"o{b}"])
            P.dma("sp", dst[tc_, :], o[:], reads=[f"o{b}"], writes=["dst"], chan=f"d_o{b}")
    P.barrier()


def all_phases(L):
    ph = [phase_rope]
    for l in range(L):
        src = (lambda c: c.x) if l == 0 else (lambda c: c.hb[1])
        ph.append(lambda c, l=l, src=src: phase_ffn(c, l, 0, src(c), c.hb[0], c.h1T))
        ph.append(lambda c, l=l: phase_gdn(c, l))
        ph.append(lambda c, l=l: phase_sb(c, l))
        ph.append(lambda c, l=l: phase_mla(c, l))
        ph.append(lambda c, l=l: phase_outproj(c, l, c.hb[0], c.hb[1]))
        ph.append(lambda c, l=l: phase_ffn(c, l, 1, c.hb[1], c.hb[0]))
        last = l == L - 1
        ph.append(lambda c, l=l, last=last: phase_ple(c, l, c.hb[0], c.out if last else c.hb[1]))
    return ph


_CACHE = {}


def kernel(**inputs):
    x = np.asarray(inputs["x"], np.float32)
    Bsz, S, _ = x.shape
    L = inputs["ffa_w_in"].shape[0]
    key = (S, L)
    if key not in _CACHE:
        _CACHE[key] = build(S, L, all_phases(L))
    nc, consts = _CACHE[key]
    w = prep_weights({k: np.asarray(v) for k, v in inputs.items()}, L)
    p = np.asarray(inputs["p"], np.float32)
    pos = np.asarray(inputs["positions"], np.int32)
    in_maps = []
    for b in range(Bsz):
        in_maps.append({"x": np.ascontiguousarray(x[b]), "p": np.ascontiguousarray(p[:, b]), "pos": np.ascontiguousarray(pos[b:b + 1]),
                        "consts": consts, **w})
    res = run_bass_kernel_spmd(nc, in_maps, core_ids=list(range(Bsz)))
    return np.stack([np.asarray(r["out"], np.float32) for r in res.results], 0)
```

```python
import numpy as np
from contextlib import ExitStack
import concourse.bass as bass
import concourse.mybir as mybir
from concourse.bass_utils import run_bass_kernel_spmd

F32 = mybir.dt.float32
BF16 = mybir.dt.bfloat16
I32 = mybir.dt.int32
AF = mybir.ActivationFunctionType
ALU = mybir.AluOpType
AX = mybir.AxisListType

D = 1024
DFF = 2816
PLE = 256
DEPTH = 2
ALPHA = (2 * DEPTH) ** 0.25
LN_EPS = 1e-5
RMS_EPS = 1e-6
IN_TOTAL = 3248
C_GQ, C_GK, C_GV, C_GZ, C_GA, C_GB = 0, 512, 1024, 1536, 2048, 2056
C_SQ, C_SK, C_SV, C_MQ, C_MKV, C_KR = 2064, 2320, 2576, 2832, 3088, 3216
SEM_ROT = 1 << 30
TWO_PI = 6.283185307179586


class Prog:
    def __init__(self, nc, es):
        self.nc = nc
        self.es = es
        self.engs = {"pe": nc.tensor, "act": nc.scalar, "dve": nc.vector, "pool": nc.gpsimd, "sp": nc.sync}
        self.nsem = 0
        self.esem = {}
        self.ecnt = {}
        self.allsems = []
        for e in ("pe", "act", "dve", "pool"):
            self.esem[e] = self._newsem("e_" + e)
            self.ecnt[e] = 0
        self.dsem = {}
        self.dcnt = {}
        self.dfree = []
        self.bar1 = None
        self.bar2 = None
        self.nbar = 0
        self.last_w = {}
        self.readers = {}
        self.known = {e: {} for e in self.engs}
        self.nwaits = 0
        self.ninst = 0
        import os
        self.limit = int(os.environ["OP_LIMIT"]) if "OP_LIMIT" in os.environ else None

    def _newsem(self, name):
        self.nsem += 1
        s = self.es.enter_context(self.nc.semaphore(f"{name}_{self.nsem}"))
        self.allsems.append([s, 0])
        return s

    def _bump(self, sem, val):
        for r in self.allsems:
            if r[0] is sem:
                r[1] = max(r[1], val)
                return

    def _deps(self, reads, writes):
        deps = {}

        def add(t):
            cur = deps.get(id(t[0]))
            if cur is None or cur[1] < t[1]:
                deps[id(t[0])] = (t[0], t[1])

        for k in reads:
            t = self.last_w.get(k)
            if t is not None:
                add(t)
        for k in writes:
            t = self.last_w.get(k)
            if t is not None:
                add(t)
            for t in self.readers.get(k, ()):
                add(t)
        return deps

    def _wait(self, eng, deps, skip_sem=None):
        e = self.engs[eng]
        kn = self.known[eng]
        for sid, (sem, val) in deps.items():
            if skip_sem is not None and sem is skip_sem:
                continue
            if kn.get(sid, 0) >= val:
                continue
            e.wait_ge(sem, val)
            self.nwaits += 1
            kn[sid] = val

    def _commit(self, reads, writes, tok):
        for k in writes:
            self.last_w[k] = tok
            self.readers[k] = []
        for k in reads:
            if k in writes:
                continue
            self.readers.setdefault(k, []).append(tok)

    def op(self, eng, fn, reads=(), writes=()):
        if self.limit is not None and self.ninst >= self.limit:
            return None
        pr = [k for k in reads if k in PSUM_KEYS]
        if pr:
            writes = list(writes) + pr
        deps = self._deps(reads, writes)
        self._wait(eng, deps, skip_sem=self.esem["pe"] if eng == "pe" else None)
        inst = fn()
        if self.ecnt[eng] >= SEM_ROT:
            self.esem[eng] = self._newsem("e_" + eng)
            self.ecnt[eng] = 0
        self.ecnt[eng] += 1
        sem = self.esem[eng]
        inst.then_inc(sem, 1)
        tok = (sem, self.ecnt[eng])
        self._bump(sem, self.ecnt[eng])
        self._commit(reads, writes, tok)
        self.ninst += 1
        return tok

    def dma(self, q, out, in_, reads=(), writes=(), chan=None, **kw):
        if self.limit is not None and self.ninst >= self.limit:
            return None
        deps = self._deps(reads, writes)
        self._wait(q, deps)
        if chan is None:
            chan = "d_" + (list(writes) + list(reads))[0]
        if q == "pool":
            chan = "sw_" + chan
            assert chan not in self.dsem
            self.dsem[chan] = self._newsem("sw")
            self.dcnt[chan] = 0
        elif chan not in self.dsem:
            self.dsem[chan] = self.dfree.pop() if self.dfree else self._newsem("d")
            self.dcnt[chan] = 0
        inst = self.engs[q].dma_start(out=out, in_=in_, **kw)
        self.dcnt[chan] += 16
        sem = self.dsem[chan]
        inst.then_inc(sem, 16)
        tok = (sem, self.dcnt[chan])
        self._bump(sem, self.dcnt[chan])
        self._commit(reads, writes, tok)
        self.ninst += 1
        return tok

    def barrier(self):
        for eng in self.engs:
            kn = self.known[eng]
            for sem, val in self.allsems:
                if val > 0 and kn.get(id(sem), 0) < val:
                    self.engs[eng].wait_ge(sem, val)
                    kn[id(sem)] = val
                    self.nwaits += 1
        self.last_w.clear()
        self.readers.clear()
        if not self.dsem:
            return
        if self.bar1 is None:
            self.bar1 = self.es.enter_context(self.nc.semaphore("bar1"))
            self.bar2 = self.es.enter_context(self.nc.semaphore("bar2"))
        self.nbar += 1
        for eng in self.engs:
            self.engs[eng].sem_inc(self.bar1, 1)
        self.engs["pool"].wait_ge(self.bar1, len(self.engs) * self.nbar)
        for chan, sem in self.dsem.items():
            if chan.startswith("sw_"):
                continue
            self.engs["pool"].sem_clear(sem)
            self.dfree.append(sem)
            for r in self.allsems:
                if r[0] is sem:
                    r[1] = 0
            for eng in self.engs:
                self.known[eng].pop(id(sem), None)
        self.engs["pool"].sem_inc(self.bar2, 1)
        for eng in self.engs:
            self.engs[eng].wait_ge(self.bar2, self.nbar)
        self.dsem.clear()
        self.dcnt.clear()


class Ctx:
    pass


_UID = [0]


def _tiles(nc, pes, P=None):
    _UID[0] += 1
    u = _UID[0]
    sb = lambda n, s, d: pes.enter_context(nc.sbuf_tensor(f"{n}_u{u}", s, d))

    def ps(n, s, d):
        PSUM_KEYS.add(n)
        return pes.enter_context(nc.psum_tensor(f"{n}_u{u}", [128, 2048 // mybir.dt.size(d)], d))
    return sb, ps


PSUM_KEYS = set()


def load_w(c, dst, src_rows, key, kchunks, c0, c1):
    n = c1 - c0
    nblk = (n + 1023) // 1024
    while n % nblk:
        nblk += 1
    w = n // nblk
    for i in range(nblk):
        c.P.dma("pool", dst[:, 0:kchunks, i * w:(i + 1) * w], src_rows[0:kchunks * 128, c0 + i * w:c0 + (i + 1) * w].rearrange("(k p) f -> p k f", p=128),
                writes=[f"{key}{kc}" for kc in range(kchunks)], chan=f"{key}_b{i}", max_dma_last_dim=4096)


def consts_phase(c, pes):
    nc, P = c.nc, c.P
    sb, ps = _tiles(nc, pes)
    return


def phase_ffn(c, l, which, src, dst, dstT=None):
    nc, P, S = c.nc, c.P, c.S
    w_in = (c.ffa_w_in if which == 0 else c.ffb_w_in)[l]
    w_out = (c.ffa_w_out if which == 0 else c.ffb_w_out)[l]
    lni = 0 if which == 0 else 2
    KC, FC, NG = D // 128, c.DFF // 128, min(512, S)
    NT = NG // 128
    dff = c.DFF
    with ExitStack() as pes:
        sb, ps = _tiles(nc, pes)
        win = sb("win", [128, KC, 2 * dff], BF16)
        wout = sb("wout", [128, FC, D], BF16)
        ident = sb("ident", [128, 128], F32)
        gb = sb("gb", [128, 2, D], F32)
        x = sb("x", [128, NT, D], F32)
        xT = sb("xT", [128, KC, NG], BF16)
        actT = sb("actT", [128, FC, NG], BF16)
        sg = [sb(f"sg{i}", [128, NG], F32) for i in range(2)]
        y = sb("y", [128, D], F32)
        o = sb("o", [128, D], F32)
        st = sb("st", [128, 2, 6], F32)
        mv = sb("mv", [128, 2], F32)
        rstd = sb("rstd", [128, 1], F32)
        xts = sb("xts", [128, 8, 128], BF16)
        pt = [ps(f"pt{i}", [128, 512], F32) for i in range(2)]
        pg = [ps(f"pg{i}", [128, 512], F32) for i in range(2)]
        pu = [ps(f"pu{i}", [128, 512], F32) for i in range(2)]
        po = [ps(f"po{i}", [128, 512], F32) for i in range(2)]
        P.dma("sp", ident[:], c.consts[:, c.CO["ident"]:c.CO["ident"] + 128], writes=["ident"])
        P.dma("sp", gb[:, 0, :], c.ln_g[l, lni, :].partition_broadcast(128), writes=["gb0"])
        P.dma("sp", gb[:, 1, :], c.ln_b[l, lni, :].partition_broadcast(128), writes=["gb1"])
        load_w(c, win, w_in, "win", KC, 0, 2 * dff)
        load_w(c, wout, w_out, "wout", FC, 0, D)
        pending = []
        for g in range(S // NG):
            t0 = g * NG
            for tt in range(NT):
                P.dma("sp", x[:, tt, :], src[t0 + tt * 128: t0 + (tt + 1) * 128, :], writes=[f"x{tt}"])
            for kc in range(KC):
                p_, pk = pt[kc % 2], f"pt{kc % 2}"
                for tt in range(NT):
                    P.op("pe", lambda p_=p_, tt=tt, kc=kc: nc.tensor.transpose(p_[:, tt * 128:(tt + 1) * 128], x[:, tt, kc * 128:(kc + 1) * 128], ident[:]),
                         reads=[f"x{tt}", "ident"], writes=[pk])
                if kc % 2 == 0:
                    P.op("act", lambda p_=p_, kc=kc: nc.scalar.copy(xT[:, kc, :], p_[:, 0:NG]), reads=[pk], writes=[f"xT{kc}"])
                else:
                    P.op("dve", lambda p_=p_, kc=kc: nc.vector.tensor_copy(xT[:, kc, :], p_[:, 0:NG]), reads=[pk], writes=[f"xT{kc}"])
            for j in range(FC):
                b = j % 2
                for kc in range(KC):
                    P.op("pe", lambda b=b, kc=kc, j=j: nc.tensor.matmul(pg[b][:, 0:NG], lhsT=win[:, kc, j * 128:(j + 1) * 128], rhs=xT[:, kc, :],
                                                                       start=(kc == 0), stop=(kc == KC - 1)),
                         reads=[f"win{kc}", f"xT{kc}"], writes=[f"pg{b}"])
                for kc in range(KC):
                    P.op("pe", lambda b=b, kc=kc, j=j: nc.tensor.matmul(pu[b][:, 0:NG], lhsT=win[:, kc, dff + j * 128: dff + (j + 1) * 128], rhs=xT[:, kc, :],
                                                                       start=(kc == 0), stop=(kc == KC - 1)),
                         reads=[f"win{kc}", f"xT{kc}"], writes=[f"pu{b}"])
                P.op("act", lambda b=b: nc.scalar.activation(out=sg[b][:], in_=pg[b][:, 0:NG], func=AF.Silu), reads=[f"pg{b}"], writes=[f"sg{b}"])
                P.op("dve", lambda b=b, j=j: nc.vector.tensor_tensor(out=actT[:, j, :], in0=pu[b][:, 0:NG], in1=sg[b][:], op=ALU.mult),
                     reads=[f"pu{b}", f"sg{b}"], writes=[f"actT{j}"])
            for tt in range(NT):
                for dh in range(2):
                    for j in range(FC):
                        P.op("pe", lambda tt=tt, dh=dh, j=j: nc.tensor.matmul(po[dh][:, :], lhsT=actT[:, j, tt * 128:(tt + 1) * 128],
                                                                             rhs=wout[:, j, dh * 512:(dh + 1) * 512], start=(j == 0), stop=(j == FC - 1)),
                             reads=[f"actT{j}", f"wout{j}"], writes=[f"po{dh}"])
                while pending:
                    pending.pop(0)()
                P.op("act", lambda tt=tt: nc.scalar.mul(y[:], x[:, tt, :], ALPHA), reads=[f"x{tt}"], writes=["y"])
                for dh in range(2):
                    P.op("dve", lambda dh=dh: nc.vector.scalar_tensor_tensor(out=y[:, dh * 512:(dh + 1) * 512], in0=po[dh][:, :], scalar=0.5,
                                                                           in1=y[:, dh * 512:(dh + 1) * 512], op0=ALU.mult, op1=ALU.add),
                         reads=[f"po{dh}", "y"], writes=["y"])
                layer_norm_tile(c, y, o, st, mv, rstd, gb)
                P.dma("sp", dst[t0 + tt * 128: t0 + (tt + 1) * 128, :], o[:], reads=["o"], writes=["dst"], chan="d_o")
                if dstT is not None:
                  def emitT(t0=t0, tt=tt):
                    for half in range(2):
                        for q in range(4):
                            kc = half * 4 + q
                            P.op("pe", lambda half=half, q=q, kc=kc: nc.tensor.transpose(pt[half][:, q * 128:(q + 1) * 128], o[:, kc * 128:(kc + 1) * 128], ident[:]),
                                 reads=["o", "ident"], writes=[f"pt{half}"])
                        if half == 0:
                            P.op("act", lambda: nc.scalar.copy(xts[:, 0:4, :].rearrange("p a b -> p (a b)"), pt[0][:, :]), reads=["pt0"], writes=["xts0"])
                        else:
                            P.op("dve", lambda: nc.vector.tensor_copy(xts[:, 4:8, :].rearrange("p a b -> p (a b)"), pt[1][:, :]), reads=["pt1"], writes=["xts1"])
                    tok = slice(t0 + tt * 128, t0 + (tt + 1) * 128)
                    P.dma("sp", dstT[:, tok].rearrange("(k p) t -> p k t", p=128), xts[:], reads=["xts0", "xts1"], writes=["dstT"], chan="d_xts")
                  pending.append(emitT)
        while pending:
            pending.pop(0)()
    P.barrier()


def layer_norm_tile(c, y, o, st, mv, rstd, gb, sfx=""):
    nc, P = c.nc, c.P
    ky, ko, kst, kmv, krs = "y" + sfx, "o" + sfx, "st" + sfx, "mv" + sfx, "rstd" + sfx
    for dh in range(2):
        P.op("dve", lambda dh=dh: nc.vector.bn_stats(out=st[:, dh, :], in_=y[:, dh * 512:(dh + 1) * 512]), reads=[ky], writes=[kst])
    P.op("dve", lambda: nc.vector.bn_aggr(out=mv[:], in_=st[:].rearrange("p a b -> p (a b)")), reads=[kst], writes=[kmv])
    P.op("act", lambda: nc.scalar.activation(out=rstd[:], in_=mv[:, 1:2], func=AF.Sqrt, bias=LN_EPS, scale=1.0), reads=[kmv], writes=[krs])
    P.op("dve", lambda: nc.vector.reciprocal(out=rstd[:], in_=rstd[:]), reads=[krs], writes=[krs])
    P.op("dve", lambda: nc.vector.tensor_scalar(out=y[:], in0=y[:], scalar1=mv[:, 0:1], scalar2=rstd[:, 0:1], op0=ALU.subtract, op1=ALU.mult),
         reads=[ky, kmv, krs], writes=[ky])
    P.op("pool", lambda: nc.gpsimd.tensor_tensor(out=o[:], in0=y[:], in1=gb[:, 0, :], op=ALU.mult), reads=[ky, "gb0"], writes=[ko])
    P.op("pool", lambda: nc.gpsimd.tensor_tensor(out=o[:], in0=o[:], in1=gb[:, 1, :], op=ALU.add), reads=[ko, "gb1"], writes=[ko])


def phase_transpose(c, src, dstT):
    nc, P, S = c.nc, c.P, c.S
    with ExitStack() as pes:
        sb, ps = _tiles(nc, pes)
        ident = sb("ident", [128, 128], F32)
        xs = [sb(f"x{i}", [128, D], F32) for i in range(2)]
        xts = [sb(f"xt{i}", [128, 8, 128], BF16) for i in range(2)]
        pt = [ps(f"pt{i}", [128, 512], F32) for i in range(4)]
        P.dma("sp", ident[:], c.consts[:, c.CO["ident"]:c.CO["ident"] + 128], writes=["ident"])
        for tt in range(S // 128):
            b = tt % 2
            P.dma("sp", xs[b][:], src[tt * 128:(tt + 1) * 128, :], writes=[f"x{b}"])
            for half in range(2):
                pp, pk = pt[2 * b + half], f"pt{2 * b + half}"
                for q in range(4):
                    kc = half * 4 + q
                    P.op("pe", lambda pp=pp, q=q, kc=kc, b=b: nc.tensor.transpose(pp[:, q * 128:(q + 1) * 128], xs[b][:, kc * 128:(kc + 1) * 128], ident[:]),
                         reads=[f"x{b}", "ident"], writes=[pk])
                eng = "act" if half == 0 else "dve"
                if half == 0:
                    P.op("act", lambda pp=pp, b=b: nc.scalar.copy(xts[b][:, 0:4, :].rearrange("p a b -> p (a b)"), pp[:, :]), reads=[pk], writes=[f"xt{b}h0"])
                else:
                    P.op("dve", lambda pp=pp, b=b: nc.vector.tensor_copy(xts[b][:, 4:8, :].rearrange("p a b -> p (a b)"), pp[:, :]), reads=[pk], writes=[f"xt{b}h1"])
            P.dma("sp", dstT[:, tt * 128:(tt + 1) * 128].rearrange("(k p) t -> p k t", p=128), xts[b][:], reads=[f"xt{b}h0", f"xt{b}h1"],
                  writes=["dstT"], chan=f"d_xt{b}")
    P.barrier()


def phase_rope(c):
    nc, P, S = c.nc, c.P, c.S
    with ExitStack() as pes:
        sb, ps = _tiles(nc, pes)
        pi_ = sb("pi", [32, S], I32)
        ang = sb("ang", [32, S], F32)
        k = sb("k", [32, S], F32)
        r = sb("r", [32, S], F32)
        rc = sb("rc", [32, S], F32)
        m = sb("m", [32, S], F32)
        cs = sb("cs", [32, S], F32)
        sn = sb("sn", [32, S], F32)
        cst = sb("cst", [32, 2], F32)
        MAGIC = 12582912.0
        C1 = 6.28125
        C2 = TWO_PI - C1
        PI_LO = 3.1415925
        P.dma("sp", pi_[:], c.pos[0, :].partition_broadcast(32), writes=["pi"])
        P.dma("sp", cst[:], c.consts[0:32, c.CO["rope"]:c.CO["rope"] + 2], writes=["cst"])
        P.op("dve", lambda: nc.vector.tensor_copy(ang[:], pi_[:]), reads=["pi"], writes=["ang"])
        P.op("dve", lambda: nc.vector.tensor_scalar(out=ang[:], in0=ang[:], scalar1=cst[:, 0:1], scalar2=None, op0=ALU.mult), reads=["ang", "cst"], writes=["ang"])
        P.op("dve", lambda: nc.vector.tensor_scalar(out=k[:], in0=ang[:], scalar1=1.0 / TWO_PI, scalar2=MAGIC, op0=ALU.mult, op1=ALU.add), reads=["ang"], writes=["k"])
        P.op("dve", lambda: nc.vector.tensor_scalar(out=k[:], in0=k[:], scalar1=MAGIC, scalar2=None, op0=ALU.subtract), reads=["k"], writes=["k"])
        P.op("dve", lambda: nc.vector.scalar_tensor_tensor(out=r[:], in0=k[:], scalar=-C1, in1=ang[:], op0=ALU.mult, op1=ALU.add), reads=["k", "ang"], writes=["r"])
        P.op("dve", lambda: nc.vector.scalar_tensor_tensor(out=r[:], in0=k[:], scalar=-C2, in1=r[:], op0=ALU.mult, op1=ALU.add), reads=["k", "r"], writes=["r"])
        P.op("dve", lambda: nc.vector.tensor_scalar(out=rc[:], in0=r[:], scalar1=np.pi / 2, scalar2=None, op0=ALU.add), reads=["r"], writes=["rc"])
        P.op("dve", lambda: nc.vector.tensor_scalar(out=m[:], in0=rc[:], scalar1=np.pi, scalar2=-TWO_PI, op0=ALU.is_gt, op1=ALU.mult), reads=["rc"], writes=["m"])
        P.op("dve", lambda: nc.vector.tensor_tensor(out=rc[:], in0=rc[:], in1=m[:], op=ALU.add), reads=["rc", "m"], writes=["rc"])
        for t_, nm in ((r, "r"), (rc, "rc")):
            P.op("dve", lambda t_=t_: nc.vector.tensor_scalar(out=t_[:], in0=t_[:], scalar1=PI_LO, scalar2=-PI_LO, op0=ALU.min, op1=ALU.max), reads=[nm], writes=[nm])
        import os
        SINF = AF.Identity if os.environ.get("ROPE_DBG") == "nosin" else AF.Sin
        P.op("act", lambda: nc.scalar.activation(out=sn[:], in_=r[:], func=SINF), reads=["r"], writes=["sn"])
        P.op("act", lambda: nc.scalar.activation(out=cs[:], in_=rc[:], func=SINF), reads=["rc"], writes=["cs"])
        P.op("dve", lambda: nc.vector.tensor_scalar(out=sn[:], in0=sn[:], scalar1=cst[:, 1:2], scalar2=None, op0=ALU.mult), reads=["sn", "cst"], writes=["sn"])
        P.dma("sp", c.ropeT[0, :, :], cs[:], reads=["cs"], writes=["ropeT0"])
        P.dma("sp", c.ropeT[1, :, :], sn[:], reads=["sn"], writes=["ropeT1"])
    P.barrier()


def phase_mla(c, l):
    nc, P, S = c.nc, c.P, c.S
    NB = max(1, S // 512)
    NG = min(512, S)
    NTT = S // 128
    SCALE = 96 ** -0.5
    H = 4
    w_in = c.mix_w_in[l]
    with ExitStack() as pes:
        sb, ps = _tiles(nc, pes)
        xTb = [sb(f"xT{i}", [128, 8, NG], BF16) for i in range(2)]
        wq = sb("wq", [128, 8, 256], BF16)
        wkv = sb("wkv", [128, 8, 128], BF16)
        wkr = sb("wkr", [128, 8, 192], BF16)
        wuq = sb("wuq", [128, 2, 768], BF16)
        wukk = sb("wukk", [128, 1, 256], BF16)
        wukv = sb("wukv", [128, 1, 256], BF16)
        nwq = sb("nwq", [128, 2], F32)
        nwkv = sb("nwkv", [128, 1], F32)
        ones = sb("ones", [128, 128], BF16)
        onesf = sb("onesf", [128, 128], F32)
        mk = sb("mk", [128, 1, 2048], BF16)
        cs = sb("cs", [96, NG], F32)
        sn = sb("sn", [96, NG], F32)
        cqn = sb("cqn", [128, 2, NG], BF16)
        ckvn = sb("ckvn", [128, NG], BF16)
        krT = sb("krT", [96, NG], BF16)
        qf = [sb(f"qf{h}", [96, S], BF16) for h in range(H)]
        kf = [sb(f"kf{h}", [96, S], BF16) for h in range(H)]
        V = sb("V", [128, NTT, 256], BF16)
        tmp = [sb(f"tmp{i}", [128, NG], F32) for i in range(3)]
        sq = [sb(f"sq{i}", [128, NG], BF16) for i in range(2)]
        rbc = sb("rbc", [128, NG], F32)
        pe_ = [sb(f"pe{i}", [128, NG], BF16) for i in range(5)]
        rec = [sb(f"rec{i}", [64, NG], F32) for i in range(2)]
        oT = [sb(f"oT{i}", [64, NG], BF16) for i in range(2)]
        pA = [ps(f"pA{i}", [128, 512], F32) for i in range(2)]
        pB = ps("pB", [128, 512], F32)
        pS = [ps(f"pS{i}", [128, 512], F32) for i in range(2)]
        pL = ps("pL", [64, 512], F32)
        CO = c.CO
        load_w(c, wq, w_in, "wq", 8, C_MQ, C_MQ + 256)
        load_w(c, wkv, w_in, "wkv", 8, C_MKV, C_MKV + 128)
        load_w(c, wkr, c.w_kr[l], "wkr", 8, 0, 192)
        load_w(c, wuq, c.w_uq_p[l], "wuq", 2, 0, 768)
        load_w(c, wukk, c.w_ukv_k[l], "wukk", 1, 0, 256)
        load_w(c, wukv, c.w_ukv_v[l], "wukv", 1, 0, 256)
        P.dma("sp", nwq[:], c.q_norm_wT[l], writes=["nwq"])
        P.dma("sp", nwkv[:], c.kv_norm_wT[l], writes=["nwkv"])
        P.dma("sp", onesf[:], c.consts[:, CO["ones"]:CO["ones"] + 128], writes=["onesf"])
        P.op("dve", lambda: nc.vector.tensor_copy(ones[:], onesf[:]), reads=["onesf"], writes=["ones"])
        load_w(c, mk, c.consts, "mk", 1, CO["maskle"], CO["maskle"] + 2048)
        wkeys = lambda n, k: [f"{n}{i}" for i in range(k)]
        for nb in range(NB):
            cols = slice(nb * NG, (nb + 1) * NG)
            xT = xTb[nb % 2]
            xk = f"xT{nb % 2}"
            P.dma("sp", xT[:], c.h1T[:, cols].rearrange("(k p) t -> p k t", p=128), writes=[xk])
            P.dma("sp", cs[64:96, :], c.ropeT[0, :, cols], writes=["cs"])
            P.dma("sp", sn[64:96, :], c.ropeT[1, :, cols], writes=["sn"])
            for j in range(2):
                pa = pA[j]
                for kc in range(8):
                    P.op("pe", lambda pa=pa, kc=kc, j=j: nc.tensor.matmul(pa[:, 0:NG], lhsT=wq[:, kc, j * 128:(j + 1) * 128], rhs=xT[:, kc, :],
                                                                         start=(kc == 0), stop=(kc == 7)), reads=[f"wq{kc}", xk], writes=[f"pA{j}"])
                P.op("act", lambda pa=pa, j=j: nc.scalar.activation(out=sq[j][:], in_=pa[:, 0:NG], func=AF.Square), reads=[f"pA{j}"], writes=[f"sq{j}"])
                P.op("dve", lambda pa=pa, j=j: nc.vector.tensor_scalar(out=tmp[j][:], in0=pa[:, 0:NG], scalar1=nwq[:, j:j + 1], scalar2=None, op0=ALU.mult),
                     reads=[f"pA{j}", "nwq"], writes=[f"tmp{j}"])
            for j in range(2):
                P.op("pe", lambda j=j: nc.tensor.matmul(pB[:, 0:NG], lhsT=ones[:], rhs=sq[j][:], start=(j == 0), stop=(j == 1)),
                     reads=["ones", f"sq{j}"], writes=["pB"])
            P.op("act", lambda: nc.scalar.activation(out=rbc[:], in_=pB[:, 0:NG], func=AF.Sqrt, bias=RMS_EPS, scale=1.0 / 256), reads=["pB"], writes=["rbc"])
            P.op("dve", lambda: nc.vector.reciprocal(out=rbc[:], in_=rbc[:]), reads=["rbc"], writes=["rbc"])
            for j in range(2):
                P.op("dve", lambda j=j: nc.vector.tensor_tensor(out=cqn[:, j, :], in0=tmp[j][:], in1=rbc[:], op=ALU.mult),
                     reads=[f"tmp{j}", "rbc"], writes=["cqn"])
            pa = pA[0]
            for kc in range(8):
                P.op("pe", lambda kc=kc: nc.tensor.matmul(pa[:, 0:NG], lhsT=wkv[:, kc, :], rhs=xT[:, kc, :], start=(kc == 0), stop=(kc == 7)),
                     reads=[f"wkv{kc}", xk], writes=["pA0"])
            P.op("act", lambda: nc.scalar.activation(out=sq[0][:], in_=pa[:, 0:NG], func=AF.Square), reads=["pA0"], writes=["sq0"])
            P.op("dve", lambda: nc.vector.tensor_scalar(out=tmp[0][:], in0=pa[:, 0:NG], scalar1=nwkv[:, 0:1], scalar2=None, op0=ALU.mult),
                 reads=["pA0", "nwkv"], writes=["tmp0"])
            P.op("pe", lambda: nc.tensor.matmul(pB[:, 0:NG], lhsT=ones[:], rhs=sq[0][:], start=True, stop=True), reads=["ones", "sq0"], writes=["pB"])
            P.op("act", lambda: nc.scalar.activation(out=rbc[:], in_=pB[:, 0:NG], func=AF.Sqrt, bias=RMS_EPS, scale=1.0 / 128), reads=["pB"], writes=["rbc"])
            P.op("dve", lambda: nc.vector.reciprocal(out=rbc[:], in_=rbc[:]), reads=["rbc"], writes=["rbc"])
            P.op("dve", lambda: nc.vector.tensor_tensor(out=ckvn[:, :], in0=tmp[0][:], in1=rbc[:], op=ALU.mult), reads=["tmp0", "rbc"], writes=["ckvn"])
            pa = pA[1]
            for kc in range(8):
                P.op("pe", lambda kc=kc: nc.tensor.matmul(pa[0:96, 0:NG], lhsT=wkr[:, kc, 0:96], rhs=xT[:, kc, :], start=(kc == 0), stop=(kc == 7)),
                     reads=[f"wkr{kc}", xk], writes=["pA1"])
            for kc in range(8):
                P.op("pe", lambda kc=kc: nc.tensor.matmul(pB[0:96, 0:NG], lhsT=wkr[:, kc, 96:192], rhs=xT[:, kc, :], start=(kc == 0), stop=(kc == 7)),
                     reads=[f"wkr{kc}", xk], writes=["pB"])
            P.op("dve", lambda: nc.vector.tensor_tensor(out=tmp[1][64:96, :], in0=pa[64:96, 0:NG], in1=cs[64:96, :], op=ALU.mult), reads=["pA1", "cs"], writes=["tmp1"])
            P.op("dve", lambda: nc.vector.tensor_tensor(out=tmp[2][64:96, :], in0=pB[64:96, 0:NG], in1=sn[64:96, :], op=ALU.mult), reads=["pB", "sn"], writes=["tmp2"])
            P.op("pool", lambda: nc.gpsimd.tensor_tensor(out=krT[64:96, :], in0=tmp[1][64:96, :], in1=tmp[2][64:96, :], op=ALU.add), reads=["tmp1", "tmp2"], writes=["krT"])
            for h in range(H):
                pa = pA[h % 2]
                pak = f"pA{h % 2}"
                for j in range(2):
                    P.op("pe", lambda pa=pa, j=j, h=h: nc.tensor.matmul(pa[0:96, 0:NG], lhsT=wuq[:, j, h * 192:h * 192 + 96], rhs=cqn[:, j, :],
                                                                       start=(j == 0), stop=(j == 1)), reads=[f"wuq{j}", "cqn"], writes=[pak])
                for j in range(2):
                    P.op("pe", lambda j=j, h=h: nc.tensor.matmul(pB[0:96, 0:NG], lhsT=wuq[:, j, h * 192 + 96:h * 192 + 192], rhs=cqn[:, j, :],
                                                                start=(j == 0), stop=(j == 1)), reads=[f"wuq{j}", "cqn"], writes=["pB"])
                P.op("act", lambda pa=pa, h=h: nc.scalar.copy(qf[h][0:64, cols], pa[0:64, 0:NG]), reads=[pak], writes=[f"qf{h}"])
                P.op("dve", lambda pa=pa: nc.vector.tensor_tensor(out=tmp[1][64:96, :], in0=pa[64:96, 0:NG], in1=cs[64:96, :], op=ALU.mult), reads=[pak, "cs"], writes=["tmp1"])
                P.op("dve", lambda: nc.vector.tensor_tensor(out=tmp[2][64:96, :], in0=pB[64:96, 0:NG], in1=sn[64:96, :], op=ALU.mult), reads=["pB", "sn"], writes=["tmp2"])
                P.op("pool", lambda h=h: nc.gpsimd.tensor_tensor(out=qf[h][64:96, cols], in0=tmp[1][64:96, :], in1=tmp[2][64:96, :], op=ALU.add),
                     reads=["tmp1", "tmp2"], writes=[f"qf{h}"])
                P.op("pe", lambda pa=pa, h=h: nc.tensor.matmul(pa[0:64, 0:NG], lhsT=wukk[:, 0, h * 64:(h + 1) * 64], rhs=ckvn[:, :], start=True, stop=True),
                     reads=["wukk0", "ckvn"], writes=[pak])
                P.op("act", lambda pa=pa, h=h: nc.scalar.copy(kf[h][0:64, cols], pa[0:64, 0:NG]), reads=[pak], writes=[f"kf{h}"])
                P.op("pool", lambda h=h: nc.gpsimd.tensor_copy(kf[h][64:96, cols], krT[64:96, :]), reads=["krT"], writes=[f"kf{h}"])
            for q in range(NG // 128):
                tt = nb * (NG // 128) + q
                P.op("pe", lambda tt=tt: nc.tensor.matmul(pB[:, 0:256], lhsT=ckvn[:, q * 128:(q + 1) * 128], rhs=wukv[:, 0, :], start=True, stop=True),
                     reads=["ckvn", "wukv0"], writes=["pB"])
                P.op("act", lambda tt=tt: nc.scalar.copy(V[:, tt, :], pB[:, 0:256]), reads=["pB"], writes=["V"])
        LA = 3
        NPE = 5
        pOb = (pA[0], pA[1])
        pOk = ("pA0", "pA1")
        pLb = (pB, pL)
        pLk = ("pB", "pL")
        for hp in range(2):
            meta = []
            for tq in range(NB):
                nkb = (tq + 1) * (NG // 128)
                for kb in range(nkb):
                    for hh in range(2):
                        meta.append(dict(tq=tq, kb=kb, hh=hh, nkb=nkb, zb=len(meta) % 2, eb=len(meta) % NPE))

            def stageA(m):
                tq, kb, hh, zb, eb = m["tq"], m["kb"], m["hh"], m["zb"], m["eb"]
                h = hp * 2 + hh
                qcols = slice(tq * NG, (tq + 1) * NG)
                P.op("pe", lambda: nc.tensor.matmul(pS[zb][:, 0:NG], lhsT=kf[h][:, kb * 128:(kb + 1) * 128], rhs=qf[h][:, qcols], start=True, stop=True),
                     reads=[f"kf{h}", f"qf{h}"], writes=[f"pS{zb}"])
                P.op("act", lambda: nc.scalar.activation(out=pe_[eb][:], in_=pS[zb][:, 0:NG], func=AF.Exp, scale=SCALE), reads=[f"pS{zb}"], writes=[f"pe{eb}"])
                r = kb - tq * (NG // 128)
                if r >= 0:
                    P.op("pool", lambda: nc.gpsimd.tensor_tensor(out=pe_[eb][:], in0=pe_[eb][:], in1=mk[:, 0, r * 512:r * 512 + NG], op=ALU.mult),
                         reads=[f"pe{eb}", "mk0"], writes=[f"pe{eb}"])

            def stageB(m):
                tq, kb, hh, nkb, eb = m["tq"], m["kb"], m["hh"], m["nkb"], m["eb"]
                h = hp * 2 + hh
                qcols = slice(tq * NG, (tq + 1) * NG)
                P.op("pe", lambda: nc.tensor.matmul(pOb[hh][0:64, 0:NG], lhsT=V[:, kb, h * 64:(h + 1) * 64], rhs=pe_[eb][:], start=(kb == 0), stop=(kb == nkb - 1)),
                     reads=["V", f"pe{eb}"], writes=[pOk[hh]])
                P.op("pe", lambda: nc.tensor.matmul(pLb[hh][0:64, 0:NG], lhsT=ones[:, 0:64], rhs=pe_[eb][:], start=(kb == 0), stop=(kb == nkb - 1)),
                     reads=["ones", f"pe{eb}"], writes=[pLk[hh]])
                if kb == nkb - 1:
                    P.op("dve", lambda: nc.vector.reciprocal(out=rec[hh][:], in_=pLb[hh][0:64, 0:NG]), reads=[pLk[hh]], writes=[f"rec{hh}"])
                    P.op("dve", lambda: nc.vector.tensor_tensor(out=oT[hh][:], in0=pOb[hh][0:64, 0:NG], in1=rec[hh][:], op=ALU.mult), reads=[pOk[hh], f"rec{hh}"], writes=[f"oT{hh}"])
                    P.dma("sp", c.mixT[768 + h * 64: 768 + (h + 1) * 64, qcols], oT[hh][:], reads=[f"oT{hh}"], writes=["mixT"], chan=f"d_oT{hh}")

            n = len(meta)
            for i in range(min(LA, n)):
                stageA(meta[i])
            for i in range(n):
                if i + LA < n:
                    stageA(meta[i + LA])
                stageB(meta[i])
    P.barrier()


def make_consts():
    CO = {}
    cols = []
    off = 0

    def add(name, arr):
        nonlocal off
        a = np.zeros((128, arr.shape[1]), np.float32)
        a[:arr.shape[0]] = arr
        CO[name] = off
        off += arr.shape[1]
        cols.append(a)

    add("ident", np.eye(128, dtype=np.float32))
    add("ones", np.ones((128, 128), np.float32))
    s = np.arange(128)[:, None]
    t = np.arange(512)[None, :]
    add("maskle", np.concatenate([((r * 128 + s) <= t).astype(np.float32) for r in range(4)], 1))
    add("masklt", np.concatenate([((r * 128 + s) < t).astype(np.float32) for r in range(4)], 1))
    j = np.arange(128)[:, None]
    i = np.arange(128)[None, :]
    add("trineg", -(j >= i).astype(np.float32))
    add("compneg", -(j < i).astype(np.float32))
    add("triinc", (j <= i).astype(np.float32))
    add("upincl", (i >= j).astype(np.float32))
    add("upstrict", (i > j).astype(np.float32))
    inv = (1.0 / (10000.0 ** (np.arange(0, 32, 2, dtype=np.float32) / 32))).astype(np.float32)
    rope = np.zeros((32, 2), np.float32)
    rope[:, 0] = np.concatenate([inv, inv])
    rope[:, 1] = np.concatenate([-np.ones(16), np.ones(16)])
    add("rope", rope)
    return np.concatenate(cols, 1), CO


def prep_weights(inp, L):
    f = lambda a: np.ascontiguousarray(a, dtype=np.float32)
    w_in = inp["mix_w_in"]
    out = {}
    z64 = np.zeros(w_in.shape[:2] + (64,), np.float32)
    out["w_kr"] = f(np.concatenate([z64, w_in[:, :, C_KR:C_KR + 32], z64, w_in[:, :, C_KR + 16:C_KR + 32], w_in[:, :, C_KR:C_KR + 16]], -1))
    uq = inp["mla_w_uq"]
    parts = []
    for h in range(4):
        b = h * 96
        parts += [uq[:, :, b:b + 64], uq[:, :, b + 64:b + 96], np.zeros(uq.shape[:2] + (64,), np.float32), uq[:, :, b + 80:b + 96], uq[:, :, b + 64:b + 80]]
    out["w_uq_p"] = f(np.concatenate(parts, -1))
    ukv = inp["mla_w_ukv"]
    out["w_ukv_k"] = f(np.concatenate([ukv[:, :, h * 128:h * 128 + 64] for h in range(4)], -1))
    out["w_ukv_v"] = f(np.concatenate([ukv[:, :, h * 128 + 64:h * 128 + 128] for h in range(4)], -1))
    out["q_norm_wT"] = f(inp["mla_q_norm_w"].reshape(L, 2, 128).transpose(0, 2, 1))
    out["kv_norm_wT"] = f(inp["mla_kv_norm_w"].reshape(L, 1, 128).transpose(0, 2, 1))
    out["conv_wT"] = f(inp["gdn_conv_w"].reshape(L, 4, 12, 128).transpose(0, 3, 2, 1))
    for k in ("ffa_w_in", "ffa_w_out", "mix_w_in", "gdn_a_log", "gdn_dt_bias", "gdn_norm_w", "mix_w_o", "ffb_w_in", "ffb_w_out",
              "ln_g", "ln_b", "ple_w_gate", "ple_w_proj"):
        out[k] = f(inp[k])
    return out


WSHAPES = lambda L, dff: {
    "ffa_w_in": [L, D, 2 * dff], "ffa_w_out": [L, dff, D], "mix_w_in": [L, D, IN_TOTAL], "w_kr": [L, D, 192],
    "w_uq_p": [L, 256, 768], "w_ukv_k": [L, 128, 256], "w_ukv_v": [L, 128, 256], "q_norm_wT": [L, 128, 2], "kv_norm_wT": [L, 128, 1],
    "conv_wT": [L, 128, 12, 4], "gdn_a_log": [L, 8], "gdn_dt_bias": [L, 8], "gdn_norm_w": [L, 64], "mix_w_o": [L, D, D],
    "ffb_w_in": [L, D, 2 * dff], "ffb_w_out": [L, dff, D], "ln_g": [L, 3, D], "ln_b": [L, 3, D], "ple_w_gate": [L, D, D], "ple_w_proj": [L, PLE, D],
}


def build(S, L, phases, dff=DFF, debug=False):
    nc = bass.Bass("TRN2", target_bir_lowering=False)
    c = Ctx()
    c.nc, c.S, c.DFF, c.L = nc, S, dff, L
    consts_np, c.CO = make_consts()
    din = lambda n, s, d=F32: nc.dram_tensor(n, s, d, kind="ExternalInput").ap()
    c.x = din("x", [S, D])
    c.p = din("p", [L, S, PLE])
    c.pos = din("pos", [1, S], I32)
    c.consts = din("consts", list(consts_np.shape))
    for k, shp in WSHAPES(L, dff).items():
        setattr(c, k, din(k, shp))
    kind = "ExternalOutput" if debug else "Internal"
    scr = lambda n, s, d: nc.dram_tensor(n, s, d, kind=kind).ap()
    c.hb = [scr("hb0", [S, D], F32), scr("hb1", [S, D], F32)]
    c.h1T = scr("h1T", [D, S], BF16)
    c.mixT = scr("mixT", [D, S], BF16)
    c.ropeT = scr("ropeT", [2, 32, S], F32)
    c.out = nc.dram_tensor("out", [S, D], F32, kind="ExternalOutput").ap()
    with ExitStack() as es:
        c.P = Prog(nc, es)
        for ph in phases:
            ph(c)
        c.P.barrier()
        print("ninst", c.P.ninst, "nwaits", c.P.nwaits, "nsem", c.P.nsem)
    return nc, consts_np


def phase_sb(c, l):
    nc, P, S = c.nc, c.P, c.S
    NG = min(512, S)
    NB = S // NG
    NTT = S // 128
    R = NG // 128
    H = 4
    w_in = c.mix_w_in[l]
    CO = c.CO
    with ExitStack() as pes:
        sb, ps = _tiles(nc, pes)
        xTb = [sb(f"xT{i}", [128, 8, NG], BF16) for i in range(2)]
        wq = sb("wq", [128, 8, 256], BF16)
        wk = sb("wk", [128, 8, 256], BF16)
        wv = sb("wv", [128, 8, 256], BF16)
        qT = [sb(f"qT{j}", [128, S], BF16) for j in range(2)]
        kT = [sb(f"kT{j}", [128, S], BF16) for j in range(2)]
        V = sb("V", [128, NTT, 256], BF16)
        cf = sb("cf", [128, 256], F32)
        tri = sb("tri", [128, 128], BF16)
        comp = sb("comp", [128, 128], BF16)
        mkf = sb("mkf", [128, 4, 512], F32)
        e = [sb(f"e{i}", [128, NG], F32) for i in range(5)]
        sp = [sb(f"sp{i}", [128, NG], BF16) for i in range(10)]
        eC = [sb(f"eC{i}", [128, NG], F32) for i in range(2)]
        A = [sb(f"A{i}", [128, NG], BF16) for i in range(4)]
        oT = [sb(f"oT{i}", [64, NG], BF16) for i in range(2)]
        pZ = [ps(f"pZ{i}", [128, 512], F32) for i in range(2)]
        pC = [ps(f"pC{i}", [128, 512], F32) for i in range(2)]
        pO = [ps(f"pO{i}", [128, 512], F32) for i in range(2)]
        pX = ps("pX", [128, 512], F32)
        load_w(c, wq, w_in, "wq", 8, C_SQ, C_SQ + 256)
        load_w(c, wk, w_in, "wk", 8, C_SK, C_SK + 256)
        load_w(c, wv, w_in, "wv", 8, C_SV, C_SV + 256)
        P.dma("sp", cf[:, 0:128], c.consts[:, CO["trineg"]:CO["trineg"] + 128], writes=["cf"])
        P.dma("sp", cf[:, 128:256], c.consts[:, CO["compneg"]:CO["compneg"] + 128], writes=["cf"])
        P.op("dve", lambda: nc.vector.tensor_copy(tri[:], cf[:, 0:128]), reads=["cf"], writes=["tri"])
        P.op("dve", lambda: nc.vector.tensor_copy(comp[:], cf[:, 128:256]), reads=["cf"], writes=["comp"])
        P.dma("sp", mkf[:].rearrange("p a b -> p (a b)"), c.consts[:, CO["masklt"]:CO["masklt"] + 2048], writes=["mkf"])
        for nb in range(NB):
            cols = slice(nb * NG, (nb + 1) * NG)
            xT = xTb[nb % 2]
            xk = f"xT{nb % 2}"
            P.dma("sp", xT[:], c.h1T[:, cols].rearrange("(k p) t -> p k t", p=128), writes=[xk])
            for (w_, dst, nm) in ((wq, qT, "qT"), (wk, kT, "kT")):
                for j in range(2):
                    for kc in range(8):
                        P.op("pe", lambda w_=w_, kc=kc, j=j: nc.tensor.matmul(pX[:, 0:NG], lhsT=w_[:, kc, j * 128:(j + 1) * 128], rhs=xT[:, kc, :],
                                                                             start=(kc == 0), stop=(kc == 7)), reads=[xk] + [f"wq{kc}", f"wk{kc}"], writes=["pX"])
                    P.op("act", lambda dst=dst, j=j: nc.scalar.copy(dst[j][:, cols], pX[:, 0:NG]), reads=["pX"], writes=[f"{nm}{j}"])
            for q in range(R):
                tt = nb * R + q
                for kc in range(8):
                    P.op("pe", lambda q=q, kc=kc: nc.tensor.matmul(pX[:, 0:256], lhsT=xT[:, kc, q * 128:(q + 1) * 128], rhs=wv[:, kc, :],
                                                                    start=(kc == 0), stop=(kc == 7)), reads=[xk, f"wv{kc}"], writes=["pX"])
                P.op("dve", lambda tt=tt: nc.vector.tensor_copy(V[:, tt, :], pX[:, 0:256]), reads=["pX"], writes=["V"])
        LA = 3
        NE, NSP = 5, 5
        for hp in range(2):
            items = []
            for tq in range(NB):
                nkb = (tq + 1) * R
                for kb in range(nkb - 1, -1, -1):
                    for hh in range(2):
                        items.append((tq, kb, hh, nkb))
            percnt = [0, 0]
            meta = []
            for idx, (tq, kb, hh, nkb) in enumerate(items):
                itn = percnt[hh]
                percnt[hh] += 1
                meta.append(dict(tq=tq, kb=kb, hh=hh, nkb=nkb, zb=idx % 2, eb=idx % NE, ab=idx % 4, spb=hh * NSP + itn % NSP, spp=hh * NSP + (itn - 1) % NSP))

            def stageA(m):
                tq, kb, hh = m["tq"], m["kb"], m["hh"]
                qcols = slice(tq * NG, (tq + 1) * NG)
                pb, j, zb, eb, spb = hh * 64, hp, m["zb"], m["eb"], m["spb"]
                r = kb - tq * R
                P.op("pe", lambda: nc.tensor.matmul(pZ[zb][:, 0:NG], lhsT=kT[j][pb:pb + 64, kb * 128:(kb + 1) * 128], rhs=qT[j][pb:pb + 64, qcols], start=True, stop=True),
                     reads=[f"kT{j}", f"qT{j}"], writes=[f"pZ{zb}"])
                P.op("act", lambda: nc.scalar.activation(out=e[eb][:], in_=pZ[zb][:, 0:NG], func=AF.Exp, scale=0.125), reads=[f"pZ{zb}"], writes=[f"e{eb}"])
                if r >= 0:
                    P.op("dve", lambda: nc.vector.tensor_tensor(out=e[eb][:], in0=e[eb][:], in1=mkf[:, r, 0:NG], op=ALU.mult), reads=[f"e{eb}", "mkf"], writes=[f"e{eb}"])
                P.op("act", lambda: nc.scalar.activation(out=sp[spb][:], in_=e[eb][:], func=AF.Ln, bias=1.0, scale=1.0), reads=[f"e{eb}"], writes=[f"sp{spb}"])

            def stageB(m):
                tq, kb, hh, nkb = m["tq"], m["kb"], m["hh"], m["nkb"]
                qcols = slice(tq * NG, (tq + 1) * NG)
                h = hp * 2 + hh
                eb, ab, spb, spp = m["eb"], m["ab"], m["spb"], m["spp"]
                first = kb == nkb - 1
                if not first:
                    P.op("pe", lambda: nc.tensor.matmul(pC[hh][:, 0:NG], lhsT=comp[:], rhs=sp[spp][:], start=False, stop=False, skip_group_check=True),
                         reads=["comp", f"sp{spp}"], writes=[f"pC{hh}"])
                P.op("pe", lambda: nc.tensor.matmul(pC[hh][:, 0:NG], lhsT=tri[:], rhs=sp[spb][:], start=first, stop=True, skip_group_check=True),
                     reads=["tri", f"sp{spb}"], writes=[f"pC{hh}"])
                P.op("act", lambda: nc.scalar.activation(out=eC[hh][:], in_=pC[hh][:, 0:NG], func=AF.Exp), reads=[f"pC{hh}"], writes=[f"eC{hh}"])
                P.op("dve", lambda: nc.vector.tensor_tensor(out=A[ab][:], in0=e[eb][:], in1=eC[hh][:], op=ALU.mult), reads=[f"e{eb}", f"eC{hh}"], writes=[f"A{ab}"])

            def stageB2(m):
                tq, kb, hh, nkb = m["tq"], m["kb"], m["hh"], m["nkb"]
                qcols = slice(tq * NG, (tq + 1) * NG)
                h = hp * 2 + hh
                ab = m["ab"]
                first = kb == nkb - 1
                P.op("pe", lambda: nc.tensor.matmul(pO[hh][0:64, 0:NG], lhsT=V[:, kb, h * 64:(h + 1) * 64], rhs=A[ab][:], start=first, stop=(kb == 0)),
                     reads=["V", f"A{ab}"], writes=[f"pO{hh}"])
                if kb == 0:
                    P.op("dve", lambda: nc.vector.tensor_copy(oT[hh][:], pO[hh][0:64, 0:NG]), reads=[f"pO{hh}"], writes=[f"oT{hh}"])
                    P.dma("sp", c.mixT[512 + h * 64: 512 + (h + 1) * 64, qcols], oT[hh][:], reads=[f"oT{hh}"], writes=["mixT"], chan=f"d_oT{hh}")

            n = len(meta)
            for i in range(min(LA, n)):
                stageA(meta[i])
            for i in range(n):
                if i + LA < n:
                    stageA(meta[i + LA])
                stageB(meta[i])
                if i >= 2:
                    stageB2(meta[i - 2])
            for i in range(max(0, n - 2), n):
                stageB2(meta[i])
    P.barrier()


def phase_gdn(c, l):
    nc, P, S = c.nc, c.P, c.S
    NG = min(512, S)
    NB = S // NG
    R = NG // 128
    w_in = c.mix_w_in[l]
    CO = c.CO
    with ExitStack() as pes:
        sb, ps = _tiles(nc, pes)
        xTb = [sb(f"xT{i}", [128, 8, NG], BF16) for i in range(2)]
        wqkv = sb("wqkv", [128, 8, 1536], BF16)
        wz = sb("wz", [128, 8, 512], BF16)
        wab = sb("wab", [128, 8, 16], BF16)
        cw = sb("cw", [128, 12, 4], F32)
        ident = sb("ident", [128, 128], F32)
        identb = sb("identb", [128, 128], BF16)
        onesf = sb("onesf", [128, 128], F32)
        triinc = sb("triinc", [128, 128], F32)
        upi = sb("upi", [128, 128], F32)
        ups = sb("ups", [128, 128], F32)
        dtb = sb("dtb", [128, 8], F32)
        nea = sb("nea", [128, 8], F32)
        nw = sb("nw", [128, 64], F32)
        halo = sb("halo", [128, 12, 3], F32)
        raw = [sb(f"raw{i}", [128, NG + 3], F32) for i in range(2)]
        acc = [sb(f"acc{i}", [128, NG], F32) for i in range(2)]
        csl = sb("csl", [128, 12, NG], F32)
        QKV = sb("QKV", [128, 1536], F32)
        sqt = sb("sqt", [128, 1024], F32)
        ss = sb("ss", [128, 16], F32)
        kqT = sb("kqT", [64, 8, 2, 128], BF16)
        gab = sb("gab", [128, 16], F32)
        beta = sb("beta", [128, 8], F32)
        nbeta = sb("nbeta", [128, 8], F32)
        g = sb("g", [128, 8], F32)
        gc = sb("gc", [128, 8], F32)
        gam = sb("gam", [128, 8], F32)
        kap = sb("kap", [128, 8], F32)
        gl = sb("gl", [128, 8], F32)
        nbk = sb("nbk", [128, 8], F32)
        G1 = sb("G1", [128, 8, 128], F32)
        dm = sb("dm", [128, 8, 128], F32)
        dTs = sb("dTs", [128, 8, 128], F32)
        dTi = sb("dTi", [128, 8, 128], F32)
        CD = F32
        A0 = sb("A0", [128, 8, 128], CD)
        qkTp = sb("qkTp", [128, 8, 128], BF16)
        Am = [[sb(f"Am{g}{i}", [128, 4, 128], BF16) for i in range(2)] for g in range(2)]
        Bm = [[sb(f"Bm{g}{i}", [128, 4, 128], BF16) for i in range(2)] for g in range(2)]
        Amf = [[sb(f"Amf{g}{i}", [128, 4, 128], F32) for i in range(2)] for g in range(2)]
        Bmf = [[sb(f"Bmf{g}{i}", [128, 4, 128], F32) for i in range(2)] for g in range(2)]
        Yb = [sb(f"Yb{g}", [128, 4, 128], BF16) for g in range(2)]
        Yf = [sb(f"Yf{g}", [128, 4, 128], F32) for g in range(2)]
        Xp = sb("Xp", [128, 8, 128], BF16)
        St = sb("St", [64, 8, 64], F32)
        Sb = sb("Sb", [64, 8, 64], BF16)
        rp = sb("rp", [128, 8, 64], BF16)
        vt = sb("vt", [128, 512], BF16)
        kd = sb("kd", [128, 512], BF16)
        o1 = sb("o1", [128, 8, 64], F32)
        of = sb("of", [128, 8, 64], F32)
        zs = sb("zs", [128, 512], F32)
        og = sb("og", [128, 512], F32)
        oTs = sb("oTs", [128, 4, 128], BF16)
        B = [ps(f"B{i}", [128, 512], F32) for i in range(8)]
        bk = lambda i: f"B{i}"

        load_w(c, wqkv, w_in, "wqkv", 8, C_GQ, C_GQ + 1536)
        load_w(c, wz, w_in, "wz", 8, C_GZ, C_GZ + 512)
        load_w(c, wab, w_in, "wab", 8, C_GA, C_GA + 16)
        P.dma("sp", cw[:], c.conv_wT[l], writes=["cw"])
        for (t_, nm) in ((ident, "ident"), (onesf, "ones"), (triinc, "triinc"), (upi, "upincl"), (ups, "upstrict")):
            P.dma("sp", t_[:], c.consts[:, CO[nm]:CO[nm] + 128], writes=[nm])
        P.op("dve", lambda: nc.vector.tensor_copy(identb[:], ident[:]), reads=["ident"], writes=["identb"])
        P.dma("sp", dtb[:], c.gdn_dt_bias[l, :].partition_broadcast(128), writes=["dtb"])
        P.dma("sp", nea[:], c.gdn_a_log[l, :].partition_broadcast(128), writes=["nea"])
        P.dma("sp", nw[:], c.gdn_norm_w[l, :].partition_broadcast(128), writes=["nw"])
        P.op("act", lambda: nc.scalar.activation(out=nea[:], in_=nea[:], func=AF.Exp), reads=["nea"], writes=["nea"])
        P.op("dve", lambda: nc.vector.tensor_scalar(out=nea[:], in0=nea[:], scalar1=-1.0, scalar2=None, op0=ALU.mult), reads=["nea"], writes=["nea"])
        P.op("pool", lambda: nc.gpsimd.memset(halo[:], 0.0), writes=["halo"])
        P.op("pool", lambda: nc.gpsimd.memset(St[:], 0.0), writes=["St"])
        P.op("pool", lambda: nc.gpsimd.memset(Sb[:], 0.0), writes=["Sb"])

        def bc_h(ap2d, n, d):
            return ap2d.unsqueeze(2).to_broadcast([128, n, d])

        def bc_m(ap2d, n, d):
            return ap2d.unsqueeze(1).to_broadcast([128, n, d])

        for nb in range(NB):
            cols = slice(nb * NG, (nb + 1) * NG)
            xT = xTb[nb % 2]
            xk = f"xT{nb % 2}"
            P.dma("sp", xT[:], c.h1T[:, cols].rearrange("(k p) t -> p k t", p=128), writes=[xk])
            for ch in range(12):
                rb = ch % 2
                for kc in range(8):
                    P.op("pe", lambda kc=kc, ch=ch: nc.tensor.matmul(B[0][:, 0:NG], lhsT=wqkv[:, kc, ch * 128:(ch + 1) * 128], rhs=xT[:, kc, :],
                                                                    start=(kc == 0), stop=(kc == 7)), reads=[xk, f"wqkv{kc}"], writes=[bk(0)])
                P.op("pool", lambda rb=rb, ch=ch: nc.gpsimd.tensor_copy(raw[rb][:, 0:3], halo[:, ch, :]), reads=["halo"], writes=[f"raw{rb}"])
                P.op("act", lambda rb=rb: nc.scalar.copy(raw[rb][:, 3:NG + 3], B[0][:, 0:NG]), reads=[bk(0)], writes=[f"raw{rb}"])
                P.op("pool", lambda rb=rb, ch=ch: nc.gpsimd.tensor_copy(halo[:, ch, :], raw[rb][:, NG:NG + 3]), reads=[f"raw{rb}"], writes=["halo"])
                P.op("dve", lambda rb=rb, ch=ch: nc.vector.tensor_scalar(out=acc[rb][:], in0=raw[rb][:, 0:NG], scalar1=cw[:, ch, 0:1], scalar2=None, op0=ALU.mult),
                     reads=[f"raw{rb}", "cw"], writes=[f"acc{rb}"])
                for j in range(1, 4):
                    P.op("dve", lambda rb=rb, ch=ch, j=j: nc.vector.scalar_tensor_tensor(out=acc[rb][:], in0=raw[rb][:, j:NG + j], scalar=cw[:, ch, j:j + 1],
                                                                                       in1=acc[rb][:], op0=ALU.mult, op1=ALU.add),
                         reads=[f"raw{rb}", "cw", f"acc{rb}"], writes=[f"acc{rb}"])
                P.op("act", lambda rb=rb, ch=ch: nc.scalar.activation(out=csl[:, ch, :], in_=acc[rb][:], func=AF.Silu), reads=[f"acc{rb}"], writes=[f"csl{ch}"])
            for q in range(R):
                tt = nb * R + q
                tcols = slice(tt * 128, (tt + 1) * 128)
                for grp in range(3):
                    for cc in range(4):
                        ch = grp * 4 + cc
                        P.op("pe", lambda cc=cc, ch=ch, q=q: nc.tensor.transpose(B[1][:, cc * 128:(cc + 1) * 128], csl[:, ch, q * 128:(q + 1) * 128], ident[:]),
                             reads=[f"csl{ch}", "ident"], writes=[bk(1)])
                    if grp % 2 == 0:
                        P.op("act", lambda grp=grp: nc.scalar.copy(QKV[:, grp * 512:(grp + 1) * 512], B[1][:, :]), reads=[bk(1)], writes=[f"QKV{grp}"])
                    else:
                        P.op("dve", lambda grp=grp: nc.vector.tensor_copy(QKV[:, grp * 512:(grp + 1) * 512], B[1][:, :]), reads=[bk(1)], writes=[f"QKV{grp}"])
                P.op("pool", lambda: nc.gpsimd.tensor_tensor(out=sqt[:], in0=QKV[:, 0:1024], in1=QKV[:, 0:1024], op=ALU.mult), reads=["QKV0", "QKV1"], writes=["sqt"])
                P.op("dve", lambda: nc.vector.tensor_reduce(out=ss[:], in_=sqt[:].rearrange("p (h d) -> p h d", d=64), axis=AX.X, op=ALU.add), reads=["sqt"], writes=["ss"])
                P.op("act", lambda: nc.scalar.activation(out=ss[:], in_=ss[:], func=AF.Sqrt, bias=RMS_EPS, scale=1.0), reads=["ss"], writes=["ss"])
                P.op("dve", lambda: nc.vector.reciprocal(out=ss[:], in_=ss[:]), reads=["ss"], writes=["ss"])
                P.op("dve", lambda: nc.vector.tensor_scalar(out=ss[:, 0:8], in0=ss[:, 0:8], scalar1=0.125, scalar2=None, op0=ALU.mult), reads=["ss"], writes=["ss"])
                P.op("dve", lambda: nc.vector.tensor_tensor(out=QKV[:, 0:1024].rearrange("p (h d) -> p h d", d=64), in0=QKV[:, 0:1024].rearrange("p (h d) -> p h d", d=64),
                                                            in1=bc_h(ss[:, 0:16], 16, 64), op=ALU.mult), reads=["QKV0", "QKV1", "ss"], writes=["QKV0", "QKV1"])
                for (src0, slot) in ((512, 0), (0, 1)):
                    for g4 in range(2):
                        for hl in range(4):
                            h = g4 * 4 + hl
                            P.op("pe", lambda hl=hl, h=h, src0=src0: nc.tensor.transpose(B[1][0:64, hl * 128:(hl + 1) * 128], QKV[:, src0 + h * 64: src0 + (h + 1) * 64], ident[:]),
                                 reads=["QKV0", "QKV1", "ident"], writes=[bk(1)])
                        if g4 == 0:
                            P.op("act", lambda slot=slot, g4=g4: nc.scalar.copy(kqT[:, g4 * 4:(g4 + 1) * 4, slot, :], B[1][0:64, :].rearrange("p (a b) -> p a b", a=4)), reads=[bk(1)], writes=["kqT"])
                        else:
                            P.op("dve", lambda slot=slot, g4=g4: nc.vector.tensor_copy(kqT[:, g4 * 4:(g4 + 1) * 4, slot, :], B[1][0:64, :].rearrange("p (a b) -> p a b", a=4)), reads=[bk(1)], writes=["kqT"])
                for kc in range(8):
                    P.op("pe", lambda kc=kc: nc.tensor.matmul(B[2][:, 0:16], lhsT=xT[:, kc, q * 128:(q + 1) * 128], rhs=wab[:, kc, :], start=(kc == 0), stop=(kc == 7)),
                         reads=[xk, f"wab{kc}"], writes=[bk(2)])
                P.op("dve", lambda: nc.vector.tensor_copy(gab[:], B[2][:, 0:16]), reads=[bk(2)], writes=["gab"])
                P.op("act", lambda: nc.scalar.activation(out=beta[:], in_=gab[:, 8:16], func=AF.Exp, scale=-1.0), reads=["gab"], writes=["beta"])
                P.op("dve", lambda: nc.vector.tensor_scalar(out=beta[:], in0=beta[:], scalar1=1.0, scalar2=None, op0=ALU.add), reads=["beta"], writes=["beta"])
                P.op("dve", lambda: nc.vector.reciprocal(out=beta[:], in_=beta[:]), reads=["beta"], writes=["beta"])
                P.op("dve", lambda: nc.vector.tensor_scalar(out=nbeta[:], in0=beta[:], scalar1=-1.0, scalar2=None, op0=ALU.mult), reads=["beta"], writes=["nbeta"])
                P.op("dve", lambda: nc.vector.tensor_tensor(out=g[:], in0=gab[:, 0:8], in1=dtb[:], op=ALU.add), reads=["gab", "dtb"], writes=["g"])
                P.op("act", lambda: nc.scalar.activation(out=g[:], in_=g[:], func=AF.Exp), reads=["g"], writes=["g"])
                P.op("act", lambda: nc.scalar.activation(out=g[:], in_=g[:], func=AF.Ln, bias=1.0, scale=1.0), reads=["g"], writes=["g"])
                P.op("dve", lambda: nc.vector.tensor_tensor(out=g[:], in0=g[:], in1=nea[:], op=ALU.mult), reads=["g", "nea"], writes=["g"])
                P.op("pe", lambda: nc.tensor.matmul(B[2][:, 16:24], lhsT=triinc[:], rhs=g[:], start=True, stop=True), reads=["triinc", "g"], writes=[bk(2)])
                P.op("pe", lambda: nc.tensor.matmul(B[2][:, 24:32], lhsT=onesf[:], rhs=g[:], start=True, stop=True), reads=["ones", "g"], writes=[bk(2)])
                P.op("dve", lambda: nc.vector.tensor_copy(gc[:], B[2][:, 16:24]), reads=[bk(2)], writes=["gc"])
                P.op("dve", lambda: nc.vector.tensor_tensor(out=kap[:], in0=B[2][:, 24:32], in1=gc[:], op=ALU.subtract), reads=[bk(2), "gc"], writes=["kap"])
                P.op("act", lambda: nc.scalar.activation(out=gl[:], in_=B[2][:, 24:32], func=AF.Exp), reads=[bk(2)], writes=["gl"])
                P.op("act", lambda: nc.scalar.activation(out=kap[:], in_=kap[:], func=AF.Exp), reads=["kap"], writes=["kap"])
                P.op("act", lambda: nc.scalar.activation(out=gam[:], in_=gc[:], func=AF.Exp), reads=["gc"], writes=["gam"])
                P.op("dve", lambda: nc.vector.tensor_tensor(out=nbk[:], in0=nbeta[:], in1=kap[:], op=ALU.mult), reads=["nbeta", "kap"], writes=["nbk"])
                P.op("pool", lambda: nc.gpsimd.tensor_tensor(out=G1[:], in0=bc_m(triinc[:], 8, 128), in1=bc_h(g[:], 8, 128), op=ALU.mult), reads=["triinc", "g"], writes=["G1"])
                for half in range(2):
                    P.op("pe", lambda half=half: nc.tensor.matmul(B[3 + half][:, :], lhsT=onesf[:], rhs=G1[:, half * 4:(half + 1) * 4, :].rearrange("p a b -> p (a b)"),
                                                                 start=True, stop=True), reads=["ones", "G1"], writes=[bk(3 + half)])
                    P.op("dve", lambda half=half: nc.vector.tensor_tensor(out=dm[:, half * 4:(half + 1) * 4, :], in0=B[3 + half][:, :].rearrange("p (a b) -> p a b", a=4),
                                                                         in1=bc_h(gc[:, half * 4:(half + 1) * 4], 4, 128), op=ALU.subtract),
                         reads=[bk(3 + half), "gc"], writes=["dm"])
                P.op("pool", lambda: nc.gpsimd.tensor_scalar(out=dm[:], in0=dm[:], scalar1=0.0, scalar2=None, op0=ALU.min), reads=["dm"], writes=["dm"])
                P.op("act", lambda: nc.scalar.activation(out=dm[:], in_=dm[:], func=AF.Exp), reads=["dm"], writes=["dm"])
                P.op("pool", lambda: nc.gpsimd.tensor_tensor(out=dTs[:], in0=dm[:], in1=bc_m(ups[:], 8, 128), op=ALU.mult), reads=["dm", "upstrict"], writes=["dTs"])
                P.op("pool", lambda: nc.gpsimd.tensor_tensor(out=dTi[:], in0=dm[:], in1=bc_m(upi[:], 8, 128), op=ALU.mult), reads=["dm", "upincl"], writes=["dTi"])
                for j2 in range(4):
                    for hh in range(2):
                        h = j2 * 2 + hh
                        P.op("pe", lambda h=h, hh=hh: nc.tensor.matmul(B[7][:, hh * 256:(hh + 1) * 256], lhsT=kqT[:, h, 0, :],
                                                                     rhs=kqT[:, h, :, :].rearrange("p a b -> p (a b)"), start=True, stop=True),
                             reads=["kqT"], writes=[bk(7)])
                    for hh in range(2):
                        h = j2 * 2 + hh
                        P.op("dve", lambda hh=hh, h=h: nc.vector.scalar_tensor_tensor(out=A0[:, h, :], in0=B[7][:, hh * 256: hh * 256 + 128], scalar=beta[:, h:h + 1],
                                                                                    in1=dTs[:, h, :], op0=ALU.mult, op1=ALU.mult),
                             reads=[bk(7), "beta", "dTs"], writes=["A0"])
                        P.op("dve", lambda hh=hh, h=h: nc.vector.scalar_tensor_tensor(out=qkTp[:, h, :], in0=B[7][:, hh * 256 + 128: hh * 256 + 256], scalar=nbeta[:, h:h + 1],
                                                                                    in1=dTi[:, h, :], op0=ALU.mult, op1=ALU.mult),
                             reads=[bk(7), "nbeta", "dTi"], writes=["qkTp"])
                NF = 4
                GB = ((5, 6, 7), (3, 4, 0))
                for g4 in range(2):
                    hs = slice(g4 * 4, (g4 + 1) * 4)
                    bB = GB[g4][1]
                    for hl in range(4):
                        P.op("pe", lambda hl=hl, g4=g4, bB=bB: nc.tensor.matmul(B[bB][:, hl * 128:(hl + 1) * 128], lhsT=A0[:, g4 * 4 + hl, :], rhs=ident[:], start=True, stop=True),
                             reads=["A0", "ident"], writes=[bk(bB)])
                    P.op("dve", lambda g4=g4, bB=bB: nc.vector.tensor_copy(Bmf[g4][0][:].rearrange("p a b -> p (a b)"), B[bB][:, :]), reads=[bk(bB)], writes=[f"Bmf{g4}0"])
                    P.op("pool", lambda hs=hs, g4=g4: nc.gpsimd.tensor_tensor(out=Yf[g4][:], in0=bc_m(ident[:], 4, 128), in1=A0[:, hs, :], op=ALU.subtract),
                         reads=["ident", "A0"], writes=[f"Yf{g4}"])
                for k in range(1, 7):
                    pv, cu = (k - 1) % 2, k % 2
                    f32lvl = k <= NF
                    for g4 in range(2):
                        hs = slice(g4 * 4, (g4 + 1) * 4)
                        bA, bB, bY = GB[g4]
                        if k == 1:
                            Aprev = lambda hl, g4=g4: A0[:, g4 * 4 + hl, :]
                            akey = "A0"
                        elif f32lvl:
                            Aprev = lambda hl, g4=g4, pv=pv: Amf[g4][pv][:, hl, :]
                            akey = f"Amf{g4}{pv}"
                        else:
                            Aprev = lambda hl, g4=g4, pv=pv: Am[g4][pv][:, hl, :]
                            akey = f"Am{g4}{pv}"
                        Bprev = (lambda hl, g4=g4, pv=pv: Bmf[g4][pv][:, hl, :]) if f32lvl else (lambda hl, g4=g4, pv=pv: Bm[g4][pv][:, hl, :])
                        bkey = f"Bmf{g4}{pv}" if f32lvl else f"Bm{g4}{pv}"
                        if k <= 5:
                            for hl in range(4):
                                P.op("pe", lambda hl=hl, Aprev=Aprev, Bprev=Bprev, bA=bA: nc.tensor.matmul(B[bA][:, hl * 128:(hl + 1) * 128], lhsT=Bprev(hl), rhs=Aprev(hl), start=True, stop=True),
                                     reads=[bkey, akey], writes=[bk(bA)])
                        if k <= NF:
                            P.op("act", lambda cu=cu, g4=g4, bA=bA: nc.scalar.copy(Amf[g4][cu][:].rearrange("p a b -> p (a b)"), B[bA][:, :]), reads=[bk(bA)], writes=[f"Amf{g4}{cu}"])
                        if NF <= k <= 5:
                            P.op("dve", lambda cu=cu, g4=g4, bA=bA: nc.vector.tensor_copy(Am[g4][cu][:].rearrange("p a b -> p (a b)"), B[bA][:, :]), reads=[bk(bA)], writes=[f"Am{g4}{cu}"])
                        for hl in range(4):
                            P.op("pe", lambda hl=hl, Aprev=Aprev, Bprev=Bprev, bB=bB: nc.tensor.matmul(B[bB][:, hl * 128:(hl + 1) * 128], lhsT=Aprev(hl), rhs=Bprev(hl), start=True, stop=True),
                                 reads=[bkey, akey], writes=[bk(bB)])
                        if k <= NF:
                            P.op("dve", lambda cu=cu, g4=g4, bB=bB: nc.vector.tensor_copy(Bmf[g4][cu][:].rearrange("p a b -> p (a b)"), B[bB][:, :]), reads=[bk(bB)], writes=[f"Bmf{g4}{cu}"])
                        if k >= NF:
                            P.op("act", lambda cu=cu, g4=g4, bB=bB: nc.scalar.copy(Bm[g4][cu][:].rearrange("p a b -> p (a b)"), B[bB][:, :]), reads=[bk(bB)], writes=[f"Bm{g4}{cu}"])
                        for hl in range(4):
                            if f32lvl:
                                P.op("pe", lambda hl=hl, cu=cu, g4=g4, bY=bY: nc.tensor.matmul(B[bY][:, hl * 128:(hl + 1) * 128], lhsT=Bmf[g4][cu][:, hl, :], rhs=Yf[g4][:, hl, :], start=True, stop=True),
                                     reads=[f"Bmf{g4}{cu}", f"Yf{g4}"], writes=[bk(bY)])
                            else:
                                P.op("pe", lambda hl=hl, cu=cu, g4=g4, bY=bY: nc.tensor.matmul(B[bY][:, hl * 128:(hl + 1) * 128], lhsT=Bm[g4][cu][:, hl, :], rhs=Yb[g4][:, hl, :], start=True, stop=True),
                                     reads=[f"Bm{g4}{cu}", f"Yb{g4}"], writes=[bk(bY)])
                        if k < 6:
                            P.op("dve", lambda g4=g4, bY=bY: nc.vector.tensor_tensor(out=Yf[g4][:].rearrange("p a b -> p (a b)"), in0=B[bY][:, :], in1=Yf[g4][:].rearrange("p a b -> p (a b)"), op=ALU.add),
                                 reads=[bk(bY), f"Yf{g4}"], writes=[f"Yf{g4}"])
                            if k >= NF:
                                P.op("act", lambda g4=g4: nc.scalar.copy(Yb[g4][:].rearrange("p a b -> p (a b)"), Yf[g4][:].rearrange("p a b -> p (a b)")), reads=[f"Yf{g4}"], writes=[f"Yb{g4}"])
                        else:
                            P.op("dve", lambda g4=g4, bY=bY, hs=hs: nc.vector.tensor_tensor(out=Xp[:, hs, :].rearrange("p a b -> p (a b)"), in0=B[bY][:, :], in1=Yf[g4][:].rearrange("p a b -> p (a b)"), op=ALU.add),
                                 reads=[bk(bY), f"Yf{g4}"], writes=["Xp"])
                for h in range(8):
                    j2, pb = h // 2, (h % 2) * 64
                    bnk = 5 + h // 4
                    col = (h % 4) * 128
                    for slot in range(2):
                        P.op("pe", lambda bnk=bnk, col=col, slot=slot, h=h: nc.tensor.matmul(B[bnk][:, col + slot * 64: col + slot * 64 + 64], lhsT=kqT[:, h, slot, :],
                                                                                           rhs=Sb[:, h, :], start=True, stop=True),
                             reads=["kqT", "Sb"], writes=[bk(bnk)])
                for h in range(8):
                    bnk, col = 5 + h // 4, (h % 4) * 128
                    P.op("dve", lambda h=h, bnk=bnk, col=col: nc.vector.scalar_tensor_tensor(out=rp[:, h, :], in0=B[bnk][:, col:col + 64], scalar=gam[:, h:h + 1],
                                                                                           in1=QKV[:, 1024 + h * 64: 1024 + (h + 1) * 64], op0=ALU.mult, op1=ALU.subtract),
                         reads=[bk(bnk), "gam", "QKV2"], writes=["rp"])
                    P.op("act", lambda h=h, bnk=bnk, col=col: nc.scalar.activation(out=o1[:, h, :], in_=B[bnk][:, col + 64:col + 128], func=AF.Identity, scale=gam[:, h:h + 1]),
                         reads=[bk(bnk), "gam"], writes=["o1"])
                for h in range(8):
                    P.op("pe", lambda h=h: nc.tensor.matmul(B[4][:, h * 64:(h + 1) * 64], lhsT=Xp[:, h, :], rhs=rp[:, h, :], start=True, stop=True),
                         reads=["Xp", "rp"], writes=[bk(4)])
                P.op("act", lambda: nc.scalar.copy(vt[:], B[4][:, :]), reads=[bk(4)], writes=["vt"])
                P.op("pool", lambda: nc.gpsimd.tensor_tensor(out=kd[:].rearrange("p (h d) -> p h d", d=64), in0=QKV[:, 512:1024].rearrange("p (h d) -> p h d", d=64),
                                                            in1=bc_h(nbk[:], 8, 64), op=ALU.mult), reads=["QKV1", "nbk"], writes=["kd"])
                for h in range(8):
                    P.op("pe", lambda h=h: nc.tensor.matmul(B[3][:, h * 64:(h + 1) * 64], lhsT=qkTp[:, h, :], rhs=vt[:, h * 64:(h + 1) * 64], start=True, stop=True),
                         reads=["qkTp", "vt"], writes=[bk(3)])
                for h in range(8):
                    j2 = h // 2
                    P.op("pe", lambda h=h: nc.tensor.matmul(B[2][0:64, h * 64:(h + 1) * 64], lhsT=kd[:, h * 64:(h + 1) * 64], rhs=vt[:, h * 64:(h + 1) * 64], start=True, stop=True),
                         reads=["kd", "vt"], writes=[bk(2)])
                P.op("dve", lambda: nc.vector.tensor_tensor(out=of[:].rearrange("p a b -> p (a b)"), in0=B[3][:, :], in1=o1[:].rearrange("p a b -> p (a b)"), op=ALU.add),
                     reads=[bk(3), "o1"], writes=["of"])
                P.op("pool", lambda: nc.gpsimd.tensor_tensor(out=St[:], in0=St[:], in1=gl[0:64, :].unsqueeze(2).to_broadcast([64, 8, 64]), op=ALU.mult), reads=["St", "gl"], writes=["St"])
                P.op("dve", lambda: nc.vector.tensor_tensor(out=St[:].rearrange("p a b -> p (a b)"), in0=St[:].rearrange("p a b -> p (a b)"), in1=B[2][0:64, :], op=ALU.add),
                     reads=["St", bk(2)], writes=["St"])
                P.op("act", lambda: nc.scalar.copy(Sb[:].rearrange("p a b -> p (a b)"), St[:].rearrange("p a b -> p (a b)")), reads=["St"], writes=["Sb"])
                for kc in range(8):
                    P.op("pe", lambda kc=kc: nc.tensor.matmul(B[0][:, :], lhsT=xT[:, kc, q * 128:(q + 1) * 128], rhs=wz[:, kc, :], start=(kc == 0), stop=(kc == 7)),
                         reads=[xk, f"wz{kc}"], writes=[bk(0)])
                P.op("act", lambda: nc.scalar.activation(out=zs[:], in_=B[0][:, :], func=AF.Silu), reads=[bk(0)], writes=["zs"])
                P.op("pool", lambda: nc.gpsimd.tensor_tensor(out=sqt[:, 0:512], in0=of[:].rearrange("p a b -> p (a b)"), in1=of[:].rearrange("p a b -> p (a b)"), op=ALU.mult),
                     reads=["of"], writes=["sqt"])
                P.op("dve", lambda: nc.vector.tensor_reduce(out=ss[:, 0:8], in_=sqt[:, 0:512].rearrange("p (h d) -> p h d", d=64), axis=AX.X, op=ALU.add), reads=["sqt"], writes=["ss"])
                P.op("act", lambda: nc.scalar.activation(out=ss[:, 0:8], in_=ss[:, 0:8], func=AF.Sqrt, bias=RMS_EPS, scale=1.0 / 64), reads=["ss"], writes=["ss"])
                P.op("dve", lambda: nc.vector.reciprocal(out=ss[:, 0:8], in_=ss[:, 0:8]), reads=["ss"], writes=["ss"])
                P.op("dve", lambda: nc.vector.tensor_tensor(out=og[:].rearrange("p (h d) -> p h d", d=64), in0=of[:], in1=bc_h(ss[:, 0:8], 8, 64), op=ALU.mult),
                     reads=["of", "ss"], writes=["og"])
                P.op("pool", lambda: nc.gpsimd.tensor_tensor(out=og[:].rearrange("p (h d) -> p h d", d=64), in0=og[:].rearrange("p (h d) -> p h d", d=64), in1=bc_m(nw[:], 8, 64), op=ALU.mult),
                     reads=["og", "nw"], writes=["og"])
                P.op("pool", lambda: nc.gpsimd.tensor_tensor(out=og[:], in0=og[:], in1=zs[:], op=ALU.mult), reads=["og", "zs"], writes=["og"])
                for j2 in range(4):
                    P.op("pe", lambda j2=j2: nc.tensor.transpose(B[1][:, j2 * 128:(j2 + 1) * 128], og[:, j2 * 128:(j2 + 1) * 128], ident[:]), reads=["og", "ident"], writes=[bk(1)])
                P.op("act", lambda: nc.scalar.copy(oTs[:].rearrange("p a b -> p (a b)"), B[1][:, :]), reads=[bk(1)], writes=["oTs"])
                P.dma("sp", c.mixT[0:512, tcols].rearrange("(a p) t -> p a t", p=128), oTs[:], reads=["oTs"], writes=["mixT"], chan="d_oTs")
    P.barrier()


def phase_outproj(c, l, src, dst):
    nc, P, S = c.nc, c.P, c.S
    with ExitStack() as pes:
        sb, ps = _tiles(nc, pes)
        wo = sb("wo", [128, 8, D], BF16)
        gb = sb("gb", [128, 2, D], F32)
        xs = [sb(f"x{i}", [128, D], F32) for i in range(2)]
        ms = [sb(f"m{i}", [128, 8, 128], BF16) for i in range(2)]
        ys = [sb(f"y{i}", [128, D], F32) for i in range(2)]
        os_ = [sb(f"o{i}", [128, D], F32) for i in range(2)]
        sts = [sb(f"st{i}", [128, 2, 6], F32) for i in range(2)]
        mvs = [sb(f"mv{i}", [128, 2], F32) for i in range(2)]
        rstds = [sb(f"rstd{i}", [128, 1], F32) for i in range(2)]
        po = [ps(f"po{i}", [128, 512], F32) for i in range(4)]
        load_w(c, wo, c.mix_w_o[l], "wo", 8, 0, D)
        P.dma("sp", gb[:, 0, :], c.ln_g[l, 1, :].partition_broadcast(128), writes=["gb0"])
        P.dma("sp", gb[:, 1, :], c.ln_b[l, 1, :].partition_broadcast(128), writes=["gb1"])
        def load(tt):
            b = tt % 2
            tc_ = slice(tt * 128, (tt + 1) * 128)
            P.dma("sp", xs[b][:], src[tc_, :], writes=[f"x{b}"])
            P.dma("sp", ms[b][:], c.mixT[:, tc_].rearrange("(k p) t -> p k t", p=128), writes=[f"m{b}"])

        NTT = S // 128
        load(0)
        if NTT > 1:
            load(1)
        for tt in range(NTT):
            b = tt % 2
            tc_ = slice(tt * 128, (tt + 1) * 128)
            y, o = ys[b], os_[b]
            for dh in range(2):
                pb_ = 2 * b + dh
                for kc in range(8):
                    P.op("pe", lambda b=b, dh=dh, kc=kc, pb_=pb_: nc.tensor.matmul(po[pb_][:, :], lhsT=ms[b][:, kc, :], rhs=wo[:, kc, dh * 512:(dh + 1) * 512],
                                                                                   start=(kc == 0), stop=(kc == 7)), reads=[f"m{b}", f"wo{kc}"], writes=[f"po{pb_}"])
            P.op("act", lambda b=b, y=y: nc.scalar.mul(y[:], xs[b][:], ALPHA), reads=[f"x{b}"], writes=[f"y{b}"])
            for dh in range(2):
                pb_ = 2 * b + dh
                P.op("dve", lambda dh=dh, y=y, pb_=pb_: nc.vector.tensor_tensor(out=y[:, dh * 512:(dh + 1) * 512], in0=po[pb_][:, :], in1=y[:, dh * 512:(dh + 1) * 512], op=ALU.add),
                     reads=[f"po{pb_}", f"y{b}"], writes=[f"y{b}"])
            layer_norm_tile(c, y, o, sts[b], mvs[b], rstds[b], gb, sfx=str(b))
            if tt + 2 < NTT:
                load(tt + 2)
            P.dma("sp", dst[tc_, :], o[:], reads=[f"o{b}"], writes=["dst"], chan=f"d_o{b}")
    P.barrier()


def phase_ple(c, l, src, dst):
    nc, P, S = c.nc, c.P, c.S
    with ExitStack() as pes:
        sb, ps = _tiles(nc, pes)
        wg = sb("wg", [128, 8, D], BF16)
        wp = sb("wp", [128, 2, D], BF16)
        ident = sb("ident", [128, 128], F32)
        xs = [sb(f"x{i}", [128, D], F32) for i in range(2)]
        pls = [sb(f"pl{i}", [128, PLE], F32) for i in range(2)]
        hTs = [sb(f"hT{i}", [128, 10, 128], BF16) for i in range(2)]
        sgs = [sb(f"sg{i}", [128, D], F32) for i in range(2)]
        os_ = [sb(f"o{i}", [128, D], F32) for i in range(2)]
        pt = [ps(f"pt{i}", [128, 512], F32) for i in range(3)]
        pg = [ps(f"pg{i}", [128, 512], F32) for i in range(2)]
        pp = [ps(f"pp{i}", [128, 512], F32) for i in range(2)]
        load_w(c, wg, c.ple_w_gate[l], "wg", 8, 0, D)
        load_w(c, wp, c.ple_w_proj[l], "wp", 2, 0, D)
        P.dma("sp", ident[:], c.consts[:, c.CO["ident"]:c.CO["ident"] + 128], writes=["ident"])
        def load(tt):
            b = tt % 2
            tc_ = slice(tt * 128, (tt + 1) * 128)
            P.dma("sp", xs[b][:], src[tc_, :], writes=[f"x{b}"])
            P.dma("sp", pls[b][:], c.p[l, tc_, :], writes=[f"pl{b}"])

        NTT = S // 128
        load(0)
        if NTT > 1:
            load(1)
        for tt in range(NTT):
            b = tt % 2
            tc_ = slice(tt * 128, (tt + 1) * 128)
            hT, sg, o = hTs[b], sgs[b], os_[b]
            for grp in range(3):
                n = 4 if grp < 2 else 2
                for q in range(n):
                    kc = grp * 4 + q
                    if grp < 2:
                        P.op("pe", lambda grp=grp, q=q, kc=kc, b=b: nc.tensor.transpose(pt[grp][:, q * 128:(q + 1) * 128], xs[b][:, kc * 128:(kc + 1) * 128], ident[:]),
                             reads=[f"x{b}", "ident"], writes=[f"pt{grp}"])
                    else:
                        P.op("pe", lambda grp=grp, q=q, b=b: nc.tensor.transpose(pt[grp][:, q * 128:(q + 1) * 128], pls[b][:, q * 128:(q + 1) * 128], ident[:]),
                             reads=[f"pl{b}", "ident"], writes=[f"pt{grp}"])
                eng = ("act", "dve", "act")[grp]
                if eng == "act":
                    P.op("act", lambda grp=grp, n=n: nc.scalar.copy(hT[:, grp * 4:grp * 4 + n, :].rearrange("p a b -> p (a b)"), pt[grp][:, 0:n * 128]),
                         reads=[f"pt{grp}"], writes=[f"hT{b}_{grp}"])
                else:
                    P.op("dve", lambda grp=grp, n=n: nc.vector.tensor_copy(hT[:, grp * 4:grp * 4 + n, :].rearrange("p a b -> p (a b)"), pt[grp][:, 0:n * 128]),
                         reads=[f"pt{grp}"], writes=[f"hT{b}_{grp}"])
            for dh in range(2):
                for kc in range(8):
                    P.op("pe", lambda dh=dh, kc=kc: nc.tensor.matmul(pg[dh][:, :], lhsT=hT[:, kc, :], rhs=wg[:, kc, dh * 512:(dh + 1) * 512], start=(kc == 0), stop=(kc == 7)),
                         reads=[f"hT{b}_0", f"hT{b}_1", f"wg{kc}"], writes=[f"pg{dh}"])
                for kc in range(2):
                    P.op("pe", lambda dh=dh, kc=kc: nc.tensor.matmul(pp[dh][:, :], lhsT=hT[:, 8 + kc, :], rhs=wp[:, kc, dh * 512:(dh + 1) * 512], start=(kc == 0), stop=(kc == 1)),
                         reads=[f"hT{b}_2", f"wp{kc}"], writes=[f"pp{dh}"])
                P.op("act", lambda dh=dh: nc.scalar.activation(out=sg[:, dh * 512:(dh + 1) * 512], in_=pg[dh][:, :], func=AF.Sigmoid), reads=[f"pg{dh}"], writes=[f"sg{b}_{dh}"])
                P.op("dve", lambda dh=dh: nc.vector.tensor_tensor(out=sg[:, dh * 512:(dh + 1) * 512], in0=pp[dh][:, :], in1=sg[:, dh * 512:(dh + 1) * 512], op=ALU.mult),
                     reads=[f"pp{dh}", f"sg{b}_{dh}"], writes=[f"sg{b}_{dh}"])
            P.op("pool", lambda b=b: nc.gpsimd.tensor_tensor(out=o[:], in0=sg[:], in1=xs[b][:], op=ALU.add), reads=[f"sg{b}_0", f"sg{b}_1", f"x{b}"], writes=[f"o{b}"])
            if tt + 2 < NTT:
                load(tt + 2)
            P.dma("sp", dst[tc_, :], o[:], reads=[f"o{b}"], writes=["dst"], chan=f"d_o{b}")
    P.barrier()


def all_phases(L):
    ph = [phase_rope]
    for l in range(L):
        src = (lambda c: c.x) if l == 0 else (lambda c: c.hb[1])
        ph.append(lambda c, l=l, src=src: phase_ffn(c, l, 0, src(c), c.hb[0], c.h1T))
        ph.append(lambda c, l=l: phase_gdn(c, l))
        ph.append(lambda c, l=l: phase_sb(c, l))
        ph.append(lambda c, l=l: phase_mla(c, l))
        ph.append(lambda c, l=l: phase_outproj(c, l, c.hb[0], c.hb[1]))
        ph.append(lambda c, l=l: phase_ffn(c, l, 1, c.hb[1], c.hb[0]))
        last = l == L - 1
        ph.append(lambda c, l=l, last=last: phase_ple(c, l, c.hb[0], c.out if last else c.hb[1]))
    return ph


_CACHE = {}


def kernel(**inputs):
    x = np.asarray(inputs["x"], np.float32)
    Bsz, S, _ = x.shape
    L = inputs["ffa_w_in"].shape[0]
    key = (S, L)
    if key not in _CACHE:
        _CACHE[key] = build(S, L, all_phases(L))
    nc, consts = _CACHE[key]
    w = prep_weights({k: np.asarray(v) for k, v in inputs.items()}, L)
    p = np.asarray(inputs["p"], np.float32)
    pos = np.asarray(inputs["positions"], np.int32)
    in_maps = []
    for b in range(Bsz):
        in_maps.append({"x": np.ascontiguousarray(x[b]), "p": np.ascontiguousarray(p[:, b]), "pos": np.ascontiguousarray(pos[b:b + 1]),
                        "consts": consts, **w})
    res = run_bass_kernel_spmd(nc, in_maps, core_ids=list(range(Bsz)))
    return np.stack([np.asarray(r["out"], np.float32) for r in res.results], 0)
```

```python
import numpy as np
from contextlib import ExitStack
import concourse.bass as bass
import concourse.mybir as mybir
from concourse.bass_utils import run_bass_kernel_spmd

F32 = mybir.dt.float32
BF16 = mybir.dt.bfloat16
I32 = mybir.dt.int32
AF = mybir.ActivationFunctionType
ALU = mybir.AluOpType
AX = mybir.AxisListType

D = 1024
DFF = 2816
PLE = 256
DEPTH = 2
ALPHA = (2 * DEPTH) ** 0.25
LN_EPS = 1e-5
RMS_EPS = 1e-6
IN_TOTAL = 3248
C_GQ, C_GK, C_GV, C_GZ, C_GA, C_GB = 0, 512, 1024, 1536, 2048, 2056
C_SQ, C_SK, C_SV, C_MQ, C_MKV, C_KR = 2064, 2320, 2576, 2832, 3088, 3216
SEM_ROT = 1 << 30
TWO_PI = 6.283185307179586


class Prog:
    def __init__(self, nc, es):
        self.nc = nc
        self.es = es
        self.engs = {"pe": nc.tensor, "act": nc.scalar, "dve": nc.vector, "pool": nc.gpsimd, "sp": nc.sync}
        self.nsem = 0
        self.esem = {}
        self.ecnt = {}
        self.allsems = []
        for e in ("pe", "act", "dve", "pool"):
            self.esem[e] = self._newsem("e_" + e)
            self.ecnt[e] = 0
        self.dsem = {}
        self.dcnt = {}
        self.dfree = []
        self.bar1 = None
        self.bar2 = None
        self.nbar = 0
        self.last_w = {}
        self.readers = {}
        self.known = {e: {} for e in self.engs}
        self.nwaits = 0
        self.ninst = 0
        import os
        self.limit = int(os.environ["OP_LIMIT"]) if "OP_LIMIT" in os.environ else None

    def _newsem(self, name):
        self.nsem += 1
        s = self.es.enter_context(self.nc.semaphore(f"{name}_{self.nsem}"))
        self.allsems.append([s, 0])
        return s

    def _bump(self, sem, val):
        for r in self.allsems:
            if r[0] is sem:
                r[1] = max(r[1], val)
                return

    def _deps(self, reads, writes):
        deps = {}

        def add(t):
            cur = deps.get(id(t[0]))
            if cur is None or cur[1] < t[1]:
                deps[id(t[0])] = (t[0], t[1])

        for k in reads:
            t = self.last_w.get(k)
            if t is not None:
                add(t)
        for k in writes:
            t = self.last_w.get(k)
            if t is not None:
                add(t)
            for t in self.readers.get(k, ()):
                add(t)
        return deps

    def _wait(self, eng, deps, skip_sem=None):
        e = self.engs[eng]
        kn = self.known[eng]
        for sid, (sem, val) in deps.items():
            if skip_sem is not None and sem is skip_sem:
                continue
            if kn.get(sid, 0) >= val:
                continue
            e.wait_ge(sem, val)
            self.nwaits += 1
            kn[sid] = val

    def _commit(self, reads, writes, tok):
        for k in writes:
            self.last_w[k] = tok
            self.readers[k] = []
        for k in reads:
            if k in writes:
                continue
            self.readers.setdefault(k, []).append(tok)

    def op(self, eng, fn, reads=(), writes=()):
        if self.limit is not None and self.ninst >= self.limit:
            return None
        pr = [k for k in reads if k in PSUM_KEYS]
        if pr:
            writes = list(writes) + pr
        deps = self._deps(reads, writes)
        self._wait(eng, deps, skip_sem=self.esem["pe"] if eng == "pe" else None)
        inst = fn()
        if self.ecnt[eng] >= SEM_ROT:
            self.esem[eng] = self._newsem("e_" + eng)
            self.ecnt[eng] = 0
        self.ecnt[eng] += 1
        sem = self.esem[eng]
        inst.then_inc(sem, 1)
        tok = (sem, self.ecnt[eng])
        self._bump(sem, self.ecnt[eng])
        self._commit(reads, writes, tok)
        self.ninst += 1
        return tok

    def dma(self, q, out, in_, reads=(), writes=(), chan=None, **kw):
        if self.limit is not None and self.ninst >= self.limit:
            return None
        deps = self._deps(reads, writes)
        self._wait(q, deps)
        if chan is None:
            chan = "d_" + (list(writes) + list(reads))[0]
        if q == "pool":
            chan = "sw_" + chan
            assert chan not in self.dsem
            self.dsem[chan] = self._newsem("sw")
            self.dcnt[chan] = 0
        elif chan not in self.dsem:
            self.dsem[chan] = self.dfree.pop() if self.dfree else self._newsem("d")
            self.dcnt[chan] = 0
        inst = self.engs[q].dma_start(out=out, in_=in_, **kw)
        self.dcnt[chan] += 16
        sem = self.dsem[chan]
        inst.then_inc(sem, 16)
        tok = (sem, self.dcnt[chan])
        self._bump(sem, self.dcnt[chan])
        self._commit(reads, writes, tok)
        self.ninst += 1
        return tok

    def barrier(self):
        for eng in self.engs:
            kn = self.known[eng]
            for sem, val in self.allsems:
                if val > 0 and kn.get(id(sem), 0) < val:
                    self.engs[eng].wait_ge(sem, val)
                    kn[id(sem)] = val
                    self.nwaits += 1
        self.last_w.clear()
        self.readers.clear()
        if not self.dsem:
            return
        if self.bar1 is None:
            self.bar1 = self.es.enter_context(self.nc.semaphore("bar1"))
            self.bar2 = self.es.enter_context(self.nc.semaphore("bar2"))
        self.nbar += 1
        for eng in self.engs:
            self.engs[eng].sem_inc(self.bar1, 1)
        self.engs["pool"].wait_ge(self.bar1, len(self.engs) * self.nbar)
        for chan, sem in self.dsem.items():
            if chan.startswith("sw_"):
                continue
            self.engs["pool"].sem_clear(sem)
            self.dfree.append(sem)
            for r in self.allsems:
                if r[0] is sem:
                    r[1] = 0
            for eng in self.engs:
                self.known[eng].pop(id(sem), None)
        self.engs["pool"].sem_inc(self.bar2, 1)
        for eng in self.engs:
            self.engs[eng].wait_ge(self.bar2, self.nbar)
        self.dsem.clear()
        self.dcnt.clear()


class Ctx:
    pass


_UID = [0]


def _tiles(nc, pes, P=None):
    _UID[0] += 1
    u = _UID[0]
    sb = lambda n, s, d: pes.enter_context(nc.sbuf_tensor(f"{n}_u{u}", s, d))

    def ps(n, s, d):
        PSUM_KEYS.add(n)
        return pes.enter_context(nc.psum_tensor(f"{n}_u{u}", [128, 2048 // mybir.dt.size(d)], d))
    return sb, ps


PSUM_KEYS = set()


def load_w(c, dst, src_rows, key, kchunks, c0, c1):
    n = c1 - c0
    nblk = (n + 1023) // 1024
    while n % nblk:
        nblk += 1
    w = n // nblk
    for i in range(nblk):
        c.P.dma("pool", dst[:, 0:kchunks, i * w:(i + 1) * w], src_rows[0:kchunks * 128, c0 + i * w:c0 + (i + 1) * w].rearrange("(k p) f -> p k f", p=128),
                writes=[f"{key}{kc}" for kc in range(kchunks)], chan=f"{key}_b{i}", max_dma_last_dim=4096)


def consts_phase(c, pes):
    nc, P = c.nc, c.P
    sb, ps = _tiles(nc, pes)
    return


def phase_ffn(c, l, which, src, dst, dstT=None):
    nc, P, S = c.nc, c.P, c.S
    w_in = (c.ffa_w_in if which == 0 else c.ffb_w_in)[l]
    w_out = (c.ffa_w_out if which == 0 else c.ffb_w_out)[l]
    lni = 0 if which == 0 else 2
    KC, FC, NG = D // 128, c.DFF // 128, min(512, S)
    NT = NG // 128
    dff = c.DFF
    with ExitStack() as pes:
        sb, ps = _tiles(nc, pes)
        win = sb("win", [128, KC, 2 * dff], BF16)
        wout = sb("wout", [128, FC, D], BF16)
        ident = sb("ident", [128, 128], F32)
        gb = sb("gb", [128, 2, D], F32)
        x = sb("x", [128, NT, D], F32)
        xT = sb("xT", [128, KC, NG], BF16)
        actT = sb("actT", [128, FC, NG], BF16)
        sg = [sb(f"sg{i}", [128, NG], F32) for i in range(2)]
        y = sb("y", [128, D], F32)
        o = sb("o", [128, D], F32)
        st = sb("st", [128, 2, 6], F32)
        mv = sb("mv", [128, 2], F32)
        rstd = sb("rstd", [128, 1], F32)
        xts = sb("xts", [128, 8, 128], BF16)
        pt = [ps(f"pt{i}", [128, 512], F32) for i in range(2)]
        pg = [ps(f"pg{i}", [128, 512], F32) for i in range(2)]
        pu = [ps(f"pu{i}", [128, 512], F32) for i in range(2)]
        po = [ps(f"po{i}", [128, 512], F32) for i in range(2)]
        P.dma("sp", ident[:], c.consts[:, c.CO["ident"]:c.CO["ident"] + 128], writes=["ident"])
        P.dma("sp", gb[:, 0, :], c.ln_g[l, lni, :].partition_broadcast(128), writes=["gb0"])
        P.dma("sp", gb[:, 1, :], c.ln_b[l, lni, :].partition_broadcast(128), writes=["gb1"])
        load_w(c, win, w_in, "win", KC, 0, 2 * dff)
        load_w(c, wout, w_out, "wout", FC, 0, D)
        pending = []
        for g in range(S // NG):
            t0 = g * NG
            for tt in range(NT):
                P.dma("sp", x[:, tt, :], src[t0 + tt * 128: t0 + (tt + 1) * 128, :], writes=[f"x{tt}"])
            for kc in range(KC):
                p_, pk = pt[kc % 2], f"pt{kc % 2}"
                for tt in range(NT):
                    P.op("pe", lambda p_=p_, tt=tt, kc=kc: nc.tensor.transpose(p_[:, tt * 128:(tt + 1) * 128], x[:, tt, kc * 128:(kc + 1) * 128], ident[:]),
                         reads=[f"x{tt}", "ident"], writes=[pk])
                if kc % 2 == 0:
                    P.op("act", lambda p_=p_, kc=kc: nc.scalar.copy(xT[:, kc, :], p_[:, 0:NG]), reads=[pk], writes=[f"xT{kc}"])
                else:
                    P.op("dve", lambda p_=p_, kc=kc: nc.vector.tensor_copy(xT[:, kc, :], p_[:, 0:NG]), reads=[pk], writes=[f"xT{kc}"])
            for j in range(FC):
                b = j % 2
                for kc in range(KC):
                    P.op("pe", lambda b=b, kc=kc, j=j: nc.tensor.matmul(pg[b][:, 0:NG], lhsT=win[:, kc, j * 128:(j + 1) * 128], rhs=xT[:, kc, :],
                                                                       start=(kc == 0), stop=(kc == KC - 1)),
                         reads=[f"win{kc}", f"xT{kc}"], writes=[f"pg{b}"])
                for kc in range(KC):
                    P.op("pe", lambda b=b, kc=kc, j=j: nc.tensor.matmul(pu[b][:, 0:NG], lhsT=win[:, kc, dff + j * 128: dff + (j + 1) * 128], rhs=xT[:, kc, :],
                                                                       start=(kc == 0), stop=(kc == KC - 1)),
                         reads=[f"win{kc}", f"xT{kc}"], writes=[f"pu{b}"])
                P.op("act", lambda b=b: nc.scalar.activation(out=sg[b][:], in_=pg[b][:, 0:NG], func=AF.Silu), reads=[f"pg{b}"], writes=[f"sg{b}"])
                P.op("dve", lambda b=b, j=j: nc.vector.tensor_tensor(out=actT[:, j, :], in0=pu[b][:, 0:NG], in1=sg[b][:], op=ALU.mult),
                     reads=[f"pu{b}", f"sg{b}"], writes=[f"actT{j}"])
            for tt in range(NT):
                for dh in range(2):
                    for j in range(FC):
                        P.op("pe", lambda tt=tt, dh=dh, j=j: nc.tensor.matmul(po[dh][:, :], lhsT=actT[:, j, tt * 128:(tt + 1) * 128],
                                                                             rhs=wout[:, j, dh * 512:(dh + 1) * 512], start=(j == 0), stop=(j == FC - 1)),
                             reads=[f"actT{j}", f"wout{j}"], writes=[f"po{dh}"])
                while pending:
                    pending.pop(0)()
                P.op("act", lambda tt=tt: nc.scalar.mul(y[:], x[:, tt, :], ALPHA), reads=[f"x{tt}"], writes=["y"])
                for dh in range(2):
                    P.op("dve", lambda dh=dh: nc.vector.scalar_tensor_tensor(out=y[:, dh * 512:(dh + 1) * 512], in0=po[dh][:, :], scalar=0.5,
                                                                           in1=y[:, dh * 512:(dh + 1) * 512], op0=ALU.mult, op1=ALU.add),
                         reads=[f"po{dh}", "y"], writes=["y"])
                layer_norm_tile(c, y, o, st, mv, rstd, gb)
                P.dma("sp", dst[t0 + tt * 128: t0 + (tt + 1) * 128, :], o[:], reads=["o"], writes=["dst"], chan="d_o")
                if dstT is not None:
                  def emitT(t0=t0, tt=tt):
                    for half in range(2):
                        for q in range(4):
                            kc = half * 4 + q
                            P.op("pe", lambda half=half, q=q, kc=kc: nc.tensor.transpose(pt[half][:, q * 128:(q + 1) * 128], o[:, kc * 128:(kc + 1) * 128], ident[:]),
                                 reads=["o", "ident"], writes=[f"pt{half}"])
                        if half == 0:
                            P.op("act", lambda: nc.scalar.copy(xts[:, 0:4, :].rearrange("p a b -> p (a b)"), pt[0][:, :]), reads=["pt0"], writes=["xts0"])
                        else:
                            P.op("dve", lambda: nc.vector.tensor_copy(xts[:, 4:8, :].rearrange("p a b -> p (a b)"), pt[1][:, :]), reads=["pt1"], writes=["xts1"])
                    tok = slice(t0 + tt * 128, t0 + (tt + 1) * 128)
                    P.dma("sp", dstT[:, tok].rearrange("(k p) t -> p k t", p=128), xts[:], reads=["xts0", "xts1"], writes=["dstT"], chan="d_xts")
                  pending.append(emitT)
        while pending:
            pending.pop(0)()
    P.barrier()


def layer_norm_tile(c, y, o, st, mv, rstd, gb, sfx=""):
    nc, P = c.nc, c.P
    ky, ko, kst, kmv, krs = "y" + sfx, "o" + sfx, "st" + sfx, "mv" + sfx, "rstd" + sfx
    for dh in range(2):
        P.op("dve", lambda dh=dh: nc.vector.bn_stats(out=st[:, dh, :], in_=y[:, dh * 512:(dh + 1) * 512]), reads=[ky], writes=[kst])
    P.op("dve", lambda: nc.vector.bn_aggr(out=mv[:], in_=st[:].rearrange("p a b -> p (a b)")), reads=[kst], writes=[kmv])
    P.op("act", lambda: nc.scalar.activation(out=rstd[:], in_=mv[:, 1:2], func=AF.Sqrt, bias=LN_EPS, scale=1.0), reads=[kmv], writes=[krs])
    P.op("dve", lambda: nc.vector.reciprocal(out=rstd[:], in_=rstd[:]), reads=[krs], writes=[krs])
    P.op("dve", lambda: nc.vector.tensor_scalar(out=y[:], in0=y[:], scalar1=mv[:, 0:1], scalar2=rstd[:, 0:1], op0=ALU.subtract, op1=ALU.mult),
         reads=[ky, kmv, krs], writes=[ky])
    P.op("pool", lambda: nc.gpsimd.tensor_tensor(out=o[:], in0=y[:], in1=gb[:, 0, :], op=ALU.mult), reads=[ky, "gb0"], writes=[ko])
    P.op("pool", lambda: nc.gpsimd.tensor_tensor(out=o[:], in0=o[:], in1=gb[:, 1, :], op=ALU.add), reads=[ko, "gb1"], writes=[ko])


def phase_transpose(c, src, dstT):
    nc, P, S = c.nc, c.P, c.S
    with ExitStack() as pes:
        sb, ps = _tiles(nc, pes)
        ident = sb("ident", [128, 128], F32)
        xs = [sb(f"x{i}", [128, D], F32) for i in range(2)]
        xts = [sb(f"xt{i}", [128, 8, 128], BF16) for i in range(2)]
        pt = [ps(f"pt{i}", [128, 512], F32) for i in range(4)]
        P.dma("sp", ident[:], c.consts[:, c.CO["ident"]:c.CO["ident"] + 128], writes=["ident"])
        for tt in range(S // 128):
            b = tt % 2
            P.dma("sp", xs[b][:], src[tt * 128:(tt + 1) * 128, :], writes=[f"x{b}"])
            for half in range(2):
                pp, pk = pt[2 * b + half], f"pt{2 * b + half}"
                for q in range(4):
                    kc = half * 4 + q
                    P.op("pe", lambda pp=pp, q=q, kc=kc, b=b: nc.tensor.transpose(pp[:, q * 128:(q + 1) * 128], xs[b][:, kc * 128:(kc + 1) * 128], ident[:]),
                         reads=[f"x{b}", "ident"], writes=[pk])
                eng = "act" if half == 0 else "dve"
                if half == 0:
                    P.op("act", lambda pp=pp, b=b: nc.scalar.copy(xts[b][:, 0:4, :].rearrange("p a b -> p (a b)"), pp[:, :]), reads=[pk], writes=[f"xt{b}h0"])
                else:
                    P.op("dve", lambda pp=pp, b=b: nc.vector.tensor_copy(xts[b][:, 4:8, :].rearrange("p a b -> p (a b)"), pp[:, :]), reads=[pk], writes=[f"xt{b}h1"])
            P.dma("sp", dstT[:, tt * 128:(tt + 1) * 128].rearrange("(k p) t -> p k t", p=128), xts[b][:], reads=[f"xt{b}h0", f"xt{b}h1"],
                  writes=["dstT"], chan=f"d_xt{b}")
    P.barrier()


def phase_rope(c):
    nc, P, S = c.nc, c.P, c.S
    with ExitStack() as pes:
        sb, ps = _tiles(nc, pes)
        pi_ = sb("pi", [32, S], I32)
        ang = sb("ang", [32, S], F32)
        k = sb("k", [32, S], F32)
        r = sb("r", [32, S], F32)
        rc = sb("rc", [32, S], F32)
        m = sb("m", [32, S], F32)
        cs = sb("cs", [32, S], F32)
        sn = sb("sn", [32, S], F32)
        cst = sb("cst", [32, 2], F32)
        MAGIC = 12582912.0
        C1 = 6.28125
        C2 = TWO_PI - C1
        PI_LO = 3.1415925
        P.dma("sp", pi_[:], c.pos[0, :].partition_broadcast(32), writes=["pi"])
        P.dma("sp", cst[:], c.consts[0:32, c.CO["rope"]:c.CO["rope"] + 2], writes=["cst"])
        P.op("dve", lambda: nc.vector.tensor_copy(ang[:], pi_[:]), reads=["pi"], writes=["ang"])
        P.op("dve", lambda: nc.vector.tensor_scalar(out=ang[:], in0=ang[:], scalar1=cst[:, 0:1], scalar2=None, op0=ALU.mult), reads=["ang", "cst"], writes=["ang"])
        P.op("dve", lambda: nc.vector.tensor_scalar(out=k[:], in0=ang[:], scalar1=1.0 / TWO_PI, scalar2=MAGIC, op0=ALU.mult, op1=ALU.add), reads=["ang"], writes=["k"])
        P.op("dve", lambda: nc.vector.tensor_scalar(out=k[:], in0=k[:], scalar1=MAGIC, scalar2=None, op0=ALU.subtract), reads=["k"], writes=["k"])
        P.op("dve", lambda: nc.vector.scalar_tensor_tensor(out=r[:], in0=k[:], scalar=-C1, in1=ang[:], op0=ALU.mult, op1=ALU.add), reads=["k", "ang"], writes=["r"])
        P.op("dve", lambda: nc.vector.scalar_tensor_tensor(out=r[:], in0=k[:], scalar=-C2, in1=r[:], op0=ALU.mult, op1=ALU.add), reads=["k", "r"], writes=["r"])
        P.op("dve", lambda: nc.vector.tensor_scalar(out=rc[:], in0=r[:], scalar1=np.pi / 2, scalar2=None, op0=ALU.add), reads=["r"], writes=["rc"])
        P.op("dve", lambda: nc.vector.tensor_scalar(out=m[:], in0=rc[:], scalar1=np.pi, scalar2=-TWO_PI, op0=ALU.is_gt, op1=ALU.mult), reads=["rc"], writes=["m"])
        P.op("dve", lambda: nc.vector.tensor_tensor(out=rc[:], in0=rc[:], in1=m[:], op=ALU.add), reads=["rc", "m"], writes=["rc"])
        for t_, nm in ((r, "r"), (rc, "rc")):
            P.op("dve", lambda t_=t_: nc.vector.tensor_scalar(out=t_[:], in0=t_[:], scalar1=PI_LO, scalar2=-PI_LO, op0=ALU.min, op1=ALU.max), reads=[nm], writes=[nm])
        import os
        SINF = AF.Identity if os.environ.get("ROPE_DBG") == "nosin" else AF.Sin
        P.op("act", lambda: nc.scalar.activation(out=sn[:], in_=r[:], func=SINF), reads=["r"], writes=["sn"])
        P.op("act", lambda: nc.scalar.activation(out=cs[:], in_=rc[:], func=SINF), reads=["rc"], writes=["cs"])
        P.op("dve", lambda: nc.vector.tensor_scalar(out=sn[:], in0=sn[:], scalar1=cst[:, 1:2], scalar2=None, op0=ALU.mult), reads=["sn", "cst"], writes=["sn"])
        P.dma("sp", c.ropeT[0, :, :], cs[:], reads=["cs"], writes=["ropeT0"])
        P.dma("sp", c.ropeT[1, :, :], sn[:], reads=["sn"], writes=["ropeT1"])
    P.barrier()


def phase_mla(c, l):
    nc, P, S = c.nc, c.P, c.S
    NB = max(1, S // 512)
    NG = min(512, S)
    NTT = S // 128
    SCALE = 96 ** -0.5
    H = 4
    w_in = c.mix_w_in[l]
    with ExitStack() as pes:
        sb, ps = _tiles(nc, pes)
        xTb = [sb(f"xT{i}", [128, 8, NG], BF16) for i in range(2)]
        wq = sb("wq", [128, 8, 256], BF16)
        wkv = sb("wkv", [128, 8, 128], BF16)
        wkr = sb("wkr", [128, 8, 192], BF16)
        wuq = sb("wuq", [128, 2, 768], BF16)
        wukk = sb("wukk", [128, 1, 256], BF16)
        wukv = sb("wukv", [128, 1, 256], BF16)
        nwq = sb("nwq", [128, 2], F32)
        nwkv = sb("nwkv", [128, 1], F32)
        ones = sb("ones", [128, 128], BF16)
        onesf = sb("onesf", [128, 128], F32)
        mk = sb("mk", [128, 1, 2048], BF16)
        cs = sb("cs", [96, NG], F32)
        sn = sb("sn", [96, NG], F32)
        cqn = sb("cqn", [128, 2, NG], BF16)
        ckvn = sb("ckvn", [128, NG], BF16)
        krT = sb("krT", [96, NG], BF16)
        qf = [sb(f"qf{h}", [96, S], BF16) for h in range(H)]
        kf = [sb(f"kf{h}", [96, S], BF16) for h in range(H)]
        V = sb("V", [128, NTT, 256], BF16)
        tmp = [sb(f"tmp{i}", [128, NG], F32) for i in range(3)]
        sq = [sb(f"sq{i}", [128, NG], BF16) for i in range(2)]
        rbc = sb("rbc", [128, NG], F32)
        pe_ = [sb(f"pe{i}", [128, NG], BF16) for i in range(5)]
        rec = [sb(f"rec{i}", [64, NG], F32) for i in range(2)]
        oT = [sb(f"oT{i}", [64, NG], BF16) for i in range(2)]
        pA = [ps(f"pA{i}", [128, 512], F32) for i in range(2)]
        pB = ps("pB", [128, 512], F32)
        pS = [ps(f"pS{i}", [128, 512], F32) for i in range(2)]
        pL = ps("pL", [64, 512], F32)
        CO = c.CO
        load_w(c, wq, w_in, "wq", 8, C_MQ, C_MQ + 256)
        load_w(c, wkv, w_in, "wkv", 8, C_MKV, C_MKV + 128)
        load_w(c, wkr, c.w_kr[l], "wkr", 8, 0, 192)
        load_w(c, wuq, c.w_uq_p[l], "wuq", 2, 0, 768)
        load_w(c, wukk, c.w_ukv_k[l], "wukk", 1, 0, 256)
        load_w(c, wukv, c.w_ukv_v[l], "wukv", 1, 0, 256)
        P.dma("sp", nwq[:], c.q_norm_wT[l], writes=["nwq"])
        P.dma("sp", nwkv[:], c.kv_norm_wT[l], writes=["nwkv"])
        P.dma("sp", onesf[:], c.consts[:, CO["ones"]:CO["ones"] + 128], writes=["onesf"])
        P.op("dve", lambda: nc.vector.tensor_copy(ones[:], onesf[:]), reads=["onesf"], writes=["ones"])
        load_w(c, mk, c.consts, "mk", 1, CO["maskle"], CO["maskle"] + 2048)
        wkeys = lambda n, k: [f"{n}{i}" for i in range(k)]
        for nb in range(NB):
            cols = slice(nb * NG, (nb + 1) * NG)
            xT = xTb[nb % 2]
            xk = f"xT{nb % 2}"
            P.dma("sp", xT[:], c.h1T[:, cols].rearrange("(k p) t -> p k t", p=128), writes=[xk])
            P.dma("sp", cs[64:96, :], c.ropeT[0, :, cols], writes=["cs"])
            P.dma("sp", sn[64:96, :], c.ropeT[1, :, cols], writes=["sn"])
            for j in range(2):
                pa = pA[j]
                for kc in range(8):
                    P.op("pe", lambda pa=pa, kc=kc, j=j: nc.tensor.matmul(pa[:, 0:NG], lhsT=wq[:, kc, j * 128:(j + 1) * 128], rhs=xT[:, kc, :],
                                                                         start=(kc == 0), stop=(kc == 7)), reads=[f"wq{kc}", xk], writes=[f"pA{j}"])
                P.op("act", lambda pa=pa, j=j: nc.scalar.activation(out=sq[j][:], in_=pa[:, 0:NG], func=AF.Square), reads=[f"pA{j}"], writes=[f"sq{j}"])
                P.op("dve", lambda pa=pa, j=j: nc.vector.tensor_scalar(out=tmp[j][:], in0=pa[:, 0:NG], scalar1=nwq[:, j:j + 1], scalar2=None, op0=ALU.mult),
                     reads=[f"pA{j}", "nwq"], writes=[f"tmp{j}"])
            for j in range(2):
                P.op("pe", lambda j=j: nc.tensor.matmul(pB[:, 0:NG], lhsT=ones[:], rhs=sq[j][:], start=(j == 0), stop=(j == 1)),
                     reads=["ones", f"sq{j}"], writes=["pB"])
            P.op("act", lambda: nc.scalar.activation(out=rbc[:], in_=pB[:, 0:NG], func=AF.Sqrt, bias=RMS_EPS, scale=1.0 / 256), reads=["pB"], writes=["rbc"])
            P.op("dve", lambda: nc.vector.reciprocal(out=rbc[:], in_=rbc[:]), reads=["rbc"], writes=["rbc"])
            for j in range(2):
                P.op("dve", lambda j=j: nc.vector.tensor_tensor(out=cqn[:, j, :], in0=tmp[j][:], in1=rbc[:], op=ALU.mult),
                     reads=[f"tmp{j}", "rbc"], writes=["cqn"])
            pa = pA[0]
            for kc in range(8):
                P.op("pe", lambda kc=kc: nc.tensor.matmul(pa[:, 0:NG], lhsT=wkv[:, kc, :], rhs=xT[:, kc, :], start=(kc == 0), stop=(kc == 7)),
                     reads=[f"wkv{kc}", xk], writes=["pA0"])
            P.op("act", lambda: nc.scalar.activation(out=sq[0][:], in_=pa[:, 0:NG], func=AF.Square), reads=["pA0"], writes=["sq0"])
            P.op("dve", lambda: nc.vector.tensor_scalar(out=tmp[0][:], in0=pa[:, 0:NG], scalar1=nwkv[:, 0:1], scalar2=None, op0=ALU.mult),
                 reads=["pA0", "nwkv"], writes=["tmp0"])
            P.op("pe", lambda: nc.tensor.matmul(pB[:, 0:NG], lhsT=ones[:], rhs=sq[0][:], start=True, stop=True), reads=["ones", "sq0"], writes=["pB"])
            P.op("act", lambda: nc.scalar.activation(out=rbc[:], in_=pB[:, 0:NG], func=AF.Sqrt, bias=RMS_EPS, scale=1.0 / 128), reads=["pB"], writes=["rbc"])
            P.op("dve", lambda: nc.vector.reciprocal(out=rbc[:], in_=rbc[:]), reads=["rbc"], writes=["rbc"])
            P.op("dve", lambda: nc.vector.tensor_tensor(out=ckvn[:, :], in0=tmp[0][:], in1=rbc[:], op=ALU.mult), reads=["tmp0", "rbc"], writes=["ckvn"])
            pa = pA[1]
            for kc in range(8):
                P.op("pe", lambda kc=kc: nc.tensor.matmul(pa[0:96, 0:NG], lhsT=wkr[:, kc, 0:96], rhs=xT[:, kc, :], start=(kc == 0), stop=(kc == 7)),
                     reads=[f"wkr{kc}", xk], writes=["pA1"])
            for kc in range(8):
                P.op("pe", lambda kc=kc: nc.tensor.matmul(pB[0:96, 0:NG], lhsT=wkr[:, kc, 96:192], rhs=xT[:, kc, :], start=(kc == 0), stop=(kc == 7)),
                     reads=[f"wkr{kc}", xk], writes=["pB"])
            P.op("dve", lambda: nc.vector.tensor_tensor(out=tmp[1][64:96, :], in0=pa[64:96, 0:NG], in1=cs[64:96, :], op=ALU.mult), reads=["pA1", "cs"], writes=["tmp1"])
            P.op("dve", lambda: nc.vector.tensor_tensor(out=tmp[2][64:96, :], in0=pB[64:96, 0:NG], in1=sn[64:96, :], op=ALU.mult), reads=["pB", "sn"], writes=["tmp2"])
            P.op("pool", lambda: nc.gpsimd.tensor_tensor(out=krT[64:96, :], in0=tmp[1][64:96, :], in1=tmp[2][64:96, :], op=ALU.add), reads=["tmp1", "tmp2"], writes=["krT"])
            for h in range(H):
                pa = pA[h % 2]
                pak = f"pA{h % 2}"
                for j in range(2):
                    P.op("pe", lambda pa=pa, j=j, h=h: nc.tensor.matmul(pa[0:96, 0:NG], lhsT=wuq[:, j, h * 192:h * 192 + 96], rhs=cqn[:, j, :],
                                                                       start=(j == 0), stop=(j == 1)), reads=[f"wuq{j}", "cqn"], writes=[pak])
                for j in range(2):
                    P.op("pe", lambda j=j, h=h: nc.tensor.matmul(pB[0:96, 0:NG], lhsT=wuq[:, j, h * 192 + 96:h * 192 + 192], rhs=cqn[:, j, :],
                                                                start=(j == 0), stop=(j == 1)), reads=[f"wuq{j}", "cqn"], writes=["pB"])
                P.op("act", lambda pa=pa, h=h: nc.scalar.copy(qf[h][0:64, cols], pa[0:64, 0:NG]), reads=[pak], writes=[f"qf{h}"])
                P.op("dve", lambda pa=pa: nc.vector.tensor_tensor(out=tmp[1][64:96, :], in0=pa[64:96, 0:NG], in1=cs[64:96, :], op=ALU.mult), reads=[pak, "cs"], writes=["tmp1"])
                P.op("dve", lambda: nc.vector.tensor_tensor(out=tmp[2][64:96, :], in0=pB[64:96, 0:NG], in1=sn[64:96, :], op=ALU.mult), reads=["pB", "sn"], writes=["tmp2"])
                P.op("pool", lambda h=h: nc.gpsimd.tensor_tensor(out=qf[h][64:96, cols], in0=tmp[1][64:96, :], in1=tmp[2][64:96, :], op=ALU.add),
                     reads=["tmp1", "tmp2"], writes=[f"qf{h}"])
                P.op("pe", lambda pa=pa, h=h: nc.tensor.matmul(pa[0:64, 0:NG], lhsT=wukk[:, 0, h * 64:(h + 1) * 64], rhs=ckvn[:, :], start=True, stop=True),
                     reads=["wukk0", "ckvn"], writes=[pak])
                P.op("act", lambda pa=pa, h=h: nc.scalar.copy(kf[h][0:64, cols], pa[0:64, 0:NG]), reads=[pak], writes=[f"kf{h}"])
                P.op("pool", lambda h=h: nc.gpsimd.tensor_copy(kf[h][64:96, cols], krT[64:96, :]), reads=["krT"], writes=[f"kf{h}"])
            for q in range(NG // 128):
                tt = nb * (NG // 128) + q
                P.op("pe", lambda tt=tt: nc.tensor.matmul(pB[:, 0:256], lhsT=ckvn[:, q * 128:(q + 1) * 128], rhs=wukv[:, 0, :], start=True, stop=True),
                     reads=["ckvn", "wukv0"], writes=["pB"])
                P.op("act", lambda tt=tt: nc.scalar.copy(V[:, tt, :], pB[:, 0:256]), reads=["pB"], writes=["V"])
        LA = 3
        NPE = 5
        pOb = (pA[0], pA[1])
        pOk = ("pA0", "pA1")
        pLb = (pB, pL)
        pLk = ("pB", "pL")
        for hp in range(2):
            meta = []
            for tq in range(NB):
                nkb = (tq + 1) * (NG // 128)
                for kb in range(nkb):
                    for hh in range(2):
                        meta.append(dict(tq=tq, kb=kb, hh=hh, nkb=nkb, zb=len(meta) % 2, eb=len(meta) % NPE))

            def stageA(m):
                tq, kb, hh, zb, eb = m["tq"], m["kb"], m["hh"], m["zb"], m["eb"]
                h = hp * 2 + hh
                qcols = slice(tq * NG, (tq + 1) * NG)
                P.op("pe", lambda: nc.tensor.matmul(pS[zb][:, 0:NG], lhsT=kf[h][:, kb * 128:(kb + 1) * 128], rhs=qf[h][:, qcols], start=True, stop=True),
                     reads=[f"kf{h}", f"qf{h}"], writes=[f"pS{zb}"])
                P.op("act", lambda: nc.scalar.activation(out=pe_[eb][:], in_=pS[zb][:, 0:NG], func=AF.Exp, scale=SCALE), reads=[f"pS{zb}"], writes=[f"pe{eb}"])
                r = kb - tq * (NG // 128)
                if r >= 0:
                    P.op("pool", lambda: nc.gpsimd.tensor_tensor(out=pe_[eb][:], in0=pe_[eb][:], in1=mk[:, 0, r * 512:r * 512 + NG], op=ALU.mult),
                         reads=[f"pe{eb}", "mk0"], writes=[f"pe{eb}"])

            def stageB(m):
                tq, kb, hh, nkb, eb = m["tq"], m["kb"], m["hh"], m["nkb"], m["eb"]
                h = hp * 2 + hh
                qcols = slice(tq * NG, (tq + 1) * NG)
                P.op("pe", lambda: nc.tensor.matmul(pOb[hh][0:64, 0:NG], lhsT=V[:, kb, h * 64:(h + 1) * 64], rhs=pe_[eb][:], start=(kb == 0), stop=(kb == nkb - 1)),
                     reads=["V", f"pe{eb}"], writes=[pOk[hh]])
                P.op("pe", lambda: nc.tensor.matmul(pLb[hh][0:64, 0:NG], lhsT=ones[:, 0:64], rhs=pe_[eb][:], start=(kb == 0), stop=(kb == nkb - 1)),
                     reads=["ones", f"pe{eb}"], writes=[pLk[hh]])
                if kb == nkb - 1:
                    P.op("dve", lambda: nc.vector.reciprocal(out=rec[hh][:], in_=pLb[hh][0:64, 0:NG]), reads=[pLk[hh]], writes=[f"rec{hh}"])
                    P.op("dve", lambda: nc.vector.tensor_tensor(out=oT[hh][:], in0=pOb[hh][0:64, 0:NG], in1=rec[hh][:], op=ALU.mult), reads=[pOk[hh], f"rec{hh}"], writes=[f"oT{hh}"])
                    P.dma("sp", c.mixT[768 + h * 64: 768 + (h + 1) * 64, qcols], oT[hh][:], reads=[f"oT{hh}"], writes=["mixT"], chan=f"d_oT{hh}")

            n = len(meta)
            for i in range(min(LA, n)):
                stageA(meta[i])
            for i in range(n):
                if i + LA < n:
                    stageA(meta[i + LA])
                stageB(meta[i])
    P.barrier()


def make_consts():
    CO = {}
    cols = []
    off = 0

    def add(name, arr):
        nonlocal off
        a = np.zeros((128, arr.shape[1]), np.float32)
        a[:arr.shape[0]] = arr
        CO[name] = off
        off += arr.shape[1]
        cols.append(a)

    add("ident", np.eye(128, dtype=np.float32))
    add("ones", np.ones((128, 128), np.float32))
    s = np.arange(128)[:, None]
    t = np.arange(512)[None, :]
    add("maskle", np.concatenate([((r * 128 + s) <= t).astype(np.float32) for r in range(4)], 1))
    add("masklt", np.concatenate([((r * 128 + s) < t).astype(np.float32) for r in range(4)], 1))
    j = np.arange(128)[:, None]
    i = np.arange(128)[None, :]
    add("trineg", -(j >= i).astype(np.float32))
    add("compneg", -(j < i).astype(np.float32))
    add("triinc", (j <= i).astype(np.float32))
    add("upincl", (i >= j).astype(np.float32))
    add("upstrict", (i > j).astype(np.float32))
    inv = (1.0 / (10000.0 ** (np.arange(0, 32, 2, dtype=np.float32) / 32))).astype(np.float32)
    rope = np.zeros((32, 2), np.float32)
    rope[:, 0] = np.concatenate([inv, inv])
    rope[:, 1] = np.concatenate([-np.ones(16), np.ones(16)])
    add("rope", rope)
    return np.concatenate(cols, 1), CO


def prep_weights(inp, L):
    f = lambda a: np.ascontiguousarray(a, dtype=np.float32)
    w_in = inp["mix_w_in"]
    out = {}
    z64 = np.zeros(w_in.shape[:2] + (64,), np.float32)
    out["w_kr"] = f(np.concatenate([z64, w_in[:, :, C_KR:C_KR + 32], z64, w_in[:, :, C_KR + 16:C_KR + 32], w_in[:, :, C_KR:C_KR + 16]], -1))
    uq = inp["mla_w_uq"]
    parts = []
    for h in range(4):
        b = h * 96
        parts += [uq[:, :, b:b + 64], uq[:, :, b + 64:b + 96], np.zeros(uq.shape[:2] + (64,), np.float32), uq[:, :, b + 80:b + 96], uq[:, :, b + 64:b + 80]]
    out["w_uq_p"] = f(np.concatenate(parts, -1))
    ukv = inp["mla_w_ukv"]
    out["w_ukv_k"] = f(np.concatenate([ukv[:, :, h * 128:h * 128 + 64] for h in range(4)], -1))
    out["w_ukv_v"] = f(np.concatenate([ukv[:, :, h * 128 + 64:h * 128 + 128] for h in range(4)], -1))
    out["q_norm_wT"] = f(inp["mla_q_norm_w"].reshape(L, 2, 128).transpose(0, 2, 1))
    out["kv_norm_wT"] = f(inp["mla_kv_norm_w"].reshape(L, 1, 128).transpose(0, 2, 1))
    out["conv_wT"] = f(inp["gdn_conv_w"].reshape(L, 4, 12, 128).transpose(0, 3, 2, 1))
    for k in ("ffa_w_in", "ffa_w_out", "mix_w_in", "gdn_a_log", "gdn_dt_bias", "gdn_norm_w", "mix_w_o", "ffb_w_in", "ffb_w_out",
              "ln_g", "ln_b", "ple_w_gate", "ple_w_proj"):
        out[k] = f(inp[k])
    return out


WSHAPES = lambda L, dff: {
    "ffa_w_in": [L, D, 2 * dff], "ffa_w_out": [L, dff, D], "mix_w_in": [L, D, IN_TOTAL], "w_kr": [L, D, 192],
    "w_uq_p": [L, 256, 768], "w_ukv_k": [L, 128, 256], "w_ukv_v": [L, 128, 256], "q_norm_wT": [L, 128, 2], "kv_norm_wT": [L, 128, 1],
    "conv_wT": [L, 128, 12, 4], "gdn_a_log": [L, 8], "gdn_dt_bias": [L, 8], "gdn_norm_w": [L, 64], "mix_w_o": [L, D, D],
    "ffb_w_in": [L, D, 2 * dff], "ffb_w_out": [L, dff, D], "ln_g": [L, 3, D], "ln_b": [L, 3, D], "ple_w_gate": [L, D, D], "ple_w_proj": [L, PLE, D],
}


def build(S, L, phases, dff=DFF, debug=False):
    nc = bass.Bass("TRN2", target_bir_lowering=False)
    c = Ctx()
    c.nc, c.S, c.DFF, c.L = nc, S, dff, L
    consts_np, c.CO = make_consts()
    din = lambda n, s, d=F32: nc.dram_tensor(n, s, d, kind="ExternalInput").ap()
    c.x = din("x", [S, D])
    c.p = din("p", [L, S, PLE])
    c.pos = din("pos", [1, S], I32)
    c.consts = din("consts", list(consts_np.shape))
    for k, shp in WSHAPES(L, dff).items():
        setattr(c, k, din(k, shp))
    kind = "ExternalOutput" if debug else "Internal"
    scr = lambda n, s, d: nc.dram_tensor(n, s, d, kind=kind).ap()
    c.hb = [scr("hb0", [S, D], F32), scr("hb1", [S, D], F32)]
    c.h1T = scr("h1T", [D, S], BF16)
    c.mixT = scr("mixT", [D, S], BF16)
    c.ropeT = scr("ropeT", [2, 32, S], F32)
    c.out = nc.dram_tensor("out", [S, D], F32, kind="ExternalOutput").ap()
    with ExitStack() as es:
        c.P = Prog(nc, es)
        for ph in phases:
            ph(c)
        c.P.barrier()
        print("ninst", c.P.ninst, "nwaits", c.P.nwaits, "nsem", c.P.nsem)
    return nc, consts_np


def phase_sb(c, l):
    nc, P, S = c.nc, c.P, c.S
    NG = min(512, S)
    NB = S // NG
    NTT = S // 128
    R = NG // 128
    H = 4
    w_in = c.mix_w_in[l]
    CO = c.CO
    with ExitStack() as pes:
        sb, ps = _tiles(nc, pes)
        xTb = [sb(f"xT{i}", [128, 8, NG], BF16) for i in range(2)]
        wq = sb("wq", [128, 8, 256], BF16)
        wk = sb("wk", [128, 8, 256], BF16)
        wv = sb("wv", [128, 8, 256], BF16)
        qT = [sb(f"qT{j}", [128, S], BF16) for j in range(2)]
        kT = [sb(f"kT{j}", [128, S], BF16) for j in range(2)]
        V = sb("V", [128, NTT, 256], BF16)
        cf = sb("cf", [128, 256], F32)
        tri = sb("tri", [128, 128], BF16)
        comp = sb("comp", [128, 128], BF16)
        mkf = sb("mkf", [128, 4, 512], F32)
        e = [sb(f"e{i}", [128, NG], F32) for i in range(5)]
        sp = [sb(f"sp{i}", [128, NG], BF16) for i in range(10)]
        eC = [sb(f"eC{i}", [128, NG], F32) for i in range(2)]
        A = [sb(f"A{i}", [128, NG], BF16) for i in range(4)]
        oT = [sb(f"oT{i}", [64, NG], BF16) for i in range(2)]
        pZ = [ps(f"pZ{i}", [128, 512], F32) for i in range(2)]
        pC = [ps(f"pC{i}", [128, 512], F32) for i in range(2)]
        pO = [ps(f"pO{i}", [128, 512], F32) for i in range(2)]
        pX = ps("pX", [128, 512], F32)
        load_w(c, wq, w_in, "wq", 8, C_SQ, C_SQ + 256)
        load_w(c, wk, w_in, "wk", 8, C_SK, C_SK + 256)
        load_w(c, wv, w_in, "wv", 8, C_SV, C_SV + 256)
        P.dma("sp", cf[:, 0:128], c.consts[:, CO["trineg"]:CO["trineg"] + 128], writes=["cf"])
        P.dma("sp", cf[:, 128:256], c.consts[:, CO["compneg"]:CO["compneg"] + 128], writes=["cf"])
        P.op("dve", lambda: nc.vector.tensor_copy(tri[:], cf[:, 0:128]), reads=["cf"], writes=["tri"])
        P.op("dve", lambda: nc.vector.tensor_copy(comp[:], cf[:, 128:256]), reads=["cf"], writes=["comp"])
        P.dma("sp", mkf[:].rearrange("p a b -> p (a b)"), c.consts[:, CO["masklt"]:CO["masklt"] + 2048], writes=["mkf"])
        for nb in range(NB):
            cols = slice(nb * NG, (nb + 1) * NG)
            xT = xTb[nb % 2]
            xk = f"xT{nb % 2}"
            P.dma("sp", xT[:], c.h1T[:, cols].rearrange("(k p) t -> p k t", p=128), writes=[xk])
            for (w_, dst, nm) in ((wq, qT, "qT"), (wk, kT, "kT")):
                for j in range(2):
                    for kc in range(8):
                        P.op("pe", lambda w_=w_, kc=kc, j=j: nc.tensor.matmul(pX[:, 0:NG], lhsT=w_[:, kc, j * 128:(j + 1) * 128], rhs=xT[:, kc, :],
                                                                             start=(kc == 0), stop=(kc == 7)), reads=[xk] + [f"wq{kc}", f"wk{kc}"], writes=["pX"])
                    P.op("act", lambda dst=dst, j=j: nc.scalar.copy(dst[j][:, cols], pX[:, 0:NG]), reads=["pX"], writes=[f"{nm}{j}"])
            for q in range(R):
                tt = nb * R + q
                for kc in range(8):
                    P.op("pe", lambda q=q, kc=kc: nc.tensor.matmul(pX[:, 0:256], lhsT=xT[:, kc, q * 128:(q + 1) * 128], rhs=wv[:, kc, :],
                                                                    start=(kc == 0), stop=(kc == 7)), reads=[xk, f"wv{kc}"], writes=["pX"])
                P.op("dve", lambda tt=tt: nc.vector.tensor_copy(V[:, tt, :], pX[:, 0:256]), reads=["pX"], writes=["V"])
        LA = 3
        NE, NSP = 5, 5
        for hp in range(2):
            items = []
            for tq in range(NB):
                nkb = (tq + 1) * R
                for kb in range(nkb - 1, -1, -1):
                    for hh in range(2):
                        items.append((tq, kb, hh, nkb))
            percnt = [0, 0]
            meta = []
            for idx, (tq, kb, hh, nkb) in enumerate(items):
                itn = percnt[hh]
                percnt[hh] += 1
                meta.append(dict(tq=tq, kb=kb, hh=hh, nkb=nkb, zb=idx % 2, eb=idx % NE, ab=idx % 4, spb=hh * NSP + itn % NSP, spp=hh * NSP + (itn - 1) % NSP))

            def stageA(m):
                tq, kb, hh = m["tq"], m["kb"], m["hh"]
                qcols = slice(tq * NG, (tq + 1) * NG)
                pb, j, zb, eb, spb = hh * 64, hp, m["zb"], m["eb"], m["spb"]
                r = kb - tq * R
                P.op("pe", lambda: nc.tensor.matmul(pZ[zb][:, 0:NG], lhsT=kT[j][pb:pb + 64, kb * 128:(kb + 1) * 128], rhs=qT[j][pb:pb + 64, qcols], start=True, stop=True),
                     reads=[f"kT{j}", f"qT{j}"], writes=[f"pZ{zb}"])
                P.op("act", lambda: nc.scalar.activation(out=e[eb][:], in_=pZ[zb][:, 0:NG], func=AF.Exp, scale=0.125), reads=[f"pZ{zb}"], writes=[f"e{eb}"])
                if r >= 0:
                    P.op("dve", lambda: nc.vector.tensor_tensor(out=e[eb][:], in0=e[eb][:], in1=mkf[:, r, 0:NG], op=ALU.mult), reads=[f"e{eb}", "mkf"], writes=[f"e{eb}"])
                P.op("act", lambda: nc.scalar.activation(out=sp[spb][:], in_=e[eb][:], func=AF.Ln, bias=1.0, scale=1.0), reads=[f"e{eb}"], writes=[f"sp{spb}"])

            def stageB(m):
                tq, kb, hh, nkb = m["tq"], m["kb"], m["hh"], m["nkb"]
                qcols = slice(tq * NG, (tq + 1) * NG)
                h = hp * 2 + hh
                eb, ab, spb, spp = m["eb"], m["ab"], m["spb"], m["spp"]
                first = kb == nkb - 1
                if not first:
                    P.op("pe", lambda: nc.tensor.matmul(pC[hh][:, 0:NG], lhsT=comp[:], rhs=sp[spp][:], start=False, stop=False, skip_group_check=True),
                         reads=["comp", f"sp{spp}"], writes=[f"pC{hh}"])
                P.op("pe", lambda: nc.tensor.matmul(pC[hh][:, 0:NG], lhsT=tri[:], rhs=sp[spb][:], start=first, stop=True, skip_group_check=True),
                     reads=["tri", f"sp{spb}"], writes=[f"pC{hh}"])
                P.op("act", lambda: nc.scalar.activation(out=eC[hh][:], in_=pC[hh][:, 0:NG], func=AF.Exp), reads=[f"pC{hh}"], writes=[f"eC{hh}"])
                P.op("dve", lambda: nc.vector.tensor_tensor(out=A[ab][:], in0=e[eb][:], in1=eC[hh][:], op=ALU.mult), reads=[f"e{eb}", f"eC{hh}"], writes=[f"A{ab}"])

            def stageB2(m):
                tq, kb, hh, nkb = m["tq"], m["kb"], m["hh"], m["nkb"]
                qcols = slice(tq * NG, (tq + 1) * NG)
                h = hp * 2 + hh
                ab = m["ab"]
                first = kb == nkb - 1
                P.op("pe", lambda: nc.tensor.matmul(pO[hh][0:64, 0:NG], lhsT=V[:, kb, h * 64:(h + 1) * 64], rhs=A[ab][:], start=first, stop=(kb == 0)),
                     reads=["V", f"A{ab}"], writes=[f"pO{hh}"])
                if kb == 0:
                    P.op("dve", lambda: nc.vector.tensor_copy(oT[hh][:], pO[hh][0:64, 0:NG]), reads=[f"pO{hh}"], writes=[f"oT{hh}"])
                    P.dma("sp", c.mixT[512 + h * 64: 512 + (h + 1) * 64, qcols], oT[hh][:], reads=[f"oT{hh}"], writes=["mixT"], chan=f"d_oT{hh}")

            n = len(meta)
            for i in range(min(LA, n)):
                stageA(meta[i])
            for i in range(n):
                if i + LA < n:
                    stageA(meta[i + LA])
                stageB(meta[i])
                if i >= 2:
                    stageB2(meta[i - 2])
            for i in range(max(0, n - 2), n):
                stageB2(meta[i])
    P.barrier()


def phase_gdn(c, l):
    nc, P, S = c.nc, c.P, c.S
    NG = min(512, S)
    NB = S // NG
    R = NG // 128
    w_in = c.mix_w_in[l]
    CO = c.CO
    with ExitStack() as pes:
        sb, ps = _tiles(nc, pes)
        xTb = [sb(f"xT{i}", [128, 8, NG], BF16) for i in range(2)]
        wqkv = sb("wqkv", [128, 8, 1536], BF16)
        wz = sb("wz", [128, 8, 512], BF16)
        wab = sb("wab", [128, 8, 16], BF16)
        cw = sb("cw", [128, 12, 4], F32)
        ident = sb("ident", [128, 128], F32)
        identb = sb("identb", [128, 128], BF16)
        onesf = sb("onesf", [128, 128], F32)
        triinc = sb("triinc", [128, 128], F32)
        upi = sb("upi", [128, 128], F32)
        ups = sb("ups", [128, 128], F32)
        dtb = sb("dtb", [128, 8], F32)
        nea = sb("nea", [128, 8], F32)
        nw = sb("nw", [128, 64], F32)
        halo = sb("halo", [128, 12, 3], F32)
        raw = [sb(f"raw{i}", [128, NG + 3], F32) for i in range(2)]
        acc = [sb(f"acc{i}", [128, NG], F32) for i in range(2)]
        csl = sb("csl", [128, 12, NG], F32)
        QKVs = [sb(f"QKV{i}", [128, 1536], F32) for i in range(2)]
        sqt2 = sb("sqt2", [128, 512], F32)
        ss2 = sb("ss2", [128, 8], F32)
        sqt = sb("sqt", [128, 1024], F32)
        ss = sb("ss", [128, 16], F32)
        kqTs = [sb(f"kqT{i}", [64, 8, 2, 128], BF16) for i in range(2)]
        gab = sb("gab", [128, 16], F32)
        beta = sb("beta", [128, 8], F32)
        nbeta = sb("nbeta", [128, 8], F32)
        g = sb("g", [128, 8], F32)
        gc = sb("gc", [128, 8], F32)
        gams = [sb(f"gam{i}", [128, 8], F32) for i in range(2)]
        kap = sb("kap", [128, 8], F32)
        gls = [sb(f"gl{i}", [128, 8], F32) for i in range(2)]
        nbks = [sb(f"nbk{i}", [128, 8], F32) for i in range(2)]
        G1 = sb("G1", [128, 8, 128], F32)
        dm = sb("dm", [128, 8, 128], F32)
        dTs = sb("dTs", [128, 8, 128], F32)
        dTi = sb("dTi", [128, 8, 128], F32)
        CD = F32
        A0s = [sb(f"A0{i}", [128, 8, 128], CD) for i in range(2)]
        qkTps = [sb(f"qkTp{i}", [128, 8, 128], BF16) for i in range(2)]
        Am = [[sb(f"Am{g}{i}", [128, 4, 128], BF16) for i in range(2)] for g in range(2)]
        Bm = [[sb(f"Bm{g}{i}", [128, 4, 128], BF16) for i in range(2)] for g in range(2)]
        Amf = [[sb(f"Amf{g}{i}", [128, 4, 128], F32) for i in range(2)] for g in range(2)]
        Bmf = [[sb(f"Bmf{g}{i}", [128, 4, 128], F32) for i in range(2)] for g in range(2)]
        Yb = [sb(f"Yb{g}", [128, 4, 128], BF16) for g in range(2)]
        Yf = [sb(f"Yf{g}", [128, 4, 128], F32) for g in range(2)]
        Xp = sb("Xp", [128, 8, 128], BF16)
        St = sb("St", [64, 8, 64], F32)
        Sb = sb("Sb", [64, 8, 64], BF16)
        rp = sb("rp", [128, 8, 64], BF16)
        vt = sb("vt", [128, 512], BF16)
        kd = sb("kd", [128, 512], BF16)
        o1 = sb("o1", [128, 8, 64], F32)
        of = sb("of", [128, 8, 64], F32)
        zs = sb("zs", [128, 512], F32)
        og = sb("og", [128, 512], F32)
        oTs = sb("oTs", [128, 4, 128], BF16)
        B = [ps(f"B{i}", [128, 512], F32) for i in range(8)]
        bk = lambda i: f"B{i}"

        load_w(c, wqkv, w_in, "wqkv", 8, C_GQ, C_GQ + 1536)
        load_w(c, wz, w_in, "wz", 8, C_GZ, C_GZ + 512)
        load_w(c, wab, w_in, "wab", 8, C_GA, C_GA + 16)
        P.dma("sp", cw[:], c.conv_wT[l], writes=["cw"])
        for (t_, nm) in ((ident, "ident"), (onesf, "ones"), (triinc, "triinc"), (upi, "upincl"), (ups, "upstrict")):
            P.dma("sp", t_[:], c.consts[:, CO[nm]:CO[nm] + 128], writes=[nm])
        P.op("dve", lambda: nc.vector.tensor_copy(identb[:], ident[:]), reads=["ident"], writes=["identb"])
        P.dma("sp", dtb[:], c.gdn_dt_bias[l, :].partition_broadcast(128), writes=["dtb"])
        P.dma("sp", nea[:], c.gdn_a_log[l, :].partition_broadcast(128), writes=["nea"])
        P.dma("sp", nw[:], c.gdn_norm_w[l, :].partition_broadcast(128), writes=["nw"])
        P.op("act", lambda: nc.scalar.activation(out=nea[:], in_=nea[:], func=AF.Exp), reads=["nea"], writes=["nea"])
        P.op("dve", lambda: nc.vector.tensor_scalar(out=nea[:], in0=nea[:], scalar1=-1.0, scalar2=None, op0=ALU.mult), reads=["nea"], writes=["nea"])
        P.op("pool", lambda: nc.gpsimd.memset(halo[:], 0.0), writes=["halo"])
        P.op("pool", lambda: nc.gpsimd.memset(St[:], 0.0), writes=["St"])
        P.op("pool", lambda: nc.gpsimd.memset(Sb[:], 0.0), writes=["Sb"])

        def bc_h(ap2d, n, d):
            return ap2d.unsqueeze(2).to_broadcast([128, n, d])

        def bc_m(ap2d, n, d):
            return ap2d.unsqueeze(1).to_broadcast([128, n, d])

        def conv(nb):
            cols = slice(nb * NG, (nb + 1) * NG)
            xT = xTb[nb % 2]
            xk = f"xT{nb % 2}"
            P.dma("sp", xT[:], c.h1T[:, cols].rearrange("(k p) t -> p k t", p=128), writes=[xk])
            for ch in range(12):
                rb = ch % 2
                for kc in range(8):
                    P.op("pe", lambda kc=kc, ch=ch: nc.tensor.matmul(B[0][:, 0:NG], lhsT=wqkv[:, kc, ch * 128:(ch + 1) * 128], rhs=xT[:, kc, :],
                                                                    start=(kc == 0), stop=(kc == 7)), reads=[xk, f"wqkv{kc}"], writes=[bk(0)])
                P.op("pool", lambda rb=rb, ch=ch: nc.gpsimd.tensor_copy(raw[rb][:, 0:3], halo[:, ch, :]), reads=["halo"], writes=[f"raw{rb}"])
                P.op("act", lambda rb=rb: nc.scalar.copy(raw[rb][:, 3:NG + 3], B[0][:, 0:NG]), reads=[bk(0)], writes=[f"raw{rb}"])
                P.op("pool", lambda rb=rb, ch=ch: nc.gpsimd.tensor_copy(halo[:, ch, :], raw[rb][:, NG:NG + 3]), reads=[f"raw{rb}"], writes=["halo"])
                P.op("dve", lambda rb=rb, ch=ch: nc.vector.tensor_scalar(out=acc[rb][:], in0=raw[rb][:, 0:NG], scalar1=cw[:, ch, 0:1], scalar2=None, op0=ALU.mult),
                     reads=[f"raw{rb}", "cw"], writes=[f"acc{rb}"])
                for j in range(1, 4):
                    P.op("dve", lambda rb=rb, ch=ch, j=j: nc.vector.scalar_tensor_tensor(out=acc[rb][:], in0=raw[rb][:, j:NG + j], scalar=cw[:, ch, j:j + 1],
                                                                                       in1=acc[rb][:], op0=ALU.mult, op1=ALU.add),
                         reads=[f"raw{rb}", "cw", f"acc{rb}"], writes=[f"acc{rb}"])
                P.op("act", lambda rb=rb, ch=ch: nc.scalar.activation(out=csl[:, ch, :], in_=acc[rb][:], func=AF.Silu), reads=[f"acc{rb}"], writes=[f"csl{ch}"])

        def pre(tt):
            nb, q = tt // R, tt % R
            xT, xk = xTb[nb % 2], f"xT{nb % 2}"
            pbuf = tt % 2
            QKV, kqT, gam, gl, nbk, A0, qkTp = QKVs[pbuf], kqTs[pbuf], gams[pbuf], gls[pbuf], nbks[pbuf], A0s[pbuf], qkTps[pbuf]
            tcols = slice(tt * 128, (tt + 1) * 128)
            for grp in range(3):
                for cc in range(4):
                    ch = grp * 4 + cc
                    P.op("pe", lambda cc=cc, ch=ch, q=q: nc.tensor.transpose(B[1][:, cc * 128:(cc + 1) * 128], csl[:, ch, q * 128:(q + 1) * 128], ident[:]),
                         reads=[f"csl{ch}", "ident"], writes=[bk(1)])
                if grp % 2 == 0:
                    P.op("act", lambda grp=grp: nc.scalar.copy(QKV[:, grp * 512:(grp + 1) * 512], B[1][:, :]), reads=[bk(1)], writes=[f"QKV{pbuf}_{grp}"])
                else:
                    P.op("dve", lambda grp=grp: nc.vector.tensor_copy(QKV[:, grp * 512:(grp + 1) * 512], B[1][:, :]), reads=[bk(1)], writes=[f"QKV{pbuf}_{grp}"])
            yield
            P.op("pool", lambda: nc.gpsimd.tensor_tensor(out=sqt[:], in0=QKV[:, 0:1024], in1=QKV[:, 0:1024], op=ALU.mult), reads=[f"QKV{pbuf}_0", f"QKV{pbuf}_1"], writes=["sqt"])
            P.op("dve", lambda: nc.vector.tensor_reduce(out=ss[:], in_=sqt[:].rearrange("p (h d) -> p h d", d=64), axis=AX.X, op=ALU.add), reads=["sqt"], writes=["ss"])
            P.op("act", lambda: nc.scalar.activation(out=ss[:], in_=ss[:], func=AF.Sqrt, bias=RMS_EPS, scale=1.0), reads=["ss"], writes=["ss"])
            P.op("dve", lambda: nc.vector.reciprocal(out=ss[:], in_=ss[:]), reads=["ss"], writes=["ss"])
            P.op("dve", lambda: nc.vector.tensor_scalar(out=ss[:, 0:8], in0=ss[:, 0:8], scalar1=0.125, scalar2=None, op0=ALU.mult), reads=["ss"], writes=["ss"])
            P.op("dve", lambda: nc.vector.tensor_tensor(out=QKV[:, 0:1024].rearrange("p (h d) -> p h d", d=64), in0=QKV[:, 0:1024].rearrange("p (h d) -> p h d", d=64),
                                                        in1=bc_h(ss[:, 0:16], 16, 64), op=ALU.mult), reads=[f"QKV{pbuf}_0", f"QKV{pbuf}_1", "ss"], writes=[f"QKV{pbuf}_0", f"QKV{pbuf}_1"])
            for (src0, slot) in ((512, 0), (0, 1)):
                for g4 in range(2):
                    for hl in range(4):
                        h = g4 * 4 + hl
                        P.op("pe", lambda hl=hl, h=h, src0=src0: nc.tensor.transpose(B[1][0:64, hl * 128:(hl + 1) * 128], QKV[:, src0 + h * 64: src0 + (h + 1) * 64], ident[:]),
                             reads=[f"QKV{pbuf}_0", f"QKV{pbuf}_1", "ident"], writes=[bk(1)])
                    if g4 == 0:
                        P.op("act", lambda slot=slot, g4=g4: nc.scalar.copy(kqT[:, g4 * 4:(g4 + 1) * 4, slot, :], B[1][0:64, :].rearrange("p (a b) -> p a b", a=4)), reads=[bk(1)], writes=[f"kqT{pbuf}"])
                    else:
                        P.op("dve", lambda slot=slot, g4=g4: nc.vector.tensor_copy(kqT[:, g4 * 4:(g4 + 1) * 4, slot, :], B[1][0:64, :].rearrange("p (a b) -> p a b", a=4)), reads=[bk(1)], writes=[f"kqT{pbuf}"])
            yield
            for kc in range(8):
                P.op("pe", lambda kc=kc: nc.tensor.matmul(B[2][:, 0:16], lhsT=xT[:, kc, q * 128:(q + 1) * 128], rhs=wab[:, kc, :], start=(kc == 0), stop=(kc == 7)),
                     reads=[xk, f"wab{kc}"], writes=[bk(2)])
            P.op("dve", lambda: nc.vector.tensor_copy(gab[:], B[2][:, 0:16]), reads=[bk(2)], writes=["gab"])
            P.op("act", lambda: nc.scalar.activation(out=beta[:], in_=gab[:, 8:16], func=AF.Exp, scale=-1.0), reads=["gab"], writes=["beta"])
            P.op("dve", lambda: nc.vector.tensor_scalar(out=beta[:], in0=beta[:], scalar1=1.0, scalar2=None, op0=ALU.add), reads=["beta"], writes=["beta"])
            P.op("dve", lambda: nc.vector.reciprocal(out=beta[:], in_=beta[:]), reads=["beta"], writes=["beta"])
            P.op("dve", lambda: nc.vector.tensor_scalar(out=nbeta[:], in0=beta[:], scalar1=-1.0, scalar2=None, op0=ALU.mult), reads=["beta"], writes=["nbeta"])
            P.op("dve", lambda: nc.vector.tensor_tensor(out=g[:], in0=gab[:, 0:8], in1=dtb[:], op=ALU.add), reads=["gab", "dtb"], writes=["g"])
            P.op("act", lambda: nc.scalar.activation(out=g[:], in_=g[:], func=AF.Exp), reads=["g"], writes=["g"])
            P.op("act", lambda: nc.scalar.activation(out=g[:], in_=g[:], func=AF.Ln, bias=1.0, scale=1.0), reads=["g"], writes=["g"])
            P.op("dve", lambda: nc.vector.tensor_tensor(out=g[:], in0=g[:], in1=nea[:], op=ALU.mult), reads=["g", "nea"], writes=["g"])
            P.op("pe", lambda: nc.tensor.matmul(B[2][:, 16:24], lhsT=triinc[:], rhs=g[:], start=True, stop=True), reads=["triinc", "g"], writes=[bk(2)])
            P.op("pe", lambda: nc.tensor.matmul(B[2][:, 24:32], lhsT=onesf[:], rhs=g[:], start=True, stop=True), reads=["ones", "g"], writes=[bk(2)])
            P.op("dve", lambda: nc.vector.tensor_copy(gc[:], B[2][:, 16:24]), reads=[bk(2)], writes=["gc"])
            P.op("dve", lambda: nc.vector.tensor_tensor(out=kap[:], in0=B[2][:, 24:32], in1=gc[:], op=ALU.subtract), reads=[bk(2), "gc"], writes=["kap"])
            P.op("act", lambda: nc.scalar.activation(out=gl[:], in_=B[2][:, 24:32], func=AF.Exp), reads=[bk(2)], writes=[f"gl{pbuf}"])
            P.op("act", lambda: nc.scalar.activation(out=kap[:], in_=kap[:], func=AF.Exp), reads=["kap"], writes=["kap"])
            P.op("act", lambda: nc.scalar.activation(out=gam[:], in_=gc[:], func=AF.Exp), reads=["gc"], writes=[f"gam{pbuf}"])
            P.op("dve", lambda: nc.vector.tensor_tensor(out=nbk[:], in0=nbeta[:], in1=kap[:], op=ALU.mult), reads=["nbeta", "kap"], writes=[f"nbk{pbuf}"])
            P.op("pool", lambda: nc.gpsimd.tensor_tensor(out=G1[:], in0=bc_m(triinc[:], 8, 128), in1=bc_h(g[:], 8, 128), op=ALU.mult), reads=["triinc", "g"], writes=["G1"])
            for half in range(2):
                P.op("pe", lambda half=half: nc.tensor.matmul(B[3 + half][:, :], lhsT=onesf[:], rhs=G1[:, half * 4:(half + 1) * 4, :].rearrange("p a b -> p (a b)"),
                                                             start=True, stop=True), reads=["ones", "G1"], writes=[bk(3 + half)])
                P.op("dve", lambda half=half: nc.vector.tensor_tensor(out=dm[:, half * 4:(half + 1) * 4, :], in0=B[3 + half][:, :].rearrange("p (a b) -> p a b", a=4),
                                                                     in1=bc_h(gc[:, half * 4:(half + 1) * 4], 4, 128), op=ALU.subtract),
                     reads=[bk(3 + half), "gc"], writes=["dm"])
            P.op("pool", lambda: nc.gpsimd.tensor_scalar(out=dm[:], in0=dm[:], scalar1=0.0, scalar2=None, op0=ALU.min), reads=["dm"], writes=["dm"])
            P.op("act", lambda: nc.scalar.activation(out=dm[:], in_=dm[:], func=AF.Exp), reads=["dm"], writes=["dm"])
            P.op("pool", lambda: nc.gpsimd.tensor_tensor(out=dTs[:], in0=dm[:], in1=bc_m(ups[:], 8, 128), op=ALU.mult), reads=["dm", "upstrict"], writes=["dTs"])
            P.op("pool", lambda: nc.gpsimd.tensor_tensor(out=dTi[:], in0=dm[:], in1=bc_m(upi[:], 8, 128), op=ALU.mult), reads=["dm", "upincl"], writes=["dTi"])
            yield
            for j2 in range(4):
                for hh in range(2):
                    h = j2 * 2 + hh
                    P.op("pe", lambda h=h, hh=hh: nc.tensor.matmul(B[7][:, hh * 256:(hh + 1) * 256], lhsT=kqT[:, h, 0, :],
                                                                 rhs=kqT[:, h, :, :].rearrange("p a b -> p (a b)"), start=True, stop=True),
                         reads=[f"kqT{pbuf}"], writes=[bk(7)])
                for hh in range(2):
                    h = j2 * 2 + hh
                    P.op("dve", lambda hh=hh, h=h: nc.vector.scalar_tensor_tensor(out=A0[:, h, :], in0=B[7][:, hh * 256: hh * 256 + 128], scalar=beta[:, h:h + 1],
                                                                                in1=dTs[:, h, :], op0=ALU.mult, op1=ALU.mult),
                         reads=[bk(7), "beta", "dTs"], writes=[f"A0{pbuf}"])
                    P.op("dve", lambda hh=hh, h=h: nc.vector.scalar_tensor_tensor(out=qkTp[:, h, :], in0=B[7][:, hh * 256 + 128: hh * 256 + 256], scalar=nbeta[:, h:h + 1],
                                                                                in1=dTi[:, h, :], op0=ALU.mult, op1=ALU.mult),
                         reads=[bk(7), "nbeta", "dTi"], writes=[f"qkTp{pbuf}"])
            yield

        def post(tt):
            nb, q = tt // R, tt % R
            xT, xk = xTb[nb % 2], f"xT{nb % 2}"
            pbuf = tt % 2
            QKV, kqT, gam, gl, nbk, A0, qkTp = QKVs[pbuf], kqTs[pbuf], gams[pbuf], gls[pbuf], nbks[pbuf], A0s[pbuf], qkTps[pbuf]
            tcols = slice(tt * 128, (tt + 1) * 128)
            NF = 4
            GB = ((5, 6, 7), (3, 4, 0))
            for g4 in range(2):
                hs = slice(g4 * 4, (g4 + 1) * 4)
                bB = GB[g4][1]
                for hl in range(4):
                    P.op("pe", lambda hl=hl, g4=g4, bB=bB: nc.tensor.matmul(B[bB][:, hl * 128:(hl + 1) * 128], lhsT=A0[:, g4 * 4 + hl, :], rhs=ident[:], start=True, stop=True),
                         reads=[f"A0{pbuf}", "ident"], writes=[bk(bB)])
                P.op("dve", lambda g4=g4, bB=bB: nc.vector.tensor_copy(Bmf[g4][0][:].rearrange("p a b -> p (a b)"), B[bB][:, :]), reads=[bk(bB)], writes=[f"Bmf{g4}0"])
                P.op("pool", lambda hs=hs, g4=g4: nc.gpsimd.tensor_tensor(out=Yf[g4][:], in0=bc_m(ident[:], 4, 128), in1=A0[:, hs, :], op=ALU.subtract),
                     reads=["ident", f"A0{pbuf}"], writes=[f"Yf{g4}"])
            for k in range(1, 7):
                pv, cu = (k - 1) % 2, k % 2
                f32lvl = k <= NF
                for g4 in range(2):
                    hs = slice(g4 * 4, (g4 + 1) * 4)
                    bA, bB, bY = GB[g4]
                    if k == 1:
                        Aprev = lambda hl, g4=g4: A0[:, g4 * 4 + hl, :]
                        akey = f"A0{pbuf}"
                    elif f32lvl:
                        Aprev = lambda hl, g4=g4, pv=pv: Amf[g4][pv][:, hl, :]
                        akey = f"Amf{g4}{pv}"
                    else:
                        Aprev = lambda hl, g4=g4, pv=pv: Am[g4][pv][:, hl, :]
                        akey = f"Am{g4}{pv}"
                    Bprev = (lambda hl, g4=g4, pv=pv: Bmf[g4][pv][:, hl, :]) if f32lvl else (lambda hl, g4=g4, pv=pv: Bm[g4][pv][:, hl, :])
                    bkey = f"Bmf{g4}{pv}" if f32lvl else f"Bm{g4}{pv}"
                    if k <= 5:
                        for hl in range(4):
                            P.op("pe", lambda hl=hl, Aprev=Aprev, Bprev=Bprev, bA=bA: nc.tensor.matmul(B[bA][:, hl * 128:(hl + 1) * 128], lhsT=Bprev(hl), rhs=Aprev(hl), start=True, stop=True),
                                 reads=[bkey, akey], writes=[bk(bA)])
                    if k <= NF:
                        P.op("act", lambda cu=cu, g4=g4, bA=bA: nc.scalar.copy(Amf[g4][cu][:].rearrange("p a b -> p (a b)"), B[bA][:, :]), reads=[bk(bA)], writes=[f"Amf{g4}{cu}"])
                    if NF <= k <= 5:
                        P.op("dve", lambda cu=cu, g4=g4, bA=bA: nc.vector.tensor_copy(Am[g4][cu][:].rearrange("p a b -> p (a b)"), B[bA][:, :]), reads=[bk(bA)], writes=[f"Am{g4}{cu}"])
                    for hl in range(4):
                        P.op("pe", lambda hl=hl, Aprev=Aprev, Bprev=Bprev, bB=bB: nc.tensor.matmul(B[bB][:, hl * 128:(hl + 1) * 128], lhsT=Aprev(hl), rhs=Bprev(hl), start=True, stop=True),
                             reads=[bkey, akey], writes=[bk(bB)])
                    if k <= NF:
                        P.op("dve", lambda cu=cu, g4=g4, bB=bB: nc.vector.tensor_copy(Bmf[g4][cu][:].rearrange("p a b -> p (a b)"), B[bB][:, :]), reads=[bk(bB)], writes=[f"Bmf{g4}{cu}"])
                    if k >= NF:
                        P.op("act", lambda cu=cu, g4=g4, bB=bB: nc.scalar.copy(Bm[g4][cu][:].rearrange("p a b -> p (a b)"), B[bB][:, :]), reads=[bk(bB)], writes=[f"Bm{g4}{cu}"])
                    for hl in range(4):
                        if f32lvl:
                            P.op("pe", lambda hl=hl, cu=cu, g4=g4, bY=bY: nc.tensor.matmul(B[bY][:, hl * 128:(hl + 1) * 128], lhsT=Bmf[g4][cu][:, hl, :], rhs=Yf[g4][:, hl, :], start=True, stop=True),
                                 reads=[f"Bmf{g4}{cu}", f"Yf{g4}"], writes=[bk(bY)])
                        else:
                            P.op("pe", lambda hl=hl, cu=cu, g4=g4, bY=bY: nc.tensor.matmul(B[bY][:, hl * 128:(hl + 1) * 128], lhsT=Bm[g4][cu][:, hl, :], rhs=Yb[g4][:, hl, :], start=True, stop=True),
                                 reads=[f"Bm{g4}{cu}", f"Yb{g4}"], writes=[bk(bY)])
                    if k < 6:
                        P.op("dve", lambda g4=g4, bY=bY: nc.vector.tensor_tensor(out=Yf[g4][:].rearrange("p a b -> p (a b)"), in0=B[bY][:, :], in1=Yf[g4][:].rearrange("p a b -> p (a b)"), op=ALU.add),
                             reads=[bk(bY), f"Yf{g4}"], writes=[f"Yf{g4}"])
                        if k >= NF:
                            P.op("act", lambda g4=g4: nc.scalar.copy(Yb[g4][:].rearrange("p a b -> p (a b)"), Yf[g4][:].rearrange("p a b -> p (a b)")), reads=[f"Yf{g4}"], writes=[f"Yb{g4}"])
                    else:
                        P.op("dve", lambda g4=g4, bY=bY, hs=hs: nc.vector.tensor_tensor(out=Xp[:, hs, :].rearrange("p a b -> p (a b)"), in0=B[bY][:, :], in1=Yf[g4][:].rearrange("p a b -> p (a b)"), op=ALU.add),
                             reads=[bk(bY), f"Yf{g4}"], writes=["Xp"])
                yield
            for h in range(8):
                j2, pb = h // 2, (h % 2) * 64
                bnk = 5 + h // 4
                col = (h % 4) * 128
                for slot in range(2):
                    P.op("pe", lambda bnk=bnk, col=col, slot=slot, h=h: nc.tensor.matmul(B[bnk][:, col + slot * 64: col + slot * 64 + 64], lhsT=kqT[:, h, slot, :],
                                                                                       rhs=Sb[:, h, :], start=True, stop=True),
                         reads=[f"kqT{pbuf}", "Sb"], writes=[bk(bnk)])
            for h in range(8):
                bnk, col = 5 + h // 4, (h % 4) * 128
                P.op("dve", lambda h=h, bnk=bnk, col=col: nc.vector.scalar_tensor_tensor(out=rp[:, h, :], in0=B[bnk][:, col:col + 64], scalar=gam[:, h:h + 1],
                                                                                       in1=QKV[:, 1024 + h * 64: 1024 + (h + 1) * 64], op0=ALU.mult, op1=ALU.subtract),
                     reads=[bk(bnk), f"gam{pbuf}", f"QKV{pbuf}_2"], writes=["rp"])
                P.op("act", lambda h=h, bnk=bnk, col=col: nc.scalar.activation(out=o1[:, h, :], in_=B[bnk][:, col + 64:col + 128], func=AF.Identity, scale=gam[:, h:h + 1]),
                     reads=[bk(bnk), f"gam{pbuf}"], writes=["o1"])
            for h in range(8):
                P.op("pe", lambda h=h: nc.tensor.matmul(B[4][:, h * 64:(h + 1) * 64], lhsT=Xp[:, h, :], rhs=rp[:, h, :], start=True, stop=True),
                     reads=["Xp", "rp"], writes=[bk(4)])
            P.op("act", lambda: nc.scalar.copy(vt[:], B[4][:, :]), reads=[bk(4)], writes=["vt"])
            P.op("pool", lambda: nc.gpsimd.tensor_tensor(out=kd[:].rearrange("p (h d) -> p h d", d=64), in0=QKV[:, 512:1024].rearrange("p (h d) -> p h d", d=64),
                                                        in1=bc_h(nbk[:], 8, 64), op=ALU.mult), reads=[f"QKV{pbuf}_1", f"nbk{pbuf}"], writes=["kd"])
            for h in range(8):
                P.op("pe", lambda h=h: nc.tensor.matmul(B[3][:, h * 64:(h + 1) * 64], lhsT=qkTp[:, h, :], rhs=vt[:, h * 64:(h + 1) * 64], start=True, stop=True),
                     reads=[f"qkTp{pbuf}", "vt"], writes=[bk(3)])
            for h in range(8):
                j2 = h // 2
                P.op("pe", lambda h=h: nc.tensor.matmul(B[2][0:64, h * 64:(h + 1) * 64], lhsT=kd[:, h * 64:(h + 1) * 64], rhs=vt[:, h * 64:(h + 1) * 64], start=True, stop=True),
                     reads=["kd", "vt"], writes=[bk(2)])
            P.op("dve", lambda: nc.vector.tensor_tensor(out=of[:].rearrange("p a b -> p (a b)"), in0=B[3][:, :], in1=o1[:].rearrange("p a b -> p (a b)"), op=ALU.add),
                 reads=[bk(3), "o1"], writes=["of"])
            P.op("pool", lambda: nc.gpsimd.tensor_tensor(out=St[:], in0=St[:], in1=gl[0:64, :].unsqueeze(2).to_broadcast([64, 8, 64]), op=ALU.mult), reads=["St", f"gl{pbuf}"], writes=["St"])
            P.op("dve", lambda: nc.vector.tensor_tensor(out=St[:].rearrange("p a b -> p (a b)"), in0=St[:].rearrange("p a b -> p (a b)"), in1=B[2][0:64, :], op=ALU.add),
                 reads=["St", bk(2)], writes=["St"])
            P.op("act", lambda: nc.scalar.copy(Sb[:].rearrange("p a b -> p (a b)"), St[:].rearrange("p a b -> p (a b)")), reads=["St"], writes=["Sb"])
            yield
            for kc in range(8):
                P.op("pe", lambda kc=kc: nc.tensor.matmul(B[0][:, :], lhsT=xT[:, kc, q * 128:(q + 1) * 128], rhs=wz[:, kc, :], start=(kc == 0), stop=(kc == 7)),
                     reads=[xk, f"wz{kc}"], writes=[bk(0)])
            P.op("act", lambda: nc.scalar.activation(out=zs[:], in_=B[0][:, :], func=AF.Silu), reads=[bk(0)], writes=["zs"])
            P.op("pool", lambda: nc.gpsimd.tensor_tensor(out=sqt2[:, 0:512], in0=of[:].rearrange("p a b -> p (a b)"), in1=of[:].rearrange("p a b -> p (a b)"), op=ALU.mult),
                 reads=["of"], writes=["sqt2"])
            P.op("dve", lambda: nc.vector.tensor_reduce(out=ss2[:, 0:8], in_=sqt2[:, 0:512].rearrange("p (h d) -> p h d", d=64), axis=AX.X, op=ALU.add), reads=["sqt2"], writes=["ss2"])
            P.op("act", lambda: nc.scalar.activation(out=ss2[:, 0:8], in_=ss2[:, 0:8], func=AF.Sqrt, bias=RMS_EPS, scale=1.0 / 64), reads=["ss2"], writes=["ss2"])
            P.op("dve", lambda: nc.vector.reciprocal(out=ss2[:, 0:8], in_=ss2[:, 0:8]), reads=["ss2"], writes=["ss2"])
            P.op("dve", lambda: nc.vector.tensor_tensor(out=og[:].rearrange("p (h d) -> p h d", d=64), in0=of[:], in1=bc_h(ss2[:, 0:8], 8, 64), op=ALU.mult),
                 reads=["of", "ss2"], writes=["og"])
            P.op("pool", lambda: nc.gpsimd.tensor_tensor(out=og[:].rearrange("p (h d) -> p h d", d=64), in0=og[:].rearrange("p (h d) -> p h d", d=64), in1=bc_m(nw[:], 8, 64), op=ALU.mult),
                 reads=["og", "nw"], writes=["og"])
            P.op("pool", lambda: nc.gpsimd.tensor_tensor(out=og[:], in0=og[:], in1=zs[:], op=ALU.mult), reads=["og", "zs"], writes=["og"])
            for j2 in range(4):
                P.op("pe", lambda j2=j2: nc.tensor.transpose(B[1][:, j2 * 128:(j2 + 1) * 128], og[:, j2 * 128:(j2 + 1) * 128], ident[:]), reads=["og", "ident"], writes=[bk(1)])
            P.op("act", lambda: nc.scalar.copy(oTs[:].rearrange("p a b -> p (a b)"), B[1][:, :]), reads=[bk(1)], writes=["oTs"])
            P.dma("sp", c.mixT[0:512, tcols].rearrange("(a p) t -> p a t", p=128), oTs[:], reads=["oTs"], writes=["mixT"], chan="d_oTs")
            yield

        NTT = S // 128
        conv(0)
        for _ in pre(0):
            pass
        for tt in range(NTT):
            gens = [post(tt)]
            if tt + 1 < NTT:
                if (tt + 1) % R == 0:
                    conv((tt + 1) // R)
                gens.append(pre(tt + 1))
            while gens:
                for g_ in list(gens):
                    try:
                        next(g_)
                    except StopIteration:
                        gens.remove(g_)

    P.barrier()


def phase_outproj(c, l, src, dst):
    nc, P, S = c.nc, c.P, c.S
    with ExitStack() as pes:
        sb, ps = _tiles(nc, pes)
        wo = sb("wo", [128, 8, D], BF16)
        gb = sb("gb", [128, 2, D], F32)
        xs = [sb(f"x{i}", [128, D], F32) for i in range(2)]
        ms = [sb(f"m{i}", [128, 8, 128], BF16) for i in range(2)]
        ys = [sb(f"y{i}", [128, D], F32) for i in range(2)]
        os_ = [sb(f"o{i}", [128, D], F32) for i in range(2)]
        sts = [sb(f"st{i}", [128, 2, 6], F32) for i in range(2)]
        mvs = [sb(f"mv{i}", [128, 2], F32) for i in range(2)]
        rstds = [sb(f"rstd{i}", [128, 1], F32) for i in range(2)]
        po = [ps(f"po{i}", [128, 512], F32) for i in range(4)]
        load_w(c, wo, c.mix_w_o[l], "wo", 8, 0, D)
        P.dma("sp", gb[:, 0, :], c.ln_g[l, 1, :].partition_broadcast(128), writes=["gb0"])
        P.dma("sp", gb[:, 1, :], c.ln_b[l, 1, :].partition_broadcast(128), writes=["gb1"])
        def load(tt):
            b = tt % 2
            tc_ = slice(tt * 128, (tt + 1) * 128)
            P.dma("sp", xs[b][:], src[tc_, :], writes=[f"x{b}"])
            P.dma("sp", ms[b][:], c.mixT[:, tc_].rearrange("(k p) t -> p k t", p=128), writes=[f"m{b}"])

        NTT = S // 128
        load(0)
        if NTT > 1:
            load(1)
        for tt in range(NTT):
            b = tt % 2
            tc_ = slice(tt * 128, (tt + 1) * 128)
            y, o = ys[b], os_[b]
            for dh in range(2):
                pb_ = 2 * b + dh
                for kc in range(8):
                    P.op("pe", lambda b=b, dh=dh, kc=kc, pb_=pb_: nc.tensor.matmul(po[pb_][:, :], lhsT=ms[b][:, kc, :], rhs=wo[:, kc, dh * 512:(dh + 1) * 512],
                                                                                   start=(kc == 0), stop=(kc == 7)), reads=[f"m{b}", f"wo{kc}"], writes=[f"po{pb_}"])
            P.op("act", lambda b=b, y=y: nc.scalar.mul(y[:], xs[b][:], ALPHA), reads=[f"x{b}"], writes=[f"y{b}"])
            for dh in range(2):
                pb_ = 2 * b + dh
                P.op("dve", lambda dh=dh, y=y, pb_=pb_: nc.vector.tensor_tensor(out=y[:, dh * 512:(dh + 1) * 512], in0=po[pb_][:, :], in1=y[:, dh * 512:(dh + 1) * 512], op=ALU.add),
                     reads=[f"po{pb_}", f"y{b}"], writes=[f"y{b}"])
            layer_norm_tile(c, y, o, sts[b], mvs[b], rstds[b], gb, sfx=str(b))
            if tt + 2 < NTT:
                load(tt + 2)
            P.dma("sp", dst[tc_, :], o[:], reads=[f"o{b}"], writes=["dst"], chan=f"d_o{b}")
    P.barrier()


def phase_ple(c, l, src, dst):
    nc, P, S = c.nc, c.P, c.S
    with ExitStack() as pes:
        sb, ps = _tiles(nc, pes)
        wg = sb("wg", [128, 8, D], BF16)
        wp = sb("wp", [128, 2, D], BF16)
        ident = sb("ident", [128, 128], F32)
        xs = [sb(f"x{i}", [128, D], F32) for i in range(2)]
        pls = [sb(f"pl{i}", [128, PLE], F32) for i in range(2)]
        hTs = [sb(f"hT{i}", [128, 10, 128], BF16) for i in range(2)]
        sgs = [sb(f"sg{i}", [128, D], F32) for i in range(2)]
        os_ = [sb(f"o{i}", [128, D], F32) for i in range(2)]
        pt = [ps(f"pt{i}", [128, 512], F32) for i in range(3)]
        pg = [ps(f"pg{i}", [128, 512], F32) for i in range(2)]
        pp = [ps(f"pp{i}", [128, 512], F32) for i in range(2)]
        load_w(c, wg, c.ple_w_gate[l], "wg", 8, 0, D)
        load_w(c, wp, c.ple_w_proj[l], "wp", 2, 0, D)
        P.dma("sp", ident[:], c.consts[:, c.CO["ident"]:c.CO["ident"] + 128], writes=["ident"])
        def load(tt):
            b = tt % 2
            tc_ = slice(tt * 128, (tt + 1) * 128)
            P.dma("sp", xs[b][:], src[tc_, :], writes=[f"x{b}"])
            P.dma("sp", pls[b][:], c.p[l, tc_, :], writes=[f"pl{b}"])

        NTT = S // 128
        load(0)
        if NTT > 1:
            load(1)
        for tt in range(NTT):
            b = tt % 2
            tc_ = slice(tt * 128, (tt + 1) * 128)
            hT, sg, o = hTs[b], sgs[b], os_[b]
            for grp in range(3):
                n = 4 if grp < 2 else 2
                for q in range(n):
                    kc = grp * 4 + q
                    if grp < 2:
                        P.op("pe", lambda grp=grp, q=q, kc=kc, b=b: nc.tensor.transpose(pt[grp][:, q * 128:(q + 1) * 128], xs[b][:, kc * 128:(kc + 1) * 128], ident[:]),
                             reads=[f"x{b}", "ident"], writes=[f"pt{grp}"])
                    else:
                        P.op("pe", lambda grp=grp, q=q, b=b: nc.tensor.transpose(pt[grp][:, q * 128:(q + 1) * 128], pls[b][:, q * 128:(q + 1) * 128], ident[:]),
                             reads=[f"pl{b}", "ident"], writes=[f"pt{grp}"])
                eng = ("act", "dve", "act")[grp]
                if eng == "act":
                    P.op("act", lambda grp=grp, n=n: nc.scalar.copy(hT[:, grp * 4:grp * 4 + n, :].rearrange("p a b -> p (a b)"), pt[grp][:, 0:n * 128]),
                         reads=[f"pt{grp}"], writes=[f"hT{b}_{grp}"])
                else:
                    P.op("dve", lambda grp=grp, n=n: nc.vector.tensor_copy(hT[:, grp * 4:grp * 4 + n, :].rearrange("p a b -> p (a b)"), pt[grp][:, 0:n * 128]),
                         reads=[f"pt{grp}"], writes=[f"hT{b}_{grp}"])
            for dh in range(2):
                for kc in range(8):
                    P.op("pe", lambda dh=dh, kc=kc: nc.tensor.matmul(pg[dh][:, :], lhsT=hT[:, kc, :], rhs=wg[:, kc, dh * 512:(dh + 1) * 512], start=(kc == 0), stop=(kc == 7)),
                         reads=[f"hT{b}_0", f"hT{b}_1", f"wg{kc}"], writes=[f"pg{dh}"])
                for kc in range(2):
                    P.op("pe", lambda dh=dh, kc=kc: nc.tensor.matmul(pp[dh][:, :], lhsT=hT[:, 8 + kc, :], rhs=wp[:, kc, dh * 512:(dh + 1) * 512], start=(kc == 0), stop=(kc == 1)),
                         reads=[f"hT{b}_2", f"wp{kc}"], writes=[f"pp{dh}"])
                P.op("act", lambda dh=dh: nc.scalar.activation(out=sg[:, dh * 512:(dh + 1) * 512], in_=pg[dh][:, :], func=AF.Sigmoid), reads=[f"pg{dh}"], writes=[f"sg{b}_{dh}"])
                P.op("dve", lambda dh=dh: nc.vector.tensor_tensor(out=sg[:, dh * 512:(dh + 1) * 512], in0=pp[dh][:, :], in1=sg[:, dh * 512:(dh + 1) * 512], op=ALU.mult),
                     reads=[f"pp{dh}", f"sg{b}_{dh}"], writes=[f"sg{b}_{dh}"])
            P.op("pool", lambda b=b: nc.gpsimd.tensor_tensor(out=o[:], in0=sg[:], in1=xs[b][:], op=ALU.add), reads=[f"sg{b}_0", f"sg{b}_1", f"x{b}"], writes=[f"o{b}"])
            if tt + 2 < NTT:
                load(tt + 2)
            P.dma("sp", dst[tc_, :], o[:], reads=[f"o{b}"], writes=["dst"], chan=f"d_o{b}")
    P.barrier()


def all_phases(L):
    ph = [phase_rope]
    for l in range(L):
        src = (lambda c: c.x) if l == 0 else (lambda c: c.hb[1])
        ph.append(lambda c, l=l, src=src: phase_ffn(c, l, 0, src(c), c.hb[0], c.h1T))
        ph.append(lambda c, l=l: phase_gdn(c, l))
        ph.append(lambda c, l=l: phase_sb(c, l))
        ph.append(lambda c, l=l: phase_mla(c, l))
        ph.append(lambda c, l=l: phase_outproj(c, l, c.hb[0], c.hb[1]))
        ph.append(lambda c, l=l: phase_ffn(c, l, 1, c.hb[1], c.hb[0]))
        last = l == L - 1
        ph.append(lambda c, l=l, last=last: phase_ple(c, l, c.hb[0], c.out if last else c.hb[1]))
    return ph


_CACHE = {}


def kernel(**inputs):
    x = np.asarray(inputs["x"], np.float32)
    Bsz, S, _ = x.shape
    L = inputs["ffa_w_in"].shape[0]
    key = (S, L)
    if key not in _CACHE:
        _CACHE[key] = build(S, L, all_phases(L))
    nc, consts = _CACHE[key]
    w = prep_weights({k: np.asarray(v) for k, v in inputs.items()}, L)
    p = np.asarray(inputs["p"], np.float32)
    pos = np.asarray(inputs["positions"], np.int32)
    in_maps = []
    for b in range(Bsz):
        in_maps.append({"x": np.ascontiguousarray(x[b]), "p": np.ascontiguousarray(p[:, b]), "pos": np.ascontiguousarray(pos[b:b + 1]),
                        "consts": consts, **w})
    res = run_bass_kernel_spmd(nc, in_maps, core_ids=list(range(Bsz)))
    return np.stack([np.asarray(r["out"], np.float32) for r in res.results], 0)
```

```python
import numpy as np
from contextlib import ExitStack
import concourse.bass as bass
import concourse.mybir as mybir
from concourse.bass_utils import run_bass_kernel_spmd

F32 = mybir.dt.float32
BF16 = mybir.dt.bfloat16
I32 = mybir.dt.int32
AF = mybir.ActivationFunctionType
ALU = mybir.AluOpType
AX = mybir.AxisListType

D = 1024
DFF = 2816
PLE = 256
DEPTH = 2
ALPHA = (2 * DEPTH) ** 0.25
LN_EPS = 1e-5
RMS_EPS = 1e-6
IN_TOTAL = 3248
C_GQ, C_GK, C_GV, C_GZ, C_GA, C_GB = 0, 512, 1024, 1536, 2048, 2056
C_SQ, C_SK, C_SV, C_MQ, C_MKV, C_KR = 2064, 2320, 2576, 2832, 3088, 3216
SEM_ROT = 1 << 30
TWO_PI = 6.283185307179586


class Prog:
    def __init__(self, nc, es):
        self.nc = nc
        self.es = es
        self.engs = {"pe": nc.tensor, "act": nc.scalar, "dve": nc.vector, "pool": nc.gpsimd, "sp": nc.sync}
        self.nsem = 0
        self.esem = {}
        self.ecnt = {}
        self.allsems = []
        for e in ("pe", "act", "dve", "pool"):
            self.esem[e] = self._newsem("e_" + e)
            self.ecnt[e] = 0
        self.dsem = {}
        self.dcnt = {}
        self.dfree = []
        self.bar1 = None
        self.bar2 = None
        self.nbar = 0
        self.last_w = {}
        self.multi_w = {}
        self.readers = {}
        self.known = {e: {} for e in self.engs}
        self.nwaits = 0
        self.ninst = 0
        import os
        self.limit = int(os.environ["OP_LIMIT"]) if "OP_LIMIT" in os.environ else None

    def _newsem(self, name):
        self.nsem += 1
        s = self.es.enter_context(self.nc.semaphore(f"{name}_{self.nsem}"))
        self.allsems.append([s, 0])
        return s

    def _bump(self, sem, val):
        for r in self.allsems:
            if r[0] is sem:
                r[1] = max(r[1], val)
                return

    def _deps(self, reads, writes):
        deps = {}

        def add(t):
            cur = deps.get(id(t[0]))
            if cur is None or cur[1] < t[1]:
                deps[id(t[0])] = (t[0], t[1])

        for k in reads:
            t = self.last_w.get(k)
            if t is not None:
                add(t)
            for t in self.multi_w.get(k, ()):
                add(t)
        for k in writes:
            t = self.last_w.get(k)
            if t is not None:
                add(t)
            for t in self.multi_w.get(k, ()):
                add(t)
            for t in self.readers.get(k, ()):
                add(t)
        return deps

    def group(self, keys, toks):
        for k in keys:
            self.multi_w[k] = list(toks)

    def _wait(self, eng, deps, skip_sem=None):
        e = self.engs[eng]
        kn = self.known[eng]
        for sid, (sem, val) in deps.items():
            if skip_sem is not None and sem is skip_sem:
                continue
            if kn.get(sid, 0) >= val:
                continue
            e.wait_ge(sem, val)
            self.nwaits += 1
            kn[sid] = val

    def _commit(self, reads, writes, tok):
        for k in writes:
            self.last_w[k] = tok
            self.readers[k] = []
            self.multi_w.pop(k, None)
        for k in reads:
            if k in writes:
                continue
            self.readers.setdefault(k, []).append(tok)

    def op(self, eng, fn, reads=(), writes=()):
        if self.limit is not None and self.ninst >= self.limit:
            return None
        pr = [k for k in reads if k in PSUM_KEYS]
        if pr:
            writes = list(writes) + pr
        deps = self._deps(reads, writes)
        self._wait(eng, deps, skip_sem=self.esem["pe"] if eng == "pe" else None)
        inst = fn()
        if self.ecnt[eng] >= SEM_ROT:
            self.esem[eng] = self._newsem("e_" + eng)
            self.ecnt[eng] = 0
        self.ecnt[eng] += 1
        sem = self.esem[eng]
        inst.then_inc(sem, 1)
        tok = (sem, self.ecnt[eng])
        self._bump(sem, self.ecnt[eng])
        self._commit(reads, writes, tok)
        self.ninst += 1
        return tok

    def dma(self, q, out, in_, reads=(), writes=(), chan=None, **kw):
        if self.limit is not None and self.ninst >= self.limit:
            return None
        deps = self._deps(reads, writes)
        self._wait(q, deps)
        if chan is None:
            chan = "d_" + (list(writes) + list(reads))[0]
        if q == "pool":
            chan = "sw_" + chan
            assert chan not in self.dsem
            self.dsem[chan] = self._newsem("sw")
            self.dcnt[chan] = 0
        elif chan not in self.dsem:
            self.dsem[chan] = self.dfree.pop() if self.dfree else self._newsem("d")
            self.dcnt[chan] = 0
        inst = self.engs[q].dma_start(out=out, in_=in_, **kw)
        self.dcnt[chan] += 16
        sem = self.dsem[chan]
        inst.then_inc(sem, 16)
        tok = (sem, self.dcnt[chan])
        self._bump(sem, self.dcnt[chan])
        self._commit(reads, writes, tok)
        self.ninst += 1
        return tok

    def barrier(self):
        for eng in self.engs:
            kn = self.known[eng]
            for sem, val in self.allsems:
                if val > 0 and kn.get(id(sem), 0) < val:
                    self.engs[eng].wait_ge(sem, val)
                    kn[id(sem)] = val
                    self.nwaits += 1
        self.last_w.clear()
        self.multi_w.clear()
        self.readers.clear()
        if not self.dsem:
            return
        if self.bar1 is None:
            self.bar1 = self.es.enter_context(self.nc.semaphore("bar1"))
            self.bar2 = self.es.enter_context(self.nc.semaphore("bar2"))
        self.nbar += 1
        for eng in self.engs:
            self.engs[eng].sem_inc(self.bar1, 1)
        self.engs["pool"].wait_ge(self.bar1, len(self.engs) * self.nbar)
        for chan, sem in self.dsem.items():
            if chan.startswith("sw_"):
                continue
            self.engs["pool"].sem_clear(sem)
            self.dfree.append(sem)
            for r in self.allsems:
                if r[0] is sem:
                    r[1] = 0
            for eng in self.engs:
                self.known[eng].pop(id(sem), None)
        self.engs["pool"].sem_inc(self.bar2, 1)
        for eng in self.engs:
            self.engs[eng].wait_ge(self.bar2, self.nbar)
        self.dsem.clear()
        self.dcnt.clear()


class Ctx:
    pass


_UID = [0]


def _tiles(nc, pes, P=None):
    _UID[0] += 1
    u = _UID[0]
    sb = lambda n, s, d: pes.enter_context(nc.sbuf_tensor(f"{n}_u{u}", s, d))

    def ps(n, s, d):
        PSUM_KEYS.add(n)
        return pes.enter_context(nc.psum_tensor(f"{n}_u{u}", [128, 2048 // mybir.dt.size(d)], d))
    return sb, ps


PSUM_KEYS = set()


def load_w(c, dst, src_rows, key, kchunks, c0, c1):
    n = c1 - c0
    nblk = (n + 1023) // 1024
    while n % nblk:
        nblk += 1
    w = n // nblk
    toks = []
    for i in range(nblk):
        toks.append(c.P.dma("pool", dst[:, 0:kchunks, i * w:(i + 1) * w], src_rows[0:kchunks * 128, c0 + i * w:c0 + (i + 1) * w].rearrange("(k p) f -> p k f", p=128),
                            writes=[f"{key}_blk{i}"], chan=f"{key}_b{i}", max_dma_last_dim=4096))
    c.P.group([f"{key}{kc}" for kc in range(kchunks)], toks)


def consts_phase(c, pes):
    nc, P = c.nc, c.P
    sb, ps = _tiles(nc, pes)
    return


def phase_ffn(c, l, which, src, dst, dstT=None):
    nc, P, S = c.nc, c.P, c.S
    w_in = (c.ffa_w_in if which == 0 else c.ffb_w_in)[l]
    w_out = (c.ffa_w_out if which == 0 else c.ffb_w_out)[l]
    lni = 0 if which == 0 else 2
    KC, FC, NG = D // 128, c.DFF // 128, min(512, S)
    NT = NG // 128
    dff = c.DFF
    with ExitStack() as pes:
        sb, ps = _tiles(nc, pes)
        win = sb("win", [128, KC, 2 * dff], BF16)
        wout = sb("wout", [128, FC, D], BF16)
        ident = sb("ident", [128, 128], F32)
        gb = sb("gb", [128, 2, D], F32)
        x = sb("x", [128, NT, D], F32)
        xT = sb("xT", [128, KC, NG], BF16)
        actT = sb("actT", [128, FC, NG], BF16)
        sg = [sb(f"sg{i}", [128, NG], F32) for i in range(2)]
        y = sb("y", [128, D], F32)
        o = sb("o", [128, D], F32)
        st = sb("st", [128, 2, 6], F32)
        mv = sb("mv", [128, 2], F32)
        rstd = sb("rstd", [128, 1], F32)
        xts = sb("xts", [128, 8, 128], BF16)
        pt = [ps(f"pt{i}", [128, 512], F32) for i in range(2)]
        pg = [ps(f"pg{i}", [128, 512], F32) for i in range(2)]
        pu = [ps(f"pu{i}", [128, 512], F32) for i in range(2)]
        po = [ps(f"po{i}", [128, 512], F32) for i in range(2)]
        P.dma("sp", ident[:], c.consts[:, c.CO["ident"]:c.CO["ident"] + 128], writes=["ident"])
        P.dma("sp", gb[:, 0, :], c.ln_g[l, lni, :].partition_broadcast(128), writes=["gb0"])
        P.dma("sp", gb[:, 1, :], c.ln_b[l, lni, :].partition_broadcast(128), writes=["gb1"])
        load_w(c, win, w_in, "win", KC, 0, 2 * dff)
        load_w(c, wout, w_out, "wout", FC, 0, D)
        pending = []
        for g in range(S // NG):
            t0 = g * NG
            for tt in range(NT):
                P.dma("act", x[:, tt, :], src[t0 + tt * 128: t0 + (tt + 1) * 128, :], writes=[f"x{tt}"])
            for kc in range(KC):
                p_, pk = pt[kc % 2], f"pt{kc % 2}"
                for tt in range(NT):
                    P.op("pe", lambda p_=p_, tt=tt, kc=kc: nc.tensor.transpose(p_[:, tt * 128:(tt + 1) * 128], x[:, tt, kc * 128:(kc + 1) * 128], ident[:]),
                         reads=[f"x{tt}", "ident"], writes=[pk])
                if kc % 2 == 0:
                    P.op("act", lambda p_=p_, kc=kc: nc.scalar.copy(xT[:, kc, :], p_[:, 0:NG]), reads=[pk], writes=[f"xT{kc}"])
                else:
                    P.op("dve", lambda p_=p_, kc=kc: nc.vector.tensor_copy(xT[:, kc, :], p_[:, 0:NG]), reads=[pk], writes=[f"xT{kc}"])
            for j in range(FC):
                b = j % 2
                for kc in range(KC):
                    P.op("pe", lambda b=b, kc=kc, j=j: nc.tensor.matmul(pg[b][:, 0:NG], lhsT=win[:, kc, j * 128:(j + 1) * 128], rhs=xT[:, kc, :],
                                                                       start=(kc == 0), stop=(kc == KC - 1)),
                         reads=[f"win{kc}", f"xT{kc}"], writes=[f"pg{b}"])
                for kc in range(KC):
                    P.op("pe", lambda b=b, kc=kc, j=j: nc.tensor.matmul(pu[b][:, 0:NG], lhsT=win[:, kc, dff + j * 128: dff + (j + 1) * 128], rhs=xT[:, kc, :],
                                                                       start=(kc == 0), stop=(kc == KC - 1)),
                         reads=[f"win{kc}", f"xT{kc}"], writes=[f"pu{b}"])
                P.op("act", lambda b=b: nc.scalar.activation(out=sg[b][:], in_=pg[b][:, 0:NG], func=AF.Silu), reads=[f"pg{b}"], writes=[f"sg{b}"])
                P.op("dve", lambda b=b, j=j: nc.vector.tensor_tensor(out=actT[:, j, :], in0=pu[b][:, 0:NG], in1=sg[b][:], op=ALU.mult),
                     reads=[f"pu{b}", f"sg{b}"], writes=[f"actT{j}"])
            for tt in range(NT):
                for dh in range(2):
                    for j in range(FC):
                        P.op("pe", lambda tt=tt, dh=dh, j=j: nc.tensor.matmul(po[dh][:, :], lhsT=actT[:, j, tt * 128:(tt + 1) * 128],
                                                                             rhs=wout[:, j, dh * 512:(dh + 1) * 512], start=(j == 0), stop=(j == FC - 1)),
                             reads=[f"actT{j}", f"wout{j}"], writes=[f"po{dh}"])
                while pending:
                    pending.pop(0)()
                P.op("act", lambda tt=tt: nc.scalar.mul(y[:], x[:, tt, :], ALPHA), reads=[f"x{tt}"], writes=["y"])
                for dh in range(2):
                    P.op("dve", lambda dh=dh: nc.vector.scalar_tensor_tensor(out=y[:, dh * 512:(dh + 1) * 512], in0=po[dh][:, :], scalar=0.5,
                                                                           in1=y[:, dh * 512:(dh + 1) * 512], op0=ALU.mult, op1=ALU.add),
                         reads=[f"po{dh}", "y"], writes=["y"])
                layer_norm_tile(c, y, o, st, mv, rstd, gb)
                P.dma("sp", dst[t0 + tt * 128: t0 + (tt + 1) * 128, :], o[:], reads=["o"], writes=["dst"], chan="d_o")
                if dstT is not None:
                  def emitT(t0=t0, tt=tt):
                    for half in range(2):
                        for q in range(4):
                            kc = half * 4 + q
                            P.op("pe", lambda half=half, q=q, kc=kc: nc.tensor.transpose(pt[half][:, q * 128:(q + 1) * 128], o[:, kc * 128:(kc + 1) * 128], ident[:]),
                                 reads=["o", "ident"], writes=[f"pt{half}"])
                        if half == 0:
                            P.op("act", lambda: nc.scalar.copy(xts[:, 0:4, :].rearrange("p a b -> p (a b)"), pt[0][:, :]), reads=["pt0"], writes=["xts0"])
                        else:
                            P.op("dve", lambda: nc.vector.tensor_copy(xts[:, 4:8, :].rearrange("p a b -> p (a b)"), pt[1][:, :]), reads=["pt1"], writes=["xts1"])
                    tok = slice(t0 + tt * 128, t0 + (tt + 1) * 128)
                    P.dma("sp", dstT[:, tok].rearrange("(k p) t -> p k t", p=128), xts[:], reads=["xts0", "xts1"], writes=["dstT"], chan="d_xts")
                  pending.append(emitT)
        while pending:
            pending.pop(0)()
    P.barrier()


def layer_norm_tile(c, y, o, st, mv, rstd, gb, sfx=""):
    nc, P = c.nc, c.P
    ky, ko, kst, kmv, krs = "y" + sfx, "o" + sfx, "st" + sfx, "mv" + sfx, "rstd" + sfx
    for dh in range(2):
        P.op("dve", lambda dh=dh: nc.vector.bn_stats(out=st[:, dh, :], in_=y[:, dh * 512:(dh + 1) * 512]), reads=[ky], writes=[kst])
    P.op("dve", lambda: nc.vector.bn_aggr(out=mv[:], in_=st[:].rearrange("p a b -> p (a b)")), reads=[kst], writes=[kmv])
    P.op("act", lambda: nc.scalar.activation(out=rstd[:], in_=mv[:, 1:2], func=AF.Sqrt, bias=LN_EPS, scale=1.0), reads=[kmv], writes=[krs])
    P.op("dve", lambda: nc.vector.reciprocal(out=rstd[:], in_=rstd[:]), reads=[krs], writes=[krs])
    P.op("dve", lambda: nc.vector.tensor_scalar(out=y[:], in0=y[:], scalar1=mv[:, 0:1], scalar2=rstd[:, 0:1], op0=ALU.subtract, op1=ALU.mult),
         reads=[ky, kmv, krs], writes=[ky])
    P.op("pool", lambda: nc.gpsimd.tensor_tensor(out=o[:], in0=y[:], in1=gb[:, 0, :], op=ALU.mult), reads=[ky, "gb0"], writes=[ko])
    P.op("pool", lambda: nc.gpsimd.tensor_tensor(out=o[:], in0=o[:], in1=gb[:, 1, :], op=ALU.add), reads=[ko, "gb1"], writes=[ko])


def phase_transpose(c, src, dstT):
    nc, P, S = c.nc, c.P, c.S
    with ExitStack() as pes:
        sb, ps = _tiles(nc, pes)
        ident = sb("ident", [128, 128], F32)
        xs = [sb(f"x{i}", [128, D], F32) for i in range(2)]
        xts = [sb(f"xt{i}", [128, 8, 128], BF16) for i in range(2)]
        pt = [ps(f"pt{i}", [128, 512], F32) for i in range(4)]
        P.dma("sp", ident[:], c.consts[:, c.CO["ident"]:c.CO["ident"] + 128], writes=["ident"])
        for tt in range(S // 128):
            b = tt % 2
            P.dma("sp", xs[b][:], src[tt * 128:(tt + 1) * 128, :], writes=[f"x{b}"])
            for half in range(2):
                pp, pk = pt[2 * b + half], f"pt{2 * b + half}"
                for q in range(4):
                    kc = half * 4 + q
                    P.op("pe", lambda pp=pp, q=q, kc=kc, b=b: nc.tensor.transpose(pp[:, q * 128:(q + 1) * 128], xs[b][:, kc * 128:(kc + 1) * 128], ident[:]),
                         reads=[f"x{b}", "ident"], writes=[pk])
                eng = "act" if half == 0 else "dve"
                if half == 0:
                    P.op("act", lambda pp=pp, b=b: nc.scalar.copy(xts[b][:, 0:4, :].rearrange("p a b -> p (a b)"), pp[:, :]), reads=[pk], writes=[f"xt{b}h0"])
                else:
                    P.op("dve", lambda pp=pp, b=b: nc.vector.tensor_copy(xts[b][:, 4:8, :].rearrange("p a b -> p (a b)"), pp[:, :]), reads=[pk], writes=[f"xt{b}h1"])
            P.dma("sp", dstT[:, tt * 128:(tt + 1) * 128].rearrange("(k p) t -> p k t", p=128), xts[b][:], reads=[f"xt{b}h0", f"xt{b}h1"],
                  writes=["dstT"], chan=f"d_xt{b}")
    P.barrier()


def phase_rope(c):
    nc, P, S = c.nc, c.P, c.S
    with ExitStack() as pes:
        sb, ps = _tiles(nc, pes)
        pi_ = sb("pi", [32, S], I32)
        ang = sb("ang", [32, S], F32)
        k = sb("k", [32, S], F32)
        r = sb("r", [32, S], F32)
        rc = sb("rc", [32, S], F32)
        m = sb("m", [32, S], F32)
        cs = sb("cs", [32, S], F32)
        sn = sb("sn", [32, S], F32)
        cst = sb("cst", [32, 2], F32)
        MAGIC = 12582912.0
        C1 = 6.28125
        C2 = TWO_PI - C1
        PI_LO = 3.1415925
        P.dma("sp", pi_[:], c.pos[0, :].partition_broadcast(32), writes=["pi"])
        P.dma("sp", cst[:], c.consts[0:32, c.CO["rope"]:c.CO["rope"] + 2], writes=["cst"])
        P.op("dve", lambda: nc.vector.tensor_copy(ang[:], pi_[:]), reads=["pi"], writes=["ang"])
        P.op("dve", lambda: nc.vector.tensor_scalar(out=ang[:], in0=ang[:], scalar1=cst[:, 0:1], scalar2=None, op0=ALU.mult), reads=["ang", "cst"], writes=["ang"])
        P.op("dve", lambda: nc.vector.tensor_scalar(out=k[:], in0=ang[:], scalar1=1.0 / TWO_PI, scalar2=MAGIC, op0=ALU.mult, op1=ALU.add), reads=["ang"], writes=["k"])
        P.op("dve", lambda: nc.vector.tensor_scalar(out=k[:], in0=k[:], scalar1=MAGIC, scalar2=None, op0=ALU.subtract), reads=["k"], writes=["k"])
        P.op("dve", lambda: nc.vector.scalar_tensor_tensor(out=r[:], in0=k[:], scalar=-C1, in1=ang[:], op0=ALU.mult, op1=ALU.add), reads=["k", "ang"], writes=["r"])
        P.op("dve", lambda: nc.vector.scalar_tensor_tensor(out=r[:], in0=k[:], scalar=-C2, in1=r[:], op0=ALU.mult, op1=ALU.add), reads=["k", "r"], writes=["r"])
        P.op("dve", lambda: nc.vector.tensor_scalar(out=rc[:], in0=r[:], scalar1=np.pi / 2, scalar2=None, op0=ALU.add), reads=["r"], writes=["rc"])
        P.op("dve", lambda: nc.vector.tensor_scalar(out=m[:], in0=rc[:], scalar1=np.pi, scalar2=-TWO_PI, op0=ALU.is_gt, op1=ALU.mult), reads=["rc"], writes=["m"])
        P.op("dve", lambda: nc.vector.tensor_tensor(out=rc[:], in0=rc[:], in1=m[:], op=ALU.add), reads=["rc", "m"], writes=["rc"])
        for t_, nm in ((r, "r"), (rc, "rc")):
            P.op("dve", lambda t_=t_: nc.vector.tensor_scalar(out=t_[:], in0=t_[:], scalar1=PI_LO, scalar2=-PI_LO, op0=ALU.min, op1=ALU.max), reads=[nm], writes=[nm])
        import os
        SINF = AF.Identity if os.environ.get("ROPE_DBG") == "nosin" else AF.Sin
        P.op("act", lambda: nc.scalar.activation(out=sn[:], in_=r[:], func=SINF), reads=["r"], writes=["sn"])
        P.op("act", lambda: nc.scalar.activation(out=cs[:], in_=rc[:], func=SINF), reads=["rc"], writes=["cs"])
        P.op("dve", lambda: nc.vector.tensor_scalar(out=sn[:], in0=sn[:], scalar1=cst[:, 1:2], scalar2=None, op0=ALU.mult), reads=["sn", "cst"], writes=["sn"])
        P.dma("sp", c.ropeT[0, :, :], cs[:], reads=["cs"], writes=["ropeT0"])
        P.dma("sp", c.ropeT[1, :, :], sn[:], reads=["sn"], writes=["ropeT1"])
    P.barrier()


def phase_mla(c, l):
    nc, P, S = c.nc, c.P, c.S
    NB = max(1, S // 512)
    NG = min(512, S)
    NTT = S // 128
    SCALE = 96 ** -0.5
    H = 4
    w_in = c.mix_w_in[l]
    with ExitStack() as pes:
        sb, ps = _tiles(nc, pes)
        xTb = [sb(f"xT{i}", [128, 8, NG], BF16) for i in range(2)]
        wq = sb("wq", [128, 8, 256], BF16)
        wkv = sb("wkv", [128, 8, 128], BF16)
        wkr = sb("wkr", [128, 8, 192], BF16)
        wuq = sb("wuq", [128, 2, 768], BF16)
        wukk = sb("wukk", [128, 1, 256], BF16)
        wukv = sb("wukv", [128, 1, 256], BF16)
        nwq = sb("nwq", [128, 2], F32)
        nwkv = sb("nwkv", [128, 1], F32)
        ones = sb("ones", [128, 128], BF16)
        onesf = sb("onesf", [128, 128], F32)
        mk = sb("mk", [128, 1, 2048], BF16)
        cs = sb("cs", [96, NG], F32)
        sn = sb("sn", [96, NG], F32)
        cqn = sb("cqn", [128, 2, NG], BF16)
        ckvn = sb("ckvn", [128, NG], BF16)
        krT = sb("krT", [96, NG], BF16)
        qf = [sb(f"qf{h}", [96, S], BF16) for h in range(H)]
        kf = [sb(f"kf{h}", [96, S], BF16) for h in range(H)]
        V = sb("V", [128, NTT, 256], BF16)
        tmp = [sb(f"tmp{i}", [128, NG], F32) for i in range(3)]
        sq = [sb(f"sq{i}", [128, NG], BF16) for i in range(2)]
        rbc = sb("rbc", [128, NG], F32)
        pe_ = [sb(f"pe{i}", [128, NG], BF16) for i in range(5)]
        rec = [sb(f"rec{i}", [64, NG], F32) for i in range(2)]
        oT = [sb(f"oT{i}", [64, NG], BF16) for i in range(2)]
        pA = [ps(f"pA{i}", [128, 512], F32) for i in range(2)]
        pB = ps("pB", [128, 512], F32)
        pS = [ps(f"pS{i}", [128, 512], F32) for i in range(2)]
        pL = ps("pL", [64, 512], F32)
        CO = c.CO
        load_w(c, wq, w_in, "wq", 8, C_MQ, C_MQ + 256)
        load_w(c, wkv, w_in, "wkv", 8, C_MKV, C_MKV + 128)
        load_w(c, wkr, c.w_kr[l], "wkr", 8, 0, 192)
        load_w(c, wuq, c.w_uq_p[l], "wuq", 2, 0, 768)
        load_w(c, wukk, c.w_ukv_k[l], "wukk", 1, 0, 256)
        load_w(c, wukv, c.w_ukv_v[l], "wukv", 1, 0, 256)
        P.dma("sp", nwq[:], c.q_norm_wT[l], writes=["nwq"])
        P.dma("sp", nwkv[:], c.kv_norm_wT[l], writes=["nwkv"])
        P.dma("sp", onesf[:], c.consts[:, CO["ones"]:CO["ones"] + 128], writes=["onesf"])
        P.op("dve", lambda: nc.vector.tensor_copy(ones[:], onesf[:]), reads=["onesf"], writes=["ones"])
        load_w(c, mk, c.consts, "mk", 1, CO["maskle"], CO["maskle"] + 2048)
        wkeys = lambda n, k: [f"{n}{i}" for i in range(k)]
        for nb in range(NB):
            cols = slice(nb * NG, (nb + 1) * NG)
            xT = xTb[nb % 2]
            xk = f"xT{nb % 2}"
            P.dma("sp", xT[:], c.h1T[:, cols].rearrange("(k p) t -> p k t", p=128), writes=[xk])
            P.dma("sp", cs[64:96, :], c.ropeT[0, :, cols], writes=["cs"])
            P.dma("sp", sn[64:96, :], c.ropeT[1, :, cols], writes=["sn"])
            for j in range(2):
                pa = pA[j]
                for kc in range(8):
                    P.op("pe", lambda pa=pa, kc=kc, j=j: nc.tensor.matmul(pa[:, 0:NG], lhsT=wq[:, kc, j * 128:(j + 1) * 128], rhs=xT[:, kc, :],
                                                                         start=(kc == 0), stop=(kc == 7)), reads=[f"wq{kc}", xk], writes=[f"pA{j}"])
                P.op("act", lambda pa=pa, j=j: nc.scalar.activation(out=sq[j][:], in_=pa[:, 0:NG], func=AF.Square), reads=[f"pA{j}"], writes=[f"sq{j}"])
                P.op("dve", lambda pa=pa, j=j: nc.vector.tensor_scalar(out=tmp[j][:], in0=pa[:, 0:NG], scalar1=nwq[:, j:j + 1], scalar2=None, op0=ALU.mult),
                     reads=[f"pA{j}", "nwq"], writes=[f"tmp{j}"])
            for j in range(2):
                P.op("pe", lambda j=j: nc.tensor.matmul(pB[:, 0:NG], lhsT=ones[:], rhs=sq[j][:], start=(j == 0), stop=(j == 1)),
                     reads=["ones", f"sq{j}"], writes=["pB"])
            P.op("act", lambda: nc.scalar.activation(out=rbc[:], in_=pB[:, 0:NG], func=AF.Sqrt, bias=RMS_EPS, scale=1.0 / 256), reads=["pB"], writes=["rbc"])
            P.op("dve", lambda: nc.vector.reciprocal(out=rbc[:], in_=rbc[:]), reads=["rbc"], writes=["rbc"])
            for j in range(2):
                P.op("dve", lambda j=j: nc.vector.tensor_tensor(out=cqn[:, j, :], in0=tmp[j][:], in1=rbc[:], op=ALU.mult),
                     reads=[f"tmp{j}", "rbc"], writes=["cqn"])
            pa = pA[0]
            for kc in range(8):
                P.op("pe", lambda kc=kc: nc.tensor.matmul(pa[:, 0:NG], lhsT=wkv[:, kc, :], rhs=xT[:, kc, :], start=(kc == 0), stop=(kc == 7)),
                     reads=[f"wkv{kc}", xk], writes=["pA0"])
            P.op("act", lambda: nc.scalar.activation(out=sq[0][:], in_=pa[:, 0:NG], func=AF.Square), reads=["pA0"], writes=["sq0"])
            P.op("dve", lambda: nc.vector.tensor_scalar(out=tmp[0][:], in0=pa[:, 0:NG], scalar1=nwkv[:, 0:1], scalar2=None, op0=ALU.mult),
                 reads=["pA0", "nwkv"], writes=["tmp0"])
            P.op("pe", lambda: nc.tensor.matmul(pB[:, 0:NG], lhsT=ones[:], rhs=sq[0][:], start=True, stop=True), reads=["ones", "sq0"], writes=["pB"])
            P.op("act", lambda: nc.scalar.activation(out=rbc[:], in_=pB[:, 0:NG], func=AF.Sqrt, bias=RMS_EPS, scale=1.0 / 128), reads=["pB"], writes=["rbc"])
            P.op("dve", lambda: nc.vector.reciprocal(out=rbc[:], in_=rbc[:]), reads=["rbc"], writes=["rbc"])
            P.op("dve", lambda: nc.vector.tensor_tensor(out=ckvn[:, :], in0=tmp[0][:], in1=rbc[:], op=ALU.mult), reads=["tmp0", "rbc"], writes=["ckvn"])
            pa = pA[1]
            for kc in range(8):
                P.op("pe", lambda kc=kc: nc.tensor.matmul(pa[0:96, 0:NG], lhsT=wkr[:, kc, 0:96], rhs=xT[:, kc, :], start=(kc == 0), stop=(kc == 7)),
                     reads=[f"wkr{kc}", xk], writes=["pA1"])
            for kc in range(8):
                P.op("pe", lambda kc=kc: nc.tensor.matmul(pB[0:96, 0:NG], lhsT=wkr[:, kc, 96:192], rhs=xT[:, kc, :], start=(kc == 0), stop=(kc == 7)),
                     reads=[f"wkr{kc}", xk], writes=["pB"])
            P.op("dve", lambda: nc.vector.tensor_tensor(out=tmp[1][64:96, :], in0=pa[64:96, 0:NG], in1=cs[64:96, :], op=ALU.mult), reads=["pA1", "cs"], writes=["tmp1"])
            P.op("dve", lambda: nc.vector.tensor_tensor(out=tmp[2][64:96, :], in0=pB[64:96, 0:NG], in1=sn[64:96, :], op=ALU.mult), reads=["pB", "sn"], writes=["tmp2"])
            P.op("pool", lambda: nc.gpsimd.tensor_tensor(out=krT[64:96, :], in0=tmp[1][64:96, :], in1=tmp[2][64:96, :], op=ALU.add), reads=["tmp1", "tmp2"], writes=["krT"])
            for h in range(H):
                pa = pA[h % 2]
                pak = f"pA{h % 2}"
                for j in range(2):
                    P.op("pe", lambda pa=pa, j=j, h=h: nc.tensor.matmul(pa[0:96, 0:NG], lhsT=wuq[:, j, h * 192:h * 192 + 96], rhs=cqn[:, j, :],
                                                                       start=(j == 0), stop=(j == 1)), reads=[f"wuq{j}", "cqn"], writes=[pak])
                for j in range(2):
                    P.op("pe", lambda j=j, h=h: nc.tensor.matmul(pB[0:96, 0:NG], lhsT=wuq[:, j, h * 192 + 96:h * 192 + 192], rhs=cqn[:, j, :],
                                                                start=(j == 0), stop=(j == 1)), reads=[f"wuq{j}", "cqn"], writes=["pB"])
                P.op("act", lambda pa=pa, h=h: nc.scalar.copy(qf[h][0:64, cols], pa[0:64, 0:NG]), reads=[pak], writes=[f"qf{h}"])
                P.op("dve", lambda pa=pa: nc.vector.tensor_tensor(out=tmp[1][64:96, :], in0=pa[64:96, 0:NG], in1=cs[64:96, :], op=ALU.mult), reads=[pak, "cs"], writes=["tmp1"])
                P.op("dve", lambda: nc.vector.tensor_tensor(out=tmp[2][64:96, :], in0=pB[64:96, 0:NG], in1=sn[64:96, :], op=ALU.mult), reads=["pB", "sn"], writes=["tmp2"])
                P.op("pool", lambda h=h: nc.gpsimd.tensor_tensor(out=qf[h][64:96, cols], in0=tmp[1][64:96, :], in1=tmp[2][64:96, :], op=ALU.add),
                     reads=["tmp1", "tmp2"], writes=[f"qf{h}"])
                P.op("pe", lambda pa=pa, h=h: nc.tensor.matmul(pa[0:64, 0:NG], lhsT=wukk[:, 0, h * 64:(h + 1) * 64], rhs=ckvn[:, :], start=True, stop=True),
                     reads=["wukk0", "ckvn"], writes=[pak])
                P.op("act", lambda pa=pa, h=h: nc.scalar.copy(kf[h][0:64, cols], pa[0:64, 0:NG]), reads=[pak], writes=[f"kf{h}"])
                P.op("pool", lambda h=h: nc.gpsimd.tensor_copy(kf[h][64:96, cols], krT[64:96, :]), reads=["krT"], writes=[f"kf{h}"])
            for q in range(NG // 128):
                tt = nb * (NG // 128) + q
                P.op("pe", lambda tt=tt: nc.tensor.matmul(pB[:, 0:256], lhsT=ckvn[:, q * 128:(q + 1) * 128], rhs=wukv[:, 0, :], start=True, stop=True),
                     reads=["ckvn", "wukv0"], writes=["pB"])
                P.op("act", lambda tt=tt: nc.scalar.copy(V[:, tt, :], pB[:, 0:256]), reads=["pB"], writes=["V"])
        LA = 3
        NPE = 5
        pOb = (pA[0], pA[1])
        pOk = ("pA0", "pA1")
        pLb = (pB, pL)
        pLk = ("pB", "pL")
        for hp in range(2):
            meta = []
            for tq in range(NB):
                nkb = (tq + 1) * (NG // 128)
                for kb in range(nkb):
                    for hh in range(2):
                        meta.append(dict(tq=tq, kb=kb, hh=hh, nkb=nkb, zb=len(meta) % 2, eb=len(meta) % NPE))

            def stageA(m):
                tq, kb, hh, zb, eb = m["tq"], m["kb"], m["hh"], m["zb"], m["eb"]
                h = hp * 2 + hh
                qcols = slice(tq * NG, (tq + 1) * NG)
                P.op("pe", lambda: nc.tensor.matmul(pS[zb][:, 0:NG], lhsT=kf[h][:, kb * 128:(kb + 1) * 128], rhs=qf[h][:, qcols], start=True, stop=True),
                     reads=[f"kf{h}", f"qf{h}"], writes=[f"pS{zb}"])
                P.op("act", lambda: nc.scalar.activation(out=pe_[eb][:], in_=pS[zb][:, 0:NG], func=AF.Exp, scale=SCALE), reads=[f"pS{zb}"], writes=[f"pe{eb}"])
                r = kb - tq * (NG // 128)
                if r >= 0:
                    P.op("pool", lambda: nc.gpsimd.tensor_tensor(out=pe_[eb][:], in0=pe_[eb][:], in1=mk[:, 0, r * 512:r * 512 + NG], op=ALU.mult),
                         reads=[f"pe{eb}", "mk0"], writes=[f"pe{eb}"])

            def stageB(m):
                tq, kb, hh, nkb, eb = m["tq"], m["kb"], m["hh"], m["nkb"], m["eb"]
                h = hp * 2 + hh
                qcols = slice(tq * NG, (tq + 1) * NG)
                P.op("pe", lambda: nc.tensor.matmul(pOb[hh][0:64, 0:NG], lhsT=V[:, kb, h * 64:(h + 1) * 64], rhs=pe_[eb][:], start=(kb == 0), stop=(kb == nkb - 1)),
                     reads=["V", f"pe{eb}"], writes=[pOk[hh]])
                P.op("pe", lambda: nc.tensor.matmul(pLb[hh][0:64, 0:NG], lhsT=ones[:, 0:64], rhs=pe_[eb][:], start=(kb == 0), stop=(kb == nkb - 1)),
                     reads=["ones", f"pe{eb}"], writes=[pLk[hh]])
                if kb == nkb - 1:
                    P.op("dve", lambda: nc.vector.reciprocal(out=rec[hh][:], in_=pLb[hh][0:64, 0:NG]), reads=[pLk[hh]], writes=[f"rec{hh}"])
                    P.op("dve", lambda: nc.vector.tensor_tensor(out=oT[hh][:], in0=pOb[hh][0:64, 0:NG], in1=rec[hh][:], op=ALU.mult), reads=[pOk[hh], f"rec{hh}"], writes=[f"oT{hh}"])
                    P.dma("sp", c.mixT[768 + h * 64: 768 + (h + 1) * 64, qcols], oT[hh][:], reads=[f"oT{hh}"], writes=["mixT"], chan=f"d_oT{hh}")

            n = len(meta)
            for i in range(min(LA, n)):
                stageA(meta[i])
            for i in range(n):
                if i + LA < n:
                    stageA(meta[i + LA])
                stageB(meta[i])
    P.barrier()


def make_consts():
    CO = {}
    cols = []
    off = 0

    def add(name, arr):
        nonlocal off
        a = np.zeros((128, arr.shape[1]), np.float32)
        a[:arr.shape[0]] = arr
        CO[name] = off
        off += arr.shape[1]
        cols.append(a)

    add("ident", np.eye(128, dtype=np.float32))
    add("ones", np.ones((128, 128), np.float32))
    s = np.arange(128)[:, None]
    t = np.arange(512)[None, :]
    add("maskle", np.concatenate([((r * 128 + s) <= t).astype(np.float32) for r in range(4)], 1))
    add("masklt", np.concatenate([((r * 128 + s) < t).astype(np.float32) for r in range(4)], 1))
    j = np.arange(128)[:, None]
    i = np.arange(128)[None, :]
    add("trineg", -(j >= i).astype(np.float32))
    add("compneg", -(j < i).astype(np.float32))
    add("triinc", (j <= i).astype(np.float32))
    add("upincl", (i >= j).astype(np.float32))
    add("upstrict", (i > j).astype(np.float32))
    inv = (1.0 / (10000.0 ** (np.arange(0, 32, 2, dtype=np.float32) / 32))).astype(np.float32)
    rope = np.zeros((32, 2), np.float32)
    rope[:, 0] = np.concatenate([inv, inv])
    rope[:, 1] = np.concatenate([-np.ones(16), np.ones(16)])
    add("rope", rope)
    return np.concatenate(cols, 1), CO


def prep_weights(inp, L):
    f = lambda a: np.ascontiguousarray(a, dtype=np.float32)
    w_in = inp["mix_w_in"]
    out = {}
    z64 = np.zeros(w_in.shape[:2] + (64,), np.float32)
    out["w_kr"] = f(np.concatenate([z64, w_in[:, :, C_KR:C_KR + 32], z64, w_in[:, :, C_KR + 16:C_KR + 32], w_in[:, :, C_KR:C_KR + 16]], -1))
    uq = inp["mla_w_uq"]
    parts = []
    for h in range(4):
        b = h * 96
        parts += [uq[:, :, b:b + 64], uq[:, :, b + 64:b + 96], np.zeros(uq.shape[:2] + (64,), np.float32), uq[:, :, b + 80:b + 96], uq[:, :, b + 64:b + 80]]
    out["w_uq_p"] = f(np.concatenate(parts, -1))
    ukv = inp["mla_w_ukv"]
    out["w_ukv_k"] = f(np.concatenate([ukv[:, :, h * 128:h * 128 + 64] for h in range(4)], -1))
    out["w_ukv_v"] = f(np.concatenate([ukv[:, :, h * 128 + 64:h * 128 + 128] for h in range(4)], -1))
    out["q_norm_wT"] = f(inp["mla_q_norm_w"].reshape(L, 2, 128).transpose(0, 2, 1))
    out["kv_norm_wT"] = f(inp["mla_kv_norm_w"].reshape(L, 1, 128).transpose(0, 2, 1))
    out["conv_wT"] = f(inp["gdn_conv_w"].reshape(L, 4, 12, 128).transpose(0, 3, 2, 1))
    for k in ("ffa_w_in", "ffa_w_out", "mix_w_in", "gdn_a_log", "gdn_dt_bias", "gdn_norm_w", "mix_w_o", "ffb_w_in", "ffb_w_out",
              "ln_g", "ln_b", "ple_w_gate", "ple_w_proj"):
        out[k] = f(inp[k])
    return out


WSHAPES = lambda L, dff: {
    "ffa_w_in": [L, D, 2 * dff], "ffa_w_out": [L, dff, D], "mix_w_in": [L, D, IN_TOTAL], "w_kr": [L, D, 192],
    "w_uq_p": [L, 256, 768], "w_ukv_k": [L, 128, 256], "w_ukv_v": [L, 128, 256], "q_norm_wT": [L, 128, 2], "kv_norm_wT": [L, 128, 1],
    "conv_wT": [L, 128, 12, 4], "gdn_a_log": [L, 8], "gdn_dt_bias": [L, 8], "gdn_norm_w": [L, 64], "mix_w_o": [L, D, D],
    "ffb_w_in": [L, D, 2 * dff], "ffb_w_out": [L, dff, D], "ln_g": [L, 3, D], "ln_b": [L, 3, D], "ple_w_gate": [L, D, D], "ple_w_proj": [L, PLE, D],
}


def build(S, L, phases, dff=DFF, debug=False):
    nc = bass.Bass("TRN2", target_bir_lowering=False)
    c = Ctx()
    c.nc, c.S, c.DFF, c.L = nc, S, dff, L
    consts_np, c.CO = make_consts()
    din = lambda n, s, d=F32: nc.dram_tensor(n, s, d, kind="ExternalInput").ap()
    c.x = din("x", [S, D])
    c.p = din("p", [L, S, PLE])
    c.pos = din("pos", [1, S], I32)
    c.consts = din("consts", list(consts_np.shape))
    for k, shp in WSHAPES(L, dff).items():
        setattr(c, k, din(k, shp))
    kind = "ExternalOutput" if debug else "Internal"
    scr = lambda n, s, d: nc.dram_tensor(n, s, d, kind=kind).ap()
    c.hb = [scr("hb0", [S, D], F32), scr("hb1", [S, D], F32)]
    c.h1T = scr("h1T", [D, S], BF16)
    c.mixT = scr("mixT", [D, S], BF16)
    c.ropeT = scr("ropeT", [2, 32, S], F32)
    c.out = nc.dram_tensor("out", [S, D], F32, kind="ExternalOutput").ap()
    with ExitStack() as es:
        c.P = Prog(nc, es)
        for ph in phases:
            ph(c)
        c.P.barrier()
        print("ninst", c.P.ninst, "nwaits", c.P.nwaits, "nsem", c.P.nsem)
    return nc, consts_np


def phase_sb(c, l):
    nc, P, S = c.nc, c.P, c.S
    NG = min(512, S)
    NB = S // NG
    NTT = S // 128
    R = NG // 128
    H = 4
    w_in = c.mix_w_in[l]
    CO = c.CO
    with ExitStack() as pes:
        sb, ps = _tiles(nc, pes)
        xTb = [sb(f"xT{i}", [128, 8, NG], BF16) for i in range(2)]
        wq = sb("wq", [128, 8, 256], BF16)
        wk = sb("wk", [128, 8, 256], BF16)
        wv = sb("wv", [128, 8, 256], BF16)
        qT = [sb(f"qT{j}", [128, S], BF16) for j in range(2)]
        kT = [sb(f"kT{j}", [128, S], BF16) for j in range(2)]
        V = sb("V", [128, NTT, 256], BF16)
        cf = sb("cf", [128, 256], F32)
        tri = sb("tri", [128, 128], BF16)
        comp = sb("comp", [128, 128], BF16)
        mkf = sb("mkf", [128, 4, 512], F32)
        e = [sb(f"e{i}", [128, NG], F32) for i in range(5)]
        sp = [sb(f"sp{i}", [128, NG], BF16) for i in range(10)]
        eC = [sb(f"eC{i}", [128, NG], F32) for i in range(2)]
        A = [sb(f"A{i}", [128, NG], BF16) for i in range(4)]
        oT = [sb(f"oT{i}", [64, NG], BF16) for i in range(2)]
        pZ = [ps(f"pZ{i}", [128, 512], F32) for i in range(2)]
        pC = [ps(f"pC{i}", [128, 512], F32) for i in range(2)]
        pO = [ps(f"pO{i}", [128, 512], F32) for i in range(2)]
        pX = ps("pX", [128, 512], F32)
        load_w(c, wq, w_in, "wq", 8, C_SQ, C_SQ + 256)
        load_w(c, wk, w_in, "wk", 8, C_SK, C_SK + 256)
        load_w(c, wv, w_in, "wv", 8, C_SV, C_SV + 256)
        P.dma("sp", cf[:, 0:128], c.consts[:, CO["trineg"]:CO["trineg"] + 128], writes=["cf"])
        P.dma("sp", cf[:, 128:256], c.consts[:, CO["compneg"]:CO["compneg"] + 128], writes=["cf"])
        P.op("dve", lambda: nc.vector.tensor_copy(tri[:], cf[:, 0:128]), reads=["cf"], writes=["tri"])
        P.op("dve", lambda: nc.vector.tensor_copy(comp[:], cf[:, 128:256]), reads=["cf"], writes=["comp"])
        P.dma("sp", mkf[:].rearrange("p a b -> p (a b)"), c.consts[:, CO["masklt"]:CO["masklt"] + 2048], writes=["mkf"])
        for nb in range(NB):
            cols = slice(nb * NG, (nb + 1) * NG)
            xT = xTb[nb % 2]
            xk = f"xT{nb % 2}"
            P.dma("sp", xT[:], c.h1T[:, cols].rearrange("(k p) t -> p k t", p=128), writes=[xk])
            for (w_, dst, nm) in ((wq, qT, "qT"), (wk, kT, "kT")):
                for j in range(2):
                    for kc in range(8):
                        P.op("pe", lambda w_=w_, kc=kc, j=j: nc.tensor.matmul(pX[:, 0:NG], lhsT=w_[:, kc, j * 128:(j + 1) * 128], rhs=xT[:, kc, :],
                                                                             start=(kc == 0), stop=(kc == 7)), reads=[xk] + [f"wq{kc}", f"wk{kc}"], writes=["pX"])
                    P.op("act", lambda dst=dst, j=j: nc.scalar.copy(dst[j][:, cols], pX[:, 0:NG]), reads=["pX"], writes=[f"{nm}{j}"])
            for q in range(R):
                tt = nb * R + q
                for kc in range(8):
                    P.op("pe", lambda q=q, kc=kc: nc.tensor.matmul(pX[:, 0:256], lhsT=xT[:, kc, q * 128:(q + 1) * 128], rhs=wv[:, kc, :],
                                                                    start=(kc == 0), stop=(kc == 7)), reads=[xk, f"wv{kc}"], writes=["pX"])
                P.op("dve", lambda tt=tt: nc.vector.tensor_copy(V[:, tt, :], pX[:, 0:256]), reads=["pX"], writes=["V"])
        LA = 3
        NE, NSP = 5, 5
        for hp in range(2):
            items = []
            for tq in range(NB):
                nkb = (tq + 1) * R
                for kb in range(nkb - 1, -1, -1):
                    for hh in range(2):
                        items.append((tq, kb, hh, nkb))
            percnt = [0, 0]
            meta = []
            for idx, (tq, kb, hh, nkb) in enumerate(items):
                itn = percnt[hh]
                percnt[hh] += 1
                meta.append(dict(tq=tq, kb=kb, hh=hh, nkb=nkb, zb=idx % 2, eb=idx % NE, ab=idx % 4, spb=hh * NSP + itn % NSP, spp=hh * NSP + (itn - 1) % NSP))

            def stageA(m):
                tq, kb, hh = m["tq"], m["kb"], m["hh"]
                qcols = slice(tq * NG, (tq + 1) * NG)
                pb, j, zb, eb, spb = hh * 64, hp, m["zb"], m["eb"], m["spb"]
                r = kb - tq * R
                P.op("pe", lambda: nc.tensor.matmul(pZ[zb][:, 0:NG], lhsT=kT[j][pb:pb + 64, kb * 128:(kb + 1) * 128], rhs=qT[j][pb:pb + 64, qcols], start=True, stop=True),
                     reads=[f"kT{j}", f"qT{j}"], writes=[f"pZ{zb}"])
                P.op("act", lambda: nc.scalar.activation(out=e[eb][:], in_=pZ[zb][:, 0:NG], func=AF.Exp, scale=0.125), reads=[f"pZ{zb}"], writes=[f"e{eb}"])
                if r >= 0:
                    P.op("dve", lambda: nc.vector.tensor_tensor(out=e[eb][:], in0=e[eb][:], in1=mkf[:, r, 0:NG], op=ALU.mult), reads=[f"e{eb}", "mkf"], writes=[f"e{eb}"])
                P.op("act", lambda: nc.scalar.activation(out=sp[spb][:], in_=e[eb][:], func=AF.Ln, bias=1.0, scale=1.0), reads=[f"e{eb}"], writes=[f"sp{spb}"])

            def stageB(m):
                tq, kb, hh, nkb = m["tq"], m["kb"], m["hh"], m["nkb"]
                qcols = slice(tq * NG, (tq + 1) * NG)
                h = hp * 2 + hh
                eb, ab, spb, spp = m["eb"], m["ab"], m["spb"], m["spp"]
                first = kb == nkb - 1
                if not first:
                    P.op("pe", lambda: nc.tensor.matmul(pC[hh][:, 0:NG], lhsT=comp[:], rhs=sp[spp][:], start=False, stop=False, skip_group_check=True),
                         reads=["comp", f"sp{spp}"], writes=[f"pC{hh}"])
                P.op("pe", lambda: nc.tensor.matmul(pC[hh][:, 0:NG], lhsT=tri[:], rhs=sp[spb][:], start=first, stop=True, skip_group_check=True),
                     reads=["tri", f"sp{spb}"], writes=[f"pC{hh}"])
                P.op("act", lambda: nc.scalar.activation(out=eC[hh][:], in_=pC[hh][:, 0:NG], func=AF.Exp), reads=[f"pC{hh}"], writes=[f"eC{hh}"])
                P.op("dve", lambda: nc.vector.tensor_tensor(out=A[ab][:], in0=e[eb][:], in1=eC[hh][:], op=ALU.mult), reads=[f"e{eb}", f"eC{hh}"], writes=[f"A{ab}"])

            def stageB2(m):
                tq, kb, hh, nkb = m["tq"], m["kb"], m["hh"], m["nkb"]
                qcols = slice(tq * NG, (tq + 1) * NG)
                h = hp * 2 + hh
                ab = m["ab"]
                first = kb == nkb - 1
                P.op("pe", lambda: nc.tensor.matmul(pO[hh][0:64, 0:NG], lhsT=V[:, kb, h * 64:(h + 1) * 64], rhs=A[ab][:], start=first, stop=(kb == 0)),
                     reads=["V", f"A{ab}"], writes=[f"pO{hh}"])
                if kb == 0:
                    P.op("dve", lambda: nc.vector.tensor_copy(oT[hh][:], pO[hh][0:64, 0:NG]), reads=[f"pO{hh}"], writes=[f"oT{hh}"])
                    P.dma("sp", c.mixT[512 + h * 64: 512 + (h + 1) * 64, qcols], oT[hh][:], reads=[f"oT{hh}"], writes=["mixT"], chan=f"d_oT{hh}")

            n = len(meta)
            for i in range(min(LA, n)):
                stageA(meta[i])
            for i in range(n):
                if i + LA < n:
                    stageA(meta[i + LA])
                stageB(meta[i])
                if i >= 2:
                    stageB2(meta[i - 2])
            for i in range(max(0, n - 2), n):
                stageB2(meta[i])
    P.barrier()


def phase_gdn(c, l):
    nc, P, S = c.nc, c.P, c.S
    NG = min(512, S)
    NB = S // NG
    R = NG // 128
    w_in = c.mix_w_in[l]
    CO = c.CO
    with ExitStack() as pes:
        sb, ps = _tiles(nc, pes)
        xTb = [sb(f"xT{i}", [128, 8, NG], BF16) for i in range(2)]
        wqkv = sb("wqkv", [128, 8, 1536], BF16)
        wz = sb("wz", [128, 8, 512], BF16)
        wab = sb("wab", [128, 8, 16], BF16)
        cw = sb("cw", [128, 12, 4], F32)
        ident = sb("ident", [128, 128], F32)
        identb = sb("identb", [128, 128], BF16)
        onesf = sb("onesf", [128, 128], F32)
        triinc = sb("triinc", [128, 128], F32)
        upi = sb("upi", [128, 128], F32)
        ups = sb("ups", [128, 128], F32)
        dtb = sb("dtb", [128, 8], F32)
        nea = sb("nea", [128, 8], F32)
        nw = sb("nw", [128, 64], F32)
        halo = sb("halo", [128, 12, 3], F32)
        raw = [sb(f"raw{i}", [128, NG + 3], F32) for i in range(2)]
        acc = [sb(f"acc{i}", [128, NG], F32) for i in range(2)]
        csl = sb("csl", [128, 12, NG], F32)
        QKVs = [sb(f"QKV{i}", [128, 1536], F32) for i in range(2)]
        sqt2 = sb("sqt2", [128, 512], F32)
        ss2 = sb("ss2", [128, 8], F32)
        sqt = sb("sqt", [128, 1024], F32)
        ss = sb("ss", [128, 16], F32)
        kqTs = [sb(f"kqT{i}", [64, 8, 2, 128], BF16) for i in range(2)]
        gab = sb("gab", [128, 16], F32)
        beta = sb("beta", [128, 8], F32)
        nbeta = sb("nbeta", [128, 8], F32)
        g = sb("g", [128, 8], F32)
        gc = sb("gc", [128, 8], F32)
        gams = [sb(f"gam{i}", [128, 8], F32) for i in range(2)]
        kap = sb("kap", [128, 8], F32)
        gls = [sb(f"gl{i}", [128, 8], F32) for i in range(2)]
        nbks = [sb(f"nbk{i}", [128, 8], F32) for i in range(2)]
        G1 = sb("G1", [128, 8, 128], F32)
        dm = sb("dm", [128, 8, 128], F32)
        dTs = sb("dTs", [128, 8, 128], F32)
        dTi = sb("dTi", [128, 8, 128], F32)
        CD = F32
        A0s = [sb(f"A0{i}", [128, 8, 128], CD) for i in range(2)]
        qkTps = [sb(f"qkTp{i}", [128, 8, 128], BF16) for i in range(2)]
        Am = [[sb(f"Am{g}{i}", [128, 4, 128], BF16) for i in range(2)] for g in range(2)]
        Bm = [[sb(f"Bm{g}{i}", [128, 4, 128], BF16) for i in range(2)] for g in range(2)]
        Amf = [[sb(f"Amf{g}{i}", [128, 4, 128], F32) for i in range(2)] for g in range(2)]
        Bmf = [[sb(f"Bmf{g}{i}", [128, 4, 128], F32) for i in range(2)] for g in range(2)]
        Yb = [sb(f"Yb{g}", [128, 4, 128], BF16) for g in range(2)]
        Yf = [sb(f"Yf{g}", [128, 4, 128], F32) for g in range(2)]
        Xp = sb("Xp", [128, 8, 128], BF16)
        St = sb("St", [64, 8, 64], F32)
        Sb = sb("Sb", [64, 8, 64], BF16)
        rp = sb("rp", [128, 8, 64], BF16)
        vt = sb("vt", [128, 512], BF16)
        kd = sb("kd", [128, 512], BF16)
        o1 = sb("o1", [128, 8, 64], F32)
        of = sb("of", [128, 8, 64], F32)
        zs = sb("zs", [128, 512], F32)
        og = sb("og", [128, 512], F32)
        oTs = sb("oTs", [128, 4, 128], BF16)
        B = [ps(f"B{i}", [128, 512], F32) for i in range(8)]
        bk = lambda i: f"B{i}"

        load_w(c, wqkv, w_in, "wqkv", 8, C_GQ, C_GQ + 1536)
        load_w(c, wz, w_in, "wz", 8, C_GZ, C_GZ + 512)
        load_w(c, wab, w_in, "wab", 8, C_GA, C_GA + 16)
        P.dma("sp", cw[:], c.conv_wT[l], writes=["cw"])
        for (t_, nm) in ((ident, "ident"), (onesf, "ones"), (triinc, "triinc"), (upi, "upincl"), (ups, "upstrict")):
            P.dma("sp", t_[:], c.consts[:, CO[nm]:CO[nm] + 128], writes=[nm])
        P.op("dve", lambda: nc.vector.tensor_copy(identb[:], ident[:]), reads=["ident"], writes=["identb"])
        P.dma("sp", dtb[:], c.gdn_dt_bias[l, :].partition_broadcast(128), writes=["dtb"])
        P.dma("sp", nea[:], c.gdn_a_log[l, :].partition_broadcast(128), writes=["nea"])
        P.dma("sp", nw[:], c.gdn_norm_w[l, :].partition_broadcast(128), writes=["nw"])
        P.op("act", lambda: nc.scalar.activation(out=nea[:], in_=nea[:], func=AF.Exp), reads=["nea"], writes=["nea"])
        P.op("dve", lambda: nc.vector.tensor_scalar(out=nea[:], in0=nea[:], scalar1=-1.0, scalar2=None, op0=ALU.mult), reads=["nea"], writes=["nea"])
        P.op("pool", lambda: nc.gpsimd.memset(halo[:], 0.0), writes=["halo"])
        P.op("pool", lambda: nc.gpsimd.memset(St[:], 0.0), writes=["St"])
        P.op("pool", lambda: nc.gpsimd.memset(Sb[:], 0.0), writes=["Sb"])

        def bc_h(ap2d, n, d):
            return ap2d.unsqueeze(2).to_broadcast([128, n, d])

        def bc_m(ap2d, n, d):
            return ap2d.unsqueeze(1).to_broadcast([128, n, d])

        def conv(nb):
            cols = slice(nb * NG, (nb + 1) * NG)
            xT = xTb[nb % 2]
            xk = f"xT{nb % 2}"
            P.dma("sp", xT[:], c.h1T[:, cols].rearrange("(k p) t -> p k t", p=128), writes=[xk])
            for ch in range(12):
                rb = ch % 2
                for kc in range(8):
                    P.op("pe", lambda kc=kc, ch=ch: nc.tensor.matmul(B[0][:, 0:NG], lhsT=wqkv[:, kc, ch * 128:(ch + 1) * 128], rhs=xT[:, kc, :],
                                                                    start=(kc == 0), stop=(kc == 7)), reads=[xk, f"wqkv{kc}"], writes=[bk(0)])
                P.op("pool", lambda rb=rb, ch=ch: nc.gpsimd.tensor_copy(raw[rb][:, 0:3], halo[:, ch, :]), reads=["halo"], writes=[f"raw{rb}"])
                P.op("act", lambda rb=rb: nc.scalar.copy(raw[rb][:, 3:NG + 3], B[0][:, 0:NG]), reads=[bk(0)], writes=[f"raw{rb}"])
                P.op("pool", lambda rb=rb, ch=ch: nc.gpsimd.tensor_copy(halo[:, ch, :], raw[rb][:, NG:NG + 3]), reads=[f"raw{rb}"], writes=["halo"])
                P.op("dve", lambda rb=rb, ch=ch: nc.vector.tensor_scalar(out=acc[rb][:], in0=raw[rb][:, 0:NG], scalar1=cw[:, ch, 0:1], scalar2=None, op0=ALU.mult),
                     reads=[f"raw{rb}", "cw"], writes=[f"acc{rb}"])
                for j in range(1, 4):
                    P.op("dve", lambda rb=rb, ch=ch, j=j: nc.vector.scalar_tensor_tensor(out=acc[rb][:], in0=raw[rb][:, j:NG + j], scalar=cw[:, ch, j:j + 1],
                                                                                       in1=acc[rb][:], op0=ALU.mult, op1=ALU.add),
                         reads=[f"raw{rb}", "cw", f"acc{rb}"], writes=[f"acc{rb}"])
                P.op("act", lambda rb=rb, ch=ch: nc.scalar.activation(out=csl[:, ch, :], in_=acc[rb][:], func=AF.Silu), reads=[f"acc{rb}"], writes=[f"csl{ch}"])

        def pre(tt):
            nb, q = tt // R, tt % R
            xT, xk = xTb[nb % 2], f"xT{nb % 2}"
            pbuf = tt % 2
            QKV, kqT, gam, gl, nbk, A0, qkTp = QKVs[pbuf], kqTs[pbuf], gams[pbuf], gls[pbuf], nbks[pbuf], A0s[pbuf], qkTps[pbuf]
            tcols = slice(tt * 128, (tt + 1) * 128)
            for grp in range(3):
                for cc in range(4):
                    ch = grp * 4 + cc
                    P.op("pe", lambda cc=cc, ch=ch, q=q: nc.tensor.transpose(B[1][:, cc * 128:(cc + 1) * 128], csl[:, ch, q * 128:(q + 1) * 128], ident[:]),
                         reads=[f"csl{ch}", "ident"], writes=[bk(1)])
                if grp % 2 == 0:
                    P.op("act", lambda grp=grp: nc.scalar.copy(QKV[:, grp * 512:(grp + 1) * 512], B[1][:, :]), reads=[bk(1)], writes=[f"QKV{pbuf}_{grp}"])
                else:
                    P.op("dve", lambda grp=grp: nc.vector.tensor_copy(QKV[:, grp * 512:(grp + 1) * 512], B[1][:, :]), reads=[bk(1)], writes=[f"QKV{pbuf}_{grp}"])
            yield
            P.op("pool", lambda: nc.gpsimd.tensor_tensor(out=sqt[:], in0=QKV[:, 0:1024], in1=QKV[:, 0:1024], op=ALU.mult), reads=[f"QKV{pbuf}_0", f"QKV{pbuf}_1"], writes=["sqt"])
            P.op("dve", lambda: nc.vector.tensor_reduce(out=ss[:], in_=sqt[:].rearrange("p (h d) -> p h d", d=64), axis=AX.X, op=ALU.add), reads=["sqt"], writes=["ss"])
            P.op("act", lambda: nc.scalar.activation(out=ss[:], in_=ss[:], func=AF.Sqrt, bias=RMS_EPS, scale=1.0), reads=["ss"], writes=["ss"])
            P.op("dve", lambda: nc.vector.reciprocal(out=ss[:], in_=ss[:]), reads=["ss"], writes=["ss"])
            P.op("dve", lambda: nc.vector.tensor_scalar(out=ss[:, 0:8], in0=ss[:, 0:8], scalar1=0.125, scalar2=None, op0=ALU.mult), reads=["ss"], writes=["ss"])
            P.op("dve", lambda: nc.vector.tensor_tensor(out=QKV[:, 0:1024].rearrange("p (h d) -> p h d", d=64), in0=QKV[:, 0:1024].rearrange("p (h d) -> p h d", d=64),
                                                        in1=bc_h(ss[:, 0:16], 16, 64), op=ALU.mult), reads=[f"QKV{pbuf}_0", f"QKV{pbuf}_1", "ss"], writes=[f"QKV{pbuf}_0", f"QKV{pbuf}_1"])
            for (src0, slot) in ((512, 0), (0, 1)):
                for g4 in range(2):
                    for hl in range(4):
                        h = g4 * 4 + hl
                        P.op("pe", lambda hl=hl, h=h, src0=src0: nc.tensor.transpose(B[1][0:64, hl * 128:(hl + 1) * 128], QKV[:, src0 + h * 64: src0 + (h + 1) * 64], ident[:]),
                             reads=[f"QKV{pbuf}_0", f"QKV{pbuf}_1", "ident"], writes=[bk(1)])
                    if g4 == 0:
                        P.op("act", lambda slot=slot, g4=g4: nc.scalar.copy(kqT[:, g4 * 4:(g4 + 1) * 4, slot, :], B[1][0:64, :].rearrange("p (a b) -> p a b", a=4)), reads=[bk(1)], writes=[f"kqT{pbuf}"])
                    else:
                        P.op("dve", lambda slot=slot, g4=g4: nc.vector.tensor_copy(kqT[:, g4 * 4:(g4 + 1) * 4, slot, :], B[1][0:64, :].rearrange("p (a b) -> p a b", a=4)), reads=[bk(1)], writes=[f"kqT{pbuf}"])
            yield
            for kc in range(8):
                P.op("pe", lambda kc=kc: nc.tensor.matmul(B[2][:, 0:16], lhsT=xT[:, kc, q * 128:(q + 1) * 128], rhs=wab[:, kc, :], start=(kc == 0), stop=(kc == 7)),
                     reads=[xk, f"wab{kc}"], writes=[bk(2)])
            P.op("dve", lambda: nc.vector.tensor_copy(gab[:], B[2][:, 0:16]), reads=[bk(2)], writes=["gab"])
            P.op("act", lambda: nc.scalar.activation(out=beta[:], in_=gab[:, 8:16], func=AF.Exp, scale=-1.0), reads=["gab"], writes=["beta"])
            P.op("dve", lambda: nc.vector.tensor_scalar(out=beta[:], in0=beta[:], scalar1=1.0, scalar2=None, op0=ALU.add), reads=["beta"], writes=["beta"])
            P.op("dve", lambda: nc.vector.reciprocal(out=beta[:], in_=beta[:]), reads=["beta"], writes=["beta"])
            P.op("dve", lambda: nc.vector.tensor_scalar(out=nbeta[:], in0=beta[:], scalar1=-1.0, scalar2=None, op0=ALU.mult), reads=["beta"], writes=["nbeta"])
            P.op("dve", lambda: nc.vector.tensor_tensor(out=g[:], in0=gab[:, 0:8], in1=dtb[:], op=ALU.add), reads=["gab", "dtb"], writes=["g"])
            P.op("act", lambda: nc.scalar.activation(out=g[:], in_=g[:], func=AF.Exp), reads=["g"], writes=["g"])
            P.op("act", lambda: nc.scalar.activation(out=g[:], in_=g[:], func=AF.Ln, bias=1.0, scale=1.0), reads=["g"], writes=["g"])
            P.op("dve", lambda: nc.vector.tensor_tensor(out=g[:], in0=g[:], in1=nea[:], op=ALU.mult), reads=["g", "nea"], writes=["g"])
            P.op("pe", lambda: nc.tensor.matmul(B[2][:, 16:24], lhsT=triinc[:], rhs=g[:], start=True, stop=True), reads=["triinc", "g"], writes=[bk(2)])
            P.op("pe", lambda: nc.tensor.matmul(B[2][:, 24:32], lhsT=onesf[:], rhs=g[:], start=True, stop=True), reads=["ones", "g"], writes=[bk(2)])
            P.op("dve", lambda: nc.vector.tensor_copy(gc[:], B[2][:, 16:24]), reads=[bk(2)], writes=["gc"])
            P.op("dve", lambda: nc.vector.tensor_tensor(out=kap[:], in0=B[2][:, 24:32], in1=gc[:], op=ALU.subtract), reads=[bk(2), "gc"], writes=["kap"])
            P.op("act", lambda: nc.scalar.activation(out=gl[:], in_=B[2][:, 24:32], func=AF.Exp), reads=[bk(2)], writes=[f"gl{pbuf}"])
            P.op("act", lambda: nc.scalar.activation(out=kap[:], in_=kap[:], func=AF.Exp), reads=["kap"], writes=["kap"])
            P.op("act", lambda: nc.scalar.activation(out=gam[:], in_=gc[:], func=AF.Exp), reads=["gc"], writes=[f"gam{pbuf}"])
            P.op("dve", lambda: nc.vector.tensor_tensor(out=nbk[:], in0=nbeta[:], in1=kap[:], op=ALU.mult), reads=["nbeta", "kap"], writes=[f"nbk{pbuf}"])
            P.op("pool", lambda: nc.gpsimd.tensor_tensor(out=G1[:], in0=bc_m(triinc[:], 8, 128), in1=bc_h(g[:], 8, 128), op=ALU.mult), reads=["triinc", "g"], writes=["G1"])
            for half in range(2):
                P.op("pe", lambda half=half: nc.tensor.matmul(B[3 + half][:, :], lhsT=onesf[:], rhs=G1[:, half * 4:(half + 1) * 4, :].rearrange("p a b -> p (a b)"),
                                                             start=True, stop=True), reads=["ones", "G1"], writes=[bk(3 + half)])
                P.op("dve", lambda half=half: nc.vector.tensor_tensor(out=dm[:, half * 4:(half + 1) * 4, :], in0=B[3 + half][:, :].rearrange("p (a b) -> p a b", a=4),
                                                                     in1=bc_h(gc[:, half * 4:(half + 1) * 4], 4, 128), op=ALU.subtract),
                     reads=[bk(3 + half), "gc"], writes=["dm"])
            P.op("pool", lambda: nc.gpsimd.tensor_scalar(out=dm[:], in0=dm[:], scalar1=0.0, scalar2=None, op0=ALU.min), reads=["dm"], writes=["dm"])
            P.op("act", lambda: nc.scalar.activation(out=dm[:], in_=dm[:], func=AF.Exp), reads=["dm"], writes=["dm"])
            P.op("pool", lambda: nc.gpsimd.tensor_tensor(out=dTs[:], in0=dm[:], in1=bc_m(ups[:], 8, 128), op=ALU.mult), reads=["dm", "upstrict"], writes=["dTs"])
            P.op("pool", lambda: nc.gpsimd.tensor_tensor(out=dTi[:], in0=dm[:], in1=bc_m(upi[:], 8, 128), op=ALU.mult), reads=["dm", "upincl"], writes=["dTi"])
            yield
            for j2 in range(4):
                for hh in range(2):
                    h = j2 * 2 + hh
                    P.op("pe", lambda h=h, hh=hh: nc.tensor.matmul(B[7][:, hh * 256:(hh + 1) * 256], lhsT=kqT[:, h, 0, :],
                                                                 rhs=kqT[:, h, :, :].rearrange("p a b -> p (a b)"), start=True, stop=True),
                         reads=[f"kqT{pbuf}"], writes=[bk(7)])
                for hh in range(2):
                    h = j2 * 2 + hh
                    P.op("dve", lambda hh=hh, h=h: nc.vector.scalar_tensor_tensor(out=A0[:, h, :], in0=B[7][:, hh * 256: hh * 256 + 128], scalar=beta[:, h:h + 1],
                                                                                in1=dTs[:, h, :], op0=ALU.mult, op1=ALU.mult),
                         reads=[bk(7), "beta", "dTs"], writes=[f"A0{pbuf}"])
                    P.op("dve", lambda hh=hh, h=h: nc.vector.scalar_tensor_tensor(out=qkTp[:, h, :], in0=B[7][:, hh * 256 + 128: hh * 256 + 256], scalar=nbeta[:, h:h + 1],
                                                                                in1=dTi[:, h, :], op0=ALU.mult, op1=ALU.mult),
                         reads=[bk(7), "nbeta", "dTi"], writes=[f"qkTp{pbuf}"])
            yield

        def post(tt):
            nb, q = tt // R, tt % R
            xT, xk = xTb[nb % 2], f"xT{nb % 2}"
            pbuf = tt % 2
            QKV, kqT, gam, gl, nbk, A0, qkTp = QKVs[pbuf], kqTs[pbuf], gams[pbuf], gls[pbuf], nbks[pbuf], A0s[pbuf], qkTps[pbuf]
            tcols = slice(tt * 128, (tt + 1) * 128)
            NF = 4
            GB = ((5, 6, 7), (3, 4, 0))
            for g4 in range(2):
                hs = slice(g4 * 4, (g4 + 1) * 4)
                bB = GB[g4][1]
                for hl in range(4):
                    P.op("pe", lambda hl=hl, g4=g4, bB=bB: nc.tensor.matmul(B[bB][:, hl * 128:(hl + 1) * 128], lhsT=A0[:, g4 * 4 + hl, :], rhs=ident[:], start=True, stop=True),
                         reads=[f"A0{pbuf}", "ident"], writes=[bk(bB)])
                P.op("dve", lambda g4=g4, bB=bB: nc.vector.tensor_copy(Bmf[g4][0][:].rearrange("p a b -> p (a b)"), B[bB][:, :]), reads=[bk(bB)], writes=[f"Bmf{g4}0"])
                P.op("pool", lambda hs=hs, g4=g4: nc.gpsimd.tensor_tensor(out=Yf[g4][:], in0=bc_m(ident[:], 4, 128), in1=A0[:, hs, :], op=ALU.subtract),
                     reads=["ident", f"A0{pbuf}"], writes=[f"Yf{g4}"])
            for k in range(1, 7):
                pv, cu = (k - 1) % 2, k % 2
                f32lvl = k <= NF
                for g4 in range(2):
                    hs = slice(g4 * 4, (g4 + 1) * 4)
                    bA, bB, bY = GB[g4]
                    if k == 1:
                        Aprev = lambda hl, g4=g4: A0[:, g4 * 4 + hl, :]
                        akey = f"A0{pbuf}"
                    elif f32lvl:
                        Aprev = lambda hl, g4=g4, pv=pv: Amf[g4][pv][:, hl, :]
                        akey = f"Amf{g4}{pv}"
                    else:
                        Aprev = lambda hl, g4=g4, pv=pv: Am[g4][pv][:, hl, :]
                        akey = f"Am{g4}{pv}"
                    Bprev = (lambda hl, g4=g4, pv=pv: Bmf[g4][pv][:, hl, :]) if f32lvl else (lambda hl, g4=g4, pv=pv: Bm[g4][pv][:, hl, :])
                    bkey = f"Bmf{g4}{pv}" if f32lvl else f"Bm{g4}{pv}"
                    if k <= 5:
                        for hl in range(4):
                            P.op("pe", lambda hl=hl, Aprev=Aprev, Bprev=Bprev, bA=bA: nc.tensor.matmul(B[bA][:, hl * 128:(hl + 1) * 128], lhsT=Bprev(hl), rhs=Aprev(hl), start=True, stop=True),
                                 reads=[bkey, akey], writes=[bk(bA)])
                    if k <= NF:
                        P.op("act", lambda cu=cu, g4=g4, bA=bA: nc.scalar.copy(Amf[g4][cu][:].rearrange("p a b -> p (a b)"), B[bA][:, :]), reads=[bk(bA)], writes=[f"Amf{g4}{cu}"])
                    if NF <= k <= 5:
                        P.op("dve", lambda cu=cu, g4=g4, bA=bA: nc.vector.tensor_copy(Am[g4][cu][:].rearrange("p a b -> p (a b)"), B[bA][:, :]), reads=[bk(bA)], writes=[f"Am{g4}{cu}"])
                    for hl in range(4):
                        P.op("pe", lambda hl=hl, Aprev=Aprev, Bprev=Bprev, bB=bB: nc.tensor.matmul(B[bB][:, hl * 128:(hl + 1) * 128], lhsT=Aprev(hl), rhs=Bprev(hl), start=True, stop=True),
                             reads=[bkey, akey], writes=[bk(bB)])
                    if k <= NF:
                        P.op("dve", lambda cu=cu, g4=g4, bB=bB: nc.vector.tensor_copy(Bmf[g4][cu][:].rearrange("p a b -> p (a b)"), B[bB][:, :]), reads=[bk(bB)], writes=[f"Bmf{g4}{cu}"])
                    if k >= NF:
                        P.op("act", lambda cu=cu, g4=g4, bB=bB: nc.scalar.copy(Bm[g4][cu][:].rearrange("p a b -> p (a b)"), B[bB][:, :]), reads=[bk(bB)], writes=[f"Bm{g4}{cu}"])
                    for hl in range(4):
                        if f32lvl:
                            P.op("pe", lambda hl=hl, cu=cu, g4=g4, bY=bY: nc.tensor.matmul(B[bY][:, hl * 128:(hl + 1) * 128], lhsT=Bmf[g4][cu][:, hl, :], rhs=Yf[g4][:, hl, :], start=True, stop=True),
                                 reads=[f"Bmf{g4}{cu}", f"Yf{g4}"], writes=[bk(bY)])
                        else:
                            P.op("pe", lambda hl=hl, cu=cu, g4=g4, bY=bY: nc.tensor.matmul(B[bY][:, hl * 128:(hl + 1) * 128], lhsT=Bm[g4][cu][:, hl, :], rhs=Yb[g4][:, hl, :], start=True, stop=True),
                                 reads=[f"Bm{g4}{cu}", f"Yb{g4}"], writes=[bk(bY)])
                    if k < 6:
                        P.op("dve", lambda g4=g4, bY=bY: nc.vector.tensor_tensor(out=Yf[g4][:].rearrange("p a b -> p (a b)"), in0=B[bY][:, :], in1=Yf[g4][:].rearrange("p a b -> p (a b)"), op=ALU.add),
                             reads=[bk(bY), f"Yf{g4}"], writes=[f"Yf{g4}"])
                        if k >= NF:
                            P.op("act", lambda g4=g4: nc.scalar.copy(Yb[g4][:].rearrange("p a b -> p (a b)"), Yf[g4][:].rearrange("p a b -> p (a b)")), reads=[f"Yf{g4}"], writes=[f"Yb{g4}"])
                    else:
                        P.op("dve", lambda g4=g4, bY=bY, hs=hs: nc.vector.tensor_tensor(out=Xp[:, hs, :].rearrange("p a b -> p (a b)"), in0=B[bY][:, :], in1=Yf[g4][:].rearrange("p a b -> p (a b)"), op=ALU.add),
                             reads=[bk(bY), f"Yf{g4}"], writes=["Xp"])
                yield
            for h in range(8):
                j2, pb = h // 2, (h % 2) * 64
                bnk = 5 + h // 4
                col = (h % 4) * 128
                for slot in range(2):
                    P.op("pe", lambda bnk=bnk, col=col, slot=slot, h=h: nc.tensor.matmul(B[bnk][:, col + slot * 64: col + slot * 64 + 64], lhsT=kqT[:, h, slot, :],
                                                                                       rhs=Sb[:, h, :], start=True, stop=True),
                         reads=[f"kqT{pbuf}", "Sb"], writes=[bk(bnk)])
            for h in range(8):
                bnk, col = 5 + h // 4, (h % 4) * 128
                P.op("dve", lambda h=h, bnk=bnk, col=col: nc.vector.scalar_tensor_tensor(out=rp[:, h, :], in0=B[bnk][:, col:col + 64], scalar=gam[:, h:h + 1],
                                                                                       in1=QKV[:, 1024 + h * 64: 1024 + (h + 1) * 64], op0=ALU.mult, op1=ALU.subtract),
                     reads=[bk(bnk), f"gam{pbuf}", f"QKV{pbuf}_2"], writes=["rp"])
                P.op("act", lambda h=h, bnk=bnk, col=col: nc.scalar.activation(out=o1[:, h, :], in_=B[bnk][:, col + 64:col + 128], func=AF.Identity, scale=gam[:, h:h + 1]),
                     reads=[bk(bnk), f"gam{pbuf}"], writes=["o1"])
            for h in range(8):
                P.op("pe", lambda h=h: nc.tensor.matmul(B[4][:, h * 64:(h + 1) * 64], lhsT=Xp[:, h, :], rhs=rp[:, h, :], start=True, stop=True),
                     reads=["Xp", "rp"], writes=[bk(4)])
            P.op("act", lambda: nc.scalar.copy(vt[:], B[4][:, :]), reads=[bk(4)], writes=["vt"])
            P.op("pool", lambda: nc.gpsimd.tensor_tensor(out=kd[:].rearrange("p (h d) -> p h d", d=64), in0=QKV[:, 512:1024].rearrange("p (h d) -> p h d", d=64),
                                                        in1=bc_h(nbk[:], 8, 64), op=ALU.mult), reads=[f"QKV{pbuf}_1", f"nbk{pbuf}"], writes=["kd"])
            for h in range(8):
                P.op("pe", lambda h=h: nc.tensor.matmul(B[3][:, h * 64:(h + 1) * 64], lhsT=qkTp[:, h, :], rhs=vt[:, h * 64:(h + 1) * 64], start=True, stop=True),
                     reads=[f"qkTp{pbuf}", "vt"], writes=[bk(3)])
            for h in range(8):
                j2 = h // 2
                P.op("pe", lambda h=h: nc.tensor.matmul(B[2][0:64, h * 64:(h + 1) * 64], lhsT=kd[:, h * 64:(h + 1) * 64], rhs=vt[:, h * 64:(h + 1) * 64], start=True, stop=True),
                     reads=["kd", "vt"], writes=[bk(2)])
            P.op("dve", lambda: nc.vector.tensor_tensor(out=of[:].rearrange("p a b -> p (a b)"), in0=B[3][:, :], in1=o1[:].rearrange("p a b -> p (a b)"), op=ALU.add),
                 reads=[bk(3), "o1"], writes=["of"])
            P.op("pool", lambda: nc.gpsimd.tensor_tensor(out=St[:], in0=St[:], in1=gl[0:64, :].unsqueeze(2).to_broadcast([64, 8, 64]), op=ALU.mult), reads=["St", f"gl{pbuf}"], writes=["St"])
            P.op("dve", lambda: nc.vector.tensor_tensor(out=St[:].rearrange("p a b -> p (a b)"), in0=St[:].rearrange("p a b -> p (a b)"), in1=B[2][0:64, :], op=ALU.add),
                 reads=["St", bk(2)], writes=["St"])
            P.op("act", lambda: nc.scalar.copy(Sb[:].rearrange("p a b -> p (a b)"), St[:].rearrange("p a b -> p (a b)")), reads=["St"], writes=["Sb"])
            yield
            for kc in range(8):
                P.op("pe", lambda kc=kc: nc.tensor.matmul(B[0][:, :], lhsT=xT[:, kc, q * 128:(q + 1) * 128], rhs=wz[:, kc, :], start=(kc == 0), stop=(kc == 7)),
                     reads=[xk, f"wz{kc}"], writes=[bk(0)])
            P.op("act", lambda: nc.scalar.activation(out=zs[:], in_=B[0][:, :], func=AF.Silu), reads=[bk(0)], writes=["zs"])
            P.op("pool", lambda: nc.gpsimd.tensor_tensor(out=sqt2[:, 0:512], in0=of[:].rearrange("p a b -> p (a b)"), in1=of[:].rearrange("p a b -> p (a b)"), op=ALU.mult),
                 reads=["of"], writes=["sqt2"])
            P.op("dve", lambda: nc.vector.tensor_reduce(out=ss2[:, 0:8], in_=sqt2[:, 0:512].rearrange("p (h d) -> p h d", d=64), axis=AX.X, op=ALU.add), reads=["sqt2"], writes=["ss2"])
            P.op("act", lambda: nc.scalar.activation(out=ss2[:, 0:8], in_=ss2[:, 0:8], func=AF.Sqrt, bias=RMS_EPS, scale=1.0 / 64), reads=["ss2"], writes=["ss2"])
            P.op("dve", lambda: nc.vector.reciprocal(out=ss2[:, 0:8], in_=ss2[:, 0:8]), reads=["ss2"], writes=["ss2"])
            P.op("dve", lambda: nc.vector.tensor_tensor(out=og[:].rearrange("p (h d) -> p h d", d=64), in0=of[:], in1=bc_h(ss2[:, 0:8], 8, 64), op=ALU.mult),
                 reads=["of", "ss2"], writes=["og"])
            P.op("pool", lambda: nc.gpsimd.tensor_tensor(out=og[:].rearrange("p (h d) -> p h d", d=64), in0=og[:].rearrange("p (h d) -> p h d", d=64), in1=bc_m(nw[:], 8, 64), op=ALU.mult),
                 reads=["og", "nw"], writes=["og"])
            P.op("pool", lambda: nc.gpsimd.tensor_tensor(out=og[:], in0=og[:], in1=zs[:], op=ALU.mult), reads=["og", "zs"], writes=["og"])
            for j2 in range(4):
                P.op("pe", lambda j2=j2: nc.tensor.transpose(B[1][:, j2 * 128:(j2 + 1) * 128], og[:, j2 * 128:(j2 + 1) * 128], ident[:]), reads=["og", "ident"], writes=[bk(1)])
            P.op("act", lambda: nc.scalar.copy(oTs[:].rearrange("p a b -> p (a b)"), B[1][:, :]), reads=[bk(1)], writes=["oTs"])
            P.dma("sp", c.mixT[0:512, tcols].rearrange("(a p) t -> p a t", p=128), oTs[:], reads=["oTs"], writes=["mixT"], chan="d_oTs")
            yield

        NTT = S // 128
        conv(0)
        for _ in pre(0):
            pass
        for tt in range(NTT):
            gens = [post(tt)]
            if tt + 1 < NTT:
                if (tt + 1) % R == 0:
                    conv((tt + 1) // R)
                gens.append(pre(tt + 1))
            while gens:
                for g_ in list(gens):
                    try:
                        next(g_)
                    except StopIteration:
                        gens.remove(g_)

    P.barrier()


def phase_outproj(c, l, src, dst):
    nc, P, S = c.nc, c.P, c.S
    with ExitStack() as pes:
        sb, ps = _tiles(nc, pes)
        wo = sb("wo", [128, 8, D], BF16)
        gb = sb("gb", [128, 2, D], F32)
        xs = [sb(f"x{i}", [128, D], F32) for i in range(2)]
        ms = [sb(f"m{i}", [128, 8, 128], BF16) for i in range(2)]
        ys = [sb(f"y{i}", [128, D], F32) for i in range(2)]
        os_ = [sb(f"o{i}", [128, D], F32) for i in range(2)]
        sts = [sb(f"st{i}", [128, 2, 6], F32) for i in range(2)]
        mvs = [sb(f"mv{i}", [128, 2], F32) for i in range(2)]
        rstds = [sb(f"rstd{i}", [128, 1], F32) for i in range(2)]
        po = [ps(f"po{i}", [128, 512], F32) for i in range(4)]
        load_w(c, wo, c.mix_w_o[l], "wo", 8, 0, D)
        P.dma("sp", gb[:, 0, :], c.ln_g[l, 1, :].partition_broadcast(128), writes=["gb0"])
        P.dma("sp", gb[:, 1, :], c.ln_b[l, 1, :].partition_broadcast(128), writes=["gb1"])
        def load(tt):
            b = tt % 2
            tc_ = slice(tt * 128, (tt + 1) * 128)
            P.dma("sp", xs[b][:], src[tc_, :], writes=[f"x{b}"])
            P.dma("sp", ms[b][:], c.mixT[:, tc_].rearrange("(k p) t -> p k t", p=128), writes=[f"m{b}"])

        NTT = S // 128
        load(0)
        if NTT > 1:
            load(1)
        for tt in range(NTT):
            b = tt % 2
            tc_ = slice(tt * 128, (tt + 1) * 128)
            y, o = ys[b], os_[b]
            for dh in range(2):
                pb_ = 2 * b + dh
                for kc in range(8):
                    P.op("pe", lambda b=b, dh=dh, kc=kc, pb_=pb_: nc.tensor.matmul(po[pb_][:, :], lhsT=ms[b][:, kc, :], rhs=wo[:, kc, dh * 512:(dh + 1) * 512],
                                                                                   start=(kc == 0), stop=(kc == 7)), reads=[f"m{b}", f"wo{kc}"], writes=[f"po{pb_}"])
            P.op("act", lambda b=b, y=y: nc.scalar.mul(y[:], xs[b][:], ALPHA), reads=[f"x{b}"], writes=[f"y{b}"])
            for dh in range(2):
                pb_ = 2 * b + dh
                P.op("dve", lambda dh=dh, y=y, pb_=pb_: nc.vector.tensor_tensor(out=y[:, dh * 512:(dh + 1) * 512], in0=po[pb_][:, :], in1=y[:, dh * 512:(dh + 1) * 512], op=ALU.add),
                     reads=[f"po{pb_}", f"y{b}"], writes=[f"y{b}"])
            layer_norm_tile(c, y, o, sts[b], mvs[b], rstds[b], gb, sfx=str(b))
            if tt + 2 < NTT:
                load(tt + 2)
            P.dma("sp", dst[tc_, :], o[:], reads=[f"o{b}"], writes=["dst"], chan=f"d_o{b}")
    P.barrier()


def phase_ple(c, l, src, dst):
    nc, P, S = c.nc, c.P, c.S
    with ExitStack() as pes:
        sb, ps = _tiles(nc, pes)
        wg = sb("wg", [128, 8, D], BF16)
        wp = sb("wp", [128, 2, D], BF16)
        ident = sb("ident", [128, 128], F32)
        xs = [sb(f"x{i}", [128, D], F32) for i in range(2)]
        pls = [sb(f"pl{i}", [128, PLE], F32) for i in range(2)]
        hTs = [sb(f"hT{i}", [128, 10, 128], BF16) for i in range(2)]
        sgs = [sb(f"sg{i}", [128, D], F32) for i in range(2)]
        os_ = [sb(f"o{i}", [128, D], F32) for i in range(2)]
        pt = [ps(f"pt{i}", [128, 512], F32) for i in range(3)]
        pg = [ps(f"pg{i}", [128, 512], F32) for i in range(2)]
        pp = [ps(f"pp{i}", [128, 512], F32) for i in range(2)]
        load_w(c, wg, c.ple_w_gate[l], "wg", 8, 0, D)
        load_w(c, wp, c.ple_w_proj[l], "wp", 2, 0, D)
        P.dma("sp", ident[:], c.consts[:, c.CO["ident"]:c.CO["ident"] + 128], writes=["ident"])
        def load(tt):
            b = tt % 2
            tc_ = slice(tt * 128, (tt + 1) * 128)
            P.dma("sp", xs[b][:], src[tc_, :], writes=[f"x{b}"])
            P.dma("sp", pls[b][:], c.p[l, tc_, :], writes=[f"pl{b}"])

        NTT = S // 128
        load(0)
        if NTT > 1:
            load(1)
        for tt in range(NTT):
            b = tt % 2
            tc_ = slice(tt * 128, (tt + 1) * 128)
            hT, sg, o = hTs[b], sgs[b], os_[b]
            for grp in range(3):
                n = 4 if grp < 2 else 2
                for q in range(n):
                    kc = grp * 4 + q
                    if grp < 2:
                        P.op("pe", lambda grp=grp, q=q, kc=kc, b=b: nc.tensor.transpose(pt[grp][:, q * 128:(q + 1) * 128], xs[b][:, kc * 128:(kc + 1) * 128], ident[:]),
                             reads=[f"x{b}", "ident"], writes=[f"pt{grp}"])
                    else:
                        P.op("pe", lambda grp=grp, q=q, b=b: nc.tensor.transpose(pt[grp][:, q * 128:(q + 1) * 128], pls[b][:, q * 128:(q + 1) * 128], ident[:]),
                             reads=[f"pl{b}", "ident"], writes=[f"pt{grp}"])
                eng = ("act", "dve", "act")[grp]
                if eng == "act":
                    P.op("act", lambda grp=grp, n=n: nc.scalar.copy(hT[:, grp * 4:grp * 4 + n, :].rearrange("p a b -> p (a b)"), pt[grp][:, 0:n * 128]),
                         reads=[f"pt{grp}"], writes=[f"hT{b}_{grp}"])
                else:
                    P.op("dve", lambda grp=grp, n=n: nc.vector.tensor_copy(hT[:, grp * 4:grp * 4 + n, :].rearrange("p a b -> p (a b)"), pt[grp][:, 0:n * 128]),
                         reads=[f"pt{grp}"], writes=[f"hT{b}_{grp}"])
            for dh in range(2):
                for kc in range(8):
                    P.op("pe", lambda dh=dh, kc=kc: nc.tensor.matmul(pg[dh][:, :], lhsT=hT[:, kc, :], rhs=wg[:, kc, dh * 512:(dh + 1) * 512], start=(kc == 0), stop=(kc == 7)),
                         reads=[f"hT{b}_0", f"hT{b}_1", f"wg{kc}"], writes=[f"pg{dh}"])
                for kc in range(2):
                    P.op("pe", lambda dh=dh, kc=kc: nc.tensor.matmul(pp[dh][:, :], lhsT=hT[:, 8 + kc, :], rhs=wp[:, kc, dh * 512:(dh + 1) * 512], start=(kc == 0), stop=(kc == 1)),
                         reads=[f"hT{b}_2", f"wp{kc}"], writes=[f"pp{dh}"])
                P.op("act", lambda dh=dh: nc.scalar.activation(out=sg[:, dh * 512:(dh + 1) * 512], in_=pg[dh][:, :], func=AF.Sigmoid), reads=[f"pg{dh}"], writes=[f"sg{b}_{dh}"])
                P.op("dve", lambda dh=dh: nc.vector.tensor_tensor(out=sg[:, dh * 512:(dh + 1) * 512], in0=pp[dh][:, :], in1=sg[:, dh * 512:(dh + 1) * 512], op=ALU.mult),
                     reads=[f"pp{dh}", f"sg{b}_{dh}"], writes=[f"sg{b}_{dh}"])
            P.op("pool", lambda b=b: nc.gpsimd.tensor_tensor(out=o[:], in0=sg[:], in1=xs[b][:], op=ALU.add), reads=[f"sg{b}_0", f"sg{b}_1", f"x{b}"], writes=[f"o{b}"])
            if tt + 2 < NTT:
                load(tt + 2)
            P.dma("sp", dst[tc_, :], o[:], reads=[f"o{b}"], writes=["dst"], chan=f"d_o{b}")
    P.barrier()


def all_phases(L):
    ph = [phase_rope]
    for l in range(L):
        src = (lambda c: c.x) if l == 0 else (lambda c: c.hb[1])
        ph.append(lambda c, l=l, src=src: phase_ffn(c, l, 0, src(c), c.hb[0], c.h1T))
        ph.append(lambda c, l=l: phase_gdn(c, l))
        ph.append(lambda c, l=l: phase_sb(c, l))
        ph.append(lambda c, l=l: phase_mla(c, l))
        ph.append(lambda c, l=l: phase_outproj(c, l, c.hb[0], c.hb[1]))
        ph.append(lambda c, l=l: phase_ffn(c, l, 1, c.hb[1], c.hb[0]))
        last = l == L - 1
        ph.append(lambda c, l=l, last=last: phase_ple(c, l, c.hb[0], c.out if last else c.hb[1]))
    return ph


_CACHE = {}


def kernel(**inputs):
    x = np.asarray(inputs["x"], np.float32)
    Bsz, S, _ = x.shape
    L = inputs["ffa_w_in"].shape[0]
    key = (S, L)
    if key not in _CACHE:
        _CACHE[key] = build(S, L, all_phases(L))
    nc, consts = _CACHE[key]
    w = prep_weights({k: np.asarray(v) for k, v in inputs.items()}, L)
    p = np.asarray(inputs["p"], np.float32)
    pos = np.asarray(inputs["positions"], np.int32)
    in_maps = []
    for b in range(Bsz):
        in_maps.append({"x": np.ascontiguousarray(x[b]), "p": np.ascontiguousarray(p[:, b]), "pos": np.ascontiguousarray(pos[b:b + 1]),
                        "consts": consts, **w})
    res = run_bass_kernel_spmd(nc, in_maps, core_ids=list(range(Bsz)))
    return np.stack([np.asarray(r["out"], np.float32) for r in res.results], 0)
```

```python
import numpy as np
from contextlib import ExitStack
import concourse.bass as bass
import concourse.mybir as mybir
from concourse.bass_utils import run_bass_kernel_spmd

F32 = mybir.dt.float32
BF16 = mybir.dt.bfloat16
I32 = mybir.dt.int32
AF = mybir.ActivationFunctionType
ALU = mybir.AluOpType
AX = mybir.AxisListType

D = 1024
DFF = 2816
PLE = 256
DEPTH = 2
ALPHA = (2 * DEPTH) ** 0.25
LN_EPS = 1e-5
RMS_EPS = 1e-6
IN_TOTAL = 3248
C_GQ, C_GK, C_GV, C_GZ, C_GA, C_GB = 0, 512, 1024, 1536, 2048, 2056
C_SQ, C_SK, C_SV, C_MQ, C_MKV, C_KR = 2064, 2320, 2576, 2832, 3088, 3216
SEM_ROT = 1 << 30
TWO_PI = 6.283185307179586


class Prog:
    def __init__(self, nc, es):
        self.nc = nc
        self.es = es
        self.engs = {"pe": nc.tensor, "act": nc.scalar, "dve": nc.vector, "pool": nc.gpsimd, "sp": nc.sync}
        self.nsem = 0
        self.esem = {}
        self.ecnt = {}
        self.allsems = []
        for e in ("pe", "act", "dve", "pool"):
            self.esem[e] = self._newsem("e_" + e)
            self.ecnt[e] = 0
        self.dsem = {}
        self.dcnt = {}
        self.dfree = []
        self.bar1 = None
        self.bar2 = None
        self.nbar = 0
        self.last_w = {}
        self.multi_w = {}
        self.readers = {}
        self.known = {e: {} for e in self.engs}
        self.nwaits = 0
        self.ninst = 0
        import os
        self.limit = int(os.environ["OP_LIMIT"]) if "OP_LIMIT" in os.environ else None

    def _newsem(self, name):
        self.nsem += 1
        s = self.es.enter_context(self.nc.semaphore(f"{name}_{self.nsem}"))
        self.allsems.append([s, 0])
        return s

    def _bump(self, sem, val):
        for r in self.allsems:
            if r[0] is sem:
                r[1] = max(r[1], val)
                return

    def _deps(self, reads, writes):
        deps = {}

        def add(t):
            cur = deps.get(id(t[0]))
            if cur is None or cur[1] < t[1]:
                deps[id(t[0])] = (t[0], t[1])

        for k in reads:
            t = self.last_w.get(k)
            if t is not None:
                add(t)
            for t in self.multi_w.get(k, ()):
                add(t)
        for k in writes:
            t = self.last_w.get(k)
            if t is not None:
                add(t)
            for t in self.multi_w.get(k, ()):
                add(t)
            for t in self.readers.get(k, ()):
                add(t)
        return deps

    def group(self, keys, toks):
        for k in keys:
            self.multi_w[k] = list(toks)

    def _wait(self, eng, deps, skip_sem=None):
        e = self.engs[eng]
        kn = self.known[eng]
        for sid, (sem, val) in deps.items():
            if skip_sem is not None and sem is skip_sem:
                continue
            if kn.get(sid, 0) >= val:
                continue
            e.wait_ge(sem, val)
            self.nwaits += 1
            kn[sid] = val

    def _commit(self, reads, writes, tok):
        for k in writes:
            self.last_w[k] = tok
            self.readers[k] = []
            self.multi_w.pop(k, None)
        for k in reads:
            if k in writes:
                continue
            self.readers.setdefault(k, []).append(tok)

    def op(self, eng, fn, reads=(), writes=()):
        if self.limit is not None and self.ninst >= self.limit:
            return None
        pr = [k for k in reads if k in PSUM_KEYS]
        if pr:
            writes = list(writes) + pr
        deps = self._deps(reads, writes)
        self._wait(eng, deps, skip_sem=self.esem["pe"] if eng == "pe" else None)
        inst = fn()
        if self.ecnt[eng] >= SEM_ROT:
            self.esem[eng] = self._newsem("e_" + eng)
            self.ecnt[eng] = 0
        self.ecnt[eng] += 1
        sem = self.esem[eng]
        inst.then_inc(sem, 1)
        tok = (sem, self.ecnt[eng])
        self._bump(sem, self.ecnt[eng])
        self._commit(reads, writes, tok)
        self.ninst += 1
        return tok

    def dma(self, q, out, in_, reads=(), writes=(), chan=None, **kw):
        if self.limit is not None and self.ninst >= self.limit:
            return None
        deps = self._deps(reads, writes)
        self._wait(q, deps)
        if chan is None:
            chan = "d_" + (list(writes) + list(reads))[0]
        if q == "pool":
            chan = "sw_" + chan
            assert chan not in self.dsem
            self.dsem[chan] = self._newsem("sw")
            self.dcnt[chan] = 0
        elif chan not in self.dsem:
            self.dsem[chan] = self.dfree.pop() if self.dfree else self._newsem("d")
            self.dcnt[chan] = 0
        inst = self.engs[q].dma_start(out=out, in_=in_, **kw)
        self.dcnt[chan] += 16
        sem = self.dsem[chan]
        inst.then_inc(sem, 16)
        tok = (sem, self.dcnt[chan])
        self._bump(sem, self.dcnt[chan])
        self._commit(reads, writes, tok)
        self.ninst += 1
        return tok

    def barrier(self):
        for eng in self.engs:
            kn = self.known[eng]
            for sem, val in self.allsems:
                if val > 0 and kn.get(id(sem), 0) < val:
                    self.engs[eng].wait_ge(sem, val)
                    kn[id(sem)] = val
                    self.nwaits += 1
        self.last_w.clear()
        self.multi_w.clear()
        self.readers.clear()
        if not self.dsem:
            return
        if self.bar1 is None:
            self.bar1 = self.es.enter_context(self.nc.semaphore("bar1"))
            self.bar2 = self.es.enter_context(self.nc.semaphore("bar2"))
        self.nbar += 1
        for eng in self.engs:
            self.engs[eng].sem_inc(self.bar1, 1)
        self.engs["pool"].wait_ge(self.bar1, len(self.engs) * self.nbar)
        for chan, sem in self.dsem.items():
            if chan.startswith("sw_"):
                continue
            self.engs["pool"].sem_clear(sem)
            self.dfree.append(sem)
            for r in self.allsems:
                if r[0] is sem:
                    r[1] = 0
            for eng in self.engs:
                self.known[eng].pop(id(sem), None)
        self.engs["pool"].sem_inc(self.bar2, 1)
        for eng in self.engs:
            self.engs[eng].wait_ge(self.bar2, self.nbar)
        self.dsem.clear()
        self.dcnt.clear()


class Ctx:
    pass


_UID = [0]


def _tiles(nc, pes, P=None):
    _UID[0] += 1
    u = _UID[0]
    sb = lambda n, s, d: pes.enter_context(nc.sbuf_tensor(f"{n}_u{u}", s, d))

    def ps(n, s, d):
        PSUM_KEYS.add(n)
        return pes.enter_context(nc.psum_tensor(f"{n}_u{u}", [128, 2048 // mybir.dt.size(d)], d))
    return sb, ps


PSUM_KEYS = set()


def load_w(c, dst, src_rows, key, kchunks, c0, c1):
    n = c1 - c0
    nblk = (n + 1023) // 1024
    while n % nblk:
        nblk += 1
    w = n // nblk
    toks = []
    for i in range(nblk):
        toks.append(c.P.dma("pool", dst[:, 0:kchunks, i * w:(i + 1) * w], src_rows[0:kchunks * 128, c0 + i * w:c0 + (i + 1) * w].rearrange("(k p) f -> p k f", p=128),
                            writes=[f"{key}_blk{i}"], chan=f"{key}_b{i}", max_dma_last_dim=4096))
    c.P.group([f"{key}{kc}" for kc in range(kchunks)], toks)


def consts_phase(c, pes):
    nc, P = c.nc, c.P
    sb, ps = _tiles(nc, pes)
    return


def phase_ffn(c, l, which, src, dst, dstT=None):
    nc, P, S = c.nc, c.P, c.S
    w_in = (c.ffa_w_in if which == 0 else c.ffb_w_in)[l]
    w_out = (c.ffa_w_out if which == 0 else c.ffb_w_out)[l]
    lni = 0 if which == 0 else 2
    KC, FC, NG = D // 128, c.DFF // 128, min(512, S)
    NT = NG // 128
    dff = c.DFF
    with ExitStack() as pes:
        sb, ps = _tiles(nc, pes)
        win = sb("win", [128, KC, 2 * dff], BF16)
        wout = sb("wout", [128, FC, D], BF16)
        ident = sb("ident", [128, 128], F32)
        gb = sb("gb", [128, 2, D], F32)
        x = sb("x", [128, NT, D], F32)
        xT = sb("xT", [128, KC, NG], BF16)
        actT = sb("actT", [128, FC, NG], BF16)
        sg = [sb(f"sg{i}", [128, NG], F32) for i in range(2)]
        y = sb("y", [128, D], F32)
        o = sb("o", [128, D], F32)
        st = sb("st", [128, 2, 6], F32)
        mv = sb("mv", [128, 2], F32)
        rstd = sb("rstd", [128, 1], F32)
        xts = sb("xts", [128, 8, 128], BF16)
        pt = [ps(f"pt{i}", [128, 512], F32) for i in range(2)]
        pg = [ps(f"pg{i}", [128, 512], F32) for i in range(2)]
        pu = [ps(f"pu{i}", [128, 512], F32) for i in range(2)]
        po = [ps(f"po{i}", [128, 512], F32) for i in range(2)]
        P.dma("sp", ident[:], c.consts[:, c.CO["ident"]:c.CO["ident"] + 128], writes=["ident"])
        P.dma("sp", gb[:, 0, :], c.ln_g[l, lni, :].partition_broadcast(128), writes=["gb0"])
        P.dma("sp", gb[:, 1, :], c.ln_b[l, lni, :].partition_broadcast(128), writes=["gb1"])
        load_w(c, win, w_in, "win", KC, 0, 2 * dff)
        load_w(c, wout, w_out, "wout", FC, 0, D)
        pending = []
        for g in range(S // NG):
            t0 = g * NG
            for tt in range(NT):
                P.dma("act", x[:, tt, :], src[t0 + tt * 128: t0 + (tt + 1) * 128, :], writes=[f"x{tt}"])
            for kc in range(KC):
                p_, pk = pt[kc % 2], f"pt{kc % 2}"
                for tt in range(NT):
                    P.op("pe", lambda p_=p_, tt=tt, kc=kc: nc.tensor.transpose(p_[:, tt * 128:(tt + 1) * 128], x[:, tt, kc * 128:(kc + 1) * 128], ident[:]),
                         reads=[f"x{tt}", "ident"], writes=[pk])
                if kc % 2 == 0:
                    P.op("act", lambda p_=p_, kc=kc: nc.scalar.copy(xT[:, kc, :], p_[:, 0:NG]), reads=[pk], writes=[f"xT{kc}"])
                else:
                    P.op("dve", lambda p_=p_, kc=kc: nc.vector.tensor_copy(xT[:, kc, :], p_[:, 0:NG]), reads=[pk], writes=[f"xT{kc}"])
            for j in range(FC):
                b = j % 2
                for kc in range(KC):
                    P.op("pe", lambda b=b, kc=kc, j=j: nc.tensor.matmul(pg[b][:, 0:NG], lhsT=win[:, kc, j * 128:(j + 1) * 128], rhs=xT[:, kc, :],
                                                                       start=(kc == 0), stop=(kc == KC - 1)),
                         reads=[f"win{kc}", f"xT{kc}"], writes=[f"pg{b}"])
                for kc in range(KC):
                    P.op("pe", lambda b=b, kc=kc, j=j: nc.tensor.matmul(pu[b][:, 0:NG], lhsT=win[:, kc, dff + j * 128: dff + (j + 1) * 128], rhs=xT[:, kc, :],
                                                                       start=(kc == 0), stop=(kc == KC - 1)),
                         reads=[f"win{kc}", f"xT{kc}"], writes=[f"pu{b}"])
                P.op("act", lambda b=b: nc.scalar.activation(out=sg[b][:], in_=pg[b][:, 0:NG], func=AF.Silu), reads=[f"pg{b}"], writes=[f"sg{b}"])
                P.op("dve", lambda b=b, j=j: nc.vector.tensor_tensor(out=actT[:, j, :], in0=pu[b][:, 0:NG], in1=sg[b][:], op=ALU.mult),
                     reads=[f"pu{b}", f"sg{b}"], writes=[f"actT{j}"])
            for tt in range(NT):
                for dh in range(2):
                    for j in range(FC):
                        P.op("pe", lambda tt=tt, dh=dh, j=j: nc.tensor.matmul(po[dh][:, :], lhsT=actT[:, j, tt * 128:(tt + 1) * 128],
                                                                             rhs=wout[:, j, dh * 512:(dh + 1) * 512], start=(j == 0), stop=(j == FC - 1)),
                             reads=[f"actT{j}", f"wout{j}"], writes=[f"po{dh}"])
                while pending:
                    pending.pop(0)()
                P.op("act", lambda tt=tt: nc.scalar.mul(y[:], x[:, tt, :], ALPHA), reads=[f"x{tt}"], writes=["y"])
                for dh in range(2):
                    P.op("dve", lambda dh=dh: nc.vector.scalar_tensor_tensor(out=y[:, dh * 512:(dh + 1) * 512], in0=po[dh][:, :], scalar=0.5,
                                                                           in1=y[:, dh * 512:(dh + 1) * 512], op0=ALU.mult, op1=ALU.add),
                         reads=[f"po{dh}", "y"], writes=["y"])
                layer_norm_tile(c, y, o, st, mv, rstd, gb)
                P.dma("sp", dst[t0 + tt * 128: t0 + (tt + 1) * 128, :], o[:], reads=["o"], writes=["dst"], chan="d_o")
                if dstT is not None:
                  def emitT(t0=t0, tt=tt):
                    for half in range(2):
                        for q in range(4):
                            kc = half * 4 + q
                            P.op("pe", lambda half=half, q=q, kc=kc: nc.tensor.transpose(pt[half][:, q * 128:(q + 1) * 128], o[:, kc * 128:(kc + 1) * 128], ident[:]),
                                 reads=["o", "ident"], writes=[f"pt{half}"])
                        if half == 0:
                            P.op("act", lambda: nc.scalar.copy(xts[:, 0:4, :].rearrange("p a b -> p (a b)"), pt[0][:, :]), reads=["pt0"], writes=["xts0"])
                        else:
                            P.op("dve", lambda: nc.vector.tensor_copy(xts[:, 4:8, :].rearrange("p a b -> p (a b)"), pt[1][:, :]), reads=["pt1"], writes=["xts1"])
                    tok = slice(t0 + tt * 128, t0 + (tt + 1) * 128)
                    P.dma("sp", dstT[:, tok].rearrange("(k p) t -> p k t", p=128), xts[:], reads=["xts0", "xts1"], writes=["dstT"], chan="d_xts")
                  pending.append(emitT)
        while pending:
            pending.pop(0)()
    P.barrier()


def layer_norm_tile(c, y, o, st, mv, rstd, gb, sfx=""):
    nc, P = c.nc, c.P
    ky, ko, kst, kmv, krs = "y" + sfx, "o" + sfx, "st" + sfx, "mv" + sfx, "rstd" + sfx
    for dh in range(2):
        P.op("dve", lambda dh=dh: nc.vector.bn_stats(out=st[:, dh, :], in_=y[:, dh * 512:(dh + 1) * 512]), reads=[ky], writes=[kst])
    P.op("dve", lambda: nc.vector.bn_aggr(out=mv[:], in_=st[:].rearrange("p a b -> p (a b)")), reads=[kst], writes=[kmv])
    P.op("act", lambda: nc.scalar.activation(out=rstd[:], in_=mv[:, 1:2], func=AF.Sqrt, bias=LN_EPS, scale=1.0), reads=[kmv], writes=[krs])
    P.op("dve", lambda: nc.vector.reciprocal(out=rstd[:], in_=rstd[:]), reads=[krs], writes=[krs])
    P.op("dve", lambda: nc.vector.tensor_scalar(out=y[:], in0=y[:], scalar1=mv[:, 0:1], scalar2=rstd[:, 0:1], op0=ALU.subtract, op1=ALU.mult),
         reads=[ky, kmv, krs], writes=[ky])
    P.op("pool", lambda: nc.gpsimd.tensor_tensor(out=o[:], in0=y[:], in1=gb[:, 0, :], op=ALU.mult), reads=[ky, "gb0"], writes=[ko])
    P.op("pool", lambda: nc.gpsimd.tensor_tensor(out=o[:], in0=o[:], in1=gb[:, 1, :], op=ALU.add), reads=[ko, "gb1"], writes=[ko])


def phase_transpose(c, src, dstT):
    nc, P, S = c.nc, c.P, c.S
    with ExitStack() as pes:
        sb, ps = _tiles(nc, pes)
        ident = sb("ident", [128, 128], F32)
        xs = [sb(f"x{i}", [128, D], F32) for i in range(2)]
        xts = [sb(f"xt{i}", [128, 8, 128], BF16) for i in range(2)]
        pt = [ps(f"pt{i}", [128, 512], F32) for i in range(4)]
        P.dma("sp", ident[:], c.consts[:, c.CO["ident"]:c.CO["ident"] + 128], writes=["ident"])
        for tt in range(S // 128):
            b = tt % 2
            P.dma("sp", xs[b][:], src[tt * 128:(tt + 1) * 128, :], writes=[f"x{b}"])
            for half in range(2):
                pp, pk = pt[2 * b + half], f"pt{2 * b + half}"
                for q in range(4):
                    kc = half * 4 + q
                    P.op("pe", lambda pp=pp, q=q, kc=kc, b=b: nc.tensor.transpose(pp[:, q * 128:(q + 1) * 128], xs[b][:, kc * 128:(kc + 1) * 128], ident[:]),
                         reads=[f"x{b}", "ident"], writes=[pk])
                eng = "act" if half == 0 else "dve"
                if half == 0:
                    P.op("act", lambda pp=pp, b=b: nc.scalar.copy(xts[b][:, 0:4, :].rearrange("p a b -> p (a b)"), pp[:, :]), reads=[pk], writes=[f"xt{b}h0"])
                else:
                    P.op("dve", lambda pp=pp, b=b: nc.vector.tensor_copy(xts[b][:, 4:8, :].rearrange("p a b -> p (a b)"), pp[:, :]), reads=[pk], writes=[f"xt{b}h1"])
            P.dma("sp", dstT[:, tt * 128:(tt + 1) * 128].rearrange("(k p) t -> p k t", p=128), xts[b][:], reads=[f"xt{b}h0", f"xt{b}h1"],
                  writes=["dstT"], chan=f"d_xt{b}")
    P.barrier()


def phase_rope(c):
    nc, P, S = c.nc, c.P, c.S
    with ExitStack() as pes:
        sb, ps = _tiles(nc, pes)
        pi_ = sb("pi", [32, S], I32)
        ang = sb("ang", [32, S], F32)
        k = sb("k", [32, S], F32)
        r = sb("r", [32, S], F32)
        rc = sb("rc", [32, S], F32)
        m = sb("m", [32, S], F32)
        cs = sb("cs", [32, S], F32)
        sn = sb("sn", [32, S], F32)
        cst = sb("cst", [32, 2], F32)
        MAGIC = 12582912.0
        C1 = 6.28125
        C2 = TWO_PI - C1
        PI_LO = 3.1415925
        P.dma("sp", pi_[:], c.pos[0, :].partition_broadcast(32), writes=["pi"])
        P.dma("sp", cst[:], c.consts[0:32, c.CO["rope"]:c.CO["rope"] + 2], writes=["cst"])
        P.op("dve", lambda: nc.vector.tensor_copy(ang[:], pi_[:]), reads=["pi"], writes=["ang"])
        P.op("dve", lambda: nc.vector.tensor_scalar(out=ang[:], in0=ang[:], scalar1=cst[:, 0:1], scalar2=None, op0=ALU.mult), reads=["ang", "cst"], writes=["ang"])
        P.op("dve", lambda: nc.vector.tensor_scalar(out=k[:], in0=ang[:], scalar1=1.0 / TWO_PI, scalar2=MAGIC, op0=ALU.mult, op1=ALU.add), reads=["ang"], writes=["k"])
        P.op("dve", lambda: nc.vector.tensor_scalar(out=k[:], in0=k[:], scalar1=MAGIC, scalar2=None, op0=ALU.subtract), reads=["k"], writes=["k"])
        P.op("dve", lambda: nc.vector.scalar_tensor_tensor(out=r[:], in0=k[:], scalar=-C1, in1=ang[:], op0=ALU.mult, op1=ALU.add), reads=["k", "ang"], writes=["r"])
        P.op("dve", lambda: nc.vector.scalar_tensor_tensor(out=r[:], in0=k[:], scalar=-C2, in1=r[:], op0=ALU.mult, op1=ALU.add), reads=["k", "r"], writes=["r"])
        P.op("dve", lambda: nc.vector.tensor_scalar(out=rc[:], in0=r[:], scalar1=np.pi / 2, scalar2=None, op0=ALU.add), reads=["r"], writes=["rc"])
        P.op("dve", lambda: nc.vector.tensor_scalar(out=m[:], in0=rc[:], scalar1=np.pi, scalar2=-TWO_PI, op0=ALU.is_gt, op1=ALU.mult), reads=["rc"], writes=["m"])
        P.op("dve", lambda: nc.vector.tensor_tensor(out=rc[:], in0=rc[:], in1=m[:], op=ALU.add), reads=["rc", "m"], writes=["rc"])
        for t_, nm in ((r, "r"), (rc, "rc")):
            P.op("dve", lambda t_=t_: nc.vector.tensor_scalar(out=t_[:], in0=t_[:], scalar1=PI_LO, scalar2=-PI_LO, op0=ALU.min, op1=ALU.max), reads=[nm], writes=[nm])
        SINF = AF.Sin
        P.op("act", lambda: nc.scalar.activation(out=sn[:], in_=r[:], func=SINF), reads=["r"], writes=["sn"])
        P.op("act", lambda: nc.scalar.activation(out=cs[:], in_=rc[:], func=SINF), reads=["rc"], writes=["cs"])
        P.op("dve", lambda: nc.vector.tensor_scalar(out=sn[:], in0=sn[:], scalar1=cst[:, 1:2], scalar2=None, op0=ALU.mult), reads=["sn", "cst"], writes=["sn"])
        P.dma("sp", c.ropeT[0, :, :], cs[:], reads=["cs"], writes=["ropeT0"])
        P.dma("sp", c.ropeT[1, :, :], sn[:], reads=["sn"], writes=["ropeT1"])
    P.barrier()


def phase_mla(c, l):
    nc, P, S = c.nc, c.P, c.S
    NB = max(1, S // 512)
    NG = min(512, S)
    NTT = S // 128
    SCALE = 96 ** -0.5
    H = 4
    w_in = c.mix_w_in[l]
    with ExitStack() as pes:
        sb, ps = _tiles(nc, pes)
        xTb = [sb(f"xT{i}", [128, 8, NG], BF16) for i in range(2)]
        wq = sb("wq", [128, 8, 256], BF16)
        wkv = sb("wkv", [128, 8, 128], BF16)
        wkr = sb("wkr", [128, 8, 192], BF16)
        wuq = sb("wuq", [128, 2, 768], BF16)
        wukk = sb("wukk", [128, 1, 256], BF16)
        wukv = sb("wukv", [128, 1, 256], BF16)
        nwq = sb("nwq", [128, 2], F32)
        nwkv = sb("nwkv", [128, 1], F32)
        ones = sb("ones", [128, 128], BF16)
        onesf = sb("onesf", [128, 128], F32)
        mk = sb("mk", [128, 1, 2048], BF16)
        cs = sb("cs", [96, NG], F32)
        sn = sb("sn", [96, NG], F32)
        cqn = sb("cqn", [128, 2, NG], BF16)
        ckvn = sb("ckvn", [128, NG], BF16)
        krT = sb("krT", [96, NG], BF16)
        qf = [sb(f"qf{h}", [96, S], BF16) for h in range(H)]
        kf = [sb(f"kf{h}", [96, S], BF16) for h in range(H)]
        V = sb("V", [128, NTT, 4, 128], BF16)
        identf = sb("identf", [128, 128], F32)
        oS = [sb(f"oS{i}", [128, NG], F32) for i in range(2)]
        tmp = [sb(f"tmp{i}", [128, NG], F32) for i in range(3)]
        sq = [sb(f"sq{i}", [128, NG], BF16) for i in range(2)]
        rbc = sb("rbc", [128, NG], F32)
        pe_ = [sb(f"pe{i}", [128, NG], BF16) for i in range(5)]
        rec = [sb(f"rec{i}", [64, NG], F32) for i in range(2)]
        oT = [sb(f"oT{i}", [64, NG], BF16) for i in range(2)]
        pA = [ps(f"pA{i}", [128, 512], F32) for i in range(2)]
        pB = ps("pB", [128, 512], F32)
        pS = [ps(f"pS{i}", [128, 512], F32) for i in range(2)]
        pL = ps("pL", [64, 512], F32)
        CO = c.CO
        load_w(c, wq, w_in, "wq", 8, C_MQ, C_MQ + 256)
        load_w(c, wkv, w_in, "wkv", 8, C_MKV, C_MKV + 128)
        load_w(c, wkr, c.w_kr[l], "wkr", 8, 0, 192)
        load_w(c, wuq, c.w_uq_p[l], "wuq", 2, 0, 768)
        load_w(c, wukk, c.w_ukv_k[l], "wukk", 1, 0, 256)
        load_w(c, wukv, c.w_ukv_v[l], "wukv", 1, 0, 256)
        P.dma("sp", nwq[:], c.q_norm_wT[l], writes=["nwq"])
        P.dma("sp", nwkv[:], c.kv_norm_wT[l], writes=["nwkv"])
        P.dma("sp", onesf[:], c.consts[:, CO["ones"]:CO["ones"] + 128], writes=["onesf"])
        P.dma("sp", identf[:], c.consts[:, CO["ident"]:CO["ident"] + 128], writes=["identf"])
        P.op("pool", lambda: nc.gpsimd.memset(V[:].rearrange("p a b c -> p (a b c)"), 1.0), writes=["V"])
        P.op("dve", lambda: nc.vector.tensor_copy(ones[:], onesf[:]), reads=["onesf"], writes=["ones"])
        load_w(c, mk, c.consts, "mk", 1, CO["maskle"], CO["maskle"] + 2048)
        wkeys = lambda n, k: [f"{n}{i}" for i in range(k)]
        for nb in range(NB):
            cols = slice(nb * NG, (nb + 1) * NG)
            xT = xTb[nb % 2]
            xk = f"xT{nb % 2}"
            P.dma("sp", xT[:], c.h1T[:, cols].rearrange("(k p) t -> p k t", p=128), writes=[xk])
            P.dma("sp", cs[64:96, :], c.ropeT[0, :, cols], writes=["cs"])
            P.dma("sp", sn[64:96, :], c.ropeT[1, :, cols], writes=["sn"])
            for j in range(2):
                pa = pA[j]
                for kc in range(8):
                    P.op("pe", lambda pa=pa, kc=kc, j=j: nc.tensor.matmul(pa[:, 0:NG], lhsT=wq[:, kc, j * 128:(j + 1) * 128], rhs=xT[:, kc, :],
                                                                         start=(kc == 0), stop=(kc == 7)), reads=[f"wq{kc}", xk], writes=[f"pA{j}"])
                P.op("act", lambda pa=pa, j=j: nc.scalar.activation(out=sq[j][:], in_=pa[:, 0:NG], func=AF.Square), reads=[f"pA{j}"], writes=[f"sq{j}"])
                P.op("dve", lambda pa=pa, j=j: nc.vector.tensor_scalar(out=tmp[j][:], in0=pa[:, 0:NG], scalar1=nwq[:, j:j + 1], scalar2=None, op0=ALU.mult),
                     reads=[f"pA{j}", "nwq"], writes=[f"tmp{j}"])
            for j in range(2):
                P.op("pe", lambda j=j: nc.tensor.matmul(pB[:, 0:NG], lhsT=ones[:], rhs=sq[j][:], start=(j == 0), stop=(j == 1)),
                     reads=["ones", f"sq{j}"], writes=["pB"])
            P.op("act", lambda: nc.scalar.activation(out=rbc[:], in_=pB[:, 0:NG], func=AF.Sqrt, bias=RMS_EPS, scale=1.0 / 256), reads=["pB"], writes=["rbc"])
            P.op("dve", lambda: nc.vector.reciprocal(out=rbc[:], in_=rbc[:]), reads=["rbc"], writes=["rbc"])
            for j in range(2):
                P.op("dve", lambda j=j: nc.vector.tensor_tensor(out=cqn[:, j, :], in0=tmp[j][:], in1=rbc[:], op=ALU.mult),
                     reads=[f"tmp{j}", "rbc"], writes=["cqn"])
            pa = pA[0]
            for kc in range(8):
                P.op("pe", lambda kc=kc: nc.tensor.matmul(pa[:, 0:NG], lhsT=wkv[:, kc, :], rhs=xT[:, kc, :], start=(kc == 0), stop=(kc == 7)),
                     reads=[f"wkv{kc}", xk], writes=["pA0"])
            P.op("act", lambda: nc.scalar.activation(out=sq[0][:], in_=pa[:, 0:NG], func=AF.Square), reads=["pA0"], writes=["sq0"])
            P.op("dve", lambda: nc.vector.tensor_scalar(out=tmp[0][:], in0=pa[:, 0:NG], scalar1=nwkv[:, 0:1], scalar2=None, op0=ALU.mult),
                 reads=["pA0", "nwkv"], writes=["tmp0"])
            P.op("pe", lambda: nc.tensor.matmul(pB[:, 0:NG], lhsT=ones[:], rhs=sq[0][:], start=True, stop=True), reads=["ones", "sq0"], writes=["pB"])
            P.op("act", lambda: nc.scalar.activation(out=rbc[:], in_=pB[:, 0:NG], func=AF.Sqrt, bias=RMS_EPS, scale=1.0 / 128), reads=["pB"], writes=["rbc"])
            P.op("dve", lambda: nc.vector.reciprocal(out=rbc[:], in_=rbc[:]), reads=["rbc"], writes=["rbc"])
            P.op("dve", lambda: nc.vector.tensor_tensor(out=ckvn[:, :], in0=tmp[0][:], in1=rbc[:], op=ALU.mult), reads=["tmp0", "rbc"], writes=["ckvn"])
            pa = pA[1]
            for kc in range(8):
                P.op("pe", lambda kc=kc: nc.tensor.matmul(pa[0:96, 0:NG], lhsT=wkr[:, kc, 0:96], rhs=xT[:, kc, :], start=(kc == 0), stop=(kc == 7)),
                     reads=[f"wkr{kc}", xk], writes=["pA1"])
            for kc in range(8):
                P.op("pe", lambda kc=kc: nc.tensor.matmul(pB[0:96, 0:NG], lhsT=wkr[:, kc, 96:192], rhs=xT[:, kc, :], start=(kc == 0), stop=(kc == 7)),
                     reads=[f"wkr{kc}", xk], writes=["pB"])
            P.op("dve", lambda: nc.vector.tensor_tensor(out=tmp[1][64:96, :], in0=pa[64:96, 0:NG], in1=cs[64:96, :], op=ALU.mult), reads=["pA1", "cs"], writes=["tmp1"])
            P.op("dve", lambda: nc.vector.tensor_tensor(out=tmp[2][64:96, :], in0=pB[64:96, 0:NG], in1=sn[64:96, :], op=ALU.mult), reads=["pB", "sn"], writes=["tmp2"])
            P.op("pool", lambda: nc.gpsimd.tensor_tensor(out=krT[64:96, :], in0=tmp[1][64:96, :], in1=tmp[2][64:96, :], op=ALU.add), reads=["tmp1", "tmp2"], writes=["krT"])
            for h in range(H):
                pa = pA[h % 2]
                pak = f"pA{h % 2}"
                for j in range(2):
                    P.op("pe", lambda pa=pa, j=j, h=h: nc.tensor.matmul(pa[0:96, 0:NG], lhsT=wuq[:, j, h * 192:h * 192 + 96], rhs=cqn[:, j, :],
                                                                       start=(j == 0), stop=(j == 1)), reads=[f"wuq{j}", "cqn"], writes=[pak])
                for j in range(2):
                    P.op("pe", lambda j=j, h=h: nc.tensor.matmul(pB[0:96, 0:NG], lhsT=wuq[:, j, h * 192 + 96:h * 192 + 192], rhs=cqn[:, j, :],
                                                                start=(j == 0), stop=(j == 1)), reads=[f"wuq{j}", "cqn"], writes=["pB"])
                P.op("act", lambda pa=pa, h=h: nc.scalar.copy(qf[h][0:64, cols], pa[0:64, 0:NG]), reads=[pak], writes=[f"qf{h}"])
                P.op("dve", lambda pa=pa: nc.vector.tensor_tensor(out=tmp[1][64:96, :], in0=pa[64:96, 0:NG], in1=cs[64:96, :], op=ALU.mult), reads=[pak, "cs"], writes=["tmp1"])
                P.op("dve", lambda: nc.vector.tensor_tensor(out=tmp[2][64:96, :], in0=pB[64:96, 0:NG], in1=sn[64:96, :], op=ALU.mult), reads=["pB", "sn"], writes=["tmp2"])
                P.op("pool", lambda h=h: nc.gpsimd.tensor_tensor(out=qf[h][64:96, cols], in0=tmp[1][64:96, :], in1=tmp[2][64:96, :], op=ALU.add),
                     reads=["tmp1", "tmp2"], writes=[f"qf{h}"])
                P.op("pe", lambda pa=pa, h=h: nc.tensor.matmul(pa[0:64, 0:NG], lhsT=wukk[:, 0, h * 64:(h + 1) * 64], rhs=ckvn[:, :], start=True, stop=True),
                     reads=["wukk0", "ckvn"], writes=[pak])
                P.op("act", lambda pa=pa, h=h: nc.scalar.copy(kf[h][0:64, cols], pa[0:64, 0:NG]), reads=[pak], writes=[f"kf{h}"])
                P.op("pool", lambda h=h: nc.gpsimd.tensor_copy(kf[h][64:96, cols], krT[64:96, :]), reads=["krT"], writes=[f"kf{h}"])
            for q in range(NG // 128):
                tt = nb * (NG // 128) + q
                P.op("pe", lambda tt=tt: nc.tensor.matmul(pB[:, 0:256], lhsT=ckvn[:, q * 128:(q + 1) * 128], rhs=wukv[:, 0, :], start=True, stop=True),
                     reads=["ckvn", "wukv0"], writes=["pB"])
                P.op("act", lambda tt=tt: nc.scalar.copy(V[:, tt, :, 0:64], pB[:, 0:256].rearrange("p (h d) -> p h d", h=4)), reads=["pB"], writes=["V"])
        LA = 3
        NPE = 5
        pOb = (pA[0], pA[1])
        pOk = ("pA0", "pA1")
        pLb = (pB, pL)
        pLk = ("pB", "pL")
        for hp in range(2):
            meta = []
            for tq in range(NB):
                nkb = (tq + 1) * (NG // 128)
                for kb in range(nkb):
                    for hh in range(2):
                        meta.append(dict(tq=tq, kb=kb, hh=hh, nkb=nkb, zb=len(meta) % 2, eb=len(meta) % NPE))

            def stageA(m):
                tq, kb, hh, zb, eb = m["tq"], m["kb"], m["hh"], m["zb"], m["eb"]
                h = hp * 2 + hh
                qcols = slice(tq * NG, (tq + 1) * NG)
                P.op("pe", lambda: nc.tensor.matmul(pS[zb][:, 0:NG], lhsT=kf[h][:, kb * 128:(kb + 1) * 128], rhs=qf[h][:, qcols], start=True, stop=True),
                     reads=[f"kf{h}", f"qf{h}"], writes=[f"pS{zb}"])
                P.op("act", lambda: nc.scalar.activation(out=pe_[eb][:], in_=pS[zb][:, 0:NG], func=AF.Exp, scale=SCALE), reads=[f"pS{zb}"], writes=[f"pe{eb}"])
                r = kb - tq * (NG // 128)
                if r >= 0:
                    P.op("pool", lambda: nc.gpsimd.tensor_tensor(out=pe_[eb][:], in0=pe_[eb][:], in1=mk[:, 0, r * 512:r * 512 + NG], op=ALU.mult),
                         reads=[f"pe{eb}", "mk0"], writes=[f"pe{eb}"])

            def stageB(m):
                tq, kb, hh, nkb, eb = m["tq"], m["kb"], m["hh"], m["nkb"], m["eb"]
                h = hp * 2 + hh
                qcols = slice(tq * NG, (tq + 1) * NG)
                P.op("pe", lambda: nc.tensor.matmul(pOb[hh][:, 0:NG], lhsT=V[:, kb, h, :], rhs=pe_[eb][:], start=(kb == 0), stop=(kb == nkb - 1)),
                     reads=["V", f"pe{eb}"], writes=[pOk[hh]])
                if kb == nkb - 1:
                    P.op("act", lambda: nc.scalar.copy(oS[hh][:], pOb[hh][:, 0:NG]), reads=[pOk[hh]], writes=[f"oS{hh}"])
                    P.op("pe", lambda: nc.tensor.matmul(pLb[hh][0:64, 0:NG], lhsT=identf[:, 64:128], rhs=oS[hh][:], start=True, stop=True),
                         reads=["identf", f"oS{hh}"], writes=[pLk[hh]])
                    P.op("dve", lambda: nc.vector.reciprocal(out=rec[hh][:], in_=pLb[hh][0:64, 0:NG]), reads=[pLk[hh]], writes=[f"rec{hh}"])
                    P.op("dve", lambda: nc.vector.tensor_tensor(out=oT[hh][:], in0=oS[hh][0:64, :], in1=rec[hh][:], op=ALU.mult), reads=[f"oS{hh}", f"rec{hh}"], writes=[f"oT{hh}"])
                    P.dma("sp", c.mixT[768 + h * 64: 768 + (h + 1) * 64, qcols], oT[hh][:], reads=[f"oT{hh}"], writes=["mixT"], chan=f"d_oT{hh}")

            n = len(meta)
            for i in range(min(LA, n)):
                stageA(meta[i])
            for i in range(n):
                if i + LA < n:
                    stageA(meta[i + LA])
                stageB(meta[i])
    P.barrier()


def make_consts():
    CO = {}
    cols = []
    off = 0

    def add(name, arr):
        nonlocal off
        a = np.zeros((128, arr.shape[1]), np.float32)
        a[:arr.shape[0]] = arr
        CO[name] = off
        off += arr.shape[1]
        cols.append(a)

    add("ident", np.eye(128, dtype=np.float32))
    add("ones", np.ones((128, 128), np.float32))
    s = np.arange(128)[:, None]
    t = np.arange(512)[None, :]
    add("maskle", np.concatenate([((r * 128 + s) <= t).astype(np.float32) for r in range(4)], 1))
    add("masklt", np.concatenate([((r * 128 + s) < t).astype(np.float32) for r in range(4)], 1))
    j = np.arange(128)[:, None]
    i = np.arange(128)[None, :]
    add("trineg", -(j >= i).astype(np.float32))
    add("compneg", -(j < i).astype(np.float32))
    add("triinc", (j <= i).astype(np.float32))
    add("upincl", (i >= j).astype(np.float32))
    add("upstrict", (i > j).astype(np.float32))
    inv = (1.0 / (10000.0 ** (np.arange(0, 32, 2, dtype=np.float32) / 32))).astype(np.float32)
    rope = np.zeros((32, 2), np.float32)
    rope[:, 0] = np.concatenate([inv, inv])
    rope[:, 1] = np.concatenate([-np.ones(16), np.ones(16)])
    add("rope", rope)
    return np.concatenate(cols, 1), CO


def prep_weights(inp, L):
    f = lambda a: np.ascontiguousarray(a, dtype=np.float32)
    w_in = inp["mix_w_in"]
    out = {}
    z64 = np.zeros(w_in.shape[:2] + (64,), np.float32)
    out["w_kr"] = f(np.concatenate([z64, w_in[:, :, C_KR:C_KR + 32], z64, w_in[:, :, C_KR + 16:C_KR + 32], w_in[:, :, C_KR:C_KR + 16]], -1))
    uq = inp["mla_w_uq"]
    parts = []
    for h in range(4):
        b = h * 96
        parts += [uq[:, :, b:b + 64], uq[:, :, b + 64:b + 96], np.zeros(uq.shape[:2] + (64,), np.float32), uq[:, :, b + 80:b + 96], uq[:, :, b + 64:b + 80]]
    out["w_uq_p"] = f(np.concatenate(parts, -1))
    ukv = inp["mla_w_ukv"]
    out["w_ukv_k"] = f(np.concatenate([ukv[:, :, h * 128:h * 128 + 64] for h in range(4)], -1))
    out["w_ukv_v"] = f(np.concatenate([ukv[:, :, h * 128 + 64:h * 128 + 128] for h in range(4)], -1))
    out["q_norm_wT"] = f(inp["mla_q_norm_w"].reshape(L, 2, 128).transpose(0, 2, 1))
    out["kv_norm_wT"] = f(inp["mla_kv_norm_w"].reshape(L, 1, 128).transpose(0, 2, 1))
    out["conv_wT"] = f(inp["gdn_conv_w"].reshape(L, 4, 12, 128).transpose(0, 3, 2, 1))
    for k in ("ffa_w_in", "ffa_w_out", "mix_w_in", "gdn_a_log", "gdn_dt_bias", "gdn_norm_w", "mix_w_o", "ffb_w_in", "ffb_w_out",
              "ln_g", "ln_b", "ple_w_gate", "ple_w_proj"):
        out[k] = f(inp[k])
    return out


WSHAPES = lambda L, dff: {
    "ffa_w_in": [L, D, 2 * dff], "ffa_w_out": [L, dff, D], "mix_w_in": [L, D, IN_TOTAL], "w_kr": [L, D, 192],
    "w_uq_p": [L, 256, 768], "w_ukv_k": [L, 128, 256], "w_ukv_v": [L, 128, 256], "q_norm_wT": [L, 128, 2], "kv_norm_wT": [L, 128, 1],
    "conv_wT": [L, 128, 12, 4], "gdn_a_log": [L, 8], "gdn_dt_bias": [L, 8], "gdn_norm_w": [L, 64], "mix_w_o": [L, D, D],
    "ffb_w_in": [L, D, 2 * dff], "ffb_w_out": [L, dff, D], "ln_g": [L, 3, D], "ln_b": [L, 3, D], "ple_w_gate": [L, D, D], "ple_w_proj": [L, PLE, D],
}


def build(S, L, phases, dff=DFF, debug=False):
    nc = bass.Bass("TRN2", target_bir_lowering=False)
    c = Ctx()
    c.nc, c.S, c.DFF, c.L = nc, S, dff, L
    consts_np, c.CO = make_consts()
    din = lambda n, s, d=F32: nc.dram_tensor(n, s, d, kind="ExternalInput").ap()
    c.x = din("x", [S, D])
    c.p = din("p", [L, S, PLE])
    c.pos = din("pos", [1, S], I32)
    c.consts = din("consts", list(consts_np.shape))
    for k, shp in WSHAPES(L, dff).items():
        setattr(c, k, din(k, shp))
    kind = "ExternalOutput" if debug else "Internal"
    scr = lambda n, s, d: nc.dram_tensor(n, s, d, kind=kind).ap()
    c.hb = [scr("hb0", [S, D], F32), scr("hb1", [S, D], F32)]
    c.h1T = scr("h1T", [D, S], BF16)
    c.mixT = scr("mixT", [D, S], BF16)
    c.ropeT = scr("ropeT", [2, 32, S], F32)
    c.out = nc.dram_tensor("out", [S, D], F32, kind="ExternalOutput").ap()
    with ExitStack() as es:
        c.P = Prog(nc, es)
        for ph in phases:
            ph(c)
        c.P.barrier()
        print("ninst", c.P.ninst, "nwaits", c.P.nwaits, "nsem", c.P.nsem)
    return nc, consts_np


def phase_sb(c, l):
    nc, P, S = c.nc, c.P, c.S
    NG = min(512, S)
    NB = S // NG
    NTT = S // 128
    R = NG // 128
    H = 4
    w_in = c.mix_w_in[l]
    CO = c.CO
    with ExitStack() as pes:
        sb, ps = _tiles(nc, pes)
        xTb = [sb(f"xT{i}", [128, 8, NG], BF16) for i in range(2)]
        wq = sb("wq", [128, 8, 256], BF16)
        wk = sb("wk", [128, 8, 256], BF16)
        wv = sb("wv", [128, 8, 256], BF16)
        qT = [sb(f"qT{j}", [128, S], BF16) for j in range(2)]
        kT = [sb(f"kT{j}", [128, S], BF16) for j in range(2)]
        V = sb("V", [128, NTT, 256], BF16)
        cf = sb("cf", [128, 256], F32)
        tri = sb("tri", [128, 128], BF16)
        comp = sb("comp", [128, 128], BF16)
        mkf = sb("mkf", [128, 4, 512], F32)
        e = [sb(f"e{i}", [128, NG], F32) for i in range(5)]
        sp = [sb(f"sp{i}", [128, NG], BF16) for i in range(10)]
        eC = [sb(f"eC{i}", [128, NG], F32) for i in range(2)]
        A = [sb(f"A{i}", [128, NG], BF16) for i in range(4)]
        oT = [sb(f"oT{i}", [64, NG], BF16) for i in range(2)]
        pZ = [ps(f"pZ{i}", [128, 512], F32) for i in range(2)]
        pC = [ps(f"pC{i}", [128, 512], F32) for i in range(2)]
        pO = [ps(f"pO{i}", [128, 512], F32) for i in range(2)]
        pX = ps("pX", [128, 512], F32)
        load_w(c, wq, w_in, "wq", 8, C_SQ, C_SQ + 256)
        load_w(c, wk, w_in, "wk", 8, C_SK, C_SK + 256)
        load_w(c, wv, w_in, "wv", 8, C_SV, C_SV + 256)
        P.dma("sp", cf[:, 0:128], c.consts[:, CO["trineg"]:CO["trineg"] + 128], writes=["cf"])
        P.dma("sp", cf[:, 128:256], c.consts[:, CO["compneg"]:CO["compneg"] + 128], writes=["cf"])
        P.op("dve", lambda: nc.vector.tensor_copy(tri[:], cf[:, 0:128]), reads=["cf"], writes=["tri"])
        P.op("dve", lambda: nc.vector.tensor_copy(comp[:], cf[:, 128:256]), reads=["cf"], writes=["comp"])
        P.dma("sp", mkf[:].rearrange("p a b -> p (a b)"), c.consts[:, CO["masklt"]:CO["masklt"] + 2048], writes=["mkf"])
        for nb in range(NB):
            cols = slice(nb * NG, (nb + 1) * NG)
            xT = xTb[nb % 2]
            xk = f"xT{nb % 2}"
            P.dma("sp", xT[:], c.h1T[:, cols].rearrange("(k p) t -> p k t", p=128), writes=[xk])
            for (w_, dst, nm) in ((wq, qT, "qT"), (wk, kT, "kT")):
                for j in range(2):
                    for kc in range(8):
                        P.op("pe", lambda w_=w_, kc=kc, j=j: nc.tensor.matmul(pX[:, 0:NG], lhsT=w_[:, kc, j * 128:(j + 1) * 128], rhs=xT[:, kc, :],
                                                                             start=(kc == 0), stop=(kc == 7)), reads=[xk] + [f"wq{kc}", f"wk{kc}"], writes=["pX"])
                    P.op("act", lambda dst=dst, j=j: nc.scalar.copy(dst[j][:, cols], pX[:, 0:NG]), reads=["pX"], writes=[f"{nm}{j}"])
            for q in range(R):
                tt = nb * R + q
                for kc in range(8):
                    P.op("pe", lambda q=q, kc=kc: nc.tensor.matmul(pX[:, 0:256], lhsT=xT[:, kc, q * 128:(q + 1) * 128], rhs=wv[:, kc, :],
                                                                    start=(kc == 0), stop=(kc == 7)), reads=[xk, f"wv{kc}"], writes=["pX"])
                P.op("dve", lambda tt=tt: nc.vector.tensor_copy(V[:, tt, :], pX[:, 0:256]), reads=["pX"], writes=["V"])
        LA = 3
        NE, NSP = 5, 5
        for hp in range(2):
            items = []
            for tq in range(NB):
                nkb = (tq + 1) * R
                for kb in range(nkb - 1, -1, -1):
                    for hh in range(2):
                        items.append((tq, kb, hh, nkb))
            percnt = [0, 0]
            meta = []
            for idx, (tq, kb, hh, nkb) in enumerate(items):
                itn = percnt[hh]
                percnt[hh] += 1
                meta.append(dict(tq=tq, kb=kb, hh=hh, nkb=nkb, zb=idx % 2, eb=idx % NE, ab=idx % 4, spb=hh * NSP + itn % NSP, spp=hh * NSP + (itn - 1) % NSP))

            def stageA(m):
                tq, kb, hh = m["tq"], m["kb"], m["hh"]
                qcols = slice(tq * NG, (tq + 1) * NG)
                pb, j, zb, eb, spb = hh * 64, hp, m["zb"], m["eb"], m["spb"]
                r = kb - tq * R
                P.op("pe", lambda: nc.tensor.matmul(pZ[zb][:, 0:NG], lhsT=kT[j][pb:pb + 64, kb * 128:(kb + 1) * 128], rhs=qT[j][pb:pb + 64, qcols], start=True, stop=True),
                     reads=[f"kT{j}", f"qT{j}"], writes=[f"pZ{zb}"])
                P.op("act", lambda: nc.scalar.activation(out=e[eb][:], in_=pZ[zb][:, 0:NG], func=AF.Exp, scale=0.125), reads=[f"pZ{zb}"], writes=[f"e{eb}"])
                if r >= 0:
                    P.op("dve", lambda: nc.vector.tensor_tensor(out=e[eb][:], in0=e[eb][:], in1=mkf[:, r, 0:NG], op=ALU.mult), reads=[f"e{eb}", "mkf"], writes=[f"e{eb}"])
                P.op("act", lambda: nc.scalar.activation(out=sp[spb][:], in_=e[eb][:], func=AF.Ln, bias=1.0, scale=1.0), reads=[f"e{eb}"], writes=[f"sp{spb}"])

            def stageB(m):
                tq, kb, hh, nkb = m["tq"], m["kb"], m["hh"], m["nkb"]
                qcols = slice(tq * NG, (tq + 1) * NG)
                h = hp * 2 + hh
                eb, ab, spb, spp = m["eb"], m["ab"], m["spb"], m["spp"]
                first = kb == nkb - 1
                if not first:
                    P.op("pe", lambda: nc.tensor.matmul(pC[hh][:, 0:NG], lhsT=comp[:], rhs=sp[spp][:], start=False, stop=False, skip_group_check=True),
                         reads=["comp", f"sp{spp}"], writes=[f"pC{hh}"])
                P.op("pe", lambda: nc.tensor.matmul(pC[hh][:, 0:NG], lhsT=tri[:], rhs=sp[spb][:], start=first, stop=True, skip_group_check=True),
                     reads=["tri", f"sp{spb}"], writes=[f"pC{hh}"])
                P.op("act", lambda: nc.scalar.activation(out=eC[hh][:], in_=pC[hh][:, 0:NG], func=AF.Exp), reads=[f"pC{hh}"], writes=[f"eC{hh}"])
                P.op("dve", lambda: nc.vector.tensor_tensor(out=A[ab][:], in0=e[eb][:], in1=eC[hh][:], op=ALU.mult), reads=[f"e{eb}", f"eC{hh}"], writes=[f"A{ab}"])

            def stageB2(m):
                tq, kb, hh, nkb = m["tq"], m["kb"], m["hh"], m["nkb"]
                qcols = slice(tq * NG, (tq + 1) * NG)
                h = hp * 2 + hh
                ab = m["ab"]
                first = kb == nkb - 1
                P.op("pe", lambda: nc.tensor.matmul(pO[hh][0:64, 0:NG], lhsT=V[:, kb, h * 64:(h + 1) * 64], rhs=A[ab][:], start=first, stop=(kb == 0)),
                     reads=["V", f"A{ab}"], writes=[f"pO{hh}"])
                if kb == 0:
                    P.op("dve", lambda: nc.vector.tensor_copy(oT[hh][:], pO[hh][0:64, 0:NG]), reads=[f"pO{hh}"], writes=[f"oT{hh}"])
                    P.dma("sp", c.mixT[512 + h * 64: 512 + (h + 1) * 64, qcols], oT[hh][:], reads=[f"oT{hh}"], writes=["mixT"], chan=f"d_oT{hh}")

            n = len(meta)
            for i in range(min(LA, n)):
                stageA(meta[i])
            for i in range(n):
                if i + LA < n:
                    stageA(meta[i + LA])
                stageB(meta[i])
                if i >= 2:
                    stageB2(meta[i - 2])
            for i in range(max(0, n - 2), n):
                stageB2(meta[i])
    P.barrier()


def phase_gdn(c, l):
    nc, P, S = c.nc, c.P, c.S
    NG = min(512, S)
    NB = S // NG
    R = NG // 128
    w_in = c.mix_w_in[l]
    CO = c.CO
    with ExitStack() as pes:
        sb, ps = _tiles(nc, pes)
        xTb = [sb(f"xT{i}", [128, 8, NG], BF16) for i in range(2)]
        wqkv = sb("wqkv", [128, 8, 1536], BF16)
        wz = sb("wz", [128, 8, 512], BF16)
        wab = sb("wab", [128, 8, 16], BF16)
        cw = sb("cw", [128, 12, 4], F32)
        ident = sb("ident", [128, 128], F32)
        identb = sb("identb", [128, 128], BF16)
        onesf = sb("onesf", [128, 128], F32)
        triinc = sb("triinc", [128, 128], F32)
        upi = sb("upi", [128, 128], F32)
        ups = sb("ups", [128, 128], F32)
        dtb = sb("dtb", [128, 8], F32)
        nea = sb("nea", [128, 8], F32)
        nw = sb("nw", [128, 64], F32)
        halo = sb("halo", [128, 12, 3], F32)
        raw = [sb(f"raw{i}", [128, NG + 3], F32) for i in range(2)]
        acc = [sb(f"acc{i}", [128, NG], F32) for i in range(2)]
        csl = sb("csl", [128, 12, NG], F32)
        QKVs = [sb(f"QKV{i}", [128, 1536], F32) for i in range(2)]
        sqt2 = sb("sqt2", [128, 512], F32)
        ss2 = sb("ss2", [128, 8], F32)
        sqt = sb("sqt", [128, 1024], F32)
        ss = sb("ss", [128, 16], F32)
        kqTs = [sb(f"kqT{i}", [64, 8, 2, 128], BF16) for i in range(2)]
        gab = sb("gab", [128, 16], F32)
        beta = sb("beta", [128, 8], F32)
        nbeta = sb("nbeta", [128, 8], F32)
        g = sb("g", [128, 8], F32)
        gc = sb("gc", [128, 8], F32)
        gams = [sb(f"gam{i}", [128, 8], F32) for i in range(2)]
        kap = sb("kap", [128, 8], F32)
        gls = [sb(f"gl{i}", [128, 8], F32) for i in range(2)]
        nbks = [sb(f"nbk{i}", [128, 8], F32) for i in range(2)]
        G1 = sb("G1", [128, 8, 128], F32)
        dm = sb("dm", [128, 8, 128], F32)
        dTs = sb("dTs", [128, 8, 128], F32)
        dTi = sb("dTi", [128, 8, 128], F32)
        CD = F32
        A0s = [sb(f"A0{i}", [128, 8, 128], CD) for i in range(2)]
        qkTps = [sb(f"qkTp{i}", [128, 8, 128], BF16) for i in range(2)]
        Am = [[sb(f"Am{g}{i}", [128, 4, 128], BF16) for i in range(2)] for g in range(2)]
        Bm = [[sb(f"Bm{g}{i}", [128, 4, 128], BF16) for i in range(2)] for g in range(2)]
        Amf = [[sb(f"Amf{g}{i}", [128, 4, 128], F32) for i in range(2)] for g in range(2)]
        Bmf = [[sb(f"Bmf{g}{i}", [128, 4, 128], F32) for i in range(2)] for g in range(2)]
        Yb = [sb(f"Yb{g}", [128, 4, 128], BF16) for g in range(2)]
        Yf = [sb(f"Yf{g}", [128, 4, 128], F32) for g in range(2)]
        Xp = sb("Xp", [128, 8, 128], BF16)
        St = sb("St", [64, 8, 64], F32)
        Sb = sb("Sb", [64, 8, 64], BF16)
        rp = sb("rp", [128, 8, 64], BF16)
        vt = sb("vt", [128, 512], BF16)
        kd = sb("kd", [128, 512], BF16)
        o1 = sb("o1", [128, 8, 64], F32)
        of = sb("of", [128, 8, 64], F32)
        zs = sb("zs", [128, 512], F32)
        og = sb("og", [128, 512], F32)
        oTs = sb("oTs", [128, 4, 128], BF16)
        B = [ps(f"B{i}", [128, 512], F32) for i in range(8)]
        bk = lambda i: f"B{i}"

        load_w(c, wqkv, w_in, "wqkv", 8, C_GQ, C_GQ + 1536)
        load_w(c, wz, w_in, "wz", 8, C_GZ, C_GZ + 512)
        load_w(c, wab, w_in, "wab", 8, C_GA, C_GA + 16)
        P.dma("sp", cw[:], c.conv_wT[l], writes=["cw"])
        for (t_, nm) in ((ident, "ident"), (onesf, "ones"), (triinc, "triinc"), (upi, "upincl"), (ups, "upstrict")):
            P.dma("sp", t_[:], c.consts[:, CO[nm]:CO[nm] + 128], writes=[nm])
        P.op("dve", lambda: nc.vector.tensor_copy(identb[:], ident[:]), reads=["ident"], writes=["identb"])
        P.dma("sp", dtb[:], c.gdn_dt_bias[l, :].partition_broadcast(128), writes=["dtb"])
        P.dma("sp", nea[:], c.gdn_a_log[l, :].partition_broadcast(128), writes=["nea"])
        P.dma("sp", nw[:], c.gdn_norm_w[l, :].partition_broadcast(128), writes=["nw"])
        P.op("act", lambda: nc.scalar.activation(out=nea[:], in_=nea[:], func=AF.Exp), reads=["nea"], writes=["nea"])
        P.op("dve", lambda: nc.vector.tensor_scalar(out=nea[:], in0=nea[:], scalar1=-1.0, scalar2=None, op0=ALU.mult), reads=["nea"], writes=["nea"])
        P.op("pool", lambda: nc.gpsimd.memset(halo[:], 0.0), writes=["halo"])
        P.op("pool", lambda: nc.gpsimd.memset(St[:], 0.0), writes=["St"])
        P.op("pool", lambda: nc.gpsimd.memset(Sb[:], 0.0), writes=["Sb"])

        def bc_h(ap2d, n, d):
            return ap2d.unsqueeze(2).to_broadcast([128, n, d])

        def bc_m(ap2d, n, d):
            return ap2d.unsqueeze(1).to_broadcast([128, n, d])

        def conv(nb):
            cols = slice(nb * NG, (nb + 1) * NG)
            xT = xTb[nb % 2]
            xk = f"xT{nb % 2}"
            P.dma("sp", xT[:], c.h1T[:, cols].rearrange("(k p) t -> p k t", p=128), writes=[xk])
            for ch in range(12):
                rb = ch % 2
                for kc in range(8):
                    P.op("pe", lambda kc=kc, ch=ch: nc.tensor.matmul(B[0][:, 0:NG], lhsT=wqkv[:, kc, ch * 128:(ch + 1) * 128], rhs=xT[:, kc, :],
                                                                    start=(kc == 0), stop=(kc == 7)), reads=[xk, f"wqkv{kc}"], writes=[bk(0)])
                P.op("pool", lambda rb=rb, ch=ch: nc.gpsimd.tensor_copy(raw[rb][:, 0:3], halo[:, ch, :]), reads=["halo"], writes=[f"raw{rb}"])
                P.op("act", lambda rb=rb: nc.scalar.copy(raw[rb][:, 3:NG + 3], B[0][:, 0:NG]), reads=[bk(0)], writes=[f"raw{rb}"])
                P.op("pool", lambda rb=rb, ch=ch: nc.gpsimd.tensor_copy(halo[:, ch, :], raw[rb][:, NG:NG + 3]), reads=[f"raw{rb}"], writes=["halo"])
                P.op("dve", lambda rb=rb, ch=ch: nc.vector.tensor_scalar(out=acc[rb][:], in0=raw[rb][:, 0:NG], scalar1=cw[:, ch, 0:1], scalar2=None, op0=ALU.mult),
                     reads=[f"raw{rb}", "cw"], writes=[f"acc{rb}"])
                for j in range(1, 4):
                    P.op("dve", lambda rb=rb, ch=ch, j=j: nc.vector.scalar_tensor_tensor(out=acc[rb][:], in0=raw[rb][:, j:NG + j], scalar=cw[:, ch, j:j + 1],
                                                                                       in1=acc[rb][:], op0=ALU.mult, op1=ALU.add),
                         reads=[f"raw{rb}", "cw", f"acc{rb}"], writes=[f"acc{rb}"])
                P.op("act", lambda rb=rb, ch=ch: nc.scalar.activation(out=csl[:, ch, :], in_=acc[rb][:], func=AF.Silu), reads=[f"acc{rb}"], writes=[f"csl{ch}"])

        def pre(tt):
            nb, q = tt // R, tt % R
            xT, xk = xTb[nb % 2], f"xT{nb % 2}"
            pbuf = tt % 2
            QKV, kqT, gam, gl, nbk, A0, qkTp = QKVs[pbuf], kqTs[pbuf], gams[pbuf], gls[pbuf], nbks[pbuf], A0s[pbuf], qkTps[pbuf]
            tcols = slice(tt * 128, (tt + 1) * 128)
            for grp in range(3):
                for cc in range(4):
                    ch = grp * 4 + cc
                    P.op("pe", lambda cc=cc, ch=ch, q=q: nc.tensor.transpose(B[1][:, cc * 128:(cc + 1) * 128], csl[:, ch, q * 128:(q + 1) * 128], ident[:]),
                         reads=[f"csl{ch}", "ident"], writes=[bk(1)])
                if grp % 2 == 0:
                    P.op("act", lambda grp=grp: nc.scalar.copy(QKV[:, grp * 512:(grp + 1) * 512], B[1][:, :]), reads=[bk(1)], writes=[f"QKV{pbuf}_{grp}"])
                else:
                    P.op("dve", lambda grp=grp: nc.vector.tensor_copy(QKV[:, grp * 512:(grp + 1) * 512], B[1][:, :]), reads=[bk(1)], writes=[f"QKV{pbuf}_{grp}"])
            yield
            P.op("pool", lambda: nc.gpsimd.tensor_tensor(out=sqt[:], in0=QKV[:, 0:1024], in1=QKV[:, 0:1024], op=ALU.mult), reads=[f"QKV{pbuf}_0", f"QKV{pbuf}_1"], writes=["sqt"])
            P.op("dve", lambda: nc.vector.tensor_reduce(out=ss[:], in_=sqt[:].rearrange("p (h d) -> p h d", d=64), axis=AX.X, op=ALU.add), reads=["sqt"], writes=["ss"])
            P.op("act", lambda: nc.scalar.activation(out=ss[:], in_=ss[:], func=AF.Sqrt, bias=RMS_EPS, scale=1.0), reads=["ss"], writes=["ss"])
            P.op("dve", lambda: nc.vector.reciprocal(out=ss[:], in_=ss[:]), reads=["ss"], writes=["ss"])
            P.op("dve", lambda: nc.vector.tensor_scalar(out=ss[:, 0:8], in0=ss[:, 0:8], scalar1=0.125, scalar2=None, op0=ALU.mult), reads=["ss"], writes=["ss"])
            P.op("dve", lambda: nc.vector.tensor_tensor(out=QKV[:, 0:1024].rearrange("p (h d) -> p h d", d=64), in0=QKV[:, 0:1024].rearrange("p (h d) -> p h d", d=64),
                                                        in1=bc_h(ss[:, 0:16], 16, 64), op=ALU.mult), reads=[f"QKV{pbuf}_0", f"QKV{pbuf}_1", "ss"], writes=[f"QKV{pbuf}_0", f"QKV{pbuf}_1"])
            for (src0, slot) in ((512, 0), (0, 1)):
                for g4 in range(2):
                    for hl in range(4):
                        h = g4 * 4 + hl
                        P.op("pe", lambda hl=hl, h=h, src0=src0: nc.tensor.transpose(B[1][0:64, hl * 128:(hl + 1) * 128], QKV[:, src0 + h * 64: src0 + (h + 1) * 64], ident[:]),
                             reads=[f"QKV{pbuf}_0", f"QKV{pbuf}_1", "ident"], writes=[bk(1)])
                    if g4 == 0:
                        P.op("act", lambda slot=slot, g4=g4: nc.scalar.copy(kqT[:, g4 * 4:(g4 + 1) * 4, slot, :], B[1][0:64, :].rearrange("p (a b) -> p a b", a=4)), reads=[bk(1)], writes=[f"kqT{pbuf}"])
                    else:
                        P.op("dve", lambda slot=slot, g4=g4: nc.vector.tensor_copy(kqT[:, g4 * 4:(g4 + 1) * 4, slot, :], B[1][0:64, :].rearrange("p (a b) -> p a b", a=4)), reads=[bk(1)], writes=[f"kqT{pbuf}"])
            yield
            for kc in range(8):
                P.op("pe", lambda kc=kc: nc.tensor.matmul(B[2][:, 0:16], lhsT=xT[:, kc, q * 128:(q + 1) * 128], rhs=wab[:, kc, :], start=(kc == 0), stop=(kc == 7)),
                     reads=[xk, f"wab{kc}"], writes=[bk(2)])
            P.op("dve", lambda: nc.vector.tensor_copy(gab[:], B[2][:, 0:16]), reads=[bk(2)], writes=["gab"])
            P.op("act", lambda: nc.scalar.activation(out=beta[:], in_=gab[:, 8:16], func=AF.Exp, scale=-1.0), reads=["gab"], writes=["beta"])
            P.op("dve", lambda: nc.vector.tensor_scalar(out=beta[:], in0=beta[:], scalar1=1.0, scalar2=None, op0=ALU.add), reads=["beta"], writes=["beta"])
            P.op("dve", lambda: nc.vector.reciprocal(out=beta[:], in_=beta[:]), reads=["beta"], writes=["beta"])
            P.op("dve", lambda: nc.vector.tensor_scalar(out=nbeta[:], in0=beta[:], scalar1=-1.0, scalar2=None, op0=ALU.mult), reads=["beta"], writes=["nbeta"])
            P.op("dve", lambda: nc.vector.tensor_tensor(out=g[:], in0=gab[:, 0:8], in1=dtb[:], op=ALU.add), reads=["gab", "dtb"], writes=["g"])
            P.op("act", lambda: nc.scalar.activation(out=g[:], in_=g[:], func=AF.Exp), reads=["g"], writes=["g"])
            P.op("act", lambda: nc.scalar.activation(out=g[:], in_=g[:], func=AF.Ln, bias=1.0, scale=1.0), reads=["g"], writes=["g"])
            P.op("dve", lambda: nc.vector.tensor_tensor(out=g[:], in0=g[:], in1=nea[:], op=ALU.mult), reads=["g", "nea"], writes=["g"])
            P.op("pe", lambda: nc.tensor.matmul(B[2][:, 16:24], lhsT=triinc[:], rhs=g[:], start=True, stop=True), reads=["triinc", "g"], writes=[bk(2)])
            P.op("pe", lambda: nc.tensor.matmul(B[2][:, 24:32], lhsT=onesf[:], rhs=g[:], start=True, stop=True), reads=["ones", "g"], writes=[bk(2)])
            P.op("dve", lambda: nc.vector.tensor_copy(gc[:], B[2][:, 16:24]), reads=[bk(2)], writes=["gc"])
            P.op("dve", lambda: nc.vector.tensor_tensor(out=kap[:], in0=B[2][:, 24:32], in1=gc[:], op=ALU.subtract), reads=[bk(2), "gc"], writes=["kap"])
            P.op("act", lambda: nc.scalar.activation(out=gl[:], in_=B[2][:, 24:32], func=AF.Exp), reads=[bk(2)], writes=[f"gl{pbuf}"])
            P.op("act", lambda: nc.scalar.activation(out=kap[:], in_=kap[:], func=AF.Exp), reads=["kap"], writes=["kap"])
            P.op("act", lambda: nc.scalar.activation(out=gam[:], in_=gc[:], func=AF.Exp), reads=["gc"], writes=[f"gam{pbuf}"])
            P.op("dve", lambda: nc.vector.tensor_tensor(out=nbk[:], in0=nbeta[:], in1=kap[:], op=ALU.mult), reads=["nbeta", "kap"], writes=[f"nbk{pbuf}"])
            P.op("pool", lambda: nc.gpsimd.tensor_tensor(out=G1[:], in0=bc_m(triinc[:], 8, 128), in1=bc_h(g[:], 8, 128), op=ALU.mult), reads=["triinc", "g"], writes=["G1"])
            for half in range(2):
                P.op("pe", lambda half=half: nc.tensor.matmul(B[3 + half][:, :], lhsT=onesf[:], rhs=G1[:, half * 4:(half + 1) * 4, :].rearrange("p a b -> p (a b)"),
                                                             start=True, stop=True), reads=["ones", "G1"], writes=[bk(3 + half)])
                P.op("dve", lambda half=half: nc.vector.tensor_tensor(out=dm[:, half * 4:(half + 1) * 4, :], in0=B[3 + half][:, :].rearrange("p (a b) -> p a b", a=4),
                                                                     in1=bc_h(gc[:, half * 4:(half + 1) * 4], 4, 128), op=ALU.subtract),
                     reads=[bk(3 + half), "gc"], writes=["dm"])
            P.op("pool", lambda: nc.gpsimd.tensor_scalar(out=dm[:], in0=dm[:], scalar1=0.0, scalar2=None, op0=ALU.min), reads=["dm"], writes=["dm"])
            P.op("act", lambda: nc.scalar.activation(out=dm[:], in_=dm[:], func=AF.Exp), reads=["dm"], writes=["dm"])
            P.op("pool", lambda: nc.gpsimd.tensor_tensor(out=dTs[:], in0=dm[:], in1=bc_m(ups[:], 8, 128), op=ALU.mult), reads=["dm", "upstrict"], writes=["dTs"])
            P.op("pool", lambda: nc.gpsimd.tensor_tensor(out=dTi[:], in0=dm[:], in1=bc_m(upi[:], 8, 128), op=ALU.mult), reads=["dm", "upincl"], writes=["dTi"])
            yield
            for j2 in range(4):
                for hh in range(2):
                    h = j2 * 2 + hh
                    P.op("pe", lambda h=h, hh=hh: nc.tensor.matmul(B[7][:, hh * 256:(hh + 1) * 256], lhsT=kqT[:, h, 0, :],
                                                                 rhs=kqT[:, h, :, :].rearrange("p a b -> p (a b)"), start=True, stop=True),
                         reads=[f"kqT{pbuf}"], writes=[bk(7)])
                for hh in range(2):
                    h = j2 * 2 + hh
                    P.op("dve", lambda hh=hh, h=h: nc.vector.scalar_tensor_tensor(out=A0[:, h, :], in0=B[7][:, hh * 256: hh * 256 + 128], scalar=beta[:, h:h + 1],
                                                                                in1=dTs[:, h, :], op0=ALU.mult, op1=ALU.mult),
                         reads=[bk(7), "beta", "dTs"], writes=[f"A0{pbuf}"])
                    P.op("dve", lambda hh=hh, h=h: nc.vector.scalar_tensor_tensor(out=qkTp[:, h, :], in0=B[7][:, hh * 256 + 128: hh * 256 + 256], scalar=nbeta[:, h:h + 1],
                                                                                in1=dTi[:, h, :], op0=ALU.mult, op1=ALU.mult),
                         reads=[bk(7), "nbeta", "dTi"], writes=[f"qkTp{pbuf}"])
            yield

        def post(tt):
            nb, q = tt // R, tt % R
            xT, xk = xTb[nb % 2], f"xT{nb % 2}"
            pbuf = tt % 2
            QKV, kqT, gam, gl, nbk, A0, qkTp = QKVs[pbuf], kqTs[pbuf], gams[pbuf], gls[pbuf], nbks[pbuf], A0s[pbuf], qkTps[pbuf]
            tcols = slice(tt * 128, (tt + 1) * 128)
            NF = 4
            GB = ((5, 6, 7), (3, 4, 0))
            for g4 in range(2):
                hs = slice(g4 * 4, (g4 + 1) * 4)
                bB = GB[g4][1]
                for hl in range(4):
                    P.op("pe", lambda hl=hl, g4=g4, bB=bB: nc.tensor.matmul(B[bB][:, hl * 128:(hl + 1) * 128], lhsT=A0[:, g4 * 4 + hl, :], rhs=ident[:], start=True, stop=True),
                         reads=[f"A0{pbuf}", "ident"], writes=[bk(bB)])
                P.op("dve", lambda g4=g4, bB=bB: nc.vector.tensor_copy(Bmf[g4][0][:].rearrange("p a b -> p (a b)"), B[bB][:, :]), reads=[bk(bB)], writes=[f"Bmf{g4}0"])
                P.op("pool", lambda hs=hs, g4=g4: nc.gpsimd.tensor_tensor(out=Yf[g4][:], in0=bc_m(ident[:], 4, 128), in1=A0[:, hs, :], op=ALU.subtract),
                     reads=["ident", f"A0{pbuf}"], writes=[f"Yf{g4}"])
            for k in range(1, 7):
                pv, cu = (k - 1) % 2, k % 2
                f32lvl = k <= NF
                for g4 in range(2):
                    hs = slice(g4 * 4, (g4 + 1) * 4)
                    bA, bB, bY = GB[g4]
                    if k == 1:
                        Aprev = lambda hl, g4=g4: A0[:, g4 * 4 + hl, :]
                        akey = f"A0{pbuf}"
                    elif f32lvl:
                        Aprev = lambda hl, g4=g4, pv=pv: Amf[g4][pv][:, hl, :]
                        akey = f"Amf{g4}{pv}"
                    else:
                        Aprev = lambda hl, g4=g4, pv=pv: Am[g4][pv][:, hl, :]
                        akey = f"Am{g4}{pv}"
                    Bprev = (lambda hl, g4=g4, pv=pv: Bmf[g4][pv][:, hl, :]) if f32lvl else (lambda hl, g4=g4, pv=pv: Bm[g4][pv][:, hl, :])
                    bkey = f"Bmf{g4}{pv}" if f32lvl else f"Bm{g4}{pv}"
                    if k <= 5:
                        for hl in range(4):
                            P.op("pe", lambda hl=hl, Aprev=Aprev, Bprev=Bprev, bA=bA: nc.tensor.matmul(B[bA][:, hl * 128:(hl + 1) * 128], lhsT=Bprev(hl), rhs=Aprev(hl), start=True, stop=True),
                                 reads=[bkey, akey], writes=[bk(bA)])
                    if k <= NF:
                        P.op("act", lambda cu=cu, g4=g4, bA=bA: nc.scalar.copy(Amf[g4][cu][:].rearrange("p a b -> p (a b)"), B[bA][:, :]), reads=[bk(bA)], writes=[f"Amf{g4}{cu}"])
                    if NF <= k <= 5:
                        P.op("dve", lambda cu=cu, g4=g4, bA=bA: nc.vector.tensor_copy(Am[g4][cu][:].rearrange("p a b -> p (a b)"), B[bA][:, :]), reads=[bk(bA)], writes=[f"Am{g4}{cu}"])
                    for hl in range(4):
                        P.op("pe", lambda hl=hl, Aprev=Aprev, Bprev=Bprev, bB=bB: nc.tensor.matmul(B[bB][:, hl * 128:(hl + 1) * 128], lhsT=Aprev(hl), rhs=Bprev(hl), start=True, stop=True),
                             reads=[bkey, akey], writes=[bk(bB)])
                    if k <= NF:
                        P.op("dve", lambda cu=cu, g4=g4, bB=bB: nc.vector.tensor_copy(Bmf[g4][cu][:].rearrange("p a b -> p (a b)"), B[bB][:, :]), reads=[bk(bB)], writes=[f"Bmf{g4}{cu}"])
                    if k >= NF:
                        P.op("act", lambda cu=cu, g4=g4, bB=bB: nc.scalar.copy(Bm[g4][cu][:].rearrange("p a b -> p (a b)"), B[bB][:, :]), reads=[bk(bB)], writes=[f"Bm{g4}{cu}"])
                    for hl in range(4):
                        if f32lvl:
                            P.op("pe", lambda hl=hl, cu=cu, g4=g4, bY=bY: nc.tensor.matmul(B[bY][:, hl * 128:(hl + 1) * 128], lhsT=Bmf[g4][cu][:, hl, :], rhs=Yf[g4][:, hl, :], start=True, stop=True),
                                 reads=[f"Bmf{g4}{cu}", f"Yf{g4}"], writes=[bk(bY)])
                        else:
                            P.op("pe", lambda hl=hl, cu=cu, g4=g4, bY=bY: nc.tensor.matmul(B[bY][:, hl * 128:(hl + 1) * 128], lhsT=Bm[g4][cu][:, hl, :], rhs=Yb[g4][:, hl, :], start=True, stop=True),
                                 reads=[f"Bm{g4}{cu}", f"Yb{g4}"], writes=[bk(bY)])
                    if k < 6:
                        P.op("dve", lambda g4=g4, bY=bY: nc.vector.tensor_tensor(out=Yf[g4][:].rearrange("p a b -> p (a b)"), in0=B[bY][:, :], in1=Yf[g4][:].rearrange("p a b -> p (a b)"), op=ALU.add),
                             reads=[bk(bY), f"Yf{g4}"], writes=[f"Yf{g4}"])
                        if k >= NF:
                            P.op("act", lambda g4=g4: nc.scalar.copy(Yb[g4][:].rearrange("p a b -> p (a b)"), Yf[g4][:].rearrange("p a b -> p (a b)")), reads=[f"Yf{g4}"], writes=[f"Yb{g4}"])
                    else:
                        P.op("dve", lambda g4=g4, bY=bY, hs=hs: nc.vector.tensor_tensor(out=Xp[:, hs, :].rearrange("p a b -> p (a b)"), in0=B[bY][:, :], in1=Yf[g4][:].rearrange("p a b -> p (a b)"), op=ALU.add),
                             reads=[bk(bY), f"Yf{g4}"], writes=["Xp"])
                yield
            for h in range(8):
                j2, pb = h // 2, (h % 2) * 64
                bnk = 5 + h // 4
                col = (h % 4) * 128
                for slot in range(2):
                    P.op("pe", lambda bnk=bnk, col=col, slot=slot, h=h: nc.tensor.matmul(B[bnk][:, col + slot * 64: col + slot * 64 + 64], lhsT=kqT[:, h, slot, :],
                                                                                       rhs=Sb[:, h, :], start=True, stop=True),
                         reads=[f"kqT{pbuf}", "Sb"], writes=[bk(bnk)])
            for h in range(8):
                bnk, col = 5 + h // 4, (h % 4) * 128
                P.op("dve", lambda h=h, bnk=bnk, col=col: nc.vector.scalar_tensor_tensor(out=rp[:, h, :], in0=B[bnk][:, col:col + 64], scalar=gam[:, h:h + 1],
                                                                                       in1=QKV[:, 1024 + h * 64: 1024 + (h + 1) * 64], op0=ALU.mult, op1=ALU.subtract),
                     reads=[bk(bnk), f"gam{pbuf}", f"QKV{pbuf}_2"], writes=["rp"])
                P.op("act", lambda h=h, bnk=bnk, col=col: nc.scalar.activation(out=o1[:, h, :], in_=B[bnk][:, col + 64:col + 128], func=AF.Identity, scale=gam[:, h:h + 1]),
                     reads=[bk(bnk), f"gam{pbuf}"], writes=["o1"])
            for h in range(8):
                P.op("pe", lambda h=h: nc.tensor.matmul(B[4][:, h * 64:(h + 1) * 64], lhsT=Xp[:, h, :], rhs=rp[:, h, :], start=True, stop=True),
                     reads=["Xp", "rp"], writes=[bk(4)])
            P.op("act", lambda: nc.scalar.copy(vt[:], B[4][:, :]), reads=[bk(4)], writes=["vt"])
            P.op("pool", lambda: nc.gpsimd.tensor_tensor(out=kd[:].rearrange("p (h d) -> p h d", d=64), in0=QKV[:, 512:1024].rearrange("p (h d) -> p h d", d=64),
                                                        in1=bc_h(nbk[:], 8, 64), op=ALU.mult), reads=[f"QKV{pbuf}_1", f"nbk{pbuf}"], writes=["kd"])
            for h in range(8):
                P.op("pe", lambda h=h: nc.tensor.matmul(B[3][:, h * 64:(h + 1) * 64], lhsT=qkTp[:, h, :], rhs=vt[:, h * 64:(h + 1) * 64], start=True, stop=True),
                     reads=[f"qkTp{pbuf}", "vt"], writes=[bk(3)])
            for h in range(8):
                j2 = h // 2
                P.op("pe", lambda h=h: nc.tensor.matmul(B[2][0:64, h * 64:(h + 1) * 64], lhsT=kd[:, h * 64:(h + 1) * 64], rhs=vt[:, h * 64:(h + 1) * 64], start=True, stop=True),
                     reads=["kd", "vt"], writes=[bk(2)])
            P.op("dve", lambda: nc.vector.tensor_tensor(out=of[:].rearrange("p a b -> p (a b)"), in0=B[3][:, :], in1=o1[:].rearrange("p a b -> p (a b)"), op=ALU.add),
                 reads=[bk(3), "o1"], writes=["of"])
            P.op("pool", lambda: nc.gpsimd.tensor_tensor(out=St[:], in0=St[:], in1=gl[0:64, :].unsqueeze(2).to_broadcast([64, 8, 64]), op=ALU.mult), reads=["St", f"gl{pbuf}"], writes=["St"])
            P.op("dve", lambda: nc.vector.tensor_tensor(out=St[:].rearrange("p a b -> p (a b)"), in0=St[:].rearrange("p a b -> p (a b)"), in1=B[2][0:64, :], op=ALU.add),
                 reads=["St", bk(2)], writes=["St"])
            P.op("act", lambda: nc.scalar.copy(Sb[:].rearrange("p a b -> p (a b)"), St[:].rearrange("p a b -> p (a b)")), reads=["St"], writes=["Sb"])
            yield
            for kc in range(8):
                P.op("pe", lambda kc=kc: nc.tensor.matmul(B[0][:, :], lhsT=xT[:, kc, q * 128:(q + 1) * 128], rhs=wz[:, kc, :], start=(kc == 0), stop=(kc == 7)),
                     reads=[xk, f"wz{kc}"], writes=[bk(0)])
            P.op("act", lambda: nc.scalar.activation(out=zs[:], in_=B[0][:, :], func=AF.Silu), reads=[bk(0)], writes=["zs"])
            P.op("pool", lambda: nc.gpsimd.tensor_tensor(out=sqt2[:, 0:512], in0=of[:].rearrange("p a b -> p (a b)"), in1=of[:].rearrange("p a b -> p (a b)"), op=ALU.mult),
                 reads=["of"], writes=["sqt2"])
            P.op("dve", lambda: nc.vector.tensor_reduce(out=ss2[:, 0:8], in_=sqt2[:, 0:512].rearrange("p (h d) -> p h d", d=64), axis=AX.X, op=ALU.add), reads=["sqt2"], writes=["ss2"])
            P.op("act", lambda: nc.scalar.activation(out=ss2[:, 0:8], in_=ss2[:, 0:8], func=AF.Sqrt, bias=RMS_EPS, scale=1.0 / 64), reads=["ss2"], writes=["ss2"])
            P.op("dve", lambda: nc.vector.reciprocal(out=ss2[:, 0:8], in_=ss2[:, 0:8]), reads=["ss2"], writes=["ss2"])
            P.op("dve", lambda: nc.vector.tensor_tensor(out=og[:].rearrange("p (h d) -> p h d", d=64), in0=of[:], in1=bc_h(ss2[:, 0:8], 8, 64), op=ALU.mult),
                 reads=["of", "ss2"], writes=["og"])
            P.op("pool", lambda: nc.gpsimd.tensor_tensor(out=og[:].rearrange("p (h d) -> p h d", d=64), in0=og[:].rearrange("p (h d) -> p h d", d=64), in1=bc_m(nw[:], 8, 64), op=ALU.mult),
                 reads=["og", "nw"], writes=["og"])
            P.op("pool", lambda: nc.gpsimd.tensor_tensor(out=og[:], in0=og[:], in1=zs[:], op=ALU.mult), reads=["og", "zs"], writes=["og"])
            for j2 in range(4):
                P.op("pe", lambda j2=j2: nc.tensor.transpose(B[1][:, j2 * 128:(j2 + 1) * 128], og[:, j2 * 128:(j2 + 1) * 128], ident[:]), reads=["og", "ident"], writes=[bk(1)])
            P.op("act", lambda: nc.scalar.copy(oTs[:].rearrange("p a b -> p (a b)"), B[1][:, :]), reads=[bk(1)], writes=["oTs"])
            P.dma("sp", c.mixT[0:512, tcols].rearrange("(a p) t -> p a t", p=128), oTs[:], reads=["oTs"], writes=["mixT"], chan="d_oTs")
            yield

        NTT = S // 128
        conv(0)
        for _ in pre(0):
            pass
        for tt in range(NTT):
            gens = [post(tt)]
            if tt + 1 < NTT:
                if (tt + 1) % R == 0:
                    conv((tt + 1) // R)
                gens.append(pre(tt + 1))
            while gens:
                for g_ in list(gens):
                    try:
                        next(g_)
                    except StopIteration:
                        gens.remove(g_)

    P.barrier()


def phase_outproj(c, l, src, dst):
    nc, P, S = c.nc, c.P, c.S
    with ExitStack() as pes:
        sb, ps = _tiles(nc, pes)
        wo = sb("wo", [128, 8, D], BF16)
        gb = sb("gb", [128, 2, D], F32)
        xs = [sb(f"x{i}", [128, D], F32) for i in range(2)]
        ms = [sb(f"m{i}", [128, 8, 128], BF16) for i in range(2)]
        ys = [sb(f"y{i}", [128, D], F32) for i in range(2)]
        os_ = [sb(f"o{i}", [128, D], F32) for i in range(2)]
        sts = [sb(f"st{i}", [128, 2, 6], F32) for i in range(2)]
        mvs = [sb(f"mv{i}", [128, 2], F32) for i in range(2)]
        rstds = [sb(f"rstd{i}", [128, 1], F32) for i in range(2)]
        po = [ps(f"po{i}", [128, 512], F32) for i in range(4)]
        load_w(c, wo, c.mix_w_o[l], "wo", 8, 0, D)
        P.dma("sp", gb[:, 0, :], c.ln_g[l, 1, :].partition_broadcast(128), writes=["gb0"])
        P.dma("sp", gb[:, 1, :], c.ln_b[l, 1, :].partition_broadcast(128), writes=["gb1"])
        def load(tt):
            b = tt % 2
            tc_ = slice(tt * 128, (tt + 1) * 128)
            P.dma("sp", xs[b][:], src[tc_, :], writes=[f"x{b}"])
            P.dma("sp", ms[b][:], c.mixT[:, tc_].rearrange("(k p) t -> p k t", p=128), writes=[f"m{b}"])

        NTT = S // 128
        load(0)
        if NTT > 1:
            load(1)
        for tt in range(NTT):
            b = tt % 2
            tc_ = slice(tt * 128, (tt + 1) * 128)
            y, o = ys[b], os_[b]
            for dh in range(2):
                pb_ = 2 * b + dh
                for kc in range(8):
                    P.op("pe", lambda b=b, dh=dh, kc=kc, pb_=pb_: nc.tensor.matmul(po[pb_][:, :], lhsT=ms[b][:, kc, :], rhs=wo[:, kc, dh * 512:(dh + 1) * 512],
                                                                                   start=(kc == 0), stop=(kc == 7)), reads=[f"m{b}", f"wo{kc}"], writes=[f"po{pb_}"])
            P.op("act", lambda b=b, y=y: nc.scalar.mul(y[:], xs[b][:], ALPHA), reads=[f"x{b}"], writes=[f"y{b}"])
            for dh in range(2):
                pb_ = 2 * b + dh
                P.op("dve", lambda dh=dh, y=y, pb_=pb_: nc.vector.tensor_tensor(out=y[:, dh * 512:(dh + 1) * 512], in0=po[pb_][:, :], in1=y[:, dh * 512:(dh + 1) * 512], op=ALU.add),
                     reads=[f"po{pb_}", f"y{b}"], writes=[f"y{b}"])
            layer_norm_tile(c, y, o, sts[b], mvs[b], rstds[b], gb, sfx=str(b))
            if tt + 2 < NTT:
                load(tt + 2)
            P.dma("sp", dst[tc_, :], o[:], reads=[f"o{b}"], writes=["dst"], chan=f"d_o{b}")
    P.barrier()


def phase_ple(c, l, src, dst):
    nc, P, S = c.nc, c.P, c.S
    with ExitStack() as pes:
        sb, ps = _tiles(nc, pes)
        wg = sb("wg", [128, 8, D], BF16)
        wp = sb("wp", [128, 2, D], BF16)
        ident = sb("ident", [128, 128], F32)
        xs = [sb(f"x{i}", [128, D], F32) for i in range(2)]
        pls = [sb(f"pl{i}", [128, PLE], F32) for i in range(2)]
        hTs = [sb(f"hT{i}", [128, 10, 128], BF16) for i in range(2)]
        sgs = [sb(f"sg{i}", [128, D], F32) for i in range(2)]
        os_ = [sb(f"o{i}", [128, D], F32) for i in range(2)]
        pt = [ps(f"pt{i}", [128, 512], F32) for i in range(3)]
        pg = [ps(f"pg{i}", [128, 512], F32) for i in range(2)]
        pp = [ps(f"pp{i}", [128, 512], F32) for i in range(2)]
        load_w(c, wg, c.ple_w_gate[l], "wg", 8, 0, D)
        load_w(c, wp, c.ple_w_proj[l], "wp", 2, 0, D)
        P.dma("sp", ident[:], c.consts[:, c.CO["ident"]:c.CO["ident"] + 128], writes=["ident"])
        def load(tt):
            b = tt % 2
            tc_ = slice(tt * 128, (tt + 1) * 128)
            P.dma("sp", xs[b][:], src[tc_, :], writes=[f"x{b}"])
            P.dma("sp", pls[b][:], c.p[l, tc_, :], writes=[f"pl{b}"])

        NTT = S // 128
        load(0)
        if NTT > 1:
            load(1)
        for tt in range(NTT):
            b = tt % 2
            tc_ = slice(tt * 128, (tt + 1) * 128)
            hT, sg, o = hTs[b], sgs[b], os_[b]
            for grp in range(3):
                n = 4 if grp < 2 else 2
                for q in range(n):
                    kc = grp * 4 + q
                    if grp < 2:
                        P.op("pe", lambda grp=grp, q=q, kc=kc, b=b: nc.tensor.transpose(pt[grp][:, q * 128:(q + 1) * 128], xs[b][:, kc * 128:(kc + 1) * 128], ident[:]),
                             reads=[f"x{b}", "ident"], writes=[f"pt{grp}"])
                    else:
                        P.op("pe", lambda grp=grp, q=q, b=b: nc.tensor.transpose(pt[grp][:, q * 128:(q + 1) * 128], pls[b][:, q * 128:(q + 1) * 128], ident[:]),
                             reads=[f"pl{b}", "ident"], writes=[f"pt{grp}"])
                eng = ("act", "dve", "act")[grp]
                if eng == "act":
                    P.op("act", lambda grp=grp, n=n: nc.scalar.copy(hT[:, grp * 4:grp * 4 + n, :].rearrange("p a b -> p (a b)"), pt[grp][:, 0:n * 128]),
                         reads=[f"pt{grp}"], writes=[f"hT{b}_{grp}"])
                else:
                    P.op("dve", lambda grp=grp, n=n: nc.vector.tensor_copy(hT[:, grp * 4:grp * 4 + n, :].rearrange("p a b -> p (a b)"), pt[grp][:, 0:n * 128]),
                         reads=[f"pt{grp}"], writes=[f"hT{b}_{grp}"])
            for dh in range(2):
                for kc in range(8):
                    P.op("pe", lambda dh=dh, kc=kc: nc.tensor.matmul(pg[dh][:, :], lhsT=hT[:, kc, :], rhs=wg[:, kc, dh * 512:(dh + 1) * 512], start=(kc == 0), stop=(kc == 7)),
                         reads=[f"hT{b}_0", f"hT{b}_1", f"wg{kc}"], writes=[f"pg{dh}"])
                for kc in range(2):
                    P.op("pe", lambda dh=dh, kc=kc: nc.tensor.matmul(pp[dh][:, :], lhsT=hT[:, 8 + kc, :], rhs=wp[:, kc, dh * 512:(dh + 1) * 512], start=(kc == 0), stop=(kc == 1)),
                         reads=[f"hT{b}_2", f"wp{kc}"], writes=[f"pp{dh}"])
                P.op("act", lambda dh=dh: nc.scalar.activation(out=sg[:, dh * 512:(dh + 1) * 512], in_=pg[dh][:, :], func=AF.Sigmoid), reads=[f"pg{dh}"], writes=[f"sg{b}_{dh}"])
                P.op("dve", lambda dh=dh: nc.vector.tensor_tensor(out=sg[:, dh * 512:(dh + 1) * 512], in0=pp[dh][:, :], in1=sg[:, dh * 512:(dh + 1) * 512], op=ALU.mult),
                     reads=[f"pp{dh}", f"sg{b}_{dh}"], writes=[f"sg{b}_{dh}"])
            P.op("pool", lambda b=b: nc.gpsimd.tensor_tensor(out=o[:], in0=sg[:], in1=xs[b][:], op=ALU.add), reads=[f"sg{b}_0", f"sg{b}_1", f"x{b}"], writes=[f"o{b}"])
            if tt + 2 < NTT:
                load(tt + 2)
            P.dma("sp", dst[tc_, :], o[:], reads=[f"o{b}"], writes=["dst"], chan=f"d_o{b}")
    P.barrier()


def all_phases(L):
    ph = [phase_rope]
    for l in range(L):
        src = (lambda c: c.x) if l == 0 else (lambda c: c.hb[1])
        ph.append(lambda c, l=l, src=src: phase_ffn(c, l, 0, src(c), c.hb[0], c.h1T))
        ph.append(lambda c, l=l: phase_gdn(c, l))
        ph.append(lambda c, l=l: phase_sb(c, l))
        ph.append(lambda c, l=l: phase_mla(c, l))
        ph.append(lambda c, l=l: phase_outproj(c, l, c.hb[0], c.hb[1]))
        ph.append(lambda c, l=l: phase_ffn(c, l, 1, c.hb[1], c.hb[0]))
        last = l == L - 1
        ph.append(lambda c, l=l, last=last: phase_ple(c, l, c.hb[0], c.out if last else c.hb[1]))
    return ph


_CACHE = {}


def kernel(**inputs):
    x = np.asarray(inputs["x"], np.float32)
    Bsz, S, _ = x.shape
    L = inputs["ffa_w_in"].shape[0]
    key = (S, L)
    if key not in _CACHE:
        _CACHE[key] = build(S, L, all_phases(L))
    nc, consts = _CACHE[key]
    w = prep_weights({k: np.asarray(v) for k, v in inputs.items()}, L)
    p = np.asarray(inputs["p"], np.float32)
    pos = np.asarray(inputs["positions"], np.int32)
    in_maps = []
    for b in range(Bsz):
        in_maps.append({"x": np.ascontiguousarray(x[b]), "p": np.ascontiguousarray(p[:, b]), "pos": np.ascontiguousarray(pos[b:b + 1]),
                        "consts": consts, **w})
    res = run_bass_kernel_spmd(nc, in_maps, core_ids=list(range(Bsz)))
    return np.stack([np.asarray(r["out"], np.float32) for r in res.results], 0)
```
